# Optimizing a Trainium2 kernel written in Bass

```python
import jax
import jax.numpy as jnp
from jax import lax
import numpy as np

D_MODEL = 1024
BATCH = 4
SEQ = 4096
DEPTH = 4

GRID_W = 64
CTX_LEN = 256
N_MIXERS = 4
MIX_RET, MIX_NAT, MIX_POOL, MIX_SWA = 0, 1, 2, 3
N_RET = len(range(MIX_RET, DEPTH, N_MIXERS))
N_NAT = len(range(MIX_NAT, DEPTH, N_MIXERS))
N_POOL = len(range(MIX_POOL, DEPTH, N_MIXERS))
N_SWA = len(range(MIX_SWA, DEPTH, N_MIXERS))
EPS = 1e-6
NEG_INF = -1e30
ROPE_BASE = 10000.0
FFN_HIDDEN = 2816
RET_HEADS = 4
RET_QK_DIM = D_MODEL // RET_HEADS
RET_V_DIM = 2 * RET_QK_DIM
RET_CHUNK = 128
NAT_HEADS = 16
NAT_HEAD_DIM = D_MODEL // NAT_HEADS
NAT_KH = 8
NAT_KW = 16
POOL_WINDOWS = (2, 4, 8, 16)
POOL_GROUPS = len(POOL_WINDOWS)
POOL_GROUP_DIM = D_MODEL // POOL_GROUPS
SWA_Q_HEADS = 16
SWA_KV_HEADS = 4
SWA_HEAD_DIM = D_MODEL // SWA_Q_HEADS
SWA_WINDOW = 128
SWA_BLOCK = 128

kernel_name = 'hybrid_interleaved_diffusion_trunk'


def rms_norm(x, g):
    xf = x.astype(jnp.float32)
    y = xf * lax.rsqrt(jnp.mean(xf * xf, axis=-1, keepdims=True) + EPS)
    return (y * g.astype(jnp.float32)).astype(x.dtype)


def ada_in(h, g, m, j):
    xn = rms_norm(h, g)
    return (xn * (1.0 + m[:, j, 1][:, None]) + m[:, j, 0][:, None]).astype(h.dtype)


def gated_residual(h, m, j, y, w):
    return h + (w * m[:, j, 2][:, None] * y).astype(h.dtype)


def swiglu(x, w_in, w_out):
    a, b = jnp.split(x @ w_in, 2, axis=-1)
    return (jax.nn.silu(a) * b) @ w_out


def rope_angles(positions, dim):
    seg = dim // len(positions)
    inv = ROPE_BASE ** (-jnp.arange(0, seg, 2, dtype=jnp.float32) / seg)
    return jnp.concatenate([jnp.tile(p.astype(jnp.float32)[:, None] * inv, (1, 2)) for p in positions], axis=-1)


def apply_rope(x, ang, n_axes):
    xf = x.astype(jnp.float32)
    segs = jnp.split(xf, 2 * n_axes, axis=-1)
    rot = jnp.concatenate([s for a in range(n_axes) for s in (-segs[2 * a + 1], segs[2 * a])], axis=-1)
    return (xf * jnp.cos(ang)[None, :, None] + rot * jnp.sin(ang)[None, :, None]).astype(x.dtype)


def retention_scan(q, k, v, log_gamma, s0, include_diag):
    B, T, H, _ = q.shape
    dv = v.shape[-1]
    C = RET_CHUNK
    N = T // C
    idx = jnp.arange(C, dtype=jnp.float32)
    diff = idx[:, None] - idx[None, :]
    keep = (diff >= 0) if include_diag else (diff > 0)
    intra = jnp.where(keep[None], jnp.exp(jnp.maximum(diff, 0.0)[None] * log_gamma[:, None, None]), 0.0)
    q_dec = jnp.exp((idx + 1.0)[None] * log_gamma[:, None])
    k_dec = jnp.exp((C - 1.0 - idx)[None] * log_gamma[:, None])
    c_dec = jnp.exp(C * log_gamma)[:, None, None]

    def chunks(a):
        return a.reshape(B, N, C, H, a.shape[-1]).transpose(1, 0, 3, 2, 4)

    def step(s, qkv):
        qn, kn, vn = qkv
        att = jnp.einsum('bhid,bhjd->bhij', qn, kn) * intra
        o = jnp.einsum('bhij,bhjv->bhiv', att, vn) + jnp.einsum('bhid,bhdv->bhiv', qn * q_dec[..., None], s)
        s = s * c_dec + jnp.einsum('bhjd,bhjv->bhdv', kn * k_dec[..., None], vn)
        return s, o

    s_fin, o = lax.scan(step, s0, (chunks(q), chunks(k), chunks(v)))
    return o.transpose(1, 0, 3, 2, 4).reshape(B, T, H, dv), s_fin


def retention_bidir(q, k, v, lg_f, lg_b, s0_f, s0_b):
    o_f, s_f = retention_scan(q, k, v, lg_f, s0_f, True)
    rev = lambda a: jnp.flip(a, axis=1)
    o_b, s_b = retention_scan(rev(q), rev(k), rev(v), lg_b, s0_b, False)
    return o_f + rev(o_b), s_f, s_b


def retention_mixer(xl, xc, w_in, w_out, gn_g, decay_f, decay_b, ctx_out):
    H, DK, DV = RET_HEADS, RET_QK_DIM, RET_V_DIM
    lg_f = jax.nn.log_sigmoid(decay_f.astype(jnp.float32))
    lg_b = jax.nn.log_sigmoid(decay_b.astype(jnp.float32))

    def project(x, ang):
        B, T, _ = x.shape
        q, k, v, g = jnp.split(x @ w_in, [H * DK, 2 * H * DK, 2 * H * DK + H * DV], axis=-1)
        q = q.reshape(B, T, H, DK)
        k = k.reshape(B, T, H, DK) * DK ** -0.5
        v = v.reshape(B, T, H, DV)
        if ang is not None:
            q = apply_rope(q, ang, 1)
            k = apply_rope(k, ang, 1)
        return q.astype(jnp.float32), k.astype(jnp.float32), v.astype(jnp.float32), g

    def read_out(o, g):
        mu = jnp.mean(o, axis=-1, keepdims=True)
        var = jnp.mean(jnp.square(o - mu), axis=-1, keepdims=True)
        on = (o - mu) * lax.rsqrt(var + EPS) * gn_g.astype(jnp.float32).reshape(H, DV)
        B, T = o.shape[:2]
        return (jax.nn.silu(g) * on.reshape(B, T, H * DV).astype(g.dtype)) @ w_out

    qc, kc, vc, gc = project(xc, None)
    Bc, L = xc.shape[:2]
    if ctx_out:
        zeros = jnp.zeros((Bc, H, DK, DV), jnp.float32)
        oc, s_f, s_b = retention_bidir(qc, kc, vc, lg_f, lg_b, zeros, zeros)
        yc = read_out(oc, gc)
    else:
        pos = jnp.arange(L, dtype=jnp.float32)[:, None]
        s_f = jnp.einsum('bthd,bthv->bhdv', kc * jnp.exp((L - 1.0 - pos) * lg_f)[None, :, :, None], vc)
        s_b = jnp.einsum('bthd,bthv->bhdv', kc * jnp.exp(pos * lg_b)[None, :, :, None], vc)
        yc = None
    T = xl.shape[1]
    ang = rope_angles([jnp.arange(T)], DK)
    ql, kl, vl, gl = project(xl, ang)
    ol, _, _ = retention_bidir(ql, kl, vl, lg_f, lg_b, s_f, s_b)
    return read_out(ol, gl), yc


def nat_mixer(xl, xc, w_qkv, w_o, rpb, ctx_out):
    H, DH = NAT_HEADS, NAT_HEAD_DIM

    def project(x):
        B, T, _ = x.shape
        q, k, v = jnp.split(x @ w_qkv, 3, axis=-1)
        return (q * DH ** -0.5).reshape(B, T, H, DH), k.reshape(B, T, H, DH), v.reshape(B, T, H, DH)

    qc, kc, vc = project(xc)
    ql, kl, vl = project(xl)
    B, T, _ = xl.shape
    rows = T // GRID_W
    kh = min(NAT_KH, rows)
    nk = kh * NAT_KW
    qg = ql.reshape(B, rows, GRID_W, H, DH)
    kg = kl.reshape(B, rows, GRID_W, H, DH)
    vg = vl.reshape(B, rows, GRID_W, H, DH)
    cols = jnp.arange(GRID_W)
    col_idx = jnp.clip(cols - NAT_KW // 2, 0, GRID_W - NAT_KW)[:, None] + jnp.arange(NAT_KW)
    col_bias_idx = col_idx - cols[:, None] + NAT_KW - 1

    def row_block(r):
        r0 = jnp.clip(r - kh // 2, 0, rows - kh)
        kw = lax.dynamic_slice_in_dim(kg, r0, kh, axis=1)[:, :, col_idx]
        vw = lax.dynamic_slice_in_dim(vg, r0, kh, axis=1)[:, :, col_idx]
        qr = lax.dynamic_index_in_dim(qg, r, axis=1, keepdims=False)
        row_bias_idx = r0 + jnp.arange(kh) - r + NAT_KH - 1
        bias = rpb[:, row_bias_idx][:, :, col_bias_idx].transpose(0, 2, 1, 3)
        s_nb = jnp.einsum('bchd,bacnhd->bhcan', qr, kw).astype(jnp.float32) + bias.astype(jnp.float32)
        s_cx = jnp.einsum('bchd,bjhd->bhcj', qr, kc).astype(jnp.float32)
        p = jax.nn.softmax(jnp.concatenate([s_nb.reshape(B, H, GRID_W, nk), s_cx], axis=-1), axis=-1).astype(vl.dtype)
        p_nb = p[..., :nk].reshape(B, H, GRID_W, kh, NAT_KW)
        return jnp.einsum('bhcan,bacnhd->bchd', p_nb, vw) + jnp.einsum('bhcj,bjhd->bchd', p[..., nk:], vc)

    o = lax.map(row_block, jnp.arange(rows))
    yl = o.transpose(1, 0, 2, 3, 4).reshape(B, T, H * DH) @ w_o
    yc = None
    if ctx_out:
        p = jax.nn.softmax(jnp.einsum('bihd,bjhd->bhij', qc, kc).astype(jnp.float32), axis=-1).astype(vc.dtype)
        oc = jnp.einsum('bhij,bjhd->bihd', p, vc)
        yc = oc.reshape(oc.shape[0], oc.shape[1], H * DH) @ w_o
    return yl, yc


def pool_mixer(x, w_grp, scale):
    B, T, D = x.shape
    G, Dg = POOL_GROUPS, POOL_GROUP_DIM
    xf = x.astype(jnp.float32).reshape(B, T, G, Dg)
    csum = jnp.concatenate([jnp.zeros((B, 1, G, Dg), jnp.float32), jnp.cumsum(xf, axis=1)], axis=1)
    half = jnp.asarray([w // 2 for w in POOL_WINDOWS])
    t = jnp.arange(T)[:, None]
    lo = jnp.clip(t - half, 0, T)
    hi = jnp.clip(t + half, 0, T)
    grp = jnp.arange(G)[None, :]
    mean = (csum[:, hi, grp] - csum[:, lo, grp]) / (hi - lo).astype(jnp.float32)[None, :, :, None]
    pooled = (mean - xf).astype(x.dtype)
    return jnp.einsum('btgc,gcd->btgd', pooled, w_grp).reshape(B, T, D) * scale


def swa_mixer(xl, xc, w_qkv, w_o, sink, ctx_out):
    HQ, HKV, DH = SWA_Q_HEADS, SWA_KV_HEADS, SWA_HEAD_DIM
    G = HQ // HKV

    def project(x, ang):
        B, T, _ = x.shape
        q, k, v = jnp.split(x @ w_qkv, [HQ * DH, (HQ + HKV) * DH], axis=-1)
        q = q.reshape(B, T, HQ, DH)
        k = k.reshape(B, T, HKV, DH)
        v = v.reshape(B, T, HKV, DH)
        if ang is not None:
            q = apply_rope(q, ang, 2)
            k = apply_rope(k, ang, 2)
        return (q * DH ** -0.5).reshape(B, T, HKV, G, DH), k, v

    B, T, _ = xl.shape
    t = jnp.arange(T)
    ang = rope_angles([t // GRID_W, t % GRID_W], DH)
    ql, kl, vl = project(xl, ang)
    qc, kc, vc = project(xc, None)
    sink_l = sink.astype(jnp.float32).reshape(HKV, G)
    nb = T // SWA_BLOCK
    nw = SWA_WINDOW // SWA_BLOCK
    nk = (2 * nw + 1) * SWA_BLOCK

    def band(a):
        ap = jnp.pad(a, ((0, 0), (SWA_WINDOW, SWA_WINDOW), (0, 0), (0, 0))).reshape(B, nb + 2 * nw, SWA_BLOCK, HKV, DH)
        return jnp.concatenate([ap[:, o:o + nb] for o in range(2 * nw + 1)], axis=2)

    kb, vb = band(kl), band(vl)
    qb = ql.reshape(B, nb, SWA_BLOCK, HKV, G, DH)
    qpos = (jnp.arange(nb)[:, None] * SWA_BLOCK + jnp.arange(SWA_BLOCK))[:, :, None]
    kpos = (jnp.arange(nb)[:, None] * SWA_BLOCK - SWA_WINDOW + jnp.arange(nk))[:, None, :]
    allowed = (jnp.abs(qpos - kpos) <= SWA_WINDOW) & (kpos >= 0) & (kpos < T)
    s_loc = jnp.where(allowed, jnp.einsum('bnikgd,bnjkd->bkgnij', qb, kb).astype(jnp.float32), NEG_INF)
    s_cx = jnp.einsum('bnikgd,bjkd->bkgnij', qb, kc).astype(jnp.float32)
    s_sk = jnp.broadcast_to(sink_l[None, :, :, None, None, None], s_loc.shape[:-1] + (1,))
    p = jax.nn.softmax(jnp.concatenate([s_loc, s_cx, s_sk], axis=-1), axis=-1)[..., :-1].astype(vl.dtype)
    o = jnp.einsum('bkgnij,bnjkd->bnikgd', p[..., :nk], vb) + jnp.einsum('bkgnij,bjkd->bnikgd', p[..., nk:], vc)
    yl = o.reshape(B, T, HQ * DH) @ w_o
    yc = None
    if ctx_out:
        s = jnp.einsum('bikgd,bjkd->bkgij', qc, kc).astype(jnp.float32)
        sk = jnp.broadcast_to(sink_l[None, :, :, None, None], s.shape[:-1] + (1,))
        pc = jax.nn.softmax(jnp.concatenate([s, sk], axis=-1), axis=-1)[..., :-1].astype(vc.dtype)
        oc = jnp.einsum('bkgij,bjkd->bikgd', pc, vc)
        yc = oc.reshape(oc.shape[0], oc.shape[1], HQ * DH) @ w_o
    return yl, yc


def setup_inputs(seed: int = 0) -> dict:
    key = jax.random.key(seed)
    ks = jax.random.split(key, 24)
    D, F = D_MODEL, FFN_HIDDEN
    nrm = lambda k, shape, s: jax.random.normal(k, shape, jnp.float32) * s
    ret_in = 2 * RET_HEADS * RET_QK_DIM + 2 * RET_HEADS * RET_V_DIM
    ret_v = RET_HEADS * RET_V_DIM
    decay_logit = jnp.log(2.0 ** (5.0 + jnp.arange(RET_HEADS, dtype=jnp.float32)) - 1.0)
    nat_w = NAT_HEADS * NAT_HEAD_DIM
    swa_in = (SWA_Q_HEADS + 2 * SWA_KV_HEADS) * SWA_HEAD_DIM
    swa_o = SWA_Q_HEADS * SWA_HEAD_DIM
    return {
        'x': nrm(ks[0], (BATCH, SEQ, D), 1.0),
        'c': nrm(ks[1], (BATCH, D), 1.0),
        'ctx': nrm(ks[2], (BATCH, CTX_LEN, D), 1.0),
        'c_ctx': nrm(ks[3], (D,), 1.0),
        'w_mod': nrm(ks[4], (DEPTH, D, 9 * D), 0.5 * D ** -0.5),
        'b_mod': nrm(ks[5], (DEPTH, 9 * D), 0.01),
        'norm_g': 1.0 + nrm(ks[6], (DEPTH, 3, D), 0.02),
        'ffn_w_in': nrm(ks[7], (DEPTH, 2, D, 2 * F), D ** -0.5),
        'ffn_w_out': nrm(ks[8], (DEPTH, 2, F, D), F ** -0.5),
        'ret_w_in': nrm(ks[9], (N_RET, D, ret_in), D ** -0.5),
        'ret_w_out': nrm(ks[10], (N_RET, ret_v, D), ret_v ** -0.5),
        'ret_gn_g': 1.0 + nrm(ks[11], (N_RET, ret_v), 0.02),
        'ret_decay_f': decay_logit + nrm(ks[12], (N_RET, RET_HEADS), 0.05),
        'ret_decay_b': decay_logit + nrm(ks[13], (N_RET, RET_HEADS), 0.05),
        'nat_w_qkv': nrm(ks[14], (N_NAT, D, 3 * nat_w), D ** -0.5),
        'nat_w_o': nrm(ks[15], (N_NAT, nat_w, D), nat_w ** -0.5),
        'nat_rpb': nrm(ks[16], (N_NAT, NAT_HEADS, 2 * NAT_KH - 1, 2 * NAT_KW - 1), 0.1),
        'pool_w': nrm(ks[17], (N_POOL, POOL_GROUPS, POOL_GROUP_DIM, POOL_GROUP_DIM), POOL_GROUP_DIM ** -0.5),
        'pool_scale': 1.0 + nrm(ks[18], (N_POOL, D), 0.02),
        'swa_w_qkv': nrm(ks[19], (N_SWA, D, swa_in), D ** -0.5),
        'swa_w_o': nrm(ks[20], (N_SWA, swa_o, D), swa_o ** -0.5),
        'swa_sink': nrm(ks[21], (N_SWA, SWA_Q_HEADS), 0.5),
        'final_norm_g': 1.0 + nrm(ks[22], (D,), 0.02),
    }


def reference(x, c, ctx, c_ctx, w_mod, b_mod, norm_g, ffn_w_in, ffn_w_out,
              ret_w_in, ret_w_out, ret_gn_g, ret_decay_f, ret_decay_b,
              nat_w_qkv, nat_w_o, nat_rpb, pool_w, pool_scale,
              swa_w_qkv, swa_w_o, swa_sink, final_norm_g):
    D = x.shape[-1]
    h, hc = x, ctx
    s_c = jax.nn.silu(c)
    s_cc = jax.nn.silu(c_ctx)[None]
    for i in range(DEPTH):
        kind, occ = i % N_MIXERS, i // N_MIXERS
        last = i == DEPTH - 1
        ctx_live = (not last) or kind != MIX_POOL
        ml = (s_c @ w_mod[i] + b_mod[i]).reshape(-1, 3, 3, D)
        h = gated_residual(h, ml, 0, swiglu(ada_in(h, norm_g[i, 0], ml, 0), ffn_w_in[i, 0], ffn_w_out[i, 0]), 0.5)
        xl = ada_in(h, norm_g[i, 1], ml, 1)
        xc = None
        if ctx_live:
            mc = (s_cc @ w_mod[i] + b_mod[i]).reshape(-1, 3, 3, D)
            hc = gated_residual(hc, mc, 0, swiglu(ada_in(hc, norm_g[i, 0], mc, 0), ffn_w_in[i, 0], ffn_w_out[i, 0]), 0.5)
            xc = ada_in(hc, norm_g[i, 1], mc, 1)
        if kind == MIX_RET:
            yl, yc = retention_mixer(xl, xc, ret_w_in[occ], ret_w_out[occ], ret_gn_g[occ],
                                     ret_decay_f[occ], ret_decay_b[occ], not last)
        elif kind == MIX_NAT:
            yl, yc = nat_mixer(xl, xc, nat_w_qkv[occ], nat_w_o[occ], nat_rpb[occ], not last)
        elif kind == MIX_POOL:
            yl = pool_mixer(xl, pool_w[occ], pool_scale[occ])
            yc = None if last else pool_mixer(xc, pool_w[occ], pool_scale[occ])
        else:
            yl, yc = swa_mixer(xl, xc, swa_w_qkv[occ], swa_w_o[occ], swa_sink[occ], not last)
        h = gated_residual(h, ml, 1, yl, 1.0)
        if not last:
            hc = gated_residual(hc, mc, 1, yc, 1.0)
            hc = gated_residual(hc, mc, 2, swiglu(ada_in(hc, norm_g[i, 2], mc, 2), ffn_w_in[i, 1], ffn_w_out[i, 1]), 0.5)
        h = gated_residual(h, ml, 2, swiglu(ada_in(h, norm_g[i, 2], ml, 2), ffn_w_in[i, 1], ffn_w_out[i, 1]), 0.5)
    return rms_norm(h, final_norm_g)
```

```python
import contextlib
import numpy as np
import concourse.bass as bass
import concourse.mybir as mybir
from concourse.bass_utils import run_bass_kernel_spmd

F32 = mybir.dt.float32
BF16 = mybir.dt.bfloat16
AF = mybir.ActivationFunctionType
ALU = mybir.AluOpType
AX = mybir.AxisListType

D = 1024
DC = 8
T_LAT = 4096
T_CTX = 256
T_ALL = T_LAT + T_CTX
DEPTH = 4
FF = 2816
FJ = 22
EPS = 1e-6
NT = 256
N_CORES = 4


class Res:
    __slots__ = ("name", "w", "r")

    def __init__(self, name=""):
        self.name = name
        self.w = None
        self.r = {}


class _Eng:
    def __init__(self, kk, name, handle):
        self.name = name
        self.h = handle
        self.sem = kk.new_sem("e_" + name)
        self.count = 0
        self.waited = {}
        self.pend_r = []
        self.pend_w = []


class K:
    def __init__(self, nc, n_dma_sems=16):
        self.nc = nc
        self.st = contextlib.ExitStack()
        self.sems = {}
        self.nsem = 0
        self.eng = {}
        for name, h in (("pe", nc.tensor), ("act", nc.scalar), ("dve", nc.vector),
                        ("pool", nc.gpsimd), ("sp", nc.sync)):
            self.eng[name] = _Eng(self, name, h)
        self.dma_pool = {}
        for q in ("sp", "pool"):
            self.dma_pool[q] = [[self.new_sem("d_%s%d" % (q, i)), 0] for i in range(n_dma_sems)]
        self.dma_rr = {"sp": 0, "pool": 0}
        self.ninst = 0

    def new_sem(self, name):
        s = self.st.enter_context(self.nc.semaphore(name))
        sid = self.nsem
        self.nsem += 1
        self.sems[sid] = s
        return sid

    def _wait(self, e, ev):
        sid, val = ev
        if e.waited.get(sid, 0) >= val:
            return
        e.waited[sid] = val
        e.h.wait_ge(self.sems[sid], val)
        self.ninst += 1

    def _deps(self, e, r, w, nowaw=False):
        evs = {}

        def add(ev):
            if ev is not None and evs.get(ev[0], 0) < ev[1]:
                evs[ev[0]] = ev[1]
        for x in r:
            add(x.w)
        for x in w:
            if not nowaw:
                add(x.w)
            for sid, val in x.r.items():
                add((sid, val))
        for sid, val in evs.items():
            if e.name == "pe" and sid == e.sem:
                continue
            self._wait(e, (sid, val))

    def _commit(self, ev, r, w):
        for x in r:
            if x.r.get(ev[0], 0) < ev[1]:
                x.r[ev[0]] = ev[1]
        for x in w:
            x.w = ev
            x.r = {}

    def op(self, eng, fn, r=(), w=(), inc=True):
        e = self.eng[eng]
        self._deps(e, r, w)
        inst = fn(e.h)
        self.ninst += 1
        if not inc:
            e.pend_r.extend(r)
            e.pend_w.extend(w)
            return None
        if e.count >= 30000:
            e.sem = self.new_sem("e_%s_%d" % (eng, self.nsem))
            e.count = 0
        e.count += 1
        inst.then_inc(self.sems[e.sem], 1)
        ev = (e.sem, e.count)
        self._commit(ev, list(r) + e.pend_r, list(w) + e.pend_w)
        e.pend_r = []
        e.pend_w = []
        return ev

    def dma(self, q, out, in_, r=(), w=(), nowaw=False):
        e = self.eng[q]
        self._deps(e, r, w, nowaw=nowaw)
        pool = self.dma_pool[q]
        i = self.dma_rr[q]
        self.dma_rr[q] = (i + 1) % len(pool)
        slot = pool[i]
        if slot[1] > 0:
            self._wait(e, (slot[0], slot[1]))
        slot[1] += 16
        e.h.dma_start(out=out, in_=in_).then_inc(self.sems[slot[0]], 16)
        self.ninst += 1
        ev = (slot[0], slot[1])
        self._commit(ev, r, w)
        return ev

    def barrier(self, engs=("pe", "act", "dve", "pool", "sp")):
        for x in engs:
            e = self.eng[x]
            for y in self.eng.values():
                if y is not e and y.count > 0:
                    self._wait(e, (y.sem, y.count))
            for pool in self.dma_pool.values():
                for sid, val in pool:
                    if val > 0:
                        self._wait(e, (sid, val))


def build(cfg=None):
    cfg = cfg or {}
    n_layers = cfg.get("n_layers", DEPTH)
    mixers = cfg.get("mixers", True)
    dbg = cfg.get("dbg", False)

    nc = bass.Bass("TRN2", target_bir_lowering=False)
    kk = K(nc)
    st = kk.st

    def dram_in(name, shape, dt=F32):
        return nc.dram_tensor(name, list(shape), dt, kind="ExternalInput").ap()

    xT = dram_in("xT", [128, DC, T_ALL])
    cT = dram_in("cT", [128, DC, 2])
    w_mod = dram_in("w_mod", [DEPTH, D, 9 * D])
    bmodT = dram_in("bmodT", [128, DEPTH, 72])
    normgT = dram_in("normgT", [128, DEPTH, 3, DC])
    fnormgT = dram_in("fnormgT", [128, DC])
    ffn_w_in = dram_in("ffn_w_in", [DEPTH, 2, D, 2 * FF])
    ffn_w_out = dram_in("ffn_w_out", [DEPTH, 2, FF, D])
    outT = nc.dram_tensor("outT", [128, DC, T_LAT], F32, kind="ExternalOutput").ap()
    hA = nc.dram_tensor("hA", [128, DC, T_ALL], F32).ap()
    hB = nc.dram_tensor("hB", [128, DC, T_ALL], F32).ap()
    hbufs = [hA, hB]
    ret_w_in = dram_in("ret_w_in", [D, 6144])
    ret_w_out = dram_in("ret_w_out", [2048, D])
    ret_gn = dram_in("ret_gn", [1, 2048])
    ret_decay = dram_in("ret_decay", [1, 8])
    ret_tabs = dram_in("ret_tabs", [128, 8, 128])
    ret_cs = dram_in("ret_cs", [128, 2, T_ALL])
    nat_w_qkv = dram_in("nat_w_qkv", [D, 3072])
    nat_w_o = dram_in("nat_w_o", [D, D])
    nat_bias = dram_in("nat_bias", [16, 128, 5, 7, 128])
    pool_w = dram_in("pool_w", [4, 256, 256])
    pool_sc = dram_in("pool_sc", [128, DC])
    pool_icnt = dram_in("pool_icnt", [4, T_ALL])
    swa_w_qkv = dram_in("swa_w_qkv", [D, 1536])
    swa_w_perm = dram_in("swa_w_perm", [D, 1280])
    swa_w_o = dram_in("swa_w_o", [D, D])
    swa_sink = dram_in("swa_sink", [1, 16])
    swa_cs = dram_in("swa_cs", [128, 2, T_ALL])
    swa_bias = dram_in("swa_bias", [128, 3, 5, 128])
    qT_d = nc.dram_tensor("qT_d", [128, DC, T_ALL], BF16).ap()
    kT_d = nc.dram_tensor("kT_d", [128, DC, T_ALL], BF16).ap()
    v_d = nc.dram_tensor("v_d", [T_ALL, 2048], BF16).ap()
    g_d = nc.dram_tensor("g_d", [T_ALL, 2048], F32).ap()
    oT_d = nc.dram_tensor("oT_d", [128, 16, T_ALL], BF16).ap()
    xl_d = nc.dram_tensor("xl_d", [128, DC, T_ALL], F32).ap()

    def sb(name, shape, dt):
        return st.enter_context(nc.sbuf_tensor(name, list(shape), dt))

    def ps(name, shape, dt=F32):
        return st.enter_context(nc.psum_tensor(name, list(shape), dt))

    _uid = [0]

    def sbt(ph, name, shape, dt):
        _uid[0] += 1
        return ph.enter_context(nc.sbuf_tensor("%s_%d" % (name, _uid[0]), list(shape), dt))

    ones_f = sb("ones_f", [128, 128], F32)
    mods = sb("mods", [128, DEPTH, 72, 2], F32)
    gsc = sb("gsc", [128, DEPTH, 3, DC, 2], F32)
    gat = sb("gat", [128, DEPTH, 3, DC, 2], F32)
    ng = sb("ng", [128, DEPTH, 3, DC], F32)
    fng = sb("fng", [128, DC], F32)
    bm = sb("bm", [128, DEPTH, 72], F32)
    sT = sb("sT", [128, DC, 2], F32)
    R_const = Res("const")
    R_mods = Res("mods")

    kk.op("dve", lambda e: e.memset(ones_f[:], 1.0), w=[R_const])
    kk.dma("sp", ng[:], normgT, w=[R_const])
    kk.dma("sp", fng[:], fnormgT, w=[R_const])
    kk.dma("sp", bm[:], bmodT, w=[R_const])
    kk.dma("sp", sT[:], cT, w=[R_const])
    kk.op("act", lambda e: e.activation(out=sT[:], in_=sT[:], func=AF.Silu), r=[R_const], w=[R_const])

    ps_stat = ps("ps_stat", [128, 512])
    ps_a = [ps("ps_a%d" % i, [128, 512]) for i in range(2)]
    ps_b = [ps("ps_b%d" % i, [128, 512]) for i in range(2)]
    ps_o = [ps("ps_o%d" % i, [128, 512]) for i in range(2)]
    R_ps_stat = Res()
    R_ps_a = [Res(), Res()]
    R_ps_b = [Res(), Res()]
    R_ps_o = [Res(), Res()]
    ps_x = ps("ps_x", [128, 512])
    R_ps_x = Res()
    poolS = [(ps_a[0], R_ps_a[0]), (ps_a[1], R_ps_a[1]), (ps_b[0], R_ps_b[0]), (ps_b[1], R_ps_b[1])]
    poolO = [(ps_o[0], R_ps_o[0]), (ps_o[1], R_ps_o[1]), (ps_x, R_ps_x)]
    _rrS = [0]
    _rrO = [0]

    def bankS():
        _rrS[0] = (_rrS[0] + 1) % len(poolS)
        return poolS[_rrS[0]]

    def bankO():
        _rrO[0] = (_rrO[0] + 1) % len(poolO)
        return poolO[_rrO[0]]

    def mm_acc(out_ap, R_out, pairs):
        last = len(pairs) - 1
        ev = None
        for idx, (l_, r_, rd) in enumerate(pairs):
            ev = kk.op("pe", lambda e, l_=l_, r_=r_, idx=idx: e.matmul(out_ap, l_, r_, start=(idx == 0), stop=(idx == last)),
                       r=rd, w=[R_out], inc=(idx == last))
        return ev

    identf = sb("identf", [128, 128], F32)
    kk.dma("sp", identf[:], ret_tabs[:, 7, :], w=[R_const])
    with contextlib.ExitStack() as ph:
        wm = [sbt(ph, "wm%d" % i, [128, DC, 1024], F32) for i in range(2)]
        R_wm = [Res(), Res()]
        modrow = sbt(ph, "modrow", [2, 9 * D], F32)
        R_modrow = Res()
        blk = 0
        for li in range(n_layers):
            for nb in range(9):
                s = blk % 2
                blk += 1
                src = w_mod[li, :, nb * 1024:(nb + 1) * 1024].rearrange("(kc p) n -> p kc n", p=128)
                kk.dma("sp", wm[s][:], src, w=[R_wm[s]])
                for half in range(2):
                    (pS, RS) = bankS()
                    mm_acc(pS[0:2, :512], RS, [(sT[:, kc, :], wm[s][:, kc, half * 512:(half + 1) * 512], [R_wm[s], R_const])
                                               for kc in range(DC)])
                    c0 = nb * 1024 + half * 512
                    kk.op("act", lambda e, pS=pS, c0=c0: e.activation(out=modrow[0:2, c0:c0 + 512], in_=pS[0:2, :512],
                                                                       func=AF.Identity), r=[RS], w=[R_modrow])
            for n in range(72):
                kk.op("pe", lambda e, n=n: e.transpose(ps_stat[:, n * 2:(n + 1) * 2], modrow[0:2, n * 128:(n + 1) * 128],
                                                       identf[0:2, 0:2]),
                      r=[R_modrow, R_const], w=[R_ps_stat], inc=(n == 71))
            kk.op("dve", lambda e, li=li: e.tensor_tensor(
                out=mods[:, li, :, :], in0=ps_stat[:, 0:144].rearrange("p (n w) -> p n w", w=2),
                in1=bm[:, li, :].unsqueeze(2).to_broadcast([128, 72, 2]),
                op=ALU.add), r=[R_ps_stat, R_const], w=[R_mods])
        kk.barrier()

    for li in range(n_layers):
        for j in range(3):
            sc = mods[:, li, (j * 3 + 1) * 8:(j * 3 + 2) * 8, :]
            gt = mods[:, li, (j * 3 + 2) * 8:(j * 3 + 3) * 8, :]
            for w_ in range(2):
                kk.op("dve", lambda e, li=li, j=j, w_=w_, sc=sc: e.scalar_tensor_tensor(
                    out=gsc[:, li, j, :, w_], in0=sc[:, :, w_], scalar=1.0, in1=ng[:, li, j, :],
                    op0=ALU.add, op1=ALU.mult), r=[R_mods, R_const], w=[R_mods])
            kk.op("dve", lambda e, li=li, j=j, gt=gt: e.tensor_scalar(
                out=gat[:, li, j, :, :], in0=gt, scalar1=(1.0 if j == 1 else 0.5), scalar2=None,
                op0=ALU.mult), r=[R_mods], w=[R_mods])
    kk.barrier()

    tiles = [(t0, NT, 0) for t0 in range(0, T_LAT, NT)] + [(T_LAT, T_CTX, 1)]

    def norm_mod(ph, name):
        o = {}
        o["sqt"] = sbt(ph, name + "_sqt", [128, DC, NT], F32)
        o["Rsqt"] = Res()
        o["ssum"] = sbt(ph, name + "_ssum", [128, NT], F32)
        o["Rssum"] = Res()
        o["rstd"] = sbt(ph, name + "_rstd", [128, NT], F32)
        o["Rrstd"] = Res()
        return o

    def _emit(steps, eng, fn, r, w):
        if steps is None:
            kk.op(eng, fn, r=r, w=w)
        else:
            steps.append(lambda: kk.op(eng, fn, r=r, w=w))

    def emit_rstd(o, ht, R_ht, n, steps=None):
        _emit(steps, "dve", lambda e: e.tensor_tensor(out=o["sqt"][:, :, :n], in0=ht[:, :, :n], in1=ht[:, :, :n], op=ALU.mult),
              [R_ht], [o["Rsqt"]])
        _emit(steps, "dve", lambda e: e.tensor_reduce(out=o["ssum"][:, :n], in_=o["sqt"][:, :, :n].rearrange("p c n -> p n c"),
                                                      axis=AX.X, op=ALU.add), [o["Rsqt"]], [o["Rssum"]])
        _emit(steps, "pe", lambda e: e.matmul(ps_stat[:, :n], ones_f[:], o["ssum"][:, :n], start=True, stop=True),
              [o["Rssum"], R_const], [R_ps_stat])
        _emit(steps, "act", lambda e: e.activation(out=o["rstd"][:, :n], in_=ps_stat[:, :n], func=AF.Sqrt,
                                                   scale=1.0 / D, bias=EPS), [R_ps_stat], [o["Rrstd"]])
        _emit(steps, "dve", lambda e: e.reciprocal(out=o["rstd"][:, :n], in_=o["rstd"][:, :n]), [o["Rrstd"]], [o["Rrstd"]])

    def emit_xl(o, ht, R_ht, n, li, j, w_, dst, R_dst, steps=None):
        _emit(steps, "dve", lambda e: e.tensor_tensor(
            out=o["sqt"][:, :, :n], in0=ht[:, :, :n], in1=o["rstd"][:, :n].unsqueeze(1).to_broadcast([128, DC, n]),
            op=ALU.mult), [R_ht, o["Rrstd"]], [o["Rsqt"]])
        for c in range(DC):
            _emit(steps, "act", lambda e, c=c: e.activation(
                out=dst[:, c, :n], in_=o["sqt"][:, c, :n], func=AF.Identity,
                scale=gsc[:, li, j, c, w_:w_ + 1], bias=mods[:, li, (j * 3) * 8 + c, w_:w_ + 1]),
                [o["Rsqt"], R_mods], [R_dst])

    def ffn_phase(li, s_, j, h_in, h_out, tl):
        with contextlib.ExitStack() as ph:
            win = sbt(ph, "win", [128, DC, 2 * FF], BF16)
            wout = sbt(ph, "wout", [128, FJ, D], BF16)
            R_win = [Res() for _ in range(DC)]
            R_wout = Res()
            for kc in range(DC):
                kk.dma("pool", win[:, kc, :], ffn_w_in[li, s_, kc * 128:(kc + 1) * 128, :], w=[R_win[kc]])
            kk.dma("pool", wout[:], ffn_w_out[li, s_].rearrange("(j p) d -> p j d", p=128), w=[R_wout])
            ht = [sbt(ph, "ht%d" % i, [128, DC, NT], F32) for i in range(2)]
            R_ht = [Res(), Res()]
            xl = [sbt(ph, "xl%d" % i, [128, DC, NT], BF16) for i in range(2)]
            R_xl = [Res(), Res()]
            g = sbt(ph, "g", [128, FJ, NT], BF16)
            R_g = [Res() for _ in range(FJ)]
            sa = [sbt(ph, "sa%d" % i, [128, NT], F32) for i in range(2)]
            R_sa = [Res(), Res()]
            o = norm_mod(ph, "f")

            def prep(ti, steps):
                t0, n, w_ = tl[ti]
                b = ti % 2
                kk.dma("sp", ht[b][:, :, :n], h_in[:, :, t0:t0 + n], w=[R_ht[b]])
                emit_rstd(o, ht[b], R_ht[b], n, steps)
                emit_xl(o, ht[b], R_ht[b], n, li, j, w_, xl[b], R_xl[b], steps)

            prep(0, None)
            for ti, (t0, n, w_) in enumerate(tl):
                b = ti % 2
                steps = []
                if ti + 1 < len(tl):
                    prep(ti + 1, steps)
                for jj in range(FJ):
                    pb = jj % 2
                    for kc in range(DC):
                        kk.op("pe", lambda e, jj=jj, kc=kc, pb=pb: e.matmul(
                            ps_a[pb][:, :n], win[:, kc, jj * 128:(jj + 1) * 128], xl[b][:, kc, :n],
                            start=(kc == 0), stop=(kc == DC - 1)),
                            r=[R_win[kc], R_xl[b]], w=[R_ps_a[pb]], inc=(kc == DC - 1))
                    for kc in range(DC):
                        kk.op("pe", lambda e, jj=jj, kc=kc, pb=pb: e.matmul(
                            ps_b[pb][:, :n], win[:, kc, FF + jj * 128:FF + (jj + 1) * 128], xl[b][:, kc, :n],
                            start=(kc == 0), stop=(kc == DC - 1)),
                            r=[R_win[kc], R_xl[b]], w=[R_ps_b[pb]], inc=(kc == DC - 1))
                    kk.op("act", lambda e, pb=pb: e.activation(out=sa[pb][:, :n], in_=ps_a[pb][:, :n], func=AF.Silu),
                          r=[R_ps_a[pb]], w=[R_sa[pb]])
                    kk.op("dve", lambda e, pb=pb, jj=jj: e.tensor_tensor(out=g[:, jj, :n], in0=sa[pb][:, :n],
                                                                         in1=ps_b[pb][:, :n], op=ALU.mult),
                          r=[R_sa[pb], R_ps_b[pb]], w=[R_g[jj]])
                    if jj >= 2 and steps:
                        steps.pop(0)()
                while steps:
                    steps.pop(0)()
                for d in range(DC):
                    pb = d % 2
                    for jj in range(FJ):
                        kk.op("pe", lambda e, jj=jj, d=d, pb=pb: e.matmul(
                            ps_o[pb][:, :n], wout[:, jj, d * 128:(d + 1) * 128], g[:, jj, :n],
                            start=(jj == 0), stop=(jj == FJ - 1)),
                            r=[R_wout, R_g[jj]], w=[R_ps_o[pb]], inc=(jj == FJ - 1))
                    kk.op("dve", lambda e, d=d, pb=pb, b=b: e.scalar_tensor_tensor(
                        out=ht[b][:, d, :n], in0=ps_o[pb][:, :n], scalar=gat[:, li, j, d, w_:w_ + 1],
                        in1=ht[b][:, d, :n], op0=ALU.mult, op1=ALU.add),
                        r=[R_ps_o[pb], R_mods, R_ht[b]], w=[R_ht[b]])
                kk.dma("sp", h_out[:, :, t0:t0 + n], ht[b][:, :, :n], r=[R_ht[b]])
            kk.barrier()

    def final_phase(h_in):
        with contextlib.ExitStack() as ph:
            ht = [sbt(ph, "fht%d" % i, [128, DC, NT], F32) for i in range(2)]
            R_ht = [Res(), Res()]
            ot = [sbt(ph, "fot%d" % i, [128, DC, NT], F32) for i in range(2)]
            R_ot = [Res(), Res()]
            o = norm_mod(ph, "fn")
            for ti, (t0, n, w_) in enumerate(tiles):
                if w_ == 1:
                    continue
                b = ti % 2
                kk.dma("sp", ht[b][:, :, :n], h_in[:, :, t0:t0 + n], w=[R_ht[b]])
                emit_rstd(o, ht[b], R_ht[b], n)
                for c in range(DC):
                    kk.op("dve", lambda e, c=c, b=b: e.scalar_tensor_tensor(
                        out=ot[b][:, c, :n], in0=ht[b][:, c, :n], scalar=fng[:, c:c + 1], in1=o["rstd"][:, :n],
                        op0=ALU.mult, op1=ALU.mult), r=[R_ht[b], o["Rrstd"], R_const], w=[R_ot[b]])
                kk.dma("sp", outT[:, :, t0:t0 + n], ot[b][:, :, :n], r=[R_ot[b]])
            kk.barrier()

    def proj_phase(li, kind, h_in, tl):
        with contextlib.ExitStack() as ph:
            if kind == "ret":
                ncol = 6144
                W = sbt(ph, "pw", [128, DC, ncol], BF16)
                R_W = Res()
                for kc in range(DC):
                    kk.dma("pool", W[:, kc, :], ret_w_in[kc * 128:(kc + 1) * 128, :], w=[R_W], nowaw=True)
                fm = [(0, 8, qT_d, "ret", 0), (1024, 8, kT_d, "ret", 0)]
                tm = [(2048, 2048, v_d, AF.Identity, "v"), (4096, 2048, g_d, AF.Silu, "g")]
                cs_d = ret_cs
            elif kind == "nat":
                ncol = 3072
                W = sbt(ph, "pw", [128, DC, ncol], BF16)
                R_W = Res()
                kk.dma("pool", W[:], nat_w_qkv.rearrange("(kc p) n -> p kc n", p=128), w=[R_W])
                fm = [(0, 8, qT_d, None, 0), (1024, 8, kT_d, None, 0)]
                tm = [(2048, 1024, v_d, AF.Identity, "v")]
                cs_d = None
            else:
                ncol = 1536 + 1280
                W = sbt(ph, "pw", [128, DC, ncol], BF16)
                R_W = Res()
                kk.dma("pool", W[:, :, 0:1536], swa_w_qkv.rearrange("(kc p) n -> p kc n", p=128), w=[R_W], nowaw=True)
                kk.dma("pool", W[:, :, 1536:2816], swa_w_perm.rearrange("(kc p) n -> p kc n", p=128), w=[R_W], nowaw=True)
                fm = [(0, 8, qT_d, "swa", 1536), (1024, 2, kT_d, "swa", 1536 + 1024)]
                tm = [(1280, 256, v_d, AF.Identity, "v")]
                cs_d = swa_cs
            ht = [sbt(ph, "pht%d" % i, [128, DC, NT], F32) for i in range(2)]
            R_ht = [Res(), Res()]
            xl = sbt(ph, "pxl", [128, DC, NT], BF16)
            R_xl = Res()
            stg = [sbt(ph, "pst%d" % i, [128, DC, NT], BF16) for i in range(2)]
            R_stg = [Res(), Res()]
            stv = sbt(ph, "pstv", [128, 2048], BF16)
            R_stv = Res()
            stgg = sbt(ph, "pstg", [128, 2048], F32)
            R_stgg = Res()
            cs = sbt(ph, "pcs", [128, 2, NT], F32)
            R_cs = Res()
            t1 = sbt(ph, "pt1", [128, NT], F32)
            t2 = sbt(ph, "pt2", [128, NT], F32)
            R_t1, R_t2 = Res(), Res()
            o = norm_mod(ph, "p")
            for ti, (t0, n, w_) in enumerate(tl):
                b = ti % 2
                kk.dma("sp", ht[b][:, :, :n], h_in[:, :, t0:t0 + n], w=[R_ht[b]])
                emit_rstd(o, ht[b], R_ht[b], n)
                emit_xl(o, ht[b], R_ht[b], n, li, 1, w_, xl, R_xl)
                if cs_d is not None:
                    kk.dma("sp", cs[:, :, :n], cs_d[:, :, t0:t0 + n], w=[R_cs])

                def proj(off, m):
                    (pa, Ra) = bankS()
                    mm_acc(pa[:, :n], Ra, [(W[:, kc, off + m * 128:off + (m + 1) * 128], xl[:, kc, :n], [R_W, R_xl])
                                            for kc in range(DC)])
                    return pa, Ra

                def tt(out, a, b_, op, r, w):
                    kk.op("dve", lambda e: e.tensor_tensor(out=out, in0=a, in1=b_, op=op), r=r, w=w)

                for fi, (off, nch, dst, mode, poff) in enumerate(fm):
                    sg, R_sg = stg[fi], R_stg[fi]
                    if mode == "ret":
                        for hh in range(nch // 2):
                            pa, Ra = proj(off, 2 * hh)
                            pb_, Rb = proj(off, 2 * hh + 1)
                            tt(t1[:, :n], pa[:, :n], cs[:, 0, :n], ALU.mult, [Ra, R_cs], [R_t1])
                            tt(t2[:, :n], pb_[:, :n], cs[:, 1, :n], ALU.mult, [Rb, R_cs], [R_t2])
                            tt(sg[:, 2 * hh, :n], t1[:, :n], t2[:, :n], ALU.subtract, [R_t1, R_t2], [R_sg])
                            tt(t1[:, :n], pb_[:, :n], cs[:, 0, :n], ALU.mult, [Rb, R_cs], [R_t1])
                            tt(t2[:, :n], pa[:, :n], cs[:, 1, :n], ALU.mult, [Ra, R_cs], [R_t2])
                            tt(sg[:, 2 * hh + 1, :n], t1[:, :n], t2[:, :n], ALU.add, [R_t1, R_t2], [R_sg])
                    elif mode == "swa":
                        for m in range(nch):
                            pa, Ra = proj(off, m)
                            pb_, Rb = proj(poff, m)
                            tt(t1[:, :n], pa[:, :n], cs[:, 0, :n], ALU.mult, [Ra, R_cs], [R_t1])
                            tt(t2[:, :n], pb_[:, :n], cs[:, 1, :n], ALU.mult, [Rb, R_cs], [R_t2])
                            tt(sg[:, m, :n], t1[:, :n], t2[:, :n], ALU.add, [R_t1, R_t2], [R_sg])
                    else:
                        for m in range(nch):
                            pa, Ra = proj(off, m)
                            kk.op("act", lambda e, pa=pa, m=m, sg=sg: e.activation(out=sg[:, m, :n], in_=pa[:, :n], func=AF.Identity),
                                  r=[Ra], w=[R_sg])
                    kk.dma("sp", dst[:, 0:nch, t0:t0 + n], sg[:, 0:nch, :n], r=[R_sg])
                for sub in range(n // 128):
                    for (off, ncols, dstd, fn, nm) in tm:
                        st_t, R_st = (stv, R_stv) if nm == "v" else (stgg, R_stgg)
                        for blk in range((ncols + 511) // 512):
                            cw = min(512, ncols - blk * 512)
                            (pa, Ra) = bankS()
                            mm_acc(pa[:, :cw], Ra, [(xl[:, kc, sub * 128:(sub + 1) * 128],
                                                     W[:, kc, off + blk * 512:off + blk * 512 + cw], [R_W, R_xl])
                                                    for kc in range(DC)])
                            kk.op("act", lambda e, pa=pa, blk=blk, cw=cw, st_t=st_t, fn=fn: e.activation(
                                out=st_t[:, blk * 512:blk * 512 + cw], in_=pa[:, :cw], func=fn), r=[Ra], w=[R_st])
                        kk.dma("sp", dstd[t0 + sub * 128:t0 + (sub + 1) * 128, 0:ncols], st_t[:, 0:ncols], r=[R_st])
            kk.barrier()

    def attn_phase(kind, do_ctx):
        with contextlib.ExitStack() as ph:
            ntyp, nch = (5, 7) if kind == "nat" else (3, 5)
            bias = sbt(ph, "abias", [128, ntyp, nch, 128], F32)
            R_bias = Res()
            ones_b = sbt(ph, "aones", [128, 64], BF16)
            R_c = Res()
            kk.op("dve", lambda e: e.memset(ones_b[:], 1.0), w=[R_c])
            sk = sbt(ph, "ask", [128, 16], F32)
            if kind == "swa":
                kk.dma("sp", bias[:], swa_bias, w=[R_bias])
                kk.dma("sp", sk[:], swa_sink.partition_broadcast(128), w=[R_c])
                kk.op("act", lambda e: e.activation(out=sk[:], in_=sk[:], func=AF.Exp), r=[R_c], w=[R_c])
            qh = [sbt(ph, "aqh%d" % i, [64, T_ALL], BF16) for i in range(2)]
            kh = [sbt(ph, "akh%d" % i, [64, T_ALL], BF16) for i in range(2)]
            vh = [sbt(ph, "avh%d" % i, [128, 34, 64], BF16) for i in range(2)]
            oh = [sbt(ph, "aoh%d" % i, [64, T_ALL], BF16) for i in range(2)]
            R_qh, R_kh, R_vh, R_oh = [Res(), Res()], [Res(), Res()], [Res(), Res()], [Res(), Res()]
            bias2 = [bias, sbt(ph, "abias2", [128, ntyp, nch, 128], F32)] if kind == "nat" else [bias, bias]
            R_bias2 = [R_bias, Res()] if kind == "nat" else [R_bias, R_bias]
            tmp = [sbt(ph, "atmp%d" % i, [128, nch * 128], F32) for i in range(2)]
            R_tmp = [Res(), Res()]
            pt = [sbt(ph, "apt%d" % i, [128, nch * 128], BF16) for i in range(2)]
            R_pt = [Res(), Res()]
            den = [sbt(ph, "aden%d" % i, [64, 128], F32) for i in range(2)]
            R_den = [Res(), Res()]
            Tq = T_ALL if do_ctx else T_LAT

            def load_head(h):
                hs = h % 2
                kvh = h if kind == "nat" else h // 4
                kk.dma("sp", qh[hs][:], qT_d[(h % 2) * 64:(h % 2) * 64 + 64, h // 2, :], w=[R_qh[hs]])
                kk.dma("sp", kh[hs][:], kT_d[(kvh % 2) * 64:(kvh % 2) * 64 + 64, kvh // 2, :], w=[R_kh[hs]])
                kk.dma("sp", vh[hs][:], v_d[:, kvh * 64:(kvh + 1) * 64].rearrange("(n p) d -> p n d", p=128), w=[R_vh[hs]])
                if kind == "nat":
                    kk.dma("sp", bias2[hs][:], nat_bias[h], w=[R_bias2[hs]])

            units = []
            for h in range(16):
                for qi in range(32):
                    if kind == "nat":
                        typ = 0 if 2 <= qi <= 29 else {0: 1, 1: 2, 30: 3, 31: 4}[qi]
                        base = min(max(qi - 2, 0), 27)
                        chunks = [base + i for i in range(5)] + [32, 33]
                    else:
                        typ = 1 if qi == 0 else (2 if qi == 31 else 0)
                        chunks = [max(qi - 1, 0), qi, min(qi + 1, 31), 32, 33]
                    units.append([h, qi * 128, chunks, typ, qi == 0, False])
                if do_ctx:
                    for qc in (32, 33):
                        units.append([h, qc * 128, [32, 33], None, False, False])
                units[-1][5] = True
            state = {}

            def s1(ui):
                h, q0, chunks, typ, first, lasth = units[ui]
                hs = h % 2
                ncu = len(chunks)
                u2 = ui % 2
                banks = []
                for c0 in range(0, ncu, 4):
                    cn = min(4, ncu - c0)
                    (pS, RS) = bankS()
                    banks.append((pS, RS, c0, cn))
                    for ci in range(c0, c0 + cn):
                        kc_ = chunks[ci]
                        kk.op("pe", lambda e, pS=pS, ci=ci, c0=c0, kc_=kc_, q0=q0, hs=hs: e.matmul(
                            pS[:, (ci - c0) * 128:(ci - c0 + 1) * 128], kh[hs][:, kc_ * 128:(kc_ + 1) * 128],
                            qh[hs][:, q0:q0 + 128], start=True, stop=True),
                            r=[R_kh[hs], R_qh[hs]], w=[RS], inc=(ci == c0 + cn - 1))
                if typ is not None:
                    for (pS, RS, c0, cn) in banks:
                        kk.op("dve", lambda e, pS=pS, c0=c0, cn=cn, typ=typ, u2=u2, hs=hs: e.scalar_tensor_tensor(
                            out=tmp[u2][:, c0 * 128:(c0 + cn) * 128].rearrange("p (c q) -> p c q", q=128),
                            in0=pS[:, :cn * 128].rearrange("p (c q) -> p c q", q=128), scalar=0.125,
                            in1=bias2[hs][:, typ, c0:c0 + cn, :], op0=ALU.mult, op1=ALU.add),
                            r=[RS, R_bias2[hs]], w=[R_tmp[u2]])
                    kk.op("act", lambda e, u2=u2, ncu=ncu: e.activation(out=pt[u2][:, :ncu * 128], in_=tmp[u2][:, :ncu * 128],
                                                                          func=AF.Exp), r=[R_tmp[u2]], w=[R_pt[u2]])
                else:
                    for (pS, RS, c0, cn) in banks:
                        kk.op("act", lambda e, pS=pS, c0=c0, cn=cn, u2=u2: e.activation(
                            out=pt[u2][:, c0 * 128:(c0 + cn) * 128], in_=pS[:, :cn * 128], func=AF.Exp, scale=0.125),
                            r=[RS], w=[R_pt[u2]])

            def s2(ui):
                h, q0, chunks, typ, first, lasth = units[ui]
                hs = h % 2
                if first and h + 1 < 16:
                    load_head(h + 1)
                ncu = len(chunks)
                u2 = ui % 2
                (pO, RO) = bankO()
                mm_acc(pO[:64, 0:128], RO, [(vh[hs][:, chunks[ci], :], pt[u2][:, ci * 128:(ci + 1) * 128], [R_vh[hs], R_pt[u2]])
                                            for ci in range(ncu)])
                mm_acc(pO[:64, 128:256], RO, [(ones_b[:, :], pt[u2][:, ci * 128:(ci + 1) * 128], [R_c, R_pt[u2]])
                                              for ci in range(ncu)])
                if kind == "swa":
                    kk.op("dve", lambda e, pO=pO, u2=u2, h=h: e.tensor_scalar(
                        out=den[u2][:], in0=pO[:64, 128:256], scalar1=sk[:64, h:h + 1], scalar2=None, op0=ALU.add),
                        r=[RO, R_c], w=[R_den[u2]])
                    kk.op("dve", lambda e, u2=u2: e.reciprocal(out=den[u2][:], in_=den[u2][:]), r=[R_den[u2]], w=[R_den[u2]])
                else:
                    kk.op("dve", lambda e, pO=pO, u2=u2: e.reciprocal(out=den[u2][:], in_=pO[:64, 128:256]),
                          r=[RO], w=[R_den[u2]])
                kk.op("dve", lambda e, pO=pO, u2=u2, q0=q0, hs=hs: e.tensor_tensor(
                    out=oh[hs][:, q0:q0 + 128], in0=pO[:64, 0:128], in1=den[u2][:], op=ALU.mult),
                    r=[RO, R_den[u2]], w=[R_oh[hs]])
                if lasth:
                    kk.dma("sp", oT_d[(h % 2) * 64:(h % 2) * 64 + 64, h // 2, 0:Tq], oh[hs][:, 0:Tq], r=[R_oh[hs]])

            LA = 1
            load_head(0)
            for k in range(len(units) + LA):
                if k < len(units):
                    s1(k)
                if k - LA >= 0:
                    s2(k - LA)
            kk.barrier()

    def oproj_phase(li, Wd, KC, h_in, h_out, tl):
        with contextlib.ExitStack() as ph:
            W = sbt(ph, "ow", [128, KC, D], BF16)
            R_W = Res()
            kk.dma("pool", W[:], Wd.rearrange("(c p) d -> p c d", p=128), w=[R_W])
            ht = [sbt(ph, "oht%d" % i, [128, DC, NT], F32) for i in range(2)]
            R_ht = [Res(), Res()]
            ot = [sbt(ph, "oot%d" % i, [128, KC, NT], BF16) for i in range(2)]
            R_ot = [Res(), Res()]
            for ti, (t0, n, w_) in enumerate(tl):
                b = ti % 2
                kk.dma("sp", ht[b][:, :, :n], h_in[:, :, t0:t0 + n], w=[R_ht[b]])
                kk.dma("sp", ot[b][:, :, :n], oT_d[:, 0:KC, t0:t0 + n], w=[R_ot[b]])
                for d in range(DC):
                    (pO, RO) = bankO()
                    mm_acc(pO[:, :n], RO, [(W[:, c, d * 128:(d + 1) * 128], ot[b][:, c, :n], [R_W, R_ot[b]]) for c in range(KC)])
                    kk.op("dve", lambda e, d=d, pO=pO, b=b: e.scalar_tensor_tensor(
                        out=ht[b][:, d, :n], in0=pO[:, :n], scalar=gat[:, li, 1, d, w_:w_ + 1],
                        in1=ht[b][:, d, :n], op0=ALU.mult, op1=ALU.add),
                        r=[RO, R_mods, R_ht[b]], w=[R_ht[b]])
                kk.dma("sp", h_out[:, :, t0:t0 + n], ht[b][:, :, :n], r=[R_ht[b]])
            kk.barrier()

    def ret_phase(do_ctx):
        with contextlib.ExitStack() as ph:
            rt = sbt(ph, "rtabs", [128, 8, 128], F32)
            R_rt = Res()
            kk.dma("sp", rt[:], ret_tabs, w=[R_rt])
            lg = sbt(ph, "rlg", [128, 8], F32)
            kk.dma("sp", lg[:], ret_decay.partition_broadcast(128), w=[R_rt])
            kk.op("act", lambda e: e.activation(out=lg[:], in_=lg[:], func=AF.Sigmoid), r=[R_rt], w=[R_rt])
            kk.op("act", lambda e: e.activation(out=lg[:], in_=lg[:], func=AF.Ln), r=[R_rt], w=[R_rt])
            ident = sbt(ph, "rident", [128, 128], BF16)
            kk.op("dve", lambda e: e.tensor_copy(out=ident[:], in_=rt[:, 7, :]), r=[R_rt], w=[R_rt])
            gng = sbt(ph, "rgng", [128, 2048], F32)
            kk.dma("sp", gng[:], ret_gn.partition_broadcast(128), w=[R_rt])
            tb = sbt(ph, "rtb", [128, 4, 5, 128], F32)
            tA = sbt(ph, "rtA", [128, 128], F32)
            tB = sbt(ph, "rtB", [128, 128], F32)
            for h in range(4):
                lf = lg[:, h:h + 1]
                lb = lg[:, 4 + h:5 + h]
                for (slot, tab, sc_) in ((0, 0, lf), (1, 1, lb), (2, 2, lf), (3, 3, lb)):
                    kk.op("act", lambda e, h=h, slot=slot, tab=tab, sc_=sc_: e.activation(
                        out=tb[:, h, slot, :], in_=rt[:, tab, :], func=AF.Exp, scale=sc_), r=[R_rt], w=[R_rt])
                kk.op("act", lambda e, lf=lf: e.activation(out=tA[:], in_=rt[:, 4, :], func=AF.Exp, scale=lf), r=[R_rt], w=[R_rt])
                kk.op("act", lambda e, lb=lb: e.activation(out=tB[:], in_=rt[:, 5, :], func=AF.Exp, scale=lb), r=[R_rt], w=[R_rt])
                kk.op("dve", lambda e: e.tensor_tensor(out=tA[:], in0=tA[:], in1=tB[:], op=ALU.subtract), r=[R_rt], w=[R_rt])
                kk.op("dve", lambda e: e.tensor_tensor(out=tA[:], in0=tA[:], in1=rt[:, 6, :], op=ALU.mult), r=[R_rt], w=[R_rt])
                kk.op("dve", lambda e, h=h: e.tensor_tensor(out=tb[:, h, 4, :], in0=tA[:], in1=tB[:], op=ALU.add), r=[R_rt], w=[R_rt])
                kk.op("dve", lambda e, h=h: e.tensor_scalar(out=tb[:, h, 2:5, :], in0=tb[:, h, 2:5, :], scalar1=1.0 / 16.0,
                                                            scalar2=None, op0=ALU.mult), r=[R_rt], w=[R_rt])
            qh = [sbt(ph, "rqh%d" % i, [128, 2, T_ALL], BF16) for i in range(2)]
            kh = [sbt(ph, "rkh%d" % i, [128, 2, T_ALL], BF16) for i in range(2)]
            vh = [sbt(ph, "rvh%d" % i, [128, 34, 512], BF16) for i in range(2)]
            R_qh, R_kh, R_vh = [Res(), Res()], [Res(), Res()], [Res(), Res()]
            NPM = 4
            pm = [sbt(ph, "rpm%d" % i, [128, 128], BF16) for i in range(NPM)]
            R_pm = [Res() for _ in range(NPM)]
            ocn = [sbt(ph, "rocn%d" % i, [128, 512], F32) for i in range(4)]
            R_ocn = [Res() for _ in range(4)]
            junk = sbt(ph, "rjunk", [128, 512], F32)
            R_junk = Res()
            gt = [sbt(ph, "rgt%d" % i, [128, 512], F32) for i in range(4)]
            R_gt = [Res() for _ in range(4)]
            gated = [sbt(ph, "rgated%d" % i, [128, 512], BF16) for i in range(4)]
            R_gated = [Res() for _ in range(4)]
            ost = [sbt(ph, "rost%d" % i, [128, 4, 128], BF16) for i in range(4)]
            R_ost = [Res() for _ in range(4)]
            sm = [sbt(ph, "rsm%d" % i, [128, 4], F32) for i in range(4)]
            R_sm = [Res() for _ in range(4)]
            psT = ps_stat[:].bitcast(BF16)
            qtiles = list(range(32)) + ([32, 33] if do_ctx else [])

            def load_head(h):
                hs = h % 2
                kk.dma("sp", qh[hs][:], qT_d[:, 2 * h:2 * h + 2, :], w=[R_qh[hs]])
                kk.dma("sp", kh[hs][:], kT_d[:, 2 * h:2 * h + 2, :], w=[R_kh[hs]])
                kk.dma("sp", vh[hs][:], v_d[:, h * 512:(h + 1) * 512].rearrange("(n p) d -> p n d", p=128), w=[R_vh[hs]])

            stream = []
            tix = 0
            for h in range(4):
                for qi in qtiles:
                    pairs = []
                    if qi < 32:
                        for m in range(2):
                            pairs.append((32 + m, "f", qi + 2 - m))
                        for kj in range(32):
                            dl = qi - kj
                            pairs.append((kj, "f" if dl > 0 else ("i" if dl == 0 else "b"), abs(dl)))
                        for m in range(2):
                            pairs.append((32 + m, "b", 32 + m - qi))
                    else:
                        for m in range(2):
                            dl = (qi - 32) - m
                            pairs.append((32 + m, "f" if dl > 0 else ("i" if dl == 0 else "b"), abs(dl)))
                    for idx, (kj, mode, dl) in enumerate(pairs):
                        stream.append((h, qi, tix, idx, len(pairs), kj, mode, dl))
                    tix += 1
            tile_bank = {}
            S_of = {}
            deferred = []

            def s1(k):
                h, qi, tx, idx, npairs, kj, mode, dl = stream[k]
                hs = h % 2
                q0 = qi * 128
                if idx == 0:
                    kk.dma("sp", gt[tx % 4][:], g_d[q0:q0 + 128, h * 512:(h + 1) * 512], w=[R_gt[tx % 4]])
                (pS, RS) = bankS()
                mm_acc(pS[:, :128], RS, [(kh[hs][:, dc, kj * 128:(kj + 1) * 128], qh[hs][:, dc, q0:q0 + 128],
                                          [R_kh[hs], R_qh[hs]]) for dc in range(2)])
                p4 = k % NPM
                if mode == "i":
                    kk.op("dve", lambda e, pS=pS, p4=p4, h=h: e.tensor_tensor(
                        out=pm[p4][:], in0=pS[:, :128], in1=tb[:, h, 4, :], op=ALU.mult),
                        r=[RS, R_rt], w=[R_pm[p4]])
                else:
                    us, ws = (2, 0) if mode == "f" else (3, 1)
                    kk.op("dve", lambda e, pS=pS, p4=p4, h=h, us=us, ws=ws, dl=dl: e.scalar_tensor_tensor(
                        out=pm[p4][:], in0=pS[:, :128], scalar=tb[:, h, us, dl - 1:dl], in1=tb[:, h, ws, :],
                        op0=ALU.mult, op1=ALU.mult), r=[RS, R_rt], w=[R_pm[p4]])

            def epiA(h, qi, tx, pO, RO):
                t2 = tx % 4
                kk.op("dve", lambda e: e.reduce_sum(out=sm[t2][:, 0:1], in_=pO[:, :512], axis=AX.X), r=[RO], w=[R_sm[t2]])
                kk.op("dve", lambda e: e.tensor_scalar(out=sm[t2][:, 0:1], in0=sm[t2][:, 0:1], scalar1=-1.0 / 512.0, scalar2=None,
                                                       op0=ALU.mult), r=[R_sm[t2]], w=[R_sm[t2]])
                kk.op("dve", lambda e: e.tensor_scalar(out=ocn[t2][:], in0=pO[:, :512], scalar1=sm[t2][:, 0:1], scalar2=None,
                                                       op0=ALU.add), r=[RO, R_sm[t2]], w=[R_ocn[t2]])
                kk.op("act", lambda e: e.activation(out=junk[:], in_=ocn[t2][:], func=AF.Square, accum_out=sm[t2][:, 1:2]),
                      r=[R_ocn[t2]], w=[R_sm[t2], R_junk])
                kk.op("act", lambda e: e.activation(out=sm[t2][:, 2:3], in_=sm[t2][:, 1:2], func=AF.Sqrt, scale=1.0 / 512.0, bias=EPS),
                      r=[R_sm[t2]], w=[R_sm[t2]])

            def epiB(h, qi, tx):
                t2 = tx % 4
                kk.op("dve", lambda e: e.reciprocal(out=sm[t2][:, 2:3], in_=sm[t2][:, 2:3]), r=[R_sm[t2]], w=[R_sm[t2]])
                kk.op("dve", lambda e: e.scalar_tensor_tensor(
                    out=ocn[t2][:], in0=ocn[t2][:], scalar=sm[t2][:, 2:3], in1=gng[:, h * 512:(h + 1) * 512],
                    op0=ALU.mult, op1=ALU.mult), r=[R_ocn[t2], R_sm[t2], R_rt], w=[R_ocn[t2]])
                kk.op("dve", lambda e: e.tensor_tensor(out=gated[t2][:], in0=ocn[t2][:], in1=gt[t2][:], op=ALU.mult),
                      r=[R_ocn[t2], R_gt[t2]], w=[R_gated[t2]])

            def epiC(h, qi, tx):
                t2 = tx % 4
                q0 = qi * 128
                for blk in range(4):
                    kk.op("pe", lambda e, blk=blk: e.transpose(psT[:, blk * 128:(blk + 1) * 128],
                                                               gated[t2][:, blk * 128:(blk + 1) * 128], ident[:]),
                          r=[R_gated[t2], R_rt], w=[R_ps_stat], inc=(blk == 3))
                kk.op("act", lambda e: e.activation(out=ost[t2][:].rearrange("p a b -> p (a b)"), in_=psT[:, 0:512],
                                                    func=AF.Identity), r=[R_ps_stat], w=[R_ost[t2]])
                kk.dma("sp", oT_d[:, 4 * h:4 * h + 4, q0:q0 + 128], ost[t2][:], r=[R_ost[t2]])

            def s2(k):
                h, qi, tx, idx, npairs, kj, mode, dl = stream[k]
                hs = h % 2
                if idx == 0:
                    tile_bank[tx] = bankO()
                    if qi == qtiles[0] and h + 1 < 4:
                        load_head(h + 1)
                (pO, RO) = tile_bank[tx]
                p4 = k % NPM
                kk.op("pe", lambda e: e.matmul(pO[:, :512], pm[p4][:], vh[hs][:, kj, :], start=(idx == 0), stop=(idx == npairs - 1)),
                      r=[R_pm[p4], R_vh[hs]], w=[RO])
                if idx == npairs - 1:
                    epiA(h, qi, tx, pO, RO)
                    deferred.append((k + 5, lambda: epiB(h, qi, tx)))
                    deferred.append((k + 10, lambda: epiC(h, qi, tx)))

            LA = 3
            load_head(0)
            for k in range(len(stream) + LA):
                if k < len(stream):
                    s1(k)
                if k - LA >= 0:
                    s2(k - LA)
                    while deferred and deferred[0][0] <= k - LA:
                        deferred.pop(0)[1]()
            while deferred:
                deferred.pop(0)[1]()
            kk.barrier()

    def pool_phase(li, h_in, h_out, tl, do_ctx):
        with contextlib.ExitStack() as ph:
            ht = [sbt(ph, "qht%d" % i, [128, DC, NT], F32) for i in range(2)]
            R_ht = [Res(), Res()]
            xf = [sbt(ph, "qxf%d" % i, [128, DC, NT], F32) for i in range(2)]
            R_xf = [Res(), Res()]
            o = norm_mod(ph, "q")
            for ti, (t0, n, w_) in enumerate(tl):
                b = ti % 2
                kk.dma("sp", ht[b][:, :, :n], h_in[:, :, t0:t0 + n], w=[R_ht[b]])
                emit_rstd(o, ht[b], R_ht[b], n)
                emit_xl(o, ht[b], R_ht[b], n, li, 1, w_, xf[b], R_xf[b])
                kk.dma("sp", xl_d[:, :, t0:t0 + n], xf[b][:, :, :n], r=[R_xf[b]])
            kk.barrier()
        with contextlib.ExitStack() as ph:
            PAD = 8
            TB = T_LAT + 2 * PAD
            X = sbt(ph, "qX", [128, 2, TB], F32)
            Y = sbt(ph, "qY", [128, 2, TB], F32)
            Zb = sbt(ph, "qZ", [128, 2, TB], F32)
            R_X, R_Y, R_Z = Res(), Res(), Res()
            icn = sbt(ph, "qicn", [128, T_LAT], F32)
            R_icn = Res()
            pbf = sbt(ph, "qpb", [128, 2, T_LAT], BF16)
            R_pb = Res()
            wg = sbt(ph, "qwg", [128, 2, 256], BF16)
            R_wg = Res()
            psc = sbt(ph, "qpsc", [128, DC], F32)
            gp = sbt(ph, "qgp", [128, DC, 2], F32)
            R_gp = Res()
            kk.dma("sp", psc[:], pool_sc, w=[R_gp])
            for w_ in range(2):
                kk.op("dve", lambda e, w_=w_: e.tensor_tensor(out=gp[:, :, w_], in0=gat[:, li, 1, :, w_], in1=psc[:], op=ALU.mult),
                      r=[R_gp, R_mods], w=[R_gp])
            hc = [sbt(ph, "qhc%d" % i, [128, 512], F32) for i in range(2)]
            R_hc = [Res(), Res()]
            seqs = [(0, T_LAT, 0)] + ([(T_LAT, T_CTX, 1)] if do_ctx else [])
            hi_ = 0
            for g_ in range(4):
                kk.dma("pool", wg[:], pool_w[g_].rearrange("(kc p) n -> p kc n", p=128), w=[R_wg])
                for (s0, T, w_) in seqs:
                    L = T + 2 * PAD
                    kk.op("pool", lambda e, L=L: e.memset(X[:, :, 0:L], 0.0), w=[R_X])
                    kk.dma("sp", X[:, :, PAD:PAD + T], xl_d[:, 2 * g_:2 * g_ + 2, s0:s0 + T], w=[R_X])
                    kk.dma("sp", icn[:, :T], pool_icnt[g_:g_ + 1, s0:s0 + T].partition_broadcast(128), w=[R_icn])
                    kk.op("dve", lambda e, L=L: e.memset(Y[:, :, 0:L], 0.0), w=[R_Y])
                    kk.op("dve", lambda e, L=L: e.tensor_tensor(
                        out=Y[:, :, 1:L], in0=X[:, :, 1:L], in1=X[:, :, 0:L - 1], op=ALU.add), r=[R_X], w=[R_Y])
                    lv_src, R_lsrc = Y, R_Y
                    sh = 1
                    for lev in range(g_):
                        a_, Ra_, b_, Rb_ = (Y, R_Y, Zb, R_Z) if lev % 2 == 0 else (Zb, R_Z, Y, R_Y)
                        kk.op("dve", lambda e, L=L, b_=b_: e.memset(b_[:, :, 0:L], 0.0), w=[Rb_])
                        kk.op("dve", lambda e, L=L, a_=a_, b_=b_, sh=sh: e.tensor_tensor(
                            out=b_[:, :, sh:L - sh], in0=a_[:, :, 0:L - 2 * sh], in1=a_[:, :, 2 * sh:L], op=ALU.add),
                            r=[Ra_], w=[Rb_])
                        lv_src, R_lsrc = b_, Rb_
                        sh *= 2
                    for c in range(2):
                        kk.op("dve", lambda e, c=c, T=T, lv_src=lv_src: e.tensor_tensor(
                            out=lv_src[:, c, PAD:PAD + T], in0=lv_src[:, c, PAD:PAD + T], in1=icn[:, :T], op=ALU.mult),
                            r=[R_lsrc, R_icn], w=[R_lsrc])
                    kk.op("dve", lambda e, T=T, lv_src=lv_src: e.tensor_tensor(
                        out=pbf[:, :, :T], in0=lv_src[:, :, PAD:PAD + T], in1=X[:, :, PAD:PAD + T], op=ALU.subtract),
                        r=[R_lsrc, R_X], w=[R_pb])
                    for m in range(2):
                        c = 2 * g_ + m
                        for tt0 in range(0, T, 512):
                            nn = min(512, T - tt0)
                            hb = hi_ % 2
                            hi_ += 1
                            kk.dma("sp", hc[hb][:, :nn], h_in[:, c, s0 + tt0:s0 + tt0 + nn], w=[R_hc[hb]])
                            (pO, RO) = bankO()
                            mm_acc(pO[:, :nn], RO, [(wg[:, kc, m * 128:(m + 1) * 128], pbf[:, kc, tt0:tt0 + nn], [R_wg, R_pb])
                                                    for kc in range(2)])
                            kk.op("dve", lambda e, pO=pO, hb=hb, nn=nn, c=c, w_=w_: e.scalar_tensor_tensor(
                                out=hc[hb][:, :nn], in0=pO[:, :nn], scalar=gp[:, c, w_:w_ + 1], in1=hc[hb][:, :nn],
                                op0=ALU.mult, op1=ALU.add), r=[RO, R_gp, R_hc[hb]], w=[R_hc[hb]])
                            kk.dma("sp", h_out[:, c, s0 + tt0:s0 + tt0 + nn], hc[hb][:, :nn], r=[R_hc[hb]])
            kk.barrier()

    kinds = cfg.get("kinds", ["ret", "nat", "pool", "swa"])
    cur = xT
    nxt = 0
    lat_tiles = [t for t in tiles if t[2] == 0]
    for li in range(n_layers):
        kind = kinds[li]
        last = (li == n_layers - 1)
        ctx_live = (not last) or kind != "pool"
        tl1 = tiles if ctx_live else lat_tiles
        tl2 = lat_tiles if last else tiles
        ffn_phase(li, 0, 0, cur, hbufs[nxt], tl1)
        cur = hbufs[nxt]
        nxt ^= 1
        if mixers:
            if kind == "pool":
                pool_phase(li, cur, hbufs[nxt], tl2, not last)
            else:
                proj_phase(li, kind, cur, tl1)
                if kind == "ret":
                    ret_phase(not last)
                    oproj_phase(li, ret_w_out, 16, cur, hbufs[nxt], tl2)
                else:
                    attn_phase(kind, not last)
                    oproj_phase(li, nat_w_o if kind == "nat" else swa_w_o, 8, cur, hbufs[nxt], tl2)
            cur = hbufs[nxt]
            nxt ^= 1
        ffn_phase(li, 1, 2, cur, hbufs[nxt], tl2)
        cur = hbufs[nxt]
        nxt ^= 1
    final_phase(cur)
    kk.barrier()
    kk.ninst_total = kk.ninst
    nc._kk = kk
    return nc


def _fm(a):
    t = a.shape[0]
    return np.ascontiguousarray(a.T.reshape(DC, 128, t).transpose(1, 0, 2))


def _vec_fm(v):
    lead = v.shape[:-1]
    x = v.reshape(*lead, DC, 128)
    x = np.moveaxis(x, -1, 0)
    return np.ascontiguousarray(x)


def _consts():
    c = {}
    p = np.arange(128, dtype=np.float64)[:, None]
    i = np.arange(128, dtype=np.float64)[None, :]
    tabs = np.zeros((128, 8, 128), np.float64)
    tabs[:, 0] = i + 0 * p
    tabs[:, 1] = 127 - i + 0 * p
    tabs[:, 2] = 128 * i + 128 - p
    tabs[:, 3] = 128 * i + p + 1
    tabs[:, 4] = np.maximum(i - p, 0)
    tabs[:, 5] = np.maximum(p - i, 0)
    tabs[:, 6] = (i >= p)
    tabs[:, 7] = (i == p)
    c["ret_tabs"] = tabs.astype(np.float32)
    t = np.arange(T_LAT, dtype=np.float32)[None, :]
    inv = (10000.0 ** (-(np.arange(0, 256, 2, dtype=np.float32)) / 256.0)).astype(np.float32)[:, None]
    ang = (t * inv).astype(np.float32)
    cs = np.zeros((128, 2, T_ALL), np.float32)
    cs[:, 0, :T_LAT] = np.cos(ang)
    cs[:, 1, :T_LAT] = np.sin(ang)
    cs[:, 0, T_LAT:] = 1.0
    c["ret_cs"] = cs
    d = np.arange(128) % 64
    tt = np.arange(T_LAT)
    pos = np.where((d < 32)[:, None], (tt // 64)[None, :], (tt % 64)[None, :]).astype(np.float32)
    inv16 = (10000.0 ** (-(np.arange(0, 32, 2, dtype=np.float32)) / 32.0)).astype(np.float32)
    invd = inv16[(d % 32) % 16][:, None]
    ang = (pos * invd).astype(np.float32)
    sign = np.where((d % 32) < 16, -1.0, 1.0).astype(np.float32)[:, None]
    cs = np.zeros((128, 2, T_ALL), np.float32)
    cs[:, 0, :T_LAT] = np.cos(ang)
    cs[:, 1, :T_LAT] = np.sin(ang) * sign
    cs[:, 0, T_LAT:] = 1.0
    c["swa_cs"] = cs
    NEG = -30000.0
    j = np.arange(128)[:, None]
    q = np.arange(128)[None, :]
    sb_ = np.zeros((128, 3, 5, 128), np.float32)
    for typ in range(3):
        sb_[:, typ, 0] = np.where(j >= q, 0.0, NEG)
        sb_[:, typ, 2] = np.where(j <= q, 0.0, NEG)
    sb_[:, 1, 0] = NEG
    sb_[:, 2, 2] = NEG
    c["swa_bias"] = sb_
    ic = np.zeros((4, T_ALL), np.float32)
    for g_, w in enumerate((2, 4, 8, 16)):
        for (s0, T) in ((0, T_LAT), (T_LAT, T_CTX)):
            tq = np.arange(T)
            lo = np.clip(tq - w // 2, 0, T)
            hi = np.clip(tq + w // 2, 0, T)
            ic[g_, s0:s0 + T] = 1.0 / (hi - lo).astype(np.float32)
    c["pool_icnt"] = ic
    return c


def _nat_bias(rpb):
    NEG = -30000.0
    out = np.zeros((16, 128, 5, 7, 128), np.float32)
    u = (np.arange(128) // 64)
    n = (np.arange(128) % 64)
    cfgs = [(2, 0), (0, 0), (1, 0), (30, 27), (31, 27)]
    for typ, (qi, base) in enumerate(cfgs):
        r = (2 * qi + u)[None, :]
        cq = n[None, :]
        r0 = np.clip(r - 4, 0, 56)
        c0 = np.clip(cq - 8, 0, 48)
        for ch in range(5):
            a = (2 * (base + ch) + u)[:, None]
            nk = n[:, None]
            valid = (a >= r0) & (a < r0 + 8) & (nk >= c0) & (nk < c0 + 16)
            ri = np.clip(a - r + 7, 0, 14)
            ci = np.clip(nk - cq + 15, 0, 30)
            vals = rpb[:, ri, ci]
            out[:, :, typ, ch, :] = np.where(valid[None], vals, NEG)
    return out


def make_in_maps(inputs, cores=range(N_CORES)):
    f = lambda a: np.ascontiguousarray(np.asarray(a, dtype=np.float32))
    x, c, ctx, c_ctx = f(inputs["x"]), f(inputs["c"]), f(inputs["ctx"]), f(inputs["c_ctx"])
    b_mod = f(inputs["b_mod"])
    shared = {
        "w_mod": f(inputs["w_mod"]),
        "bmodT": np.ascontiguousarray(b_mod.reshape(DEPTH, 72, 128).transpose(2, 0, 1)),
        "normgT": _vec_fm(f(inputs["norm_g"])),
        "fnormgT": _vec_fm(f(inputs["final_norm_g"])),
        "ffn_w_in": f(inputs["ffn_w_in"]),
        "ffn_w_out": f(inputs["ffn_w_out"]),
        "ret_w_in": f(inputs["ret_w_in"][0]),
        "ret_w_out": f(inputs["ret_w_out"][0]),
        "ret_gn": f(inputs["ret_gn_g"][0:1]),
        "ret_decay": np.ascontiguousarray(np.concatenate([f(inputs["ret_decay_f"][0]), f(inputs["ret_decay_b"][0])])[None, :]),
        "nat_w_qkv": f(inputs["nat_w_qkv"][0]),
        "nat_w_o": f(inputs["nat_w_o"][0]),
        "nat_bias": _nat_bias(f(inputs["nat_rpb"][0])),
        "pool_w": f(inputs["pool_w"][0]),
        "pool_sc": _vec_fm(f(inputs["pool_scale"][0])),
        "swa_w_qkv": f(inputs["swa_w_qkv"][0]),
        "swa_w_o": f(inputs["swa_w_o"][0]),
        "swa_sink": f(inputs["swa_sink"][0:1]),
    }
    wq = shared["swa_w_qkv"]
    dd = np.arange(64)
    partner = np.where((dd % 32) < 16, dd + 16, dd - 16)
    colq = (np.arange(16)[:, None] * 64 + partner[None, :]).reshape(-1)
    colk = 1024 + (np.arange(4)[:, None] * 64 + partner[None, :]).reshape(-1)
    shared["swa_w_perm"] = np.ascontiguousarray(wq[:, np.concatenate([colq, colk])])
    shared.update(_consts())
    maps = []
    for b in cores:
        m = dict(shared)
        m["xT"] = _fm(np.concatenate([x[b], ctx[b]], axis=0))
        m["cT"] = np.ascontiguousarray(np.stack([c[b], c_ctx], axis=0).reshape(2, DC, 128).transpose(2, 1, 0))
        maps.append(m)
    return maps


_NC_CACHE = {}


def kernel(**inputs):
    if "nc" not in _NC_CACHE:
        _NC_CACHE["nc"] = build()
    nc = _NC_CACHE["nc"]
    in_maps = make_in_maps(inputs)
    res = run_bass_kernel_spmd(nc, in_maps, core_ids=list(range(N_CORES)))
    outs = []
    for b in range(N_CORES):
        o = res.results[b]["outT"]
        outs.append(o.transpose(1, 0, 2).reshape(D, T_LAT).T)
    return np.ascontiguousarray(np.stack(outs, axis=0).astype(np.float32))
```

```python
import contextlib
import numpy as np
import concourse.bass as bass
import concourse.mybir as mybir
from concourse.bass_utils import run_bass_kernel_spmd

F32 = mybir.dt.float32
BF16 = mybir.dt.bfloat16
AF = mybir.ActivationFunctionType
ALU = mybir.AluOpType
AX = mybir.AxisListType

D = 1024
DC = 8
T_LAT = 4096
T_CTX = 256
T_ALL = T_LAT + T_CTX
DEPTH = 4
FF = 2816
FJ = 22
EPS = 1e-6
NT = 256
N_CORES = 4


class Res:
    __slots__ = ("name", "w", "r")

    def __init__(self, name=""):
        self.name = name
        self.w = None
        self.r = {}


class _Eng:
    def __init__(self, kk, name, handle):
        self.name = name
        self.h = handle
        self.sem = kk.new_sem("e_" + name)
        self.count = 0
        self.waited = {}
        self.pend_r = []
        self.pend_w = []


class K:
    def __init__(self, nc, n_dma_sems=16):
        self.nc = nc
        self.st = contextlib.ExitStack()
        self.sems = {}
        self.nsem = 0
        self.eng = {}
        for name, h in (("pe", nc.tensor), ("act", nc.scalar), ("dve", nc.vector),
                        ("pool", nc.gpsimd), ("sp", nc.sync)):
            self.eng[name] = _Eng(self, name, h)
        self.dma_pool = {}
        for q in ("sp", "pool"):
            self.dma_pool[q] = [[self.new_sem("d_%s%d" % (q, i)), 0] for i in range(n_dma_sems)]
        self.dma_rr = {"sp": 0, "pool": 0}
        self.ninst = 0

    def new_sem(self, name):
        s = self.st.enter_context(self.nc.semaphore(name))
        sid = self.nsem
        self.nsem += 1
        self.sems[sid] = s
        return sid

    def _wait(self, e, ev):
        sid, val = ev
        if e.waited.get(sid, 0) >= val:
            return
        e.waited[sid] = val
        e.h.wait_ge(self.sems[sid], val)
        self.ninst += 1

    def _deps(self, e, r, w, nowaw=False):
        evs = {}

        def add(ev):
            if ev is not None and evs.get(ev[0], 0) < ev[1]:
                evs[ev[0]] = ev[1]
        for x in r:
            add(x.w)
        for x in w:
            if not nowaw:
                add(x.w)
            for sid, val in x.r.items():
                add((sid, val))
        for sid, val in evs.items():
            if e.name == "pe" and sid == e.sem:
                continue
            self._wait(e, (sid, val))

    def _commit(self, ev, r, w):
        for x in r:
            if x.r.get(ev[0], 0) < ev[1]:
                x.r[ev[0]] = ev[1]
        for x in w:
            x.w = ev
            x.r = {}

    def op(self, eng, fn, r=(), w=(), inc=True):
        e = self.eng[eng]
        self._deps(e, r, w)
        inst = fn(e.h)
        self.ninst += 1
        if not inc:
            e.pend_r.extend(r)
            e.pend_w.extend(w)
            return None
        if e.count >= 30000:
            e.sem = self.new_sem("e_%s_%d" % (eng, self.nsem))
            e.count = 0
        e.count += 1
        inst.then_inc(self.sems[e.sem], 1)
        ev = (e.sem, e.count)
        self._commit(ev, list(r) + e.pend_r, list(w) + e.pend_w)
        e.pend_r = []
        e.pend_w = []
        return ev

    def dma(self, q, out, in_, r=(), w=(), nowaw=False):
        e = self.eng[q]
        self._deps(e, r, w, nowaw=nowaw)
        pool = self.dma_pool[q]
        i = self.dma_rr[q]
        self.dma_rr[q] = (i + 1) % len(pool)
        slot = pool[i]
        if slot[1] > 0:
            self._wait(e, (slot[0], slot[1]))
        slot[1] += 16
        e.h.dma_start(out=out, in_=in_).then_inc(self.sems[slot[0]], 16)
        self.ninst += 1
        ev = (slot[0], slot[1])
        self._commit(ev, r, w)
        return ev

    def barrier(self, engs=("pe", "act", "dve", "pool", "sp")):
        for x in engs:
            e = self.eng[x]
            for y in self.eng.values():
                if y is not e and y.count > 0:
                    self._wait(e, (y.sem, y.count))
            for pool in self.dma_pool.values():
                for sid, val in pool:
                    if val > 0:
                        self._wait(e, (sid, val))


def build(cfg=None):
    cfg = cfg or {}
    n_layers = cfg.get("n_layers", DEPTH)
    mixers = cfg.get("mixers", True)
    dbg = cfg.get("dbg", False)

    nc = bass.Bass("TRN2", target_bir_lowering=False)
    kk = K(nc)
    st = kk.st

    def dram_in(name, shape, dt=F32):
        return nc.dram_tensor(name, list(shape), dt, kind="ExternalInput").ap()

    xT = dram_in("xT", [128, DC, T_ALL])
    cT = dram_in("cT", [128, DC, 2])
    w_mod = dram_in("w_mod", [DEPTH, D, 9 * D])
    bmodT = dram_in("bmodT", [128, DEPTH, 72])
    normgT = dram_in("normgT", [128, DEPTH, 3, DC])
    fnormgT = dram_in("fnormgT", [128, DC])
    ffn_w_in = dram_in("ffn_w_in", [DEPTH, 2, D, 2 * FF])
    ffn_w_out = dram_in("ffn_w_out", [DEPTH, 2, FF, D])
    outT = nc.dram_tensor("outT", [128, DC, T_LAT], F32, kind="ExternalOutput").ap()
    hA = nc.dram_tensor("hA", [128, DC, T_ALL], F32).ap()
    hB = nc.dram_tensor("hB", [128, DC, T_ALL], F32).ap()
    hbufs = [hA, hB]
    ret_w_in = dram_in("ret_w_in", [D, 6144])
    ret_w_out = dram_in("ret_w_out", [2048, D])
    ret_gn = dram_in("ret_gn", [1, 2048])
    ret_decay = dram_in("ret_decay", [1, 8])
    ret_tabs = dram_in("ret_tabs", [128, 8, 128])
    ret_cs = dram_in("ret_cs", [128, 2, T_ALL])
    nat_w_qkv = dram_in("nat_w_qkv", [D, 3072])
    nat_w_o = dram_in("nat_w_o", [D, D])
    nat_bias = dram_in("nat_bias", [16, 128, 5, 7, 128])
    pool_w = dram_in("pool_w", [4, 256, 256])
    pool_sc = dram_in("pool_sc", [128, DC])
    pool_icnt = dram_in("pool_icnt", [4, T_ALL])
    swa_w_qkv = dram_in("swa_w_qkv", [D, 1536])
    swa_w_perm = dram_in("swa_w_perm", [D, 1280])
    swa_w_o = dram_in("swa_w_o", [D, D])
    swa_sink = dram_in("swa_sink", [1, 16])
    swa_cs = dram_in("swa_cs", [128, 2, T_ALL])
    swa_bias = dram_in("swa_bias", [128, 3, 5, 128])
    qT_d = nc.dram_tensor("qT_d", [128, DC, T_ALL], BF16).ap()
    kT_d = nc.dram_tensor("kT_d", [128, DC, T_ALL], BF16).ap()
    v_d = nc.dram_tensor("v_d", [T_ALL, 2048], BF16).ap()
    g_d = nc.dram_tensor("g_d", [T_ALL, 2048], F32).ap()
    oT_d = nc.dram_tensor("oT_d", [128, 16, T_ALL], BF16).ap()
    xl_d = nc.dram_tensor("xl_d", [128, DC, T_ALL], F32).ap()

    def sb(name, shape, dt):
        return st.enter_context(nc.sbuf_tensor(name, list(shape), dt))

    def ps(name, shape, dt=F32):
        return st.enter_context(nc.psum_tensor(name, list(shape), dt))

    _uid = [0]

    def sbt(ph, name, shape, dt):
        _uid[0] += 1
        return ph.enter_context(nc.sbuf_tensor("%s_%d" % (name, _uid[0]), list(shape), dt))

    ones_f = sb("ones_f", [128, 128], F32)
    mods = sb("mods", [128, DEPTH, 72, 2], F32)
    gsc = sb("gsc", [128, DEPTH, 3, DC, 2], F32)
    gat = sb("gat", [128, DEPTH, 3, DC, 2], F32)
    ng = sb("ng", [128, DEPTH, 3, DC], F32)
    fng = sb("fng", [128, DC], F32)
    bm = sb("bm", [128, DEPTH, 72], F32)
    sT = sb("sT", [128, DC, 2], F32)
    R_const = Res("const")
    R_mods = Res("mods")

    kk.op("dve", lambda e: e.memset(ones_f[:], 1.0), w=[R_const])
    kk.dma("sp", ng[:], normgT, w=[R_const])
    kk.dma("sp", fng[:], fnormgT, w=[R_const])
    kk.dma("sp", bm[:], bmodT, w=[R_const])
    kk.dma("sp", sT[:], cT, w=[R_const])
    kk.op("act", lambda e: e.activation(out=sT[:], in_=sT[:], func=AF.Silu), r=[R_const], w=[R_const])

    ps_stat = ps("ps_stat", [128, 512])
    ps_a = [ps("ps_a%d" % i, [128, 512]) for i in range(2)]
    ps_b = [ps("ps_b%d" % i, [128, 512]) for i in range(2)]
    ps_o = [ps("ps_o%d" % i, [128, 512]) for i in range(2)]
    R_ps_stat = Res()
    R_ps_a = [Res(), Res()]
    R_ps_b = [Res(), Res()]
    R_ps_o = [Res(), Res()]
    ps_x = ps("ps_x", [128, 512])
    R_ps_x = Res()
    poolS = [(ps_a[0], R_ps_a[0]), (ps_a[1], R_ps_a[1]), (ps_b[0], R_ps_b[0]), (ps_b[1], R_ps_b[1])]
    poolO = [(ps_o[0], R_ps_o[0]), (ps_o[1], R_ps_o[1]), (ps_x, R_ps_x)]
    _rrS = [0]
    _rrO = [0]

    def bankS():
        _rrS[0] = (_rrS[0] + 1) % len(poolS)
        return poolS[_rrS[0]]

    def bankO():
        _rrO[0] = (_rrO[0] + 1) % len(poolO)
        return poolO[_rrO[0]]

    def mm_acc(out_ap, R_out, pairs):
        last = len(pairs) - 1
        ev = None
        for idx, (l_, r_, rd) in enumerate(pairs):
            ev = kk.op("pe", lambda e, l_=l_, r_=r_, idx=idx: e.matmul(out_ap, l_, r_, start=(idx == 0), stop=(idx == last)),
                       r=rd, w=[R_out], inc=(idx == last))
        return ev

    identf = sb("identf", [128, 128], F32)
    kk.dma("sp", identf[:], ret_tabs[:, 7, :], w=[R_const])
    with contextlib.ExitStack() as ph:
        wm = [sbt(ph, "wm%d" % i, [128, DC, 1024], F32) for i in range(2)]
        R_wm = [Res(), Res()]
        modrow = sbt(ph, "modrow", [2, 9 * D], F32)
        R_modrow = Res()
        blk = 0
        for li in range(n_layers):
            for nb in range(9):
                s = blk % 2
                blk += 1
                src = w_mod[li, :, nb * 1024:(nb + 1) * 1024].rearrange("(kc p) n -> p kc n", p=128)
                kk.dma("sp", wm[s][:], src, w=[R_wm[s]])
                for half in range(2):
                    (pS, RS) = bankS()
                    mm_acc(pS[0:2, :512], RS, [(sT[:, kc, :], wm[s][:, kc, half * 512:(half + 1) * 512], [R_wm[s], R_const])
                                               for kc in range(DC)])
                    c0 = nb * 1024 + half * 512
                    kk.op("act", lambda e, pS=pS, c0=c0: e.activation(out=modrow[0:2, c0:c0 + 512], in_=pS[0:2, :512],
                                                                       func=AF.Identity), r=[RS], w=[R_modrow])
            for n in range(72):
                kk.op("pe", lambda e, n=n: e.transpose(ps_stat[:, n * 2:(n + 1) * 2], modrow[0:2, n * 128:(n + 1) * 128],
                                                       identf[0:2, 0:2]),
                      r=[R_modrow, R_const], w=[R_ps_stat], inc=(n == 71))
            kk.op("dve", lambda e, li=li: e.tensor_tensor(
                out=mods[:, li, :, :], in0=ps_stat[:, 0:144].rearrange("p (n w) -> p n w", w=2),
                in1=bm[:, li, :].unsqueeze(2).to_broadcast([128, 72, 2]),
                op=ALU.add), r=[R_ps_stat, R_const], w=[R_mods])
        kk.barrier()

    for li in range(n_layers):
        for j in range(3):
            sc = mods[:, li, (j * 3 + 1) * 8:(j * 3 + 2) * 8, :]
            gt = mods[:, li, (j * 3 + 2) * 8:(j * 3 + 3) * 8, :]
            for w_ in range(2):
                kk.op("dve", lambda e, li=li, j=j, w_=w_, sc=sc: e.scalar_tensor_tensor(
                    out=gsc[:, li, j, :, w_], in0=sc[:, :, w_], scalar=1.0, in1=ng[:, li, j, :],
                    op0=ALU.add, op1=ALU.mult), r=[R_mods, R_const], w=[R_mods])
            kk.op("dve", lambda e, li=li, j=j, gt=gt: e.tensor_scalar(
                out=gat[:, li, j, :, :], in0=gt, scalar1=(1.0 if j == 1 else 0.5), scalar2=None,
                op0=ALU.mult), r=[R_mods], w=[R_mods])
    kk.barrier()

    tiles = [(t0, NT, 0) for t0 in range(0, T_LAT, NT)] + [(T_LAT, T_CTX, 1)]

    def norm_mod(ph, name):
        o = {}
        o["sqt"] = sbt(ph, name + "_sqt", [128, DC, NT], F32)
        o["Rsqt"] = Res()
        o["ssum"] = sbt(ph, name + "_ssum", [128, NT], F32)
        o["Rssum"] = Res()
        o["rstd"] = sbt(ph, name + "_rstd", [128, NT], F32)
        o["Rrstd"] = Res()
        return o

    def _emit(steps, eng, fn, r, w):
        if steps is None:
            kk.op(eng, fn, r=r, w=w)
        else:
            steps.append(lambda: kk.op(eng, fn, r=r, w=w))

    def emit_rstd(o, ht, R_ht, n, steps=None):
        _emit(steps, "dve", lambda e: e.tensor_tensor(out=o["sqt"][:, :, :n], in0=ht[:, :, :n], in1=ht[:, :, :n], op=ALU.mult),
              [R_ht], [o["Rsqt"]])
        _emit(steps, "dve", lambda e: e.tensor_reduce(out=o["ssum"][:, :n], in_=o["sqt"][:, :, :n].rearrange("p c n -> p n c"),
                                                      axis=AX.X, op=ALU.add), [o["Rsqt"]], [o["Rssum"]])
        _emit(steps, "pe", lambda e: e.matmul(ps_stat[:, :n], ones_f[:], o["ssum"][:, :n], start=True, stop=True),
              [o["Rssum"], R_const], [R_ps_stat])
        _emit(steps, "act", lambda e: e.activation(out=o["rstd"][:, :n], in_=ps_stat[:, :n], func=AF.Sqrt,
                                                   scale=1.0 / D, bias=EPS), [R_ps_stat], [o["Rrstd"]])
        _emit(steps, "dve", lambda e: e.reciprocal(out=o["rstd"][:, :n], in_=o["rstd"][:, :n]), [o["Rrstd"]], [o["Rrstd"]])

    def emit_xl(o, ht, R_ht, n, li, j, w_, dst, R_dst, steps=None):
        _emit(steps, "dve", lambda e: e.tensor_tensor(
            out=o["sqt"][:, :, :n], in0=ht[:, :, :n], in1=o["rstd"][:, :n].unsqueeze(1).to_broadcast([128, DC, n]),
            op=ALU.mult), [R_ht, o["Rrstd"]], [o["Rsqt"]])
        for c in range(DC):
            _emit(steps, "act", lambda e, c=c: e.activation(
                out=dst[:, c, :n], in_=o["sqt"][:, c, :n], func=AF.Identity,
                scale=gsc[:, li, j, c, w_:w_ + 1], bias=mods[:, li, (j * 3) * 8 + c, w_:w_ + 1]),
                [o["Rsqt"], R_mods], [R_dst])

    def ffn_phase(li, s_, j, h_in, h_out, tl):
        with contextlib.ExitStack() as ph:
            win = sbt(ph, "win", [128, DC, 2 * FF], BF16)
            wout = sbt(ph, "wout", [128, FJ, D], BF16)
            R_win = [Res() for _ in range(DC)]
            R_wout = Res()
            for kc in range(DC):
                kk.dma("pool", win[:, kc, :], ffn_w_in[li, s_, kc * 128:(kc + 1) * 128, :], w=[R_win[kc]])
            kk.dma("pool", wout[:], ffn_w_out[li, s_].rearrange("(j p) d -> p j d", p=128), w=[R_wout])
            ht = [sbt(ph, "ht%d" % i, [128, DC, NT], F32) for i in range(2)]
            R_ht = [Res(), Res()]
            xl = [sbt(ph, "xl%d" % i, [128, DC, NT], BF16) for i in range(2)]
            R_xl = [Res(), Res()]
            g = sbt(ph, "g", [128, FJ, NT], BF16)
            R_g = [Res() for _ in range(FJ)]
            sa = [sbt(ph, "sa%d" % i, [128, NT], F32) for i in range(2)]
            R_sa = [Res(), Res()]
            o = norm_mod(ph, "f")

            def prep(ti, steps):
                t0, n, w_ = tl[ti]
                b = ti % 2
                kk.dma("sp", ht[b][:, :, :n], h_in[:, :, t0:t0 + n], w=[R_ht[b]])
                emit_rstd(o, ht[b], R_ht[b], n, steps)
                emit_xl(o, ht[b], R_ht[b], n, li, j, w_, xl[b], R_xl[b], steps)

            prep(0, None)
            for ti, (t0, n, w_) in enumerate(tl):
                b = ti % 2
                steps = []
                if ti + 1 < len(tl):
                    prep(ti + 1, steps)
                for jj in range(FJ):
                    pb = jj % 2
                    for kc in range(DC):
                        kk.op("pe", lambda e, jj=jj, kc=kc, pb=pb: e.matmul(
                            ps_a[pb][:, :n], win[:, kc, jj * 128:(jj + 1) * 128], xl[b][:, kc, :n],
                            start=(kc == 0), stop=(kc == DC - 1)),
                            r=[R_win[kc], R_xl[b]], w=[R_ps_a[pb]], inc=(kc == DC - 1))
                    for kc in range(DC):
                        kk.op("pe", lambda e, jj=jj, kc=kc, pb=pb: e.matmul(
                            ps_b[pb][:, :n], win[:, kc, FF + jj * 128:FF + (jj + 1) * 128], xl[b][:, kc, :n],
                            start=(kc == 0), stop=(kc == DC - 1)),
                            r=[R_win[kc], R_xl[b]], w=[R_ps_b[pb]], inc=(kc == DC - 1))
                    kk.op("act", lambda e, pb=pb: e.activation(out=sa[pb][:, :n], in_=ps_a[pb][:, :n], func=AF.Silu),
                          r=[R_ps_a[pb]], w=[R_sa[pb]])
                    kk.op("dve", lambda e, pb=pb, jj=jj: e.tensor_tensor(out=g[:, jj, :n], in0=sa[pb][:, :n],
                                                                         in1=ps_b[pb][:, :n], op=ALU.mult),
                          r=[R_sa[pb], R_ps_b[pb]], w=[R_g[jj]])
                    if jj >= 2 and steps:
                        steps.pop(0)()
                while steps:
                    steps.pop(0)()
                for d in range(DC):
                    pb = d % 2
                    for jj in range(FJ):
                        kk.op("pe", lambda e, jj=jj, d=d, pb=pb: e.matmul(
                            ps_o[pb][:, :n], wout[:, jj, d * 128:(d + 1) * 128], g[:, jj, :n],
                            start=(jj == 0), stop=(jj == FJ - 1)),
                            r=[R_wout, R_g[jj]], w=[R_ps_o[pb]], inc=(jj == FJ - 1))
                    kk.op("dve", lambda e, d=d, pb=pb, b=b: e.scalar_tensor_tensor(
                        out=ht[b][:, d, :n], in0=ps_o[pb][:, :n], scalar=gat[:, li, j, d, w_:w_ + 1],
                        in1=ht[b][:, d, :n], op0=ALU.mult, op1=ALU.add),
                        r=[R_ps_o[pb], R_mods, R_ht[b]], w=[R_ht[b]])
                kk.dma("sp", h_out[:, :, t0:t0 + n], ht[b][:, :, :n], r=[R_ht[b]])
            kk.barrier()

    def final_phase(h_in):
        with contextlib.ExitStack() as ph:
            ht = [sbt(ph, "fht%d" % i, [128, DC, NT], F32) for i in range(2)]
            R_ht = [Res(), Res()]
            ot = [sbt(ph, "fot%d" % i, [128, DC, NT], F32) for i in range(2)]
            R_ot = [Res(), Res()]
            o = norm_mod(ph, "fn")
            for ti, (t0, n, w_) in enumerate(tiles):
                if w_ == 1:
                    continue
                b = ti % 2
                kk.dma("sp", ht[b][:, :, :n], h_in[:, :, t0:t0 + n], w=[R_ht[b]])
                emit_rstd(o, ht[b], R_ht[b], n)
                for c in range(DC):
                    kk.op("dve", lambda e, c=c, b=b: e.scalar_tensor_tensor(
                        out=ot[b][:, c, :n], in0=ht[b][:, c, :n], scalar=fng[:, c:c + 1], in1=o["rstd"][:, :n],
                        op0=ALU.mult, op1=ALU.mult), r=[R_ht[b], o["Rrstd"], R_const], w=[R_ot[b]])
                kk.dma("sp", outT[:, :, t0:t0 + n], ot[b][:, :, :n], r=[R_ot[b]])
            kk.barrier()

    def proj_phase(li, kind, h_in, tl):
        with contextlib.ExitStack() as ph:
            if kind == "ret":
                ncol = 6144
                W = sbt(ph, "pw", [128, DC, ncol], BF16)
                R_W = Res()
                for kc in range(DC):
                    kk.dma("pool", W[:, kc, :], ret_w_in[kc * 128:(kc + 1) * 128, :], w=[R_W], nowaw=True)
                fm = [(0, 8, qT_d, "ret", 0), (1024, 8, kT_d, "ret", 0)]
                tm = [(2048, 2048, v_d, AF.Identity, "v"), (4096, 2048, g_d, AF.Silu, "g")]
                cs_d = ret_cs
            elif kind == "nat":
                ncol = 3072
                W = sbt(ph, "pw", [128, DC, ncol], BF16)
                R_W = Res()
                kk.dma("pool", W[:], nat_w_qkv.rearrange("(kc p) n -> p kc n", p=128), w=[R_W])
                fm = [(0, 8, qT_d, None, 0), (1024, 8, kT_d, None, 0)]
                tm = [(2048, 1024, v_d, AF.Identity, "v")]
                cs_d = None
            else:
                ncol = 1536 + 1280
                W = sbt(ph, "pw", [128, DC, ncol], BF16)
                R_W = Res()
                kk.dma("pool", W[:, :, 0:1536], swa_w_qkv.rearrange("(kc p) n -> p kc n", p=128), w=[R_W], nowaw=True)
                kk.dma("pool", W[:, :, 1536:2816], swa_w_perm.rearrange("(kc p) n -> p kc n", p=128), w=[R_W], nowaw=True)
                fm = [(0, 8, qT_d, "swa", 1536), (1024, 2, kT_d, "swa", 1536 + 1024)]
                tm = [(1280, 256, v_d, AF.Identity, "v")]
                cs_d = swa_cs
            ht = [sbt(ph, "pht%d" % i, [128, DC, NT], F32) for i in range(2)]
            R_ht = [Res(), Res()]
            xl = sbt(ph, "pxl", [128, DC, NT], BF16)
            R_xl = Res()
            stg = [sbt(ph, "pst%d" % i, [128, DC, NT], BF16) for i in range(2)]
            R_stg = [Res(), Res()]
            stv = sbt(ph, "pstv", [128, 2048], BF16)
            R_stv = Res()
            stgg = sbt(ph, "pstg", [128, 2048], F32)
            R_stgg = Res()
            cs = sbt(ph, "pcs", [128, 2, NT], F32)
            R_cs = Res()
            t1 = sbt(ph, "pt1", [128, NT], F32)
            t2 = sbt(ph, "pt2", [128, NT], F32)
            R_t1, R_t2 = Res(), Res()
            o = norm_mod(ph, "p")
            for ti, (t0, n, w_) in enumerate(tl):
                b = ti % 2
                kk.dma("sp", ht[b][:, :, :n], h_in[:, :, t0:t0 + n], w=[R_ht[b]])
                emit_rstd(o, ht[b], R_ht[b], n)
                emit_xl(o, ht[b], R_ht[b], n, li, 1, w_, xl, R_xl)
                if cs_d is not None:
                    kk.dma("sp", cs[:, :, :n], cs_d[:, :, t0:t0 + n], w=[R_cs])

                def proj(off, m):
                    (pa, Ra) = bankS()
                    mm_acc(pa[:, :n], Ra, [(W[:, kc, off + m * 128:off + (m + 1) * 128], xl[:, kc, :n], [R_W, R_xl])
                                            for kc in range(DC)])
                    return pa, Ra

                def tt(out, a, b_, op, r, w):
                    kk.op("dve", lambda e: e.tensor_tensor(out=out, in0=a, in1=b_, op=op), r=r, w=w)

                for fi, (off, nch, dst, mode, poff) in enumerate(fm):
                    sg, R_sg = stg[fi], R_stg[fi]
                    if mode == "ret":
                        for hh in range(nch // 2):
                            pa, Ra = proj(off, 2 * hh)
                            pb_, Rb = proj(off, 2 * hh + 1)
                            tt(t1[:, :n], pa[:, :n], cs[:, 0, :n], ALU.mult, [Ra, R_cs], [R_t1])
                            tt(t2[:, :n], pb_[:, :n], cs[:, 1, :n], ALU.mult, [Rb, R_cs], [R_t2])
                            tt(sg[:, 2 * hh, :n], t1[:, :n], t2[:, :n], ALU.subtract, [R_t1, R_t2], [R_sg])
                            tt(t1[:, :n], pb_[:, :n], cs[:, 0, :n], ALU.mult, [Rb, R_cs], [R_t1])
                            tt(t2[:, :n], pa[:, :n], cs[:, 1, :n], ALU.mult, [Ra, R_cs], [R_t2])
                            tt(sg[:, 2 * hh + 1, :n], t1[:, :n], t2[:, :n], ALU.add, [R_t1, R_t2], [R_sg])
                    elif mode == "swa":
                        for m in range(nch):
                            pa, Ra = proj(off, m)
                            pb_, Rb = proj(poff, m)
                            tt(t1[:, :n], pa[:, :n], cs[:, 0, :n], ALU.mult, [Ra, R_cs], [R_t1])
                            tt(t2[:, :n], pb_[:, :n], cs[:, 1, :n], ALU.mult, [Rb, R_cs], [R_t2])
                            tt(sg[:, m, :n], t1[:, :n], t2[:, :n], ALU.add, [R_t1, R_t2], [R_sg])
                    else:
                        for m in range(nch):
                            pa, Ra = proj(off, m)
                            kk.op("act", lambda e, pa=pa, m=m, sg=sg: e.activation(out=sg[:, m, :n], in_=pa[:, :n], func=AF.Identity),
                                  r=[Ra], w=[R_sg])
                    kk.dma("sp", dst[:, 0:nch, t0:t0 + n], sg[:, 0:nch, :n], r=[R_sg])
                for sub in range(n // 128):
                    for (off, ncols, dstd, fn, nm) in tm:
                        st_t, R_st = (stv, R_stv) if nm == "v" else (stgg, R_stgg)
                        for blk in range((ncols + 511) // 512):
                            cw = min(512, ncols - blk * 512)
                            (pa, Ra) = bankS()
                            mm_acc(pa[:, :cw], Ra, [(xl[:, kc, sub * 128:(sub + 1) * 128],
                                                     W[:, kc, off + blk * 512:off + blk * 512 + cw], [R_W, R_xl])
                                                    for kc in range(DC)])
                            kk.op("act", lambda e, pa=pa, blk=blk, cw=cw, st_t=st_t, fn=fn: e.activation(
                                out=st_t[:, blk * 512:blk * 512 + cw], in_=pa[:, :cw], func=fn), r=[Ra], w=[R_st])
                        kk.dma("sp", dstd[t0 + sub * 128:t0 + (sub + 1) * 128, 0:ncols], st_t[:, 0:ncols], r=[R_st])
            kk.barrier()

    def attn_phase(kind, do_ctx):
        with contextlib.ExitStack() as ph:
            ntyp, nch = (5, 7) if kind == "nat" else (3, 5)
            bias = sbt(ph, "abias", [128, ntyp, nch, 128], F32)
            R_bias = Res()
            ones_b = sbt(ph, "aones", [128, 64], BF16)
            R_c = Res()
            kk.op("dve", lambda e: e.memset(ones_b[:], 1.0), w=[R_c])
            sk = sbt(ph, "ask", [128, 16], F32)
            if kind == "swa":
                kk.dma("sp", bias[:], swa_bias, w=[R_bias])
                kk.dma("sp", sk[:], swa_sink.partition_broadcast(128), w=[R_c])
                kk.op("act", lambda e: e.activation(out=sk[:], in_=sk[:], func=AF.Exp), r=[R_c], w=[R_c])
            qh = [sbt(ph, "aqh%d" % i, [64, T_ALL], BF16) for i in range(2)]
            kh = [sbt(ph, "akh%d" % i, [64, T_ALL], BF16) for i in range(2)]
            vh = [sbt(ph, "avh%d" % i, [128, 34, 65], BF16) for i in range(2)]
            R_qh, R_kh, R_vh = [Res(), Res()], [Res(), Res()], [Res(), Res()]
            for i in range(2):
                kk.op("dve", lambda e, i=i: e.memset(vh[i][:, :, 64:65], 1.0), w=[R_vh[i]])
            otm = [sbt(ph, "aotm%d" % i, [128, 34, 128], BF16) for i in range(2)]
            R_otm = [Res(), Res()]
            ostg = sbt(ph, "aostg", [128, 34 * 128], BF16)
            R_ostg = Res()
            ident_b = sbt(ph, "aident", [128, 128], BF16)
            kk.op("dve", lambda e: e.tensor_copy(out=ident_b[:], in_=identf[:]), r=[R_const], w=[R_c])
            psT = ps_stat[:].bitcast(BF16)
            bias2 = [bias, sbt(ph, "abias2", [128, ntyp, nch, 128], F32)] if kind == "nat" else [bias, bias]
            R_bias2 = [R_bias, Res()] if kind == "nat" else [R_bias, R_bias]
            tmp = [sbt(ph, "atmp%d" % i, [128, nch * 128], F32) for i in range(2)]
            R_tmp = [Res(), Res()]
            pt = [sbt(ph, "apt%d" % i, [128, nch * 128], BF16) for i in range(2)]
            R_pt = [Res(), Res()]
            den = [sbt(ph, "aden%d" % i, [128, 1], F32) for i in range(2)]
            R_den = [Res(), Res()]
            Tq = T_ALL if do_ctx else T_LAT

            def load_head(h):
                hs = h % 2
                kvh = h if kind == "nat" else h // 4
                kk.dma("sp", qh[hs][:], qT_d[(h % 2) * 64:(h % 2) * 64 + 64, h // 2, :], w=[R_qh[hs]])
                kk.dma("sp", kh[hs][:], kT_d[(kvh % 2) * 64:(kvh % 2) * 64 + 64, kvh // 2, :], w=[R_kh[hs]])
                kk.dma("sp", vh[hs][:, :, 0:64], v_d[:, kvh * 64:(kvh + 1) * 64].rearrange("(n p) d -> p n d", p=128),
                       w=[R_vh[hs]])
                if kind == "nat":
                    kk.dma("sp", bias2[hs][:], nat_bias[h], w=[R_bias2[hs]])

            units = []
            for h in range(16):
                for qi in range(32):
                    if kind == "nat":
                        typ = 0 if 2 <= qi <= 29 else {0: 1, 1: 2, 30: 3, 31: 4}[qi]
                        base = min(max(qi - 2, 0), 27)
                        chunks = [base + i for i in range(5)] + [32, 33]
                    else:
                        typ = 1 if qi == 0 else (2 if qi == 31 else 0)
                        chunks = [max(qi - 1, 0), qi, min(qi + 1, 31), 32, 33]
                    units.append([h, qi * 128, chunks, typ, qi == 0, False])
                if do_ctx:
                    for qc in (32, 33):
                        units.append([h, qc * 128, [32, 33], None, False, False])
                units[-1][5] = True
            state = {}

            def s1(ui):
                h, q0, chunks, typ, first, lasth = units[ui]
                hs = h % 2
                ncu = len(chunks)
                u2 = ui % 2
                banks = []
                for c0 in range(0, ncu, 4):
                    cn = min(4, ncu - c0)
                    (pS, RS) = bankS()
                    banks.append((pS, RS, c0, cn))
                    for ci in range(c0, c0 + cn):
                        kc_ = chunks[ci]
                        kk.op("pe", lambda e, pS=pS, ci=ci, c0=c0, kc_=kc_, q0=q0, hs=hs: e.matmul(
                            pS[:, (ci - c0) * 128:(ci - c0 + 1) * 128], kh[hs][:, kc_ * 128:(kc_ + 1) * 128],
                            qh[hs][:, q0:q0 + 128], start=True, stop=True),
                            r=[R_kh[hs], R_qh[hs]], w=[RS], inc=(ci == c0 + cn - 1))
                if typ is not None:
                    for (pS, RS, c0, cn) in banks:
                        kk.op("dve", lambda e, pS=pS, c0=c0, cn=cn, typ=typ, u2=u2, hs=hs: e.scalar_tensor_tensor(
                            out=tmp[u2][:, c0 * 128:(c0 + cn) * 128].rearrange("p (c q) -> p c q", q=128),
                            in0=pS[:, :cn * 128].rearrange("p (c q) -> p c q", q=128), scalar=0.125,
                            in1=bias2[hs][:, typ, c0:c0 + cn, :], op0=ALU.mult, op1=ALU.add),
                            r=[RS, R_bias2[hs]], w=[R_tmp[u2]])
                    kk.op("act", lambda e, u2=u2, ncu=ncu: e.activation(out=pt[u2][:, :ncu * 128], in_=tmp[u2][:, :ncu * 128],
                                                                          func=AF.Exp), r=[R_tmp[u2]], w=[R_pt[u2]])
                else:
                    for (pS, RS, c0, cn) in banks:
                        kk.op("act", lambda e, pS=pS, c0=c0, cn=cn, u2=u2: e.activation(
                            out=pt[u2][:, c0 * 128:(c0 + cn) * 128], in_=pS[:, :cn * 128], func=AF.Exp, scale=0.125),
                            r=[RS], w=[R_pt[u2]])

            ntile = 34 if do_ctx else 32

            def s2(ui):
                h, q0, chunks, typ, first, lasth = units[ui]
                hs = h % 2
                if first and h + 1 < 16:
                    load_head(h + 1)
                ncu = len(chunks)
                u2 = ui % 2
                pp = (h // 2) % 2
                par = h % 2
                qt = q0 // 128
                (pO, RO) = bankO()
                mm_acc(pO[:, 0:65], RO, [(pt[u2][:, ci * 128:(ci + 1) * 128], vh[hs][:, chunks[ci], 0:65], [R_vh[hs], R_pt[u2]])
                                         for ci in range(ncu)])
                if kind == "swa":
                    kk.op("dve", lambda e, pO=pO, u2=u2, h=h: e.tensor_scalar(
                        out=den[u2][:], in0=pO[:, 64:65], scalar1=sk[:, h:h + 1], scalar2=None, op0=ALU.add),
                        r=[RO, R_c], w=[R_den[u2]])
                    kk.op("dve", lambda e, u2=u2: e.reciprocal(out=den[u2][:], in_=den[u2][:]), r=[R_den[u2]], w=[R_den[u2]])
                else:
                    kk.op("dve", lambda e, pO=pO, u2=u2: e.reciprocal(out=den[u2][:], in_=pO[:, 64:65]),
                          r=[RO], w=[R_den[u2]])
                kk.op("dve", lambda e, pO=pO, u2=u2, pp=pp, par=par, qt=qt: e.tensor_scalar(
                    out=otm[pp][:, qt, par * 64:(par + 1) * 64], in0=pO[:, 0:64], scalar1=den[u2][:, 0:1], scalar2=None,
                    op0=ALU.mult), r=[RO, R_den[u2]], w=[R_otm[pp]])
                if lasth and par == 1:
                    for t in range(ntile):
                        reg = (t % 8) * 128
                        kk.op("pe", lambda e, t=t, reg=reg, pp=pp: e.transpose(psT[:, reg:reg + 128], otm[pp][:, t, :], ident_b[:]),
                              r=[R_otm[pp], R_c], w=[R_ps_stat], inc=(t % 4 == 3 or t == ntile - 1))
                        if t % 4 == 3 or t == ntile - 1:
                            t0_ = t - (t % 4)
                            half = ((t0_ % 8) // 4) * 512
                            nn = (t - t0_ + 1) * 128
                            kk.op("act", lambda e, t0_=t0_, half=half, nn=nn: e.activation(
                                out=ostg[:, t0_ * 128:t0_ * 128 + nn], in_=psT[:, half:half + nn], func=AF.Identity),
                                r=[R_ps_stat], w=[R_ostg])
                    kk.dma("sp", oT_d[:, h // 2, 0:Tq], ostg[:, 0:Tq], r=[R_ostg])

            LA = 1
            load_head(0)
            for k in range(len(units) + LA):
                if k < len(units):
                    s1(k)
                if k - LA >= 0:
                    s2(k - LA)
            kk.barrier()

    def oproj_phase(li, Wd, KC, h_in, h_out, tl):
        with contextlib.ExitStack() as ph:
            W = sbt(ph, "ow", [128, KC, D], BF16)
            R_W = Res()
            kk.dma("pool", W[:], Wd.rearrange("(c p) d -> p c d", p=128), w=[R_W])
            ht = [sbt(ph, "oht%d" % i, [128, DC, NT], F32) for i in range(2)]
            R_ht = [Res(), Res()]
            ot = [sbt(ph, "oot%d" % i, [128, KC, NT], BF16) for i in range(2)]
            R_ot = [Res(), Res()]
            for ti, (t0, n, w_) in enumerate(tl):
                b = ti % 2
                kk.dma("sp", ht[b][:, :, :n], h_in[:, :, t0:t0 + n], w=[R_ht[b]])
                kk.dma("sp", ot[b][:, :, :n], oT_d[:, 0:KC, t0:t0 + n], w=[R_ot[b]])
                for d in range(DC):
                    (pO, RO) = bankO()
                    mm_acc(pO[:, :n], RO, [(W[:, c, d * 128:(d + 1) * 128], ot[b][:, c, :n], [R_W, R_ot[b]]) for c in range(KC)])
                    kk.op("dve", lambda e, d=d, pO=pO, b=b: e.scalar_tensor_tensor(
                        out=ht[b][:, d, :n], in0=pO[:, :n], scalar=gat[:, li, 1, d, w_:w_ + 1],
                        in1=ht[b][:, d, :n], op0=ALU.mult, op1=ALU.add),
                        r=[RO, R_mods, R_ht[b]], w=[R_ht[b]])
                kk.dma("sp", h_out[:, :, t0:t0 + n], ht[b][:, :, :n], r=[R_ht[b]])
            kk.barrier()

    def ret_phase(do_ctx):
        with contextlib.ExitStack() as ph:
            rt = sbt(ph, "rtabs", [128, 8, 128], F32)
            R_rt = Res()
            kk.dma("sp", rt[:], ret_tabs, w=[R_rt])
            lg = sbt(ph, "rlg", [128, 8], F32)
            kk.dma("sp", lg[:], ret_decay.partition_broadcast(128), w=[R_rt])
            kk.op("act", lambda e: e.activation(out=lg[:], in_=lg[:], func=AF.Sigmoid), r=[R_rt], w=[R_rt])
            kk.op("act", lambda e: e.activation(out=lg[:], in_=lg[:], func=AF.Ln), r=[R_rt], w=[R_rt])
            lgn = sbt(ph, "rlgn", [128, 8], F32)
            kk.op("dve", lambda e: e.tensor_scalar(out=lgn[:], in0=lg[:], scalar1=-1.0, scalar2=None, op0=ALU.mult),
                  r=[R_rt], w=[R_rt])
            ident = sbt(ph, "rident", [128, 128], BF16)
            kk.op("dve", lambda e: e.tensor_copy(out=ident[:], in_=rt[:, 7, :]), r=[R_rt], w=[R_rt])
            gng = sbt(ph, "rgng", [128, 2048], F32)
            kk.dma("sp", gng[:], ret_gn.partition_broadcast(128), w=[R_rt])
            tb = sbt(ph, "rtb", [128, 4, 5, 128], F32)
            tA = sbt(ph, "rtA", [128, 128], F32)
            tB = sbt(ph, "rtB", [128, 128], F32)
            for h in range(4):
                lf = lg[:, h:h + 1]
                lb = lg[:, 4 + h:5 + h]
                for (slot, tab, sc_) in ((0, 0, lf), (1, 1, lb), (2, 2, lf), (3, 3, lb)):
                    kk.op("act", lambda e, h=h, slot=slot, tab=tab, sc_=sc_: e.activation(
                        out=tb[:, h, slot, :], in_=rt[:, tab, :], func=AF.Exp, scale=sc_), r=[R_rt], w=[R_rt])
                kk.op("act", lambda e, lf=lf: e.activation(out=tA[:], in_=rt[:, 4, :], func=AF.Exp, scale=lf), r=[R_rt], w=[R_rt])
                kk.op("act", lambda e, lb=lb: e.activation(out=tB[:], in_=rt[:, 5, :], func=AF.Exp, scale=lb), r=[R_rt], w=[R_rt])
                kk.op("dve", lambda e: e.tensor_tensor(out=tA[:], in0=tA[:], in1=tB[:], op=ALU.subtract), r=[R_rt], w=[R_rt])
                kk.op("dve", lambda e: e.tensor_tensor(out=tA[:], in0=tA[:], in1=rt[:, 6, :], op=ALU.mult), r=[R_rt], w=[R_rt])
                kk.op("dve", lambda e, h=h: e.tensor_tensor(out=tb[:, h, 4, :], in0=tA[:], in1=tB[:], op=ALU.add), r=[R_rt], w=[R_rt])
                kk.op("dve", lambda e, h=h: e.tensor_scalar(out=tb[:, h, 2:5, :], in0=tb[:, h, 2:5, :], scalar1=1.0 / 16.0,
                                                            scalar2=None, op0=ALU.mult), r=[R_rt], w=[R_rt])
                kk.op("act", lambda e, h=h: e.activation(out=tA[:], in_=rt[:, 0, :], func=AF.Exp, scale=lgn[:, h:h + 1]),
                      r=[R_rt], w=[R_rt])
                kk.op("dve", lambda e, h=h: e.tensor_tensor(out=tb[:, h, 4, :], in0=tb[:, h, 4, :], in1=tA[:], op=ALU.mult),
                      r=[R_rt], w=[R_rt])
            qh = sbt(ph, "rqh", [128, 2, T_ALL], BF16)
            qf = sbt(ph, "rqf", [128, 2, T_ALL], BF16)
            qb = sbt(ph, "rqb", [128, 2, T_ALL], BF16)
            kh = [sbt(ph, "rkh%d" % i, [128, 2, T_ALL], BF16) for i in range(2)]
            vh = sbt(ph, "rvh", [128, 34, 512], BF16)
            R_qh, R_qf, R_qb, R_kh, R_vh = Res(), Res(), Res(), [Res(), Res()], Res()
            NPM = 4
            pm = [sbt(ph, "rpm%d" % i, [128, 128], BF16) for i in range(NPM)]
            R_pm = [Res() for _ in range(NPM)]
            ocn = [sbt(ph, "rocn%d" % i, [128, 512], F32) for i in range(4)]
            R_ocn = [Res() for _ in range(4)]
            junk = sbt(ph, "rjunk", [128, 512], F32)
            R_junk = Res()
            gt = [sbt(ph, "rgt%d" % i, [128, 512], F32) for i in range(4)]
            R_gt = [Res() for _ in range(4)]
            gated = [sbt(ph, "rgated%d" % i, [128, 512], BF16) for i in range(4)]
            R_gated = [Res() for _ in range(4)]
            ost = [sbt(ph, "rost%d" % i, [128, 4, 128], BF16) for i in range(4)]
            R_ost = [Res() for _ in range(4)]
            sm = [sbt(ph, "rsm%d" % i, [128, 4], F32) for i in range(4)]
            R_sm = [Res() for _ in range(4)]
            psT = ps_stat[:].bitcast(BF16)
            qtiles = list(range(32)) + ([32, 33] if do_ctx else [])

            def load_qk(h):
                hs = h % 2
                kk.dma("sp", qh[:], qT_d[:, 2 * h:2 * h + 2, :], w=[R_qh])
                kk.dma("sp", kh[hs][:], kT_d[:, 2 * h:2 * h + 2, :], w=[R_kh[hs]])

            def load_v(h):
                kk.dma("sp", vh[:], v_d[:, h * 512:(h + 1) * 512].rearrange("(n p) d -> p n d", p=128), w=[R_vh])

            def prescale_q(h):
                for (dst, R_dst, slot, eng) in ((qf, R_qf, 0, "dve"), (qb, R_qb, 1, "pool")):
                    kk.op(eng, lambda e, dst=dst, slot=slot: e.tensor_tensor(
                        out=dst[:].rearrange("p c (n i) -> p c n i", i=128),
                        in0=qh[:].rearrange("p c (n i) -> p c n i", i=128),
                        in1=tb[:, h, slot, :].unsqueeze(1).unsqueeze(1).to_broadcast([128, 2, 34, 128]), op=ALU.mult),
                        r=[R_qh, R_rt], w=[R_dst])

            stream = []
            tix = 0
            for h in range(4):
                for qi in qtiles:
                    pairs = []
                    if qi < 32:
                        for m in range(2):
                            pairs.append((32 + m, "f", qi + 2 - m))
                        for kj in range(32):
                            dl = qi - kj
                            pairs.append((kj, "f" if dl > 0 else ("i" if dl == 0 else "b"), abs(dl)))
                        for m in range(2):
                            pairs.append((32 + m, "b", 32 + m - qi))
                    else:
                        for m in range(2):
                            dl = (qi - 32) - m
                            pairs.append((32 + m, "f" if dl > 0 else ("i" if dl == 0 else "b"), abs(dl)))
                    for idx, (kj, mode, dl) in enumerate(pairs):
                        stream.append((h, qi, tix, idx, len(pairs), kj, mode, dl))
                    tix += 1
            tile_bank = {}
            S_of = {}
            deferred = []

            def s1(k):
                h, qi, tx, idx, npairs, kj, mode, dl = stream[k]
                hs = h % 2
                q0 = qi * 128
                if idx == 0:
                    if qi == qtiles[0]:
                        prescale_q(h)
                    kk.dma("sp", gt[tx % 4][:], g_d[q0:q0 + 128, h * 512:(h + 1) * 512], w=[R_gt[tx % 4]])
                (pS, RS) = bankS()
                qq, R_qq = (qb, R_qb) if mode == "b" else (qf, R_qf)
                mm_acc(pS[:, :128], RS, [(kh[hs][:, dc, kj * 128:(kj + 1) * 128], qq[:, dc, q0:q0 + 128],
                                          [R_kh[hs], R_qq]) for dc in range(2)])
                p4 = k % NPM
                if mode == "i":
                    kk.op("dve", lambda e, pS=pS, p4=p4, h=h: e.tensor_tensor(
                        out=pm[p4][:], in0=pS[:, :128], in1=tb[:, h, 4, :], op=ALU.mult),
                        r=[RS, R_rt], w=[R_pm[p4]])
                else:
                    us = 2 if mode == "f" else 3
                    if k % 2 == 0:
                        kk.op("act", lambda e, pS=pS, p4=p4, h=h, us=us, dl=dl: e.activation(
                            out=pm[p4][:], in_=pS[:, :128], func=AF.Identity, scale=tb[:, h, us, dl - 1:dl]),
                            r=[RS, R_rt], w=[R_pm[p4]])
                    else:
                        kk.op("dve", lambda e, pS=pS, p4=p4, h=h, us=us, dl=dl: e.tensor_scalar(
                            out=pm[p4][:], in0=pS[:, :128], scalar1=tb[:, h, us, dl - 1:dl], scalar2=None, op0=ALU.mult),
                            r=[RS, R_rt], w=[R_pm[p4]])

            def epiA(h, qi, tx, pO, RO):
                t2 = tx % 4
                kk.op("dve", lambda e: e.reduce_sum(out=sm[t2][:, 0:1], in_=pO[:, :512], axis=AX.X), r=[RO], w=[R_sm[t2]])
                kk.op("dve", lambda e: e.tensor_scalar(out=sm[t2][:, 0:1], in0=sm[t2][:, 0:1], scalar1=-1.0 / 512.0, scalar2=None,
                                                       op0=ALU.mult), r=[R_sm[t2]], w=[R_sm[t2]])
                kk.op("dve", lambda e: e.tensor_scalar(out=ocn[t2][:], in0=pO[:, :512], scalar1=sm[t2][:, 0:1], scalar2=None,
                                                       op0=ALU.add), r=[RO, R_sm[t2]], w=[R_ocn[t2]])
                kk.op("act", lambda e: e.activation(out=junk[:], in_=ocn[t2][:], func=AF.Square, accum_out=sm[t2][:, 1:2]),
                      r=[R_ocn[t2]], w=[R_sm[t2], R_junk])
                kk.op("act", lambda e: e.activation(out=sm[t2][:, 2:3], in_=sm[t2][:, 1:2], func=AF.Sqrt, scale=1.0 / 512.0, bias=EPS),
                      r=[R_sm[t2]], w=[R_sm[t2]])

            def epiB(h, qi, tx):
                t2 = tx % 4
                kk.op("dve", lambda e: e.reciprocal(out=sm[t2][:, 2:3], in_=sm[t2][:, 2:3]), r=[R_sm[t2]], w=[R_sm[t2]])
                kk.op("dve", lambda e: e.scalar_tensor_tensor(
                    out=ocn[t2][:], in0=ocn[t2][:], scalar=sm[t2][:, 2:3], in1=gng[:, h * 512:(h + 1) * 512],
                    op0=ALU.mult, op1=ALU.mult), r=[R_ocn[t2], R_sm[t2], R_rt], w=[R_ocn[t2]])
                kk.op("dve", lambda e: e.tensor_tensor(out=gated[t2][:], in0=ocn[t2][:], in1=gt[t2][:], op=ALU.mult),
                      r=[R_ocn[t2], R_gt[t2]], w=[R_gated[t2]])

            def epiC(h, qi, tx):
                t2 = tx % 4
                q0 = qi * 128
                for blk in range(4):
                    kk.op("pe", lambda e, blk=blk: e.transpose(psT[:, blk * 128:(blk + 1) * 128],
                                                               gated[t2][:, blk * 128:(blk + 1) * 128], ident[:]),
                          r=[R_gated[t2], R_rt], w=[R_ps_stat], inc=(blk == 3))
                kk.op("act", lambda e: e.activation(out=ost[t2][:].rearrange("p a b -> p (a b)"), in_=psT[:, 0:512],
                                                    func=AF.Identity), r=[R_ps_stat], w=[R_ost[t2]])
                kk.dma("sp", oT_d[:, 4 * h:4 * h + 4, q0:q0 + 128], ost[t2][:], r=[R_ost[t2]])

            def s2(k):
                h, qi, tx, idx, npairs, kj, mode, dl = stream[k]
                hs = h % 2
                if idx == 0:
                    tile_bank[tx] = bankO()
                    if qi == qtiles[0] and h + 1 < 4:
                        load_qk(h + 1)
                (pO, RO) = tile_bank[tx]
                p4 = k % NPM
                kk.op("pe", lambda e: e.matmul(pO[:, :512], pm[p4][:], vh[:, kj, :], start=(idx == 0), stop=(idx == npairs - 1)),
                      r=[R_pm[p4], R_vh], w=[RO])
                if idx == npairs - 1 and qi == qtiles[-1] and h + 1 < 4:
                    load_v(h + 1)
                if idx == npairs - 1:
                    epiA(h, qi, tx, pO, RO)
                    deferred.append((k + 5, lambda: epiB(h, qi, tx)))
                    deferred.append((k + 10, lambda: epiC(h, qi, tx)))

            LA = 3
            load_qk(0)
            load_v(0)
            for k in range(len(stream) + LA):
                if k < len(stream):
                    s1(k)
                if k - LA >= 0:
                    s2(k - LA)
                    while deferred and deferred[0][0] <= k - LA:
                        deferred.pop(0)[1]()
            while deferred:
                deferred.pop(0)[1]()
            kk.barrier()

    def pool_phase(li, h_in, h_out, tl, do_ctx):
        with contextlib.ExitStack() as ph:
            ht = [sbt(ph, "qht%d" % i, [128, DC, NT], F32) for i in range(2)]
            R_ht = [Res(), Res()]
            xf = [sbt(ph, "qxf%d" % i, [128, DC, NT], F32) for i in range(2)]
            R_xf = [Res(), Res()]
            o = norm_mod(ph, "q")
            for ti, (t0, n, w_) in enumerate(tl):
                b = ti % 2
                kk.dma("sp", ht[b][:, :, :n], h_in[:, :, t0:t0 + n], w=[R_ht[b]])
                emit_rstd(o, ht[b], R_ht[b], n)
                emit_xl(o, ht[b], R_ht[b], n, li, 1, w_, xf[b], R_xf[b])
                kk.dma("sp", xl_d[:, :, t0:t0 + n], xf[b][:, :, :n], r=[R_xf[b]])
            kk.barrier()
        with contextlib.ExitStack() as ph:
            PAD = 8
            TB = T_LAT + 2 * PAD
            X = sbt(ph, "qX", [128, 2, TB], F32)
            Y = sbt(ph, "qY", [128, 2, TB], F32)
            Zb = sbt(ph, "qZ", [128, 2, TB], F32)
            R_X, R_Y, R_Z = Res(), Res(), Res()
            icn = sbt(ph, "qicn", [128, T_LAT], F32)
            R_icn = Res()
            pbf = sbt(ph, "qpb", [128, 2, T_LAT], BF16)
            R_pb = Res()
            wg = sbt(ph, "qwg", [128, 2, 256], BF16)
            R_wg = Res()
            psc = sbt(ph, "qpsc", [128, DC], F32)
            gp = sbt(ph, "qgp", [128, DC, 2], F32)
            R_gp = Res()
            kk.dma("sp", psc[:], pool_sc, w=[R_gp])
            for w_ in range(2):
                kk.op("dve", lambda e, w_=w_: e.tensor_tensor(out=gp[:, :, w_], in0=gat[:, li, 1, :, w_], in1=psc[:], op=ALU.mult),
                      r=[R_gp, R_mods], w=[R_gp])
            hc = [sbt(ph, "qhc%d" % i, [128, 512], F32) for i in range(2)]
            R_hc = [Res(), Res()]
            seqs = [(0, T_LAT, 0)] + ([(T_LAT, T_CTX, 1)] if do_ctx else [])
            hi_ = 0
            for g_ in range(4):
                kk.dma("pool", wg[:], pool_w[g_].rearrange("(kc p) n -> p kc n", p=128), w=[R_wg])
                for (s0, T, w_) in seqs:
                    L = T + 2 * PAD
                    kk.op("pool", lambda e, L=L: e.memset(X[:, :, 0:L], 0.0), w=[R_X])
                    kk.dma("sp", X[:, :, PAD:PAD + T], xl_d[:, 2 * g_:2 * g_ + 2, s0:s0 + T], w=[R_X])
                    kk.dma("sp", icn[:, :T], pool_icnt[g_:g_ + 1, s0:s0 + T].partition_broadcast(128), w=[R_icn])
                    kk.op("dve", lambda e, L=L: e.memset(Y[:, :, 0:L], 0.0), w=[R_Y])
                    kk.op("dve", lambda e, L=L: e.tensor_tensor(
                        out=Y[:, :, 1:L], in0=X[:, :, 1:L], in1=X[:, :, 0:L - 1], op=ALU.add), r=[R_X], w=[R_Y])
                    lv_src, R_lsrc = Y, R_Y
                    sh = 1
                    for lev in range(g_):
                        a_, Ra_, b_, Rb_ = (Y, R_Y, Zb, R_Z) if lev % 2 == 0 else (Zb, R_Z, Y, R_Y)
                        kk.op("dve", lambda e, L=L, b_=b_: e.memset(b_[:, :, 0:L], 0.0), w=[Rb_])
                        kk.op("dve", lambda e, L=L, a_=a_, b_=b_, sh=sh: e.tensor_tensor(
                            out=b_[:, :, sh:L - sh], in0=a_[:, :, 0:L - 2 * sh], in1=a_[:, :, 2 * sh:L], op=ALU.add),
                            r=[Ra_], w=[Rb_])
                        lv_src, R_lsrc = b_, Rb_
                        sh *= 2
                    for c in range(2):
                        kk.op("dve", lambda e, c=c, T=T, lv_src=lv_src: e.tensor_tensor(
                            out=lv_src[:, c, PAD:PAD + T], in0=lv_src[:, c, PAD:PAD + T], in1=icn[:, :T], op=ALU.mult),
                            r=[R_lsrc, R_icn], w=[R_lsrc])
                    kk.op("dve", lambda e, T=T, lv_src=lv_src: e.tensor_tensor(
                        out=pbf[:, :, :T], in0=lv_src[:, :, PAD:PAD + T], in1=X[:, :, PAD:PAD + T], op=ALU.subtract),
                        r=[R_lsrc, R_X], w=[R_pb])
                    for m in range(2):
                        c = 2 * g_ + m
                        for tt0 in range(0, T, 512):
                            nn = min(512, T - tt0)
                            hb = hi_ % 2
                            hi_ += 1
                            kk.dma("sp", hc[hb][:, :nn], h_in[:, c, s0 + tt0:s0 + tt0 + nn], w=[R_hc[hb]])
                            (pO, RO) = bankO()
                            mm_acc(pO[:, :nn], RO, [(wg[:, kc, m * 128:(m + 1) * 128], pbf[:, kc, tt0:tt0 + nn], [R_wg, R_pb])
                                                    for kc in range(2)])
                            kk.op("dve", lambda e, pO=pO, hb=hb, nn=nn, c=c, w_=w_: e.scalar_tensor_tensor(
                                out=hc[hb][:, :nn], in0=pO[:, :nn], scalar=gp[:, c, w_:w_ + 1], in1=hc[hb][:, :nn],
                                op0=ALU.mult, op1=ALU.add), r=[RO, R_gp, R_hc[hb]], w=[R_hc[hb]])
                            kk.dma("sp", h_out[:, c, s0 + tt0:s0 + tt0 + nn], hc[hb][:, :nn], r=[R_hc[hb]])
            kk.barrier()

    kinds = cfg.get("kinds", ["ret", "nat", "pool", "swa"])
    cur = xT
    nxt = 0
    lat_tiles = [t for t in tiles if t[2] == 0]
    for li in range(n_layers):
        kind = kinds[li]
        last = (li == n_layers - 1)
        ctx_live = (not last) or kind != "pool"
        tl1 = tiles if ctx_live else lat_tiles
        tl2 = lat_tiles if last else tiles
        ffn_phase(li, 0, 0, cur, hbufs[nxt], tl1)
        cur = hbufs[nxt]
        nxt ^= 1
        if mixers:
            if kind == "pool":
                pool_phase(li, cur, hbufs[nxt], tl2, not last)
            else:
                proj_phase(li, kind, cur, tl1)
                if kind == "ret":
                    ret_phase(not last)
                    oproj_phase(li, ret_w_out, 16, cur, hbufs[nxt], tl2)
                else:
                    attn_phase(kind, not last)
                    oproj_phase(li, nat_w_o if kind == "nat" else swa_w_o, 8, cur, hbufs[nxt], tl2)
            cur = hbufs[nxt]
            nxt ^= 1
        ffn_phase(li, 1, 2, cur, hbufs[nxt], tl2)
        cur = hbufs[nxt]
        nxt ^= 1
    final_phase(cur)
    kk.barrier()
    kk.ninst_total = kk.ninst
    nc._kk = kk
    return nc


def _fm(a):
    t = a.shape[0]
    return np.ascontiguousarray(a.T.reshape(DC, 128, t).transpose(1, 0, 2))


def _vec_fm(v):
    lead = v.shape[:-1]
    x = v.reshape(*lead, DC, 128)
    x = np.moveaxis(x, -1, 0)
    return np.ascontiguousarray(x)


def _consts():
    c = {}
    p = np.arange(128, dtype=np.float64)[:, None]
    i = np.arange(128, dtype=np.float64)[None, :]
    tabs = np.zeros((128, 8, 128), np.float64)
    tabs[:, 0] = i + 0 * p
    tabs[:, 1] = 127 - i + 0 * p
    tabs[:, 2] = 128 * i + 128 - p
    tabs[:, 3] = 128 * i + p + 1
    tabs[:, 4] = np.maximum(i - p, 0)
    tabs[:, 5] = np.maximum(p - i, 0)
    tabs[:, 6] = (i >= p)
    tabs[:, 7] = (i == p)
    c["ret_tabs"] = tabs.astype(np.float32)
    t = np.arange(T_LAT, dtype=np.float32)[None, :]
    inv = (10000.0 ** (-(np.arange(0, 256, 2, dtype=np.float32)) / 256.0)).astype(np.float32)[:, None]
    ang = (t * inv).astype(np.float32)
    cs = np.zeros((128, 2, T_ALL), np.float32)
    cs[:, 0, :T_LAT] = np.cos(ang)
    cs[:, 1, :T_LAT] = np.sin(ang)
    cs[:, 0, T_LAT:] = 1.0
    c["ret_cs"] = cs
    d = np.arange(128) % 64
    tt = np.arange(T_LAT)
    pos = np.where((d < 32)[:, None], (tt // 64)[None, :], (tt % 64)[None, :]).astype(np.float32)
    inv16 = (10000.0 ** (-(np.arange(0, 32, 2, dtype=np.float32)) / 32.0)).astype(np.float32)
    invd = inv16[(d % 32) % 16][:, None]
    ang = (pos * invd).astype(np.float32)
    sign = np.where((d % 32) < 16, -1.0, 1.0).astype(np.float32)[:, None]
    cs = np.zeros((128, 2, T_ALL), np.float32)
    cs[:, 0, :T_LAT] = np.cos(ang)
    cs[:, 1, :T_LAT] = np.sin(ang) * sign
    cs[:, 0, T_LAT:] = 1.0
    c["swa_cs"] = cs
    NEG = -30000.0
    j = np.arange(128)[:, None]
    q = np.arange(128)[None, :]
    sb_ = np.zeros((128, 3, 5, 128), np.float32)
    for typ in range(3):
        sb_[:, typ, 0] = np.where(j >= q, 0.0, NEG)
        sb_[:, typ, 2] = np.where(j <= q, 0.0, NEG)
    sb_[:, 1, 0] = NEG
    sb_[:, 2, 2] = NEG
    c["swa_bias"] = sb_
    ic = np.zeros((4, T_ALL), np.float32)
    for g_, w in enumerate((2, 4, 8, 16)):
        for (s0, T) in ((0, T_LAT), (T_LAT, T_CTX)):
            tq = np.arange(T)
            lo = np.clip(tq - w // 2, 0, T)
            hi = np.clip(tq + w // 2, 0, T)
            ic[g_, s0:s0 + T] = 1.0 / (hi - lo).astype(np.float32)
    c["pool_icnt"] = ic
    return c


def _nat_bias(rpb):
    NEG = -30000.0
    out = np.zeros((16, 128, 5, 7, 128), np.float32)
    u = (np.arange(128) // 64)
    n = (np.arange(128) % 64)
    cfgs = [(2, 0), (0, 0), (1, 0), (30, 27), (31, 27)]
    for typ, (qi, base) in enumerate(cfgs):
        r = (2 * qi + u)[None, :]
        cq = n[None, :]
        r0 = np.clip(r - 4, 0, 56)
        c0 = np.clip(cq - 8, 0, 48)
        for ch in range(5):
            a = (2 * (base + ch) + u)[:, None]
            nk = n[:, None]
            valid = (a >= r0) & (a < r0 + 8) & (nk >= c0) & (nk < c0 + 16)
            ri = np.clip(a - r + 7, 0, 14)
            ci = np.clip(nk - cq + 15, 0, 30)
            vals = rpb[:, ri, ci]
            out[:, :, typ, ch, :] = np.where(valid[None], vals, NEG)
    return out


def make_in_maps(inputs, cores=range(N_CORES)):
    f = lambda a: np.ascontiguousarray(np.asarray(a, dtype=np.float32))
    x, c, ctx, c_ctx = f(inputs["x"]), f(inputs["c"]), f(inputs["ctx"]), f(inputs["c_ctx"])
    b_mod = f(inputs["b_mod"])
    shared = {
        "w_mod": f(inputs["w_mod"]),
        "bmodT": np.ascontiguousarray(b_mod.reshape(DEPTH, 72, 128).transpose(2, 0, 1)),
        "normgT": _vec_fm(f(inputs["norm_g"])),
        "fnormgT": _vec_fm(f(inputs["final_norm_g"])),
        "ffn_w_in": f(inputs["ffn_w_in"]),
        "ffn_w_out": f(inputs["ffn_w_out"]),
        "ret_w_in": f(inputs["ret_w_in"][0]),
        "ret_w_out": f(inputs["ret_w_out"][0]),
        "ret_gn": f(inputs["ret_gn_g"][0:1]),
        "ret_decay": np.ascontiguousarray(np.concatenate([f(inputs["ret_decay_f"][0]), f(inputs["ret_decay_b"][0])])[None, :]),
        "nat_w_qkv": f(inputs["nat_w_qkv"][0]),
        "nat_w_o": f(inputs["nat_w_o"][0]),
        "nat_bias": _nat_bias(f(inputs["nat_rpb"][0])),
        "pool_w": f(inputs["pool_w"][0]),
        "pool_sc": _vec_fm(f(inputs["pool_scale"][0])),
        "swa_w_qkv": f(inputs["swa_w_qkv"][0]),
        "swa_w_o": f(inputs["swa_w_o"][0]),
        "swa_sink": f(inputs["swa_sink"][0:1]),
    }
    wq = shared["swa_w_qkv"]
    dd = np.arange(64)
    partner = np.where((dd % 32) < 16, dd + 16, dd - 16)
    colq = (np.arange(16)[:, None] * 64 + partner[None, :]).reshape(-1)
    colk = 1024 + (np.arange(4)[:, None] * 64 + partner[None, :]).reshape(-1)
    shared["swa_w_perm"] = np.ascontiguousarray(wq[:, np.concatenate([colq, colk])])
    shared.update(_consts())
    maps = []
    for b in cores:
        m = dict(shared)
        m["xT"] = _fm(np.concatenate([x[b], ctx[b]], axis=0))
        m["cT"] = np.ascontiguousarray(np.stack([c[b], c_ctx], axis=0).reshape(2, DC, 128).transpose(2, 1, 0))
        maps.append(m)
    return maps


_NC_CACHE = {}


def kernel(**inputs):
    if "nc" not in _NC_CACHE:
        _NC_CACHE["nc"] = build()
    nc = _NC_CACHE["nc"]
    in_maps = make_in_maps(inputs)
    res = run_bass_kernel_spmd(nc, in_maps, core_ids=list(range(N_CORES)))
    outs = []
    for b in range(N_CORES):
        o = res.results[b]["outT"]
        outs.append(o.transpose(1, 0, 2).reshape(D, T_LAT).T)
    return np.ascontiguousarray(np.stack(outs, axis=0).astype(np.float32))
```

```python
import contextlib
import numpy as np
import concourse.bass as bass
import concourse.mybir as mybir
from concourse.bass_utils import run_bass_kernel_spmd

F32 = mybir.dt.float32
BF16 = mybir.dt.bfloat16
AF = mybir.ActivationFunctionType
ALU = mybir.AluOpType
AX = mybir.AxisListType

D = 1024
DC = 8
T_LAT = 4096
T_CTX = 256
T_ALL = T_LAT + T_CTX
DEPTH = 4
FF = 2816
FJ = 22
EPS = 1e-6
NT = 256
N_CORES = 4


class Res:
    __slots__ = ("name", "w", "r")

    def __init__(self, name=""):
        self.name = name
        self.w = None
        self.r = {}


class _Eng:
    def __init__(self, kk, name, handle):
        self.name = name
        self.h = handle
        self.sem = kk.new_sem("e_" + name)
        self.count = 0
        self.waited = {}
        self.pend_r = []
        self.pend_w = []


class K:
    def __init__(self, nc, n_dma_sems=16):
        self.nc = nc
        self.st = contextlib.ExitStack()
        self.sems = {}
        self.nsem = 0
        self.eng = {}
        for name, h in (("pe", nc.tensor), ("act", nc.scalar), ("dve", nc.vector),
                        ("pool", nc.gpsimd), ("sp", nc.sync)):
            self.eng[name] = _Eng(self, name, h)
        self.dma_pool = {}
        for q in ("sp", "pool"):
            self.dma_pool[q] = [[self.new_sem("d_%s%d" % (q, i)), 0] for i in range(n_dma_sems)]
        self.dma_rr = {"sp": 0, "pool": 0}
        self.ninst = 0

    def new_sem(self, name):
        s = self.st.enter_context(self.nc.semaphore(name))
        sid = self.nsem
        self.nsem += 1
        self.sems[sid] = s
        return sid

    def _wait(self, e, ev):
        sid, val = ev
        if e.waited.get(sid, 0) >= val:
            return
        e.waited[sid] = val
        e.h.wait_ge(self.sems[sid], val)
        self.ninst += 1

    def _deps(self, e, r, w, nowaw=False):
        evs = {}

        def add(ev):
            if ev is not None and evs.get(ev[0], 0) < ev[1]:
                evs[ev[0]] = ev[1]
        for x in r:
            add(x.w)
        for x in w:
            if not nowaw:
                add(x.w)
            for sid, val in x.r.items():
                add((sid, val))
        for sid, val in evs.items():
            if e.name == "pe" and sid == e.sem:
                continue
            self._wait(e, (sid, val))

    def _commit(self, ev, r, w):
        for x in r:
            if x.r.get(ev[0], 0) < ev[1]:
                x.r[ev[0]] = ev[1]
        for x in w:
            x.w = ev
            x.r = {}

    def op(self, eng, fn, r=(), w=(), inc=True):
        e = self.eng[eng]
        self._deps(e, r, w)
        inst = fn(e.h)
        self.ninst += 1
        if not inc:
            e.pend_r.extend(r)
            e.pend_w.extend(w)
            return None
        if e.count >= 30000:
            e.sem = self.new_sem("e_%s_%d" % (eng, self.nsem))
            e.count = 0
        e.count += 1
        inst.then_inc(self.sems[e.sem], 1)
        ev = (e.sem, e.count)
        self._commit(ev, list(r) + e.pend_r, list(w) + e.pend_w)
        e.pend_r = []
        e.pend_w = []
        return ev

    def dma(self, q, out, in_, r=(), w=(), nowaw=False):
        e = self.eng[q]
        self._deps(e, r, w, nowaw=nowaw)
        pool = self.dma_pool[q]
        i = self.dma_rr[q]
        self.dma_rr[q] = (i + 1) % len(pool)
        slot = pool[i]
        if slot[1] > 0:
            self._wait(e, (slot[0], slot[1]))
        slot[1] += 16
        e.h.dma_start(out=out, in_=in_).then_inc(self.sems[slot[0]], 16)
        self.ninst += 1
        ev = (slot[0], slot[1])
        self._commit(ev, r, w)
        return ev

    def barrier(self, engs=("pe", "act", "dve", "pool", "sp")):
        for x in engs:
            e = self.eng[x]
            for y in self.eng.values():
                if y is not e and y.count > 0:
                    self._wait(e, (y.sem, y.count))
            for pool in self.dma_pool.values():
                for sid, val in pool:
                    if val > 0:
                        self._wait(e, (sid, val))


def build(cfg=None):
    cfg = cfg or {}
    n_layers = cfg.get("n_layers", DEPTH)
    mixers = cfg.get("mixers", True)
    dbg = cfg.get("dbg", False)

    nc = bass.Bass("TRN2", target_bir_lowering=False)
    kk = K(nc)
    st = kk.st

    def dram_in(name, shape, dt=F32):
        return nc.dram_tensor(name, list(shape), dt, kind="ExternalInput").ap()

    xT = dram_in("xT", [128, DC, T_ALL])
    cT = dram_in("cT", [128, DC, 2])
    w_mod = dram_in("w_mod", [DEPTH, D, 9 * D])
    bmodT = dram_in("bmodT", [128, DEPTH, 72])
    normgT = dram_in("normgT", [128, DEPTH, 3, DC])
    fnormgT = dram_in("fnormgT", [128, DC])
    ffn_w_in = dram_in("ffn_w_in", [DEPTH, 2, D, 2 * FF])
    ffn_w_out = dram_in("ffn_w_out", [DEPTH, 2, FF, D])
    outT = nc.dram_tensor("outT", [128, DC, T_LAT], F32, kind="ExternalOutput").ap()
    hA = nc.dram_tensor("hA", [128, DC, T_ALL], F32).ap()
    hB = nc.dram_tensor("hB", [128, DC, T_ALL], F32).ap()
    hbufs = [hA, hB]
    ret_w_in = dram_in("ret_w_in", [D, 6144])
    ret_w_out = dram_in("ret_w_out", [2048, D])
    ret_gn = dram_in("ret_gn", [1, 2048])
    ret_decay = dram_in("ret_decay", [1, 8])
    ret_tabs = dram_in("ret_tabs", [128, 8, 128])
    ret_cs = dram_in("ret_cs", [128, 2, T_ALL])
    nat_w_qkv = dram_in("nat_w_qkv", [D, 3072])
    nat_w_o = dram_in("nat_w_o", [D, D])
    nat_bias = dram_in("nat_bias", [16, 128, 5, 7, 128])
    pool_w = dram_in("pool_w", [4, 256, 256])
    pool_sc = dram_in("pool_sc", [128, DC])
    pool_icnt = dram_in("pool_icnt", [4, T_ALL])
    swa_w_qkv = dram_in("swa_w_qkv", [D, 1536])
    swa_w_perm = dram_in("swa_w_perm", [D, 1280])
    swa_w_o = dram_in("swa_w_o", [D, D])
    swa_sink = dram_in("swa_sink", [1, 16])
    swa_cs = dram_in("swa_cs", [128, 2, T_ALL])
    swa_bias = dram_in("swa_bias", [128, 3, 5, 128])
    qT_d = nc.dram_tensor("qT_d", [128, DC, T_ALL], BF16).ap()
    kT_d = nc.dram_tensor("kT_d", [128, DC, T_ALL], BF16).ap()
    v_d = nc.dram_tensor("v_d", [T_ALL, 2048], BF16).ap()
    g_d = nc.dram_tensor("g_d", [T_ALL, 2048], F32).ap()
    oT_d = nc.dram_tensor("oT_d", [128, 16, T_ALL], BF16).ap()
    xl_d = nc.dram_tensor("xl_d", [128, DC, T_ALL], F32).ap()
    sb_d = nc.dram_tensor("sb_d", [34, 128, 2, 512], BF16).ap()

    def sb(name, shape, dt):
        return st.enter_context(nc.sbuf_tensor(name, list(shape), dt))

    def ps(name, shape, dt=F32):
        return st.enter_context(nc.psum_tensor(name, list(shape), dt))

    _uid = [0]

    def sbt(ph, name, shape, dt):
        _uid[0] += 1
        return ph.enter_context(nc.sbuf_tensor("%s_%d" % (name, _uid[0]), list(shape), dt))

    ones_f = sb("ones_f", [128, 128], F32)
    mods = sb("mods", [128, DEPTH, 72, 2], F32)
    gsc = sb("gsc", [128, DEPTH, 3, DC, 2], F32)
    gat = sb("gat", [128, DEPTH, 3, DC, 2], F32)
    ng = sb("ng", [128, DEPTH, 3, DC], F32)
    fng = sb("fng", [128, DC], F32)
    bm = sb("bm", [128, DEPTH, 72], F32)
    sT = sb("sT", [128, DC, 2], F32)
    R_const = Res("const")
    R_mods = Res("mods")

    kk.op("dve", lambda e: e.memset(ones_f[:], 1.0), w=[R_const])
    kk.dma("sp", ng[:], normgT, w=[R_const])
    kk.dma("sp", fng[:], fnormgT, w=[R_const])
    kk.dma("sp", bm[:], bmodT, w=[R_const])
    kk.dma("sp", sT[:], cT, w=[R_const])
    kk.op("act", lambda e: e.activation(out=sT[:], in_=sT[:], func=AF.Silu), r=[R_const], w=[R_const])

    ps_stat = ps("ps_stat", [128, 512])
    ps_a = [ps("ps_a%d" % i, [128, 512]) for i in range(2)]
    ps_b = [ps("ps_b%d" % i, [128, 512]) for i in range(2)]
    ps_o = [ps("ps_o%d" % i, [128, 512]) for i in range(2)]
    R_ps_stat = Res()
    R_ps_a = [Res(), Res()]
    R_ps_b = [Res(), Res()]
    R_ps_o = [Res(), Res()]
    ps_x = ps("ps_x", [128, 512])
    R_ps_x = Res()
    poolS = [(ps_a[0], R_ps_a[0]), (ps_a[1], R_ps_a[1]), (ps_b[0], R_ps_b[0]), (ps_b[1], R_ps_b[1])]
    poolO = [(ps_o[0], R_ps_o[0]), (ps_o[1], R_ps_o[1]), (ps_x, R_ps_x)]
    _rrS = [0]
    _rrO = [0]

    def bankS():
        _rrS[0] = (_rrS[0] + 1) % len(poolS)
        return poolS[_rrS[0]]

    def bankO():
        _rrO[0] = (_rrO[0] + 1) % len(poolO)
        return poolO[_rrO[0]]

    def mm_acc(out_ap, R_out, pairs):
        last = len(pairs) - 1
        ev = None
        for idx, (l_, r_, rd) in enumerate(pairs):
            ev = kk.op("pe", lambda e, l_=l_, r_=r_, idx=idx: e.matmul(out_ap, l_, r_, start=(idx == 0), stop=(idx == last)),
                       r=rd, w=[R_out], inc=(idx == last))
        return ev

    identf = sb("identf", [128, 128], F32)
    kk.dma("sp", identf[:], ret_tabs[:, 7, :], w=[R_const])
    with contextlib.ExitStack() as ph:
        wm = [sbt(ph, "wm%d" % i, [128, DC, 1024], F32) for i in range(2)]
        R_wm = [Res(), Res()]
        modrow = sbt(ph, "modrow", [2, 9 * D], F32)
        R_modrow = Res()
        blk = 0
        for li in range(n_layers):
            for nb in range(9):
                s = blk % 2
                blk += 1
                src = w_mod[li, :, nb * 1024:(nb + 1) * 1024].rearrange("(kc p) n -> p kc n", p=128)
                kk.dma("sp", wm[s][:], src, w=[R_wm[s]])
                for half in range(2):
                    (pS, RS) = bankS()
                    mm_acc(pS[0:2, :512], RS, [(sT[:, kc, :], wm[s][:, kc, half * 512:(half + 1) * 512], [R_wm[s], R_const])
                                               for kc in range(DC)])
                    c0 = nb * 1024 + half * 512
                    kk.op("act", lambda e, pS=pS, c0=c0: e.activation(out=modrow[0:2, c0:c0 + 512], in_=pS[0:2, :512],
                                                                       func=AF.Identity), r=[RS], w=[R_modrow])
            for n in range(72):
                kk.op("pe", lambda e, n=n: e.transpose(ps_stat[:, n * 2:(n + 1) * 2], modrow[0:2, n * 128:(n + 1) * 128],
                                                       identf[0:2, 0:2]),
                      r=[R_modrow, R_const], w=[R_ps_stat], inc=(n == 71))
            kk.op("dve", lambda e, li=li: e.tensor_tensor(
                out=mods[:, li, :, :], in0=ps_stat[:, 0:144].rearrange("p (n w) -> p n w", w=2),
                in1=bm[:, li, :].unsqueeze(2).to_broadcast([128, 72, 2]),
                op=ALU.add), r=[R_ps_stat, R_const], w=[R_mods])
        kk.barrier()

    for li in range(n_layers):
        for j in range(3):
            sc = mods[:, li, (j * 3 + 1) * 8:(j * 3 + 2) * 8, :]
            gt = mods[:, li, (j * 3 + 2) * 8:(j * 3 + 3) * 8, :]
            for w_ in range(2):
                kk.op("dve", lambda e, li=li, j=j, w_=w_, sc=sc: e.scalar_tensor_tensor(
                    out=gsc[:, li, j, :, w_], in0=sc[:, :, w_], scalar=1.0, in1=ng[:, li, j, :],
                    op0=ALU.add, op1=ALU.mult), r=[R_mods, R_const], w=[R_mods])
            kk.op("dve", lambda e, li=li, j=j, gt=gt: e.tensor_scalar(
                out=gat[:, li, j, :, :], in0=gt, scalar1=(1.0 if j == 1 else 0.5), scalar2=None,
                op0=ALU.mult), r=[R_mods], w=[R_mods])
    kk.barrier()

    tiles = [(t0, NT, 0) for t0 in range(0, T_LAT, NT)] + [(T_LAT, T_CTX, 1)]

    def norm_mod(ph, name):
        o = {}
        o["sqt"] = sbt(ph, name + "_sqt", [128, DC, NT], F32)
        o["Rsqt"] = Res()
        o["ssum"] = sbt(ph, name + "_ssum", [128, NT], F32)
        o["Rssum"] = Res()
        o["rstd"] = sbt(ph, name + "_rstd", [128, NT], F32)
        o["Rrstd"] = Res()
        return o

    def _emit(steps, eng, fn, r, w):
        if steps is None:
            kk.op(eng, fn, r=r, w=w)
        else:
            steps.append(lambda: kk.op(eng, fn, r=r, w=w))

    def emit_rstd(o, ht, R_ht, n, steps=None):
        _emit(steps, "dve", lambda e: e.tensor_tensor(out=o["sqt"][:, :, :n], in0=ht[:, :, :n], in1=ht[:, :, :n], op=ALU.mult),
              [R_ht], [o["Rsqt"]])
        _emit(steps, "dve", lambda e: e.tensor_reduce(out=o["ssum"][:, :n], in_=o["sqt"][:, :, :n].rearrange("p c n -> p n c"),
                                                      axis=AX.X, op=ALU.add), [o["Rsqt"]], [o["Rssum"]])
        _emit(steps, "pe", lambda e: e.matmul(ps_stat[:, :n], ones_f[:], o["ssum"][:, :n], start=True, stop=True),
              [o["Rssum"], R_const], [R_ps_stat])
        _emit(steps, "act", lambda e: e.activation(out=o["rstd"][:, :n], in_=ps_stat[:, :n], func=AF.Sqrt,
                                                   scale=1.0 / D, bias=EPS), [R_ps_stat], [o["Rrstd"]])
        _emit(steps, "dve", lambda e: e.reciprocal(out=o["rstd"][:, :n], in_=o["rstd"][:, :n]), [o["Rrstd"]], [o["Rrstd"]])

    def emit_xl(o, ht, R_ht, n, li, j, w_, dst, R_dst, steps=None):
        _emit(steps, "dve", lambda e: e.tensor_tensor(
            out=o["sqt"][:, :, :n], in0=ht[:, :, :n], in1=o["rstd"][:, :n].unsqueeze(1).to_broadcast([128, DC, n]),
            op=ALU.mult), [R_ht, o["Rrstd"]], [o["Rsqt"]])
        for c in range(DC):
            _emit(steps, "act", lambda e, c=c: e.activation(
                out=dst[:, c, :n], in_=o["sqt"][:, c, :n], func=AF.Identity,
                scale=gsc[:, li, j, c, w_:w_ + 1], bias=mods[:, li, (j * 3) * 8 + c, w_:w_ + 1]),
                [o["Rsqt"], R_mods], [R_dst])

    def ffn_phase(li, s_, j, h_in, h_out, tl):
        with contextlib.ExitStack() as ph:
            win = sbt(ph, "win", [128, DC, 2 * FF], BF16)
            wout = sbt(ph, "wout", [128, FJ, D], BF16)
            R_win = [Res() for _ in range(DC)]
            R_wout = Res()
            for kc in range(DC):
                kk.dma("pool", win[:, kc, :], ffn_w_in[li, s_, kc * 128:(kc + 1) * 128, :], w=[R_win[kc]])
            kk.dma("pool", wout[:], ffn_w_out[li, s_].rearrange("(j p) d -> p j d", p=128), w=[R_wout])
            ht = [sbt(ph, "ht%d" % i, [128, DC, NT], F32) for i in range(2)]
            R_ht = [Res(), Res()]
            xl = [sbt(ph, "xl%d" % i, [128, DC, NT], BF16) for i in range(2)]
            R_xl = [Res(), Res()]
            g = sbt(ph, "g", [128, FJ, NT], BF16)
            R_g = [Res() for _ in range(FJ)]
            sa = [sbt(ph, "sa%d" % i, [128, NT], F32) for i in range(2)]
            R_sa = [Res(), Res()]
            o = norm_mod(ph, "f")

            def prep(ti, steps):
                t0, n, w_ = tl[ti]
                b = ti % 2
                kk.dma("sp", ht[b][:, :, :n], h_in[:, :, t0:t0 + n], w=[R_ht[b]])
                emit_rstd(o, ht[b], R_ht[b], n, steps)
                emit_xl(o, ht[b], R_ht[b], n, li, j, w_, xl[b], R_xl[b], steps)

            prep(0, None)
            for ti, (t0, n, w_) in enumerate(tl):
                b = ti % 2
                steps = []
                if ti + 1 < len(tl):
                    prep(ti + 1, steps)
                for jj in range(FJ):
                    pb = jj % 2
                    for kc in range(DC):
                        kk.op("pe", lambda e, jj=jj, kc=kc, pb=pb: e.matmul(
                            ps_a[pb][:, :n], win[:, kc, jj * 128:(jj + 1) * 128], xl[b][:, kc, :n],
                            start=(kc == 0), stop=(kc == DC - 1)),
                            r=[R_win[kc], R_xl[b]], w=[R_ps_a[pb]], inc=(kc == DC - 1))
                    for kc in range(DC):
                        kk.op("pe", lambda e, jj=jj, kc=kc, pb=pb: e.matmul(
                            ps_b[pb][:, :n], win[:, kc, FF + jj * 128:FF + (jj + 1) * 128], xl[b][:, kc, :n],
                            start=(kc == 0), stop=(kc == DC - 1)),
                            r=[R_win[kc], R_xl[b]], w=[R_ps_b[pb]], inc=(kc == DC - 1))
                    kk.op("act", lambda e, pb=pb: e.activation(out=sa[pb][:, :n], in_=ps_a[pb][:, :n], func=AF.Silu),
                          r=[R_ps_a[pb]], w=[R_sa[pb]])
                    kk.op("dve", lambda e, pb=pb, jj=jj: e.tensor_tensor(out=g[:, jj, :n], in0=sa[pb][:, :n],
                                                                         in1=ps_b[pb][:, :n], op=ALU.mult),
                          r=[R_sa[pb], R_ps_b[pb]], w=[R_g[jj]])
                    if jj >= 2 and steps:
                        steps.pop(0)()
                while steps:
                    steps.pop(0)()
                for d in range(DC):
                    pb = d % 2
                    for jj in range(FJ):
                        kk.op("pe", lambda e, jj=jj, d=d, pb=pb: e.matmul(
                            ps_o[pb][:, :n], wout[:, jj, d * 128:(d + 1) * 128], g[:, jj, :n],
                            start=(jj == 0), stop=(jj == FJ - 1)),
                            r=[R_wout, R_g[jj]], w=[R_ps_o[pb]], inc=(jj == FJ - 1))
                    kk.op("dve", lambda e, d=d, pb=pb, b=b: e.scalar_tensor_tensor(
                        out=ht[b][:, d, :n], in0=ps_o[pb][:, :n], scalar=gat[:, li, j, d, w_:w_ + 1],
                        in1=ht[b][:, d, :n], op0=ALU.mult, op1=ALU.add),
                        r=[R_ps_o[pb], R_mods, R_ht[b]], w=[R_ht[b]])
                kk.dma("sp", h_out[:, :, t0:t0 + n], ht[b][:, :, :n], r=[R_ht[b]])
            kk.barrier()

    def final_phase(h_in):
        with contextlib.ExitStack() as ph:
            ht = [sbt(ph, "fht%d" % i, [128, DC, NT], F32) for i in range(2)]
            R_ht = [Res(), Res()]
            ot = [sbt(ph, "fot%d" % i, [128, DC, NT], F32) for i in range(2)]
            R_ot = [Res(), Res()]
            o = norm_mod(ph, "fn")
            for ti, (t0, n, w_) in enumerate(tiles):
                if w_ == 1:
                    continue
                b = ti % 2
                kk.dma("sp", ht[b][:, :, :n], h_in[:, :, t0:t0 + n], w=[R_ht[b]])
                emit_rstd(o, ht[b], R_ht[b], n)
                for c in range(DC):
                    kk.op("dve", lambda e, c=c, b=b: e.scalar_tensor_tensor(
                        out=ot[b][:, c, :n], in0=ht[b][:, c, :n], scalar=fng[:, c:c + 1], in1=o["rstd"][:, :n],
                        op0=ALU.mult, op1=ALU.mult), r=[R_ht[b], o["Rrstd"], R_const], w=[R_ot[b]])
                kk.dma("sp", outT[:, :, t0:t0 + n], ot[b][:, :, :n], r=[R_ot[b]])
            kk.barrier()

    def proj_phase(li, kind, h_in, tl):
        with contextlib.ExitStack() as ph:
            if kind == "ret":
                ncol = 6144
                W = sbt(ph, "pw", [128, DC, ncol], BF16)
                R_Wl = [Res() for _ in range(DC)]
                for kc in range(DC):
                    kk.dma("pool", W[:, kc, :], ret_w_in[kc * 128:(kc + 1) * 128, :], w=[R_Wl[kc]])
                fm = [(0, 8, qT_d, "ret", 0), (1024, 8, kT_d, "ret", 0)]
                tm = [(2048, 2048, v_d, AF.Identity, "v"), (4096, 2048, g_d, AF.Silu, "g")]
                cs_d = ret_cs
            elif kind == "nat":
                ncol = 3072
                W = sbt(ph, "pw", [128, DC, ncol], BF16)
                R_Wl = [Res()]
                kk.dma("pool", W[:], nat_w_qkv.rearrange("(kc p) n -> p kc n", p=128), w=[R_Wl[0]])
                fm = [(0, 8, qT_d, None, 0), (1024, 8, kT_d, None, 0)]
                tm = [(2048, 1024, v_d, AF.Identity, "v")]
                cs_d = None
            else:
                ncol = 1536 + 1280
                W = sbt(ph, "pw", [128, DC, ncol], BF16)
                R_Wl = [Res(), Res()]
                kk.dma("pool", W[:, :, 0:1536], swa_w_qkv.rearrange("(kc p) n -> p kc n", p=128), w=[R_Wl[0]])
                kk.dma("pool", W[:, :, 1536:2816], swa_w_perm.rearrange("(kc p) n -> p kc n", p=128), w=[R_Wl[1]])
                fm = [(0, 8, qT_d, "swa", 1536), (1024, 2, kT_d, "swa", 1536 + 1024)]
                tm = [(1280, 256, v_d, AF.Identity, "v")]
                cs_d = swa_cs
            ht = [sbt(ph, "pht%d" % i, [128, DC, NT], F32) for i in range(2)]
            R_ht = [Res(), Res()]
            xl = sbt(ph, "pxl", [128, DC, NT], BF16)
            R_xl = Res()
            stg = [sbt(ph, "pst%d" % i, [128, DC, NT], BF16) for i in range(2)]
            R_stg = [Res(), Res()]
            stv = sbt(ph, "pstv", [128, 2048], BF16)
            R_stv = Res()
            stgg = sbt(ph, "pstg", [128, 2048], F32)
            R_stgg = Res()
            cs = sbt(ph, "pcs", [128, 2, NT], F32)
            R_cs = Res()
            t1 = sbt(ph, "pt1", [128, NT], F32)
            t2 = sbt(ph, "pt2", [128, NT], F32)
            R_t1, R_t2 = Res(), Res()
            o = norm_mod(ph, "p")
            for ti, (t0, n, w_) in enumerate(tl):
                b = ti % 2
                kk.dma("sp", ht[b][:, :, :n], h_in[:, :, t0:t0 + n], w=[R_ht[b]])
                emit_rstd(o, ht[b], R_ht[b], n)
                emit_xl(o, ht[b], R_ht[b], n, li, 1, w_, xl, R_xl)
                if cs_d is not None:
                    kk.dma("sp", cs[:, :, :n], cs_d[:, :, t0:t0 + n], w=[R_cs])

                def proj(off, m):
                    (pa, Ra) = bankS()
                    mm_acc(pa[:, :n], Ra, [(W[:, kc, off + m * 128:off + (m + 1) * 128], xl[:, kc, :n], R_Wl + [R_xl])
                                            for kc in range(DC)])
                    return pa, Ra

                def tt(out, a, b_, op, r, w):
                    kk.op("dve", lambda e: e.tensor_tensor(out=out, in0=a, in1=b_, op=op), r=r, w=w)

                for fi, (off, nch, dst, mode, poff) in enumerate(fm):
                    sg, R_sg = stg[fi], R_stg[fi]
                    if mode == "ret":
                        for hh in range(nch // 2):
                            pa, Ra = proj(off, 2 * hh)
                            pb_, Rb = proj(off, 2 * hh + 1)
                            tt(t1[:, :n], pa[:, :n], cs[:, 0, :n], ALU.mult, [Ra, R_cs], [R_t1])
                            tt(t2[:, :n], pb_[:, :n], cs[:, 1, :n], ALU.mult, [Rb, R_cs], [R_t2])
                            tt(sg[:, 2 * hh, :n], t1[:, :n], t2[:, :n], ALU.subtract, [R_t1, R_t2], [R_sg])
                            tt(t1[:, :n], pb_[:, :n], cs[:, 0, :n], ALU.mult, [Rb, R_cs], [R_t1])
                            tt(t2[:, :n], pa[:, :n], cs[:, 1, :n], ALU.mult, [Ra, R_cs], [R_t2])
                            tt(sg[:, 2 * hh + 1, :n], t1[:, :n], t2[:, :n], ALU.add, [R_t1, R_t2], [R_sg])
                    elif mode == "swa":
                        for m in range(nch):
                            pa, Ra = proj(off, m)
                            pb_, Rb = proj(poff, m)
                            tt(t1[:, :n], pa[:, :n], cs[:, 0, :n], ALU.mult, [Ra, R_cs], [R_t1])
                            tt(t2[:, :n], pb_[:, :n], cs[:, 1, :n], ALU.mult, [Rb, R_cs], [R_t2])
                            tt(sg[:, m, :n], t1[:, :n], t2[:, :n], ALU.add, [R_t1, R_t2], [R_sg])
                    else:
                        for m in range(nch):
                            pa, Ra = proj(off, m)
                            kk.op("act", lambda e, pa=pa, m=m, sg=sg: e.activation(out=sg[:, m, :n], in_=pa[:, :n], func=AF.Identity),
                                  r=[Ra], w=[R_sg])
                    kk.dma("sp", dst[:, 0:nch, t0:t0 + n], sg[:, 0:nch, :n], r=[R_sg])
                for sub in range(n // 128):
                    for (off, ncols, dstd, fn, nm) in tm:
                        st_t, R_st = (stv, R_stv) if nm == "v" else (stgg, R_stgg)
                        for blk in range((ncols + 511) // 512):
                            cw = min(512, ncols - blk * 512)
                            (pa, Ra) = bankS()
                            mm_acc(pa[:, :cw], Ra, [(xl[:, kc, sub * 128:(sub + 1) * 128],
                                                     W[:, kc, off + blk * 512:off + blk * 512 + cw], R_Wl + [R_xl])
                                                    for kc in range(DC)])
                            kk.op("act", lambda e, pa=pa, blk=blk, cw=cw, st_t=st_t, fn=fn: e.activation(
                                out=st_t[:, blk * 512:blk * 512 + cw], in_=pa[:, :cw], func=fn), r=[Ra], w=[R_st])
                        kk.dma("sp", dstd[t0 + sub * 128:t0 + (sub + 1) * 128, 0:ncols], st_t[:, 0:ncols], r=[R_st])
            kk.barrier()

    def attn_phase(kind, do_ctx):
        with contextlib.ExitStack() as ph:
            ntyp, nch = (5, 7) if kind == "nat" else (3, 5)
            bias = sbt(ph, "abias", [128, ntyp, nch, 128], F32)
            R_bias = Res()
            ones_b = sbt(ph, "aones", [128, 64], BF16)
            R_c = Res()
            kk.op("dve", lambda e: e.memset(ones_b[:], 1.0), w=[R_c])
            sk = sbt(ph, "ask", [128, 16], F32)
            if kind == "swa":
                kk.dma("sp", bias[:], swa_bias, w=[R_bias])
                kk.dma("sp", sk[:], swa_sink.partition_broadcast(128), w=[R_c])
                kk.op("act", lambda e: e.activation(out=sk[:], in_=sk[:], func=AF.Exp), r=[R_c], w=[R_c])
            qh = [sbt(ph, "aqh%d" % i, [64, T_ALL], BF16) for i in range(2)]
            kh = [sbt(ph, "akh%d" % i, [64, T_ALL], BF16) for i in range(2)]
            vh = [sbt(ph, "avh%d" % i, [128, 34, 65], BF16) for i in range(2)]
            R_qh, R_kh, R_vh = [Res(), Res()], [Res(), Res()], [Res(), Res()]
            for i in range(2):
                kk.op("dve", lambda e, i=i: e.memset(vh[i][:, :, 64:65], 1.0), w=[R_vh[i]])
            otm = [sbt(ph, "aotm%d" % i, [128, 34, 128], BF16) for i in range(2)]
            R_otm = [Res(), Res()]
            ostg = sbt(ph, "aostg", [128, 34 * 128], BF16)
            R_ostg = Res()
            ident_b = sbt(ph, "aident", [128, 128], BF16)
            kk.op("dve", lambda e: e.tensor_copy(out=ident_b[:], in_=identf[:]), r=[R_const], w=[R_c])
            psT = ps_stat[:].bitcast(BF16)
            bias2 = [bias, sbt(ph, "abias2", [128, ntyp, nch, 128], F32)] if kind == "nat" else [bias, bias]
            R_bias2 = [R_bias, Res()] if kind == "nat" else [R_bias, R_bias]
            tmp = [sbt(ph, "atmp%d" % i, [128, nch * 128], F32) for i in range(2)]
            R_tmp = [Res(), Res()]
            pt = [sbt(ph, "apt%d" % i, [128, nch * 128], BF16) for i in range(2)]
            R_pt = [Res(), Res()]
            den = [sbt(ph, "aden%d" % i, [128, 1], F32) for i in range(2)]
            R_den = [Res(), Res()]
            Tq = T_ALL if do_ctx else T_LAT

            def load_head(h):
                hs = h % 2
                kvh = h if kind == "nat" else h // 4
                kk.dma("sp", qh[hs][:], qT_d[(h % 2) * 64:(h % 2) * 64 + 64, h // 2, :], w=[R_qh[hs]])
                kk.dma("sp", kh[hs][:], kT_d[(kvh % 2) * 64:(kvh % 2) * 64 + 64, kvh // 2, :], w=[R_kh[hs]])
                kk.dma("sp", vh[hs][:, :, 0:64], v_d[:, kvh * 64:(kvh + 1) * 64].rearrange("(n p) d -> p n d", p=128),
                       w=[R_vh[hs]])
                if kind == "nat":
                    kk.dma("sp", bias2[hs][:], nat_bias[h], w=[R_bias2[hs]])

            units = []
            for h in range(16):
                for qi in range(32):
                    if kind == "nat":
                        typ = 0 if 2 <= qi <= 29 else {0: 1, 1: 2, 30: 3, 31: 4}[qi]
                        base = min(max(qi - 2, 0), 27)
                        chunks = [base + i for i in range(5)] + [32, 33]
                    else:
                        typ = 1 if qi == 0 else (2 if qi == 31 else 0)
                        chunks = [max(qi - 1, 0), qi, min(qi + 1, 31), 32, 33]
                    units.append([h, qi * 128, chunks, typ, qi == 0, False])
                if do_ctx:
                    for qc in (32, 33):
                        units.append([h, qc * 128, [32, 33], None, False, False])
                units[-1][5] = True
            state = {}

            def s1(ui):
                h, q0, chunks, typ, first, lasth = units[ui]
                hs = h % 2
                ncu = len(chunks)
                u2 = ui % 2
                banks = []
                for c0 in range(0, ncu, 4):
                    cn = min(4, ncu - c0)
                    (pS, RS) = bankS()
                    banks.append((pS, RS, c0, cn))
                    for ci in range(c0, c0 + cn):
                        kc_ = chunks[ci]
                        kk.op("pe", lambda e, pS=pS, ci=ci, c0=c0, kc_=kc_, q0=q0, hs=hs: e.matmul(
                            pS[:, (ci - c0) * 128:(ci - c0 + 1) * 128], kh[hs][:, kc_ * 128:(kc_ + 1) * 128],
                            qh[hs][:, q0:q0 + 128], start=True, stop=True),
                            r=[R_kh[hs], R_qh[hs]], w=[RS], inc=(ci == c0 + cn - 1))
                if typ is not None:
                    for (pS, RS, c0, cn) in banks:
                        kk.op("dve", lambda e, pS=pS, c0=c0, cn=cn, typ=typ, u2=u2, hs=hs: e.scalar_tensor_tensor(
                            out=tmp[u2][:, c0 * 128:(c0 + cn) * 128].rearrange("p (c q) -> p c q", q=128),
                            in0=pS[:, :cn * 128].rearrange("p (c q) -> p c q", q=128), scalar=0.125,
                            in1=bias2[hs][:, typ, c0:c0 + cn, :], op0=ALU.mult, op1=ALU.add),
                            r=[RS, R_bias2[hs]], w=[R_tmp[u2]])
                    kk.op("act", lambda e, u2=u2, ncu=ncu: e.activation(out=pt[u2][:, :ncu * 128], in_=tmp[u2][:, :ncu * 128],
                                                                          func=AF.Exp), r=[R_tmp[u2]], w=[R_pt[u2]])
                else:
                    for (pS, RS, c0, cn) in banks:
                        kk.op("act", lambda e, pS=pS, c0=c0, cn=cn, u2=u2: e.activation(
                            out=pt[u2][:, c0 * 128:(c0 + cn) * 128], in_=pS[:, :cn * 128], func=AF.Exp, scale=0.125),
                            r=[RS], w=[R_pt[u2]])

            ntile = 34 if do_ctx else 32

            def s2(ui):
                h, q0, chunks, typ, first, lasth = units[ui]
                hs = h % 2
                if first and h + 1 < 16:
                    load_head(h + 1)
                ncu = len(chunks)
                u2 = ui % 2
                pp = (h // 2) % 2
                par = h % 2
                qt = q0 // 128
                (pO, RO) = bankO()
                mm_acc(pO[:, 0:65], RO, [(pt[u2][:, ci * 128:(ci + 1) * 128], vh[hs][:, chunks[ci], 0:65], [R_vh[hs], R_pt[u2]])
                                         for ci in range(ncu)])
                if kind == "swa":
                    kk.op("dve", lambda e, pO=pO, u2=u2, h=h: e.tensor_scalar(
                        out=den[u2][:], in0=pO[:, 64:65], scalar1=sk[:, h:h + 1], scalar2=None, op0=ALU.add),
                        r=[RO, R_c], w=[R_den[u2]])
                    kk.op("dve", lambda e, u2=u2: e.reciprocal(out=den[u2][:], in_=den[u2][:]), r=[R_den[u2]], w=[R_den[u2]])
                else:
                    kk.op("dve", lambda e, pO=pO, u2=u2: e.reciprocal(out=den[u2][:], in_=pO[:, 64:65]),
                          r=[RO], w=[R_den[u2]])
                kk.op("dve", lambda e, pO=pO, u2=u2, pp=pp, par=par, qt=qt: e.tensor_scalar(
                    out=otm[pp][:, qt, par * 64:(par + 1) * 64], in0=pO[:, 0:64], scalar1=den[u2][:, 0:1], scalar2=None,
                    op0=ALU.mult), r=[RO, R_den[u2]], w=[R_otm[pp]])
                if lasth and par == 1:
                    for t in range(ntile):
                        reg = (t % 8) * 128
                        kk.op("pe", lambda e, t=t, reg=reg, pp=pp: e.transpose(psT[:, reg:reg + 128], otm[pp][:, t, :], ident_b[:]),
                              r=[R_otm[pp], R_c], w=[R_ps_stat], inc=(t % 4 == 3 or t == ntile - 1))
                        if t % 4 == 3 or t == ntile - 1:
                            t0_ = t - (t % 4)
                            half = ((t0_ % 8) // 4) * 512
                            nn = (t - t0_ + 1) * 128
                            kk.op("act", lambda e, t0_=t0_, half=half, nn=nn: e.activation(
                                out=ostg[:, t0_ * 128:t0_ * 128 + nn], in_=psT[:, half:half + nn], func=AF.Identity),
                                r=[R_ps_stat], w=[R_ostg])
                    kk.dma("sp", oT_d[:, h // 2, 0:Tq], ostg[:, 0:Tq], r=[R_ostg])

            LA = 1
            load_head(0)
            for k in range(len(units) + LA):
                if k < len(units):
                    s1(k)
                if k - LA >= 0:
                    s2(k - LA)
            kk.barrier()

    def oproj_phase(li, Wd, KC, h_in, h_out, tl):
        with contextlib.ExitStack() as ph:
            W = sbt(ph, "ow", [128, KC, D], BF16)
            R_W = Res()
            kk.dma("pool", W[:], Wd.rearrange("(c p) d -> p c d", p=128), w=[R_W])
            ht = [sbt(ph, "oht%d" % i, [128, DC, NT], F32) for i in range(2)]
            R_ht = [Res(), Res()]
            ot = [sbt(ph, "oot%d" % i, [128, KC, NT], BF16) for i in range(2)]
            R_ot = [Res(), Res()]
            for ti, (t0, n, w_) in enumerate(tl):
                b = ti % 2
                kk.dma("sp", ht[b][:, :, :n], h_in[:, :, t0:t0 + n], w=[R_ht[b]])
                kk.dma("sp", ot[b][:, :, :n], oT_d[:, 0:KC, t0:t0 + n], w=[R_ot[b]])
                for d in range(DC):
                    (pO, RO) = bankO()
                    mm_acc(pO[:, :n], RO, [(W[:, c, d * 128:(d + 1) * 128], ot[b][:, c, :n], [R_W, R_ot[b]]) for c in range(KC)])
                    kk.op("dve", lambda e, d=d, pO=pO, b=b: e.scalar_tensor_tensor(
                        out=ht[b][:, d, :n], in0=pO[:, :n], scalar=gat[:, li, 1, d, w_:w_ + 1],
                        in1=ht[b][:, d, :n], op0=ALU.mult, op1=ALU.add),
                        r=[RO, R_mods, R_ht[b]], w=[R_ht[b]])
                kk.dma("sp", h_out[:, :, t0:t0 + n], ht[b][:, :, :n], r=[R_ht[b]])
            kk.barrier()

    def ret_phase(do_ctx):
        with contextlib.ExitStack() as ph:
            rt = sbt(ph, "rtabs", [128, 8, 128], F32)
            R_rt = Res()
            kk.dma("sp", rt[:], ret_tabs, w=[R_rt])
            lg = sbt(ph, "rlg", [128, 8], F32)
            kk.dma("sp", lg[:], ret_decay.partition_broadcast(128), w=[R_rt])
            kk.op("act", lambda e: e.activation(out=lg[:], in_=lg[:], func=AF.Sigmoid), r=[R_rt], w=[R_rt])
            kk.op("act", lambda e: e.activation(out=lg[:], in_=lg[:], func=AF.Ln), r=[R_rt], w=[R_rt])
            lgn = sbt(ph, "rlgn", [128, 8], F32)
            kk.op("dve", lambda e: e.tensor_scalar(out=lgn[:], in0=lg[:], scalar1=-1.0, scalar2=None, op0=ALU.mult),
                  r=[R_rt], w=[R_rt])
            ident = sbt(ph, "rident", [128, 128], BF16)
            kk.op("dve", lambda e: e.tensor_copy(out=ident[:], in_=rt[:, 7, :]), r=[R_rt], w=[R_rt])
            gng = sbt(ph, "rgng", [128, 2048], F32)
            kk.dma("sp", gng[:], ret_gn.partition_broadcast(128), w=[R_rt])
            tb = sbt(ph, "rtb", [128, 4, 5, 128], F32)
            tA = sbt(ph, "rtA", [128, 128], F32)
            tB = sbt(ph, "rtB", [128, 128], F32)
            for h in range(4):
                lf = lg[:, h:h + 1]
                lb = lg[:, 4 + h:5 + h]
                for (slot, tab, sc_) in ((0, 0, lf), (1, 1, lb), (2, 2, lf), (3, 3, lb)):
                    kk.op("act", lambda e, h=h, slot=slot, tab=tab, sc_=sc_: e.activation(
                        out=tb[:, h, slot, :], in_=rt[:, tab, :], func=AF.Exp, scale=sc_), r=[R_rt], w=[R_rt])
                kk.op("act", lambda e, lf=lf: e.activation(out=tA[:], in_=rt[:, 4, :], func=AF.Exp, scale=lf), r=[R_rt], w=[R_rt])
                kk.op("act", lambda e, lb=lb: e.activation(out=tB[:], in_=rt[:, 5, :], func=AF.Exp, scale=lb), r=[R_rt], w=[R_rt])
                kk.op("dve", lambda e: e.tensor_tensor(out=tA[:], in0=tA[:], in1=tB[:], op=ALU.subtract), r=[R_rt], w=[R_rt])
                kk.op("dve", lambda e: e.tensor_tensor(out=tA[:], in0=tA[:], in1=rt[:, 6, :], op=ALU.mult), r=[R_rt], w=[R_rt])
                kk.op("dve", lambda e, h=h: e.tensor_tensor(out=tb[:, h, 4, :], in0=tA[:], in1=tB[:], op=ALU.add), r=[R_rt], w=[R_rt])
                kk.op("dve", lambda e, h=h: e.tensor_scalar(out=tb[:, h, 2:5, :], in0=tb[:, h, 2:5, :], scalar1=1.0 / 16.0,
                                                            scalar2=None, op0=ALU.mult), r=[R_rt], w=[R_rt])
                kk.op("act", lambda e, h=h: e.activation(out=tA[:], in_=rt[:, 0, :], func=AF.Exp, scale=lgn[:, h:h + 1]),
                      r=[R_rt], w=[R_rt])
                kk.op("dve", lambda e, h=h: e.tensor_tensor(out=tb[:, h, 4, :], in0=tb[:, h, 4, :], in1=tA[:], op=ALU.mult),
                      r=[R_rt], w=[R_rt])
            cc = sbt(ph, "rcc", [128, 8], F32)
            kk.op("act", lambda e: e.activation(out=cc[:], in_=lg[:], func=AF.Exp, scale=128.0), r=[R_rt], w=[R_rt])
            qh = sbt(ph, "rqh", [128, 2, T_ALL], BF16)
            qf = sbt(ph, "rqf", [128, 2, T_ALL], BF16)
            qb = sbt(ph, "rqb", [128, 2, T_ALL], BF16)
            kh = sbt(ph, "rkh", [128, 2, T_ALL], BF16)
            vh = sbt(ph, "rvh", [128, 34, 512], BF16)
            R_qh, R_qf, R_qb, R_kh, R_vh = Res(), Res(), Res(), Res(), Res()
            Sf = sbt(ph, "rSf", [128, 2, 512], F32)
            Sbp = [sbt(ph, "rSb%d" % i, [128, 2, 512], F32) for i in range(2)]
            R_Sf, R_Sbp = Res(), [Res(), Res()]
            Sfb = [sbt(ph, "rSfb%d" % i, [128, 2, 512], BF16) for i in range(2)]
            R_Sfb = [Res(), Res()]
            sstg = [sbt(ph, "rsstg%d" % i, [128, 2, 512], BF16) for i in range(2)]
            R_sstg = [Res(), Res()]
            sbl = [sbt(ph, "rsbl%d" % i, [128, 2, 512], BF16) for i in range(3)]
            R_sbl = [Res(), Res(), Res()]
            kt = [sbt(ph, "rkt%d" % i, [128, 256], BF16) for i in range(2)]
            R_kt = [Res(), Res()]
            pm = [sbt(ph, "rpm%d" % i, [128, 128], BF16) for i in range(2)]
            R_pm = [Res(), Res()]
            ocn = [sbt(ph, "rocn%d" % i, [128, 512], F32) for i in range(4)]
            R_ocn = [Res() for _ in range(4)]
            junk = sbt(ph, "rjunk", [128, 512], F32)
            R_junk = Res()
            gt = [sbt(ph, "rgt%d" % i, [128, 512], F32) for i in range(4)]
            R_gt = [Res() for _ in range(4)]
            gated = [sbt(ph, "rgated%d" % i, [128, 512], BF16) for i in range(4)]
            R_gated = [Res() for _ in range(4)]
            ost = [sbt(ph, "rost%d" % i, [128, 4, 128], BF16) for i in range(4)]
            R_ost = [Res() for _ in range(4)]
            sm = [sbt(ph, "rsm%d" % i, [128, 4], F32) for i in range(4)]
            R_sm = [Res() for _ in range(4)]
            psT = ps_stat[:].bitcast(BF16)
            psXb = ps_x[:].bitcast(BF16)
            R_sbd = [Res() for _ in range(34)]
            Ubank = [[(ps_a[0], R_ps_a[0]), (ps_a[1], R_ps_a[1])], [(ps_b[0], R_ps_b[0]), (ps_b[1], R_ps_b[1])]]
            Obank = [(ps_o[0], R_ps_o[0]), (ps_o[1], R_ps_o[1])]

            def prescale_q(h):
                for (dst, R_dst, slot, eng) in ((qf, R_qf, 0, "dve"), (qb, R_qb, 1, "pool")):
                    kk.op(eng, lambda e, dst=dst, slot=slot: e.tensor_tensor(
                        out=dst[:].rearrange("p c (n i) -> p c n i", i=128),
                        in0=qh[:].rearrange("p c (n i) -> p c n i", i=128),
                        in1=tb[:, h, slot, :].unsqueeze(1).unsqueeze(1).to_broadcast([128, 2, 34, 128]), op=ALU.mult),
                        r=[R_qh, R_rt], w=[R_dst])

            def u_prep(h, m, direction, par):
                reg = 512 + par * 256
                for dc in range(2):
                    kk.op("pe", lambda e, dc=dc: e.transpose(psXb[:, reg + dc * 128:reg + (dc + 1) * 128],
                                                             kh[:, dc, m * 128:(m + 1) * 128], ident[:]),
                          r=[R_kh, R_rt], w=[R_ps_x], inc=(dc == 1))
                slot = 2 if direction == "f" else 3
                kk.op("act", lambda e: e.activation(out=kt[par][:], in_=psXb[:, reg:reg + 256], func=AF.Identity,
                                                    scale=tb[:, h, slot, 0:1]), r=[R_ps_x, R_rt], w=[R_kt[par]])
                for dc in range(2):
                    (pU, RU) = Ubank[par][dc]
                    kk.op("pe", lambda e, dc=dc, pU=pU: e.matmul(pU[:, :512], kt[par][:, dc * 128:(dc + 1) * 128], vh[:, m, :],
                                                                  start=True, stop=True), r=[R_kt[par], R_vh], w=[RU])

            def s_update(h, S, R_S, direction, par, S2=None, R_S2=None):
                cidx = h if direction == "f" else 4 + h
                if S2 is None:
                    S2, R_S2 = S, R_S
                for dc in range(2):
                    (pU, RU) = Ubank[par][dc]
                    kk.op("dve", lambda e, dc=dc, pU=pU: e.scalar_tensor_tensor(
                        out=S2[:, dc, :], in0=S[:, dc, :], scalar=cc[:, cidx:cidx + 1], in1=pU[:, :512],
                        op0=ALU.mult, op1=ALU.add), r=[RU, R_S, R_rt], w=[R_S2])

            def epiA(h, q0, tx, pO, RO):
                t2 = tx % 4
                kk.op("dve", lambda e: e.reduce_sum(out=sm[t2][:, 0:1], in_=pO[:, :512], axis=AX.X), r=[RO], w=[R_sm[t2]])
                kk.op("dve", lambda e: e.tensor_scalar(out=sm[t2][:, 0:1], in0=sm[t2][:, 0:1], scalar1=-1.0 / 512.0, scalar2=None,
                                                       op0=ALU.mult), r=[R_sm[t2]], w=[R_sm[t2]])
                kk.op("act", lambda e: e.activation(out=ocn[t2][:], in_=pO[:, :512], func=AF.Identity, bias=sm[t2][:, 0:1]),
                      r=[RO, R_sm[t2]], w=[R_ocn[t2]])
                kk.op("act", lambda e: e.activation(out=junk[:], in_=ocn[t2][:], func=AF.Square, accum_out=sm[t2][:, 1:2]),
                      r=[R_ocn[t2]], w=[R_sm[t2], R_junk])
                kk.op("act", lambda e: e.activation(out=sm[t2][:, 2:3], in_=sm[t2][:, 1:2], func=AF.Sqrt, scale=1.0 / 512.0, bias=EPS),
                      r=[R_sm[t2]], w=[R_sm[t2]])

            def epiB(h, q0, tx):
                t2 = tx % 4
                kk.op("dve", lambda e: e.reciprocal(out=sm[t2][:, 2:3], in_=sm[t2][:, 2:3]), r=[R_sm[t2]], w=[R_sm[t2]])
                kk.op("dve", lambda e: e.scalar_tensor_tensor(
                    out=ocn[t2][:], in0=ocn[t2][:], scalar=sm[t2][:, 2:3], in1=gng[:, h * 512:(h + 1) * 512],
                    op0=ALU.mult, op1=ALU.mult), r=[R_ocn[t2], R_sm[t2], R_rt], w=[R_ocn[t2]])
                kk.op("dve", lambda e: e.tensor_tensor(out=gated[t2][:], in0=ocn[t2][:], in1=gt[t2][:], op=ALU.mult),
                      r=[R_ocn[t2], R_gt[t2]], w=[R_gated[t2]])

            def epiC(h, q0, tx):
                t2 = tx % 4
                for blk in range(4):
                    kk.op("pe", lambda e, blk=blk: e.transpose(psT[:, blk * 128:(blk + 1) * 128],
                                                               gated[t2][:, blk * 128:(blk + 1) * 128], ident[:]),
                          r=[R_gated[t2], R_rt], w=[R_ps_stat], inc=(blk == 3))
                kk.op("act", lambda e: e.activation(out=ost[t2][:].rearrange("p a b -> p (a b)"), in_=psT[:, 0:512],
                                                    func=AF.Identity), r=[R_ps_stat], w=[R_ost[t2]])
                kk.dma("sp", oT_d[:, 4 * h:4 * h + 4, q0:q0 + 128], ost[t2][:], r=[R_ost[t2]])

            tx = 0
            pending = []

            def tick():
                for it in pending:
                    it[0] -= 1
                while pending and pending[0][0] <= 0:
                    pending.pop(0)[1]()

            for h in range(4):
                kk.dma("sp", qh[:], qT_d[:, 2 * h:2 * h + 2, :], w=[R_qh])
                kk.dma("sp", kh[:], kT_d[:, 2 * h:2 * h + 2, :], w=[R_kh])
                kk.dma("sp", vh[:], v_d[:, h * 512:(h + 1) * 512].rearrange("(n p) d -> p n d", p=128), w=[R_vh])
                prescale_q(h)
                kk.op("dve", lambda e: e.memset(Sbp[0][:], 0.0), w=[R_Sbp[0]])
                orderB = [33, 32] + list(range(31, -1, -1))
                for bi, m in enumerate(orderB):
                    par = bi % 2
                    cur = bi % 2
                    u_prep(h, m, "b", par)
                    need = (m < 32) or (do_ctx and m == 32)
                    if need:
                        sg = bi % 2
                        kk.op("act", lambda e, sg=sg, cur=cur: e.activation(out=sstg[sg][:], in_=Sbp[cur][:], func=AF.Identity),
                              r=[R_Sbp[cur]], w=[R_sstg[sg]])
                        kk.dma("sp", sb_d[m], sstg[sg][:], r=[R_sstg[sg]], w=[R_sbd[m]])
                    if bi < len(orderB) - 1:
                        s_update(h, Sbp[cur], R_Sbp[cur], "b", par, Sbp[1 - cur], R_Sbp[1 - cur])
                kk.op("dve", lambda e: e.memset(Sf[:], 0.0), w=[R_Sf])
                orderF = [32, 33] + list(range(32))
                u_prep(h, orderF[0], "f", 0)
                for fi, n in enumerate(orderF):
                    par = fi % 2
                    if fi + 1 < len(orderF):
                        u_prep(h, orderF[fi + 1], "f", 1 - par)
                    out_needed = (n < 32) or do_ctx
                    if out_needed:
                        q0 = n * 128
                        t4 = tx % 4
                        kk.dma("sp", gt[t4][:], g_d[q0:q0 + 128, h * 512:(h + 1) * 512], w=[R_gt[t4]])
                        has_b = not (n == 33)
                        has_f = fi > 0
                        if has_b:
                            sl = tx % 3
                            kk.dma("sp", sbl[sl][:], sb_d[n], r=[R_sbd[n]], w=[R_sbl[sl]])
                        (pO, RO) = Obank[tx % 2]
                        mm_acc(ps_x[:, :128], R_ps_x, [(kh[:, dc, q0:q0 + 128], qf[:, dc, q0:q0 + 128], [R_kh, R_qf]) for dc in range(2)])
                        p2 = tx % 2
                        kk.op("dve", lambda e, p2=p2: e.tensor_tensor(out=pm[p2][:], in0=ps_x[:, :128], in1=tb[:, h, 4, :], op=ALU.mult),
                              r=[R_ps_x, R_rt], w=[R_pm[p2]])
                        terms = [(pm[p2][:], vh[:, n, :], [R_pm[p2], R_vh])]
                        if has_b:
                            terms += [(qb[:, dc, q0:q0 + 128], sbl[sl][:, dc, :], [R_qb, R_sbl[sl]]) for dc in range(2)]
                        if has_f:
                            fb = (fi - 1) % 2
                            terms += [(qf[:, dc, q0:q0 + 128], Sfb[fb][:, dc, :], [R_qf, R_Sfb[fb]]) for dc in range(2)]
                        for ti_, (l_, r_, rd) in enumerate(terms):
                            kk.op("pe", lambda e, l_=l_, r_=r_, ti_=ti_, nt_=len(terms), pO=pO: e.matmul(
                                pO[:, :512], l_, r_, start=(ti_ == 0), stop=(ti_ == nt_ - 1)), r=rd, w=[RO],
                                inc=(ti_ == len(terms) - 1))
                    if fi < len(orderF) - 1:
                        s_update(h, Sf, R_Sf, "f", par)
                        fb = fi % 2
                        kk.op("act", lambda e, fb=fb: e.activation(out=Sfb[fb][:], in_=Sf[:], func=AF.Identity),
                              r=[R_Sf], w=[R_Sfb[fb]])
                    while pending:
                        pending.pop(0)[1]()
                    if out_needed:
                        epiA(h, q0, tx, pO, RO)
                        pending.append([1, (lambda h=h, q0=q0, tx=tx: epiB(h, q0, tx))])
                        pending.append([2, (lambda h=h, q0=q0, tx=tx: epiC(h, q0, tx))])
                        tx += 1
            while pending:
                pending.pop(0)[1]()
            kk.barrier()


    def pool_phase(li, h_in, h_out, tl, do_ctx):
        with contextlib.ExitStack() as ph:
            ht = [sbt(ph, "qht%d" % i, [128, DC, NT], F32) for i in range(2)]
            R_ht = [Res(), Res()]
            xf = [sbt(ph, "qxf%d" % i, [128, DC, NT], F32) for i in range(2)]
            R_xf = [Res(), Res()]
            o = norm_mod(ph, "q")
            for ti, (t0, n, w_) in enumerate(tl):
                b = ti % 2
                kk.dma("sp", ht[b][:, :, :n], h_in[:, :, t0:t0 + n], w=[R_ht[b]])
                emit_rstd(o, ht[b], R_ht[b], n)
                emit_xl(o, ht[b], R_ht[b], n, li, 1, w_, xf[b], R_xf[b])
                kk.dma("sp", xl_d[:, :, t0:t0 + n], xf[b][:, :, :n], r=[R_xf[b]])
            kk.barrier()
        with contextlib.ExitStack() as ph:
            PAD = 8
            TB = T_LAT + 2 * PAD
            X = sbt(ph, "qX", [128, 2, TB], F32)
            Y = sbt(ph, "qY", [128, 2, TB], F32)
            Zb = sbt(ph, "qZ", [128, 2, TB], F32)
            R_X, R_Y, R_Z = Res(), Res(), Res()
            icn = sbt(ph, "qicn", [128, T_LAT], F32)
            R_icn = Res()
            pbf = sbt(ph, "qpb", [128, 2, T_LAT], BF16)
            R_pb = Res()
            wg = sbt(ph, "qwg", [128, 2, 256], BF16)
            R_wg = Res()
            psc = sbt(ph, "qpsc", [128, DC], F32)
            gp = sbt(ph, "qgp", [128, DC, 2], F32)
            R_gp = Res()
            kk.dma("sp", psc[:], pool_sc, w=[R_gp])
            for w_ in range(2):
                kk.op("dve", lambda e, w_=w_: e.tensor_tensor(out=gp[:, :, w_], in0=gat[:, li, 1, :, w_], in1=psc[:], op=ALU.mult),
                      r=[R_gp, R_mods], w=[R_gp])
            hc = [sbt(ph, "qhc%d" % i, [128, 512], F32) for i in range(2)]
            R_hc = [Res(), Res()]
            seqs = [(0, T_LAT, 0)] + ([(T_LAT, T_CTX, 1)] if do_ctx else [])
            hi_ = 0
            for g_ in range(4):
                kk.dma("pool", wg[:], pool_w[g_].rearrange("(kc p) n -> p kc n", p=128), w=[R_wg])
                for (s0, T, w_) in seqs:
                    L = T + 2 * PAD
                    kk.op("pool", lambda e, L=L: e.memset(X[:, :, 0:L], 0.0), w=[R_X])
                    kk.dma("sp", X[:, :, PAD:PAD + T], xl_d[:, 2 * g_:2 * g_ + 2, s0:s0 + T], w=[R_X])
                    kk.dma("sp", icn[:, :T], pool_icnt[g_:g_ + 1, s0:s0 + T].partition_broadcast(128), w=[R_icn])
                    kk.op("dve", lambda e, L=L: e.memset(Y[:, :, 0:L], 0.0), w=[R_Y])
                    kk.op("dve", lambda e, L=L: e.tensor_tensor(
                        out=Y[:, :, 1:L], in0=X[:, :, 1:L], in1=X[:, :, 0:L - 1], op=ALU.add), r=[R_X], w=[R_Y])
                    lv_src, R_lsrc = Y, R_Y
                    sh = 1
                    for lev in range(g_):
                        a_, Ra_, b_, Rb_ = (Y, R_Y, Zb, R_Z) if lev % 2 == 0 else (Zb, R_Z, Y, R_Y)
                        kk.op("dve", lambda e, L=L, b_=b_: e.memset(b_[:, :, 0:L], 0.0), w=[Rb_])
                        kk.op("dve", lambda e, L=L, a_=a_, b_=b_, sh=sh: e.tensor_tensor(
                            out=b_[:, :, sh:L - sh], in0=a_[:, :, 0:L - 2 * sh], in1=a_[:, :, 2 * sh:L], op=ALU.add),
                            r=[Ra_], w=[Rb_])
                        lv_src, R_lsrc = b_, Rb_
                        sh *= 2
                    for c in range(2):
                        kk.op("dve", lambda e, c=c, T=T, lv_src=lv_src: e.tensor_tensor(
                            out=lv_src[:, c, PAD:PAD + T], in0=lv_src[:, c, PAD:PAD + T], in1=icn[:, :T], op=ALU.mult),
                            r=[R_lsrc, R_icn], w=[R_lsrc])
                    kk.op("dve", lambda e, T=T, lv_src=lv_src: e.tensor_tensor(
                        out=pbf[:, :, :T], in0=lv_src[:, :, PAD:PAD + T], in1=X[:, :, PAD:PAD + T], op=ALU.subtract),
                        r=[R_lsrc, R_X], w=[R_pb])
                    for m in range(2):
                        c = 2 * g_ + m
                        for tt0 in range(0, T, 512):
                            nn = min(512, T - tt0)
                            hb = hi_ % 2
                            hi_ += 1
                            kk.dma("sp", hc[hb][:, :nn], h_in[:, c, s0 + tt0:s0 + tt0 + nn], w=[R_hc[hb]])
                            (pO, RO) = bankO()
                            mm_acc(pO[:, :nn], RO, [(wg[:, kc, m * 128:(m + 1) * 128], pbf[:, kc, tt0:tt0 + nn], [R_wg, R_pb])
                                                    for kc in range(2)])
                            kk.op("dve", lambda e, pO=pO, hb=hb, nn=nn, c=c, w_=w_: e.scalar_tensor_tensor(
                                out=hc[hb][:, :nn], in0=pO[:, :nn], scalar=gp[:, c, w_:w_ + 1], in1=hc[hb][:, :nn],
                                op0=ALU.mult, op1=ALU.add), r=[RO, R_gp, R_hc[hb]], w=[R_hc[hb]])
                            kk.dma("sp", h_out[:, c, s0 + tt0:s0 + tt0 + nn], hc[hb][:, :nn], r=[R_hc[hb]])
            kk.barrier()

    kinds = cfg.get("kinds", ["ret", "nat", "pool", "swa"])
    cur = xT
    nxt = 0
    lat_tiles = [t for t in tiles if t[2] == 0]
    for li in range(n_layers):
        kind = kinds[li]
        last = (li == n_layers - 1)
        ctx_live = (not last) or kind != "pool"
        tl1 = tiles if ctx_live else lat_tiles
        tl2 = lat_tiles if last else tiles
        ffn_phase(li, 0, 0, cur, hbufs[nxt], tl1)
        cur = hbufs[nxt]
        nxt ^= 1
        if mixers:
            if kind == "pool":
                pool_phase(li, cur, hbufs[nxt], tl2, not last)
            else:
                proj_phase(li, kind, cur, tl1)
                if kind == "ret":
                    ret_phase(not last)
                    oproj_phase(li, ret_w_out, 16, cur, hbufs[nxt], tl2)
                else:
                    attn_phase(kind, not last)
                    oproj_phase(li, nat_w_o if kind == "nat" else swa_w_o, 8, cur, hbufs[nxt], tl2)
            cur = hbufs[nxt]
            nxt ^= 1
        ffn_phase(li, 1, 2, cur, hbufs[nxt], tl2)
        cur = hbufs[nxt]
        nxt ^= 1
    final_phase(cur)
    kk.barrier()
    kk.ninst_total = kk.ninst
    nc._kk = kk
    return nc


def _fm(a):
    t = a.shape[0]
    return np.ascontiguousarray(a.T.reshape(DC, 128, t).transpose(1, 0, 2))


def _vec_fm(v):
    lead = v.shape[:-1]
    x = v.reshape(*lead, DC, 128)
    x = np.moveaxis(x, -1, 0)
    return np.ascontiguousarray(x)


def _consts():
    c = {}
    p = np.arange(128, dtype=np.float64)[:, None]
    i = np.arange(128, dtype=np.float64)[None, :]
    tabs = np.zeros((128, 8, 128), np.float64)
    tabs[:, 0] = i + 0 * p
    tabs[:, 1] = 127 - i + 0 * p
    tabs[:, 2] = 128 * i + 128 - p
    tabs[:, 3] = 128 * i + p + 1
    tabs[:, 4] = np.maximum(i - p, 0)
    tabs[:, 5] = np.maximum(p - i, 0)
    tabs[:, 6] = (i >= p)
    tabs[:, 7] = (i == p)
    c["ret_tabs"] = tabs.astype(np.float32)
    t = np.arange(T_LAT, dtype=np.float32)[None, :]
    inv = (10000.0 ** (-(np.arange(0, 256, 2, dtype=np.float32)) / 256.0)).astype(np.float32)[:, None]
    ang = (t * inv).astype(np.float32)
    cs = np.zeros((128, 2, T_ALL), np.float32)
    cs[:, 0, :T_LAT] = np.cos(ang)
    cs[:, 1, :T_LAT] = np.sin(ang)
    cs[:, 0, T_LAT:] = 1.0
    c["ret_cs"] = cs
    d = np.arange(128) % 64
    tt = np.arange(T_LAT)
    pos = np.where((d < 32)[:, None], (tt // 64)[None, :], (tt % 64)[None, :]).astype(np.float32)
    inv16 = (10000.0 ** (-(np.arange(0, 32, 2, dtype=np.float32)) / 32.0)).astype(np.float32)
    invd = inv16[(d % 32) % 16][:, None]
    ang = (pos * invd).astype(np.float32)
    sign = np.where((d % 32) < 16, -1.0, 1.0).astype(np.float32)[:, None]
    cs = np.zeros((128, 2, T_ALL), np.float32)
    cs[:, 0, :T_LAT] = np.cos(ang)
    cs[:, 1, :T_LAT] = np.sin(ang) * sign
    cs[:, 0, T_LAT:] = 1.0
    c["swa_cs"] = cs
    NEG = -30000.0
    j = np.arange(128)[:, None]
    q = np.arange(128)[None, :]
    sb_ = np.zeros((128, 3, 5, 128), np.float32)
    for typ in range(3):
        sb_[:, typ, 0] = np.where(j >= q, 0.0, NEG)
        sb_[:, typ, 2] = np.where(j <= q, 0.0, NEG)
    sb_[:, 1, 0] = NEG
    sb_[:, 2, 2] = NEG
    c["swa_bias"] = sb_
    ic = np.zeros((4, T_ALL), np.float32)
    for g_, w in enumerate((2, 4, 8, 16)):
        for (s0, T) in ((0, T_LAT), (T_LAT, T_CTX)):
            tq = np.arange(T)
            lo = np.clip(tq - w // 2, 0, T)
            hi = np.clip(tq + w // 2, 0, T)
            ic[g_, s0:s0 + T] = 1.0 / (hi - lo).astype(np.float32)
    c["pool_icnt"] = ic
    return c


def _nat_bias(rpb):
    NEG = -30000.0
    out = np.zeros((16, 128, 5, 7, 128), np.float32)
    u = (np.arange(128) // 64)
    n = (np.arange(128) % 64)
    cfgs = [(2, 0), (0, 0), (1, 0), (30, 27), (31, 27)]
    for typ, (qi, base) in enumerate(cfgs):
        r = (2 * qi + u)[None, :]
        cq = n[None, :]
        r0 = np.clip(r - 4, 0, 56)
        c0 = np.clip(cq - 8, 0, 48)
        for ch in range(5):
            a = (2 * (base + ch) + u)[:, None]
            nk = n[:, None]
            valid = (a >= r0) & (a < r0 + 8) & (nk >= c0) & (nk < c0 + 16)
            ri = np.clip(a - r + 7, 0, 14)
            ci = np.clip(nk - cq + 15, 0, 30)
            vals = rpb[:, ri, ci]
            out[:, :, typ, ch, :] = np.where(valid[None], vals, NEG)
    return out


def make_in_maps(inputs, cores=range(N_CORES)):
    f = lambda a: np.ascontiguousarray(np.asarray(a, dtype=np.float32))
    x, c, ctx, c_ctx = f(inputs["x"]), f(inputs["c"]), f(inputs["ctx"]), f(inputs["c_ctx"])
    b_mod = f(inputs["b_mod"])
    shared = {
        "w_mod": f(inputs["w_mod"]),
        "bmodT": np.ascontiguousarray(b_mod.reshape(DEPTH, 72, 128).transpose(2, 0, 1)),
        "normgT": _vec_fm(f(inputs["norm_g"])),
        "fnormgT": _vec_fm(f(inputs["final_norm_g"])),
        "ffn_w_in": f(inputs["ffn_w_in"]),
        "ffn_w_out": f(inputs["ffn_w_out"]),
        "ret_w_in": f(inputs["ret_w_in"][0]),
        "ret_w_out": f(inputs["ret_w_out"][0]),
        "ret_gn": f(inputs["ret_gn_g"][0:1]),
        "ret_decay": np.ascontiguousarray(np.concatenate([f(inputs["ret_decay_f"][0]), f(inputs["ret_decay_b"][0])])[None, :]),
        "nat_w_qkv": f(inputs["nat_w_qkv"][0]),
        "nat_w_o": f(inputs["nat_w_o"][0]),
        "nat_bias": _nat_bias(f(inputs["nat_rpb"][0])),
        "pool_w": f(inputs["pool_w"][0]),
        "pool_sc": _vec_fm(f(inputs["pool_scale"][0])),
        "swa_w_qkv": f(inputs["swa_w_qkv"][0]),
        "swa_w_o": f(inputs["swa_w_o"][0]),
        "swa_sink": f(inputs["swa_sink"][0:1]),
    }
    wq = shared["swa_w_qkv"]
    dd = np.arange(64)
    partner = np.where((dd % 32) < 16, dd + 16, dd - 16)
    colq = (np.arange(16)[:, None] * 64 + partner[None, :]).reshape(-1)
    colk = 1024 + (np.arange(4)[:, None] * 64 + partner[None, :]).reshape(-1)
    shared["swa_w_perm"] = np.ascontiguousarray(wq[:, np.concatenate([colq, colk])])
    shared.update(_consts())
    maps = []
    for b in cores:
        m = dict(shared)
        m["xT"] = _fm(np.concatenate([x[b], ctx[b]], axis=0))
        m["cT"] = np.ascontiguousarray(np.stack([c[b], c_ctx], axis=0).reshape(2, DC, 128).transpose(2, 1, 0))
        maps.append(m)
    return maps


_NC_CACHE = {}


def kernel(**inputs):
    if "nc" not in _NC_CACHE:
        _NC_CACHE["nc"] = build()
    nc = _NC_CACHE["nc"]
    in_maps = make_in_maps(inputs)
    res = run_bass_kernel_spmd(nc, in_maps, core_ids=list(range(N_CORES)))
    outs = []
    for b in range(N_CORES):
        o = res.results[b]["outT"]
        outs.append(o.transpose(1, 0, 2).reshape(D, T_LAT).T)
    return np.ascontiguousarray(np.stack(outs, axis=0).astype(np.float32))
```

```python
import contextlib
import numpy as np
import concourse.bass as bass
import concourse.mybir as mybir
from concourse.bass_utils import run_bass_kernel_spmd

F32 = mybir.dt.float32
BF16 = mybir.dt.bfloat16
AF = mybir.ActivationFunctionType
ALU = mybir.AluOpType
AX = mybir.AxisListType

D = 1024
DC = 8
T_LAT = 4096
T_CTX = 256
T_ALL = T_LAT + T_CTX
DEPTH = 4
FF = 2816
FJ = 22
EPS = 1e-6
NT = 256
N_CORES = 4


class Res:
    __slots__ = ("name", "w", "r")

    def __init__(self, name=""):
        self.name = name
        self.w = None
        self.r = {}


class _Eng:
    def __init__(self, kk, name, handle):
        self.name = name
        self.h = handle
        self.sem = kk.new_sem("e_" + name)
        self.count = 0
        self.waited = {}
        self.pend_r = []
        self.pend_w = []


class K:
    def __init__(self, nc, n_dma_sems=16):
        self.nc = nc
        self.st = contextlib.ExitStack()
        self.sems = {}
        self.nsem = 0
        self.eng = {}
        for name, h in (("pe", nc.tensor), ("act", nc.scalar), ("dve", nc.vector),
                        ("pool", nc.gpsimd), ("sp", nc.sync)):
            self.eng[name] = _Eng(self, name, h)
        self.dma_pool = {}
        for q in ("sp", "pool"):
            self.dma_pool[q] = [[self.new_sem("d_%s%d" % (q, i)), 0] for i in range(n_dma_sems)]
        self.dma_rr = {"sp": 0, "pool": 0}
        self.ninst = 0

    def new_sem(self, name):
        s = self.st.enter_context(self.nc.semaphore(name))
        sid = self.nsem
        self.nsem += 1
        self.sems[sid] = s
        return sid

    def _wait(self, e, ev):
        sid, val = ev
        if e.waited.get(sid, 0) >= val:
            return
        e.waited[sid] = val
        e.h.wait_ge(self.sems[sid], val)
        self.ninst += 1

    def _deps(self, e, r, w, nowaw=False):
        evs = {}

        def add(ev):
            if ev is not None and evs.get(ev[0], 0) < ev[1]:
                evs[ev[0]] = ev[1]
        for x in r:
            add(x.w)
        for x in w:
            if not nowaw:
                add(x.w)
            for sid, val in x.r.items():
                add((sid, val))
        for sid, val in evs.items():
            if e.name == "pe" and sid == e.sem:
                continue
            self._wait(e, (sid, val))

    def _commit(self, ev, r, w):
        for x in r:
            if x.r.get(ev[0], 0) < ev[1]:
                x.r[ev[0]] = ev[1]
        for x in w:
            x.w = ev
            x.r = {}

    def op(self, eng, fn, r=(), w=(), inc=True):
        e = self.eng[eng]
        self._deps(e, r, w)
        inst = fn(e.h)
        self.ninst += 1
        if not inc:
            e.pend_r.extend(r)
            e.pend_w.extend(w)
            return None
        if e.count >= 30000:
            e.sem = self.new_sem("e_%s_%d" % (eng, self.nsem))
            e.count = 0
        e.count += 1
        inst.then_inc(self.sems[e.sem], 1)
        ev = (e.sem, e.count)
        self._commit(ev, list(r) + e.pend_r, list(w) + e.pend_w)
        e.pend_r = []
        e.pend_w = []
        return ev

    def dma(self, q, out, in_, r=(), w=(), nowaw=False):
        e = self.eng[q]
        self._deps(e, r, w, nowaw=nowaw)
        pool = self.dma_pool[q]
        i = self.dma_rr[q]
        self.dma_rr[q] = (i + 1) % len(pool)
        slot = pool[i]
        if slot[1] > 0:
            self._wait(e, (slot[0], slot[1]))
        slot[1] += 16
        e.h.dma_start(out=out, in_=in_).then_inc(self.sems[slot[0]], 16)
        self.ninst += 1
        ev = (slot[0], slot[1])
        self._commit(ev, r, w)
        return ev

    def barrier(self, engs=("pe", "act", "dve", "pool", "sp")):
        for x in engs:
            e = self.eng[x]
            for y in self.eng.values():
                if y is not e and y.count > 0:
                    self._wait(e, (y.sem, y.count))
            for pool in self.dma_pool.values():
                for sid, val in pool:
                    if val > 0:
                        self._wait(e, (sid, val))


def build(cfg=None):
    cfg = cfg or {}
    n_layers = cfg.get("n_layers", DEPTH)
    mixers = cfg.get("mixers", True)
    dbg = cfg.get("dbg", False)

    nc = bass.Bass("TRN2", target_bir_lowering=False)
    kk = K(nc)
    st = kk.st

    def dram_in(name, shape, dt=F32):
        return nc.dram_tensor(name, list(shape), dt, kind="ExternalInput").ap()

    xT = dram_in("xT", [128, DC, T_ALL])
    cT = dram_in("cT", [128, DC, 2])
    w_mod = dram_in("w_mod", [DEPTH, D, 9 * D])
    bmodT = dram_in("bmodT", [128, DEPTH, 72])
    normgT = dram_in("normgT", [128, DEPTH, 3, DC])
    fnormgT = dram_in("fnormgT", [128, DC])
    ffn_w_in = dram_in("ffn_w_in", [DEPTH, 2, D, 2 * FF])
    ffn_w_out = dram_in("ffn_w_out", [DEPTH, 2, FF, D])
    outT = nc.dram_tensor("outT", [128, DC, T_LAT], F32, kind="ExternalOutput").ap()
    hA = nc.dram_tensor("hA", [128, DC, T_ALL], F32).ap()
    hB = nc.dram_tensor("hB", [128, DC, T_ALL], F32).ap()
    hbufs = [hA, hB]
    ret_w_in = dram_in("ret_w_in", [D, 6144])
    ret_w_out = dram_in("ret_w_out", [2048, D])
    ret_gn = dram_in("ret_gn", [1, 2048])
    ret_decay = dram_in("ret_decay", [1, 8])
    ret_tabs = dram_in("ret_tabs", [128, 8, 128])
    ret_cs = dram_in("ret_cs", [128, 2, T_ALL])
    nat_w_qkv = dram_in("nat_w_qkv", [D, 3072])
    nat_w_o = dram_in("nat_w_o", [D, D])
    nat_bias = dram_in("nat_bias", [16, 128, 5, 7, 128])
    pool_w = dram_in("pool_w", [4, 256, 256])
    pool_sc = dram_in("pool_sc", [128, DC])
    pool_icnt = dram_in("pool_icnt", [4, T_ALL])
    swa_w_qkv = dram_in("swa_w_qkv", [D, 1536])
    swa_w_perm = dram_in("swa_w_perm", [D, 1280])
    swa_w_o = dram_in("swa_w_o", [D, D])
    swa_sink = dram_in("swa_sink", [1, 16])
    swa_cs = dram_in("swa_cs", [128, 2, T_ALL])
    swa_bias = dram_in("swa_bias", [128, 3, 5, 128])
    qT_d = nc.dram_tensor("qT_d", [128, DC, T_ALL], BF16).ap()
    kT_d = nc.dram_tensor("kT_d", [128, DC, T_ALL], BF16).ap()
    v_d = nc.dram_tensor("v_d", [T_ALL, 2048], BF16).ap()
    g_d = nc.dram_tensor("g_d", [T_ALL, 2048], F32).ap()
    oT_d = nc.dram_tensor("oT_d", [128, 16, T_ALL], BF16).ap()
    xl_d = nc.dram_tensor("xl_d", [128, DC, T_ALL], F32).ap()
    sb_d = nc.dram_tensor("sb_d", [34, 128, 2, 512], BF16).ap()

    def sb(name, shape, dt):
        return st.enter_context(nc.sbuf_tensor(name, list(shape), dt))

    def ps(name, shape, dt=F32):
        return st.enter_context(nc.psum_tensor(name, list(shape), dt))

    _uid = [0]

    def sbt(ph, name, shape, dt):
        _uid[0] += 1
        return ph.enter_context(nc.sbuf_tensor("%s_%d" % (name, _uid[0]), list(shape), dt))

    ones_f = sb("ones_f", [128, 128], F32)
    mods = sb("mods", [128, DEPTH, 72, 2], F32)
    gsc = sb("gsc", [128, DEPTH, 3, DC, 2], F32)
    gat = sb("gat", [128, DEPTH, 3, DC, 2], F32)
    ng = sb("ng", [128, DEPTH, 3, DC], F32)
    fng = sb("fng", [128, DC], F32)
    bm = sb("bm", [128, DEPTH, 72], F32)
    sT = sb("sT", [128, DC, 2], F32)
    R_const = Res("const")
    R_mods = Res("mods")

    kk.op("dve", lambda e: e.memset(ones_f[:], 1.0), w=[R_const])
    kk.dma("sp", ng[:], normgT, w=[R_const])
    kk.dma("sp", fng[:], fnormgT, w=[R_const])
    kk.dma("sp", bm[:], bmodT, w=[R_const])
    kk.dma("sp", sT[:], cT, w=[R_const])
    kk.op("act", lambda e: e.activation(out=sT[:], in_=sT[:], func=AF.Silu), r=[R_const], w=[R_const])

    ps_stat = ps("ps_stat", [128, 512])
    ps_a = [ps("ps_a%d" % i, [128, 512]) for i in range(2)]
    ps_b = [ps("ps_b%d" % i, [128, 512]) for i in range(2)]
    ps_o = [ps("ps_o%d" % i, [128, 512]) for i in range(2)]
    R_ps_stat = Res()
    R_ps_a = [Res(), Res()]
    R_ps_b = [Res(), Res()]
    R_ps_o = [Res(), Res()]
    ps_x = ps("ps_x", [128, 512])
    R_ps_x = Res()
    poolS = [(ps_a[0], R_ps_a[0]), (ps_a[1], R_ps_a[1]), (ps_b[0], R_ps_b[0]), (ps_b[1], R_ps_b[1])]
    poolO = [(ps_o[0], R_ps_o[0]), (ps_o[1], R_ps_o[1]), (ps_x, R_ps_x)]
    _rrS = [0]
    _rrO = [0]

    def bankS():
        _rrS[0] = (_rrS[0] + 1) % len(poolS)
        return poolS[_rrS[0]]

    def bankO():
        _rrO[0] = (_rrO[0] + 1) % len(poolO)
        return poolO[_rrO[0]]

    def mm_acc(out_ap, R_out, pairs):
        last = len(pairs) - 1
        ev = None
        for idx, (l_, r_, rd) in enumerate(pairs):
            ev = kk.op("pe", lambda e, l_=l_, r_=r_, idx=idx: e.matmul(out_ap, l_, r_, start=(idx == 0), stop=(idx == last)),
                       r=rd, w=[R_out], inc=(idx == last))
        return ev

    identf = sb("identf", [128, 128], F32)
    kk.dma("sp", identf[:], ret_tabs[:, 7, :], w=[R_const])
    with contextlib.ExitStack() as ph:
        wm = [sbt(ph, "wm%d" % i, [128, DC, 1024], F32) for i in range(2)]
        R_wm = [Res(), Res()]
        modrow = sbt(ph, "modrow", [2, 9 * D], F32)
        R_modrow = Res()
        blk = 0
        for li in range(n_layers):
            for nb in range(9):
                s = blk % 2
                blk += 1
                src = w_mod[li, :, nb * 1024:(nb + 1) * 1024].rearrange("(kc p) n -> p kc n", p=128)
                kk.dma("sp", wm[s][:], src, w=[R_wm[s]])
                for half in range(2):
                    (pS, RS) = bankS()
                    mm_acc(pS[0:2, :512], RS, [(sT[:, kc, :], wm[s][:, kc, half * 512:(half + 1) * 512], [R_wm[s], R_const])
                                               for kc in range(DC)])
                    c0 = nb * 1024 + half * 512
                    kk.op("act", lambda e, pS=pS, c0=c0: e.activation(out=modrow[0:2, c0:c0 + 512], in_=pS[0:2, :512],
                                                                       func=AF.Identity), r=[RS], w=[R_modrow])
            for n in range(72):
                kk.op("pe", lambda e, n=n: e.transpose(ps_stat[:, n * 2:(n + 1) * 2], modrow[0:2, n * 128:(n + 1) * 128],
                                                       identf[0:2, 0:2]),
                      r=[R_modrow, R_const], w=[R_ps_stat], inc=(n == 71))
            kk.op("dve", lambda e, li=li: e.tensor_tensor(
                out=mods[:, li, :, :], in0=ps_stat[:, 0:144].rearrange("p (n w) -> p n w", w=2),
                in1=bm[:, li, :].unsqueeze(2).to_broadcast([128, 72, 2]),
                op=ALU.add), r=[R_ps_stat, R_const], w=[R_mods])
        kk.barrier()

    for li in range(n_layers):
        for j in range(3):
            sc = mods[:, li, (j * 3 + 1) * 8:(j * 3 + 2) * 8, :]
            gt = mods[:, li, (j * 3 + 2) * 8:(j * 3 + 3) * 8, :]
            for w_ in range(2):
                kk.op("dve", lambda e, li=li, j=j, w_=w_, sc=sc: e.scalar_tensor_tensor(
                    out=gsc[:, li, j, :, w_], in0=sc[:, :, w_], scalar=1.0, in1=ng[:, li, j, :],
                    op0=ALU.add, op1=ALU.mult), r=[R_mods, R_const], w=[R_mods])
            kk.op("dve", lambda e, li=li, j=j, gt=gt: e.tensor_scalar(
                out=gat[:, li, j, :, :], in0=gt, scalar1=(1.0 if j == 1 else 0.5), scalar2=None,
                op0=ALU.mult), r=[R_mods], w=[R_mods])
    kk.barrier()

    tiles = [(t0, NT, 0) for t0 in range(0, T_LAT, NT)] + [(T_LAT, T_CTX, 1)]

    def norm_mod(ph, name):
        o = {}
        o["sqt"] = sbt(ph, name + "_sqt", [128, DC, NT], F32)
        o["Rsqt"] = Res()
        o["ssum"] = sbt(ph, name + "_ssum", [128, NT], F32)
        o["Rssum"] = Res()
        o["rstd"] = sbt(ph, name + "_rstd", [128, NT], F32)
        o["Rrstd"] = Res()
        return o

    def _emit(steps, eng, fn, r, w):
        if steps is None:
            kk.op(eng, fn, r=r, w=w)
        else:
            steps.append(lambda: kk.op(eng, fn, r=r, w=w))

    def emit_rstd(o, ht, R_ht, n, steps=None):
        _emit(steps, "dve", lambda e: e.tensor_tensor(out=o["sqt"][:, :, :n], in0=ht[:, :, :n], in1=ht[:, :, :n], op=ALU.mult),
              [R_ht], [o["Rsqt"]])
        _emit(steps, "dve", lambda e: e.tensor_reduce(out=o["ssum"][:, :n], in_=o["sqt"][:, :, :n].rearrange("p c n -> p n c"),
                                                      axis=AX.X, op=ALU.add), [o["Rsqt"]], [o["Rssum"]])
        _emit(steps, "pe", lambda e: e.matmul(ps_stat[:, :n], ones_f[:], o["ssum"][:, :n], start=True, stop=True),
              [o["Rssum"], R_const], [R_ps_stat])
        _emit(steps, "act", lambda e: e.activation(out=o["rstd"][:, :n], in_=ps_stat[:, :n], func=AF.Sqrt,
                                                   scale=1.0 / D, bias=EPS), [R_ps_stat], [o["Rrstd"]])
        _emit(steps, "dve", lambda e: e.reciprocal(out=o["rstd"][:, :n], in_=o["rstd"][:, :n]), [o["Rrstd"]], [o["Rrstd"]])

    def emit_xl(o, ht, R_ht, n, li, j, w_, dst, R_dst, steps=None):
        _emit(steps, "dve", lambda e: e.tensor_tensor(
            out=o["sqt"][:, :, :n], in0=ht[:, :, :n], in1=o["rstd"][:, :n].unsqueeze(1).to_broadcast([128, DC, n]),
            op=ALU.mult), [R_ht, o["Rrstd"]], [o["Rsqt"]])
        for c in range(DC):
            _emit(steps, "act", lambda e, c=c: e.activation(
                out=dst[:, c, :n], in_=o["sqt"][:, c, :n], func=AF.Identity,
                scale=gsc[:, li, j, c, w_:w_ + 1], bias=mods[:, li, (j * 3) * 8 + c, w_:w_ + 1]),
                [o["Rsqt"], R_mods], [R_dst])

    def ffn_phase(li, s_, j, h_in, h_out, tl):
        with contextlib.ExitStack() as ph:
            win = sbt(ph, "win", [128, DC, 2 * FF], BF16)
            wout = sbt(ph, "wout", [128, FJ, D], BF16)
            jblocks = [(0, 2), (2, 5), (5, 8), (8, 11), (11, 14), (14, 17), (17, 20), (20, 22)]
            blk_of_j = {}
            for bi_, (j0, j1) in enumerate(jblocks):
                for jx in range(j0, j1):
                    blk_of_j[jx] = bi_
            R_wina = [Res() for _ in jblocks]
            R_winb = [Res() for _ in jblocks]
            R_woutb = [Res() for _ in range(4)]
            w_in_src = ffn_w_in[li, s_].rearrange("(kc p) n -> p kc n", p=128)
            for bi_, (j0, j1) in enumerate(jblocks):
                for base, RR in ((0, R_wina), (FF, R_winb)):
                    c0, c1 = base + j0 * 128, base + j1 * 128
                    kk.dma("pool", win[:, :, c0:c1], w_in_src[:, :, c0:c1], w=[RR[bi_]])
                if bi_ == 1:
                    w_out_src = ffn_w_out[li, s_].rearrange("(j p) d -> p j d", p=128)
                    for ob in range(4):
                        o0, o1 = ob * 6, min(FJ, ob * 6 + 6)
                        kk.dma("pool", wout[:, o0:o1, :], w_out_src[:, o0:o1, :], w=[R_woutb[ob]])
            ht = [sbt(ph, "ht%d" % i, [128, DC, NT], F32) for i in range(2)]
            R_ht = [Res(), Res()]
            xl = [sbt(ph, "xl%d" % i, [128, DC, NT], BF16) for i in range(2)]
            R_xl = [Res(), Res()]
            g = sbt(ph, "g", [128, FJ, NT], BF16)
            R_g = [Res() for _ in range(FJ)]
            sa = [sbt(ph, "sa%d" % i, [128, NT], F32) for i in range(2)]
            R_sa = [Res(), Res()]
            o = norm_mod(ph, "f")

            def prep(ti, steps):
                t0, n, w_ = tl[ti]
                b = ti % 2
                kk.dma("sp", ht[b][:, :, :n], h_in[:, :, t0:t0 + n], w=[R_ht[b]])
                emit_rstd(o, ht[b], R_ht[b], n, steps)
                emit_xl(o, ht[b], R_ht[b], n, li, j, w_, xl[b], R_xl[b], steps)

            prep(0, None)
            for ti, (t0, n, w_) in enumerate(tl):
                b = ti % 2
                steps = []
                if ti + 1 < len(tl):
                    prep(ti + 1, steps)
                for jj in range(FJ):
                    pb = jj % 2
                    for kc in range(DC):
                        kk.op("pe", lambda e, jj=jj, kc=kc, pb=pb: e.matmul(
                            ps_a[pb][:, :n], win[:, kc, jj * 128:(jj + 1) * 128], xl[b][:, kc, :n],
                            start=(kc == 0), stop=(kc == DC - 1)),
                            r=[R_wina[blk_of_j[jj]], R_xl[b]], w=[R_ps_a[pb]], inc=(kc == DC - 1))
                    for kc in range(DC):
                        kk.op("pe", lambda e, jj=jj, kc=kc, pb=pb: e.matmul(
                            ps_b[pb][:, :n], win[:, kc, FF + jj * 128:FF + (jj + 1) * 128], xl[b][:, kc, :n],
                            start=(kc == 0), stop=(kc == DC - 1)),
                            r=[R_winb[blk_of_j[jj]], R_xl[b]], w=[R_ps_b[pb]], inc=(kc == DC - 1))
                    kk.op("act", lambda e, pb=pb: e.activation(out=sa[pb][:, :n], in_=ps_a[pb][:, :n], func=AF.Silu),
                          r=[R_ps_a[pb]], w=[R_sa[pb]])
                    kk.op("dve", lambda e, pb=pb, jj=jj: e.tensor_tensor(out=g[:, jj, :n], in0=sa[pb][:, :n],
                                                                         in1=ps_b[pb][:, :n], op=ALU.mult),
                          r=[R_sa[pb], R_ps_b[pb]], w=[R_g[jj]])
                    if jj >= 2 and steps:
                        steps.pop(0)()
                while steps:
                    steps.pop(0)()
                for d in range(DC):
                    pb = d % 2
                    for jj in range(FJ):
                        kk.op("pe", lambda e, jj=jj, d=d, pb=pb: e.matmul(
                            ps_o[pb][:, :n], wout[:, jj, d * 128:(d + 1) * 128], g[:, jj, :n],
                            start=(jj == 0), stop=(jj == FJ - 1)),
                            r=[R_woutb[jj // 6], R_g[jj]], w=[R_ps_o[pb]], inc=(jj == FJ - 1))
                    kk.op("dve", lambda e, d=d, pb=pb, b=b: e.scalar_tensor_tensor(
                        out=ht[b][:, d, :n], in0=ps_o[pb][:, :n], scalar=gat[:, li, j, d, w_:w_ + 1],
                        in1=ht[b][:, d, :n], op0=ALU.mult, op1=ALU.add),
                        r=[R_ps_o[pb], R_mods, R_ht[b]], w=[R_ht[b]])
                kk.dma("sp", h_out[:, :, t0:t0 + n], ht[b][:, :, :n], r=[R_ht[b]])
            kk.barrier()

    def final_phase(h_in):
        with contextlib.ExitStack() as ph:
            ht = [sbt(ph, "fht%d" % i, [128, DC, NT], F32) for i in range(2)]
            R_ht = [Res(), Res()]
            ot = [sbt(ph, "fot%d" % i, [128, DC, NT], F32) for i in range(2)]
            R_ot = [Res(), Res()]
            o = norm_mod(ph, "fn")
            for ti, (t0, n, w_) in enumerate(tiles):
                if w_ == 1:
                    continue
                b = ti % 2
                kk.dma("sp", ht[b][:, :, :n], h_in[:, :, t0:t0 + n], w=[R_ht[b]])
                emit_rstd(o, ht[b], R_ht[b], n)
                for c in range(DC):
                    kk.op("dve", lambda e, c=c, b=b: e.scalar_tensor_tensor(
                        out=ot[b][:, c, :n], in0=ht[b][:, c, :n], scalar=fng[:, c:c + 1], in1=o["rstd"][:, :n],
                        op0=ALU.mult, op1=ALU.mult), r=[R_ht[b], o["Rrstd"], R_const], w=[R_ot[b]])
                kk.dma("sp", outT[:, :, t0:t0 + n], ot[b][:, :, :n], r=[R_ot[b]])
            kk.barrier()

    def proj_phase(li, kind, h_in, tl):
        with contextlib.ExitStack() as ph:
            if kind == "ret":
                ncol = 6144
                W = sbt(ph, "pw", [128, DC, ncol], BF16)
                R_Wl = [Res() for _ in range(DC)]
                for kc in range(DC):
                    kk.dma("pool", W[:, kc, :], ret_w_in[kc * 128:(kc + 1) * 128, :], w=[R_Wl[kc]])
                fm = [(0, 8, qT_d, "ret", 0), (1024, 8, kT_d, "ret", 0)]
                tm = [(2048, 2048, v_d, AF.Identity, "v"), (4096, 2048, g_d, AF.Silu, "g")]
                cs_d = ret_cs
            elif kind == "nat":
                ncol = 3072
                W = sbt(ph, "pw", [128, DC, ncol], BF16)
                R_Wl = [Res()]
                kk.dma("pool", W[:], nat_w_qkv.rearrange("(kc p) n -> p kc n", p=128), w=[R_Wl[0]])
                fm = [(0, 8, qT_d, None, 0), (1024, 8, kT_d, None, 0)]
                tm = [(2048, 1024, v_d, AF.Identity, "v")]
                cs_d = None
            else:
                ncol = 1536 + 1280
                W = sbt(ph, "pw", [128, DC, ncol], BF16)
                R_Wl = [Res(), Res()]
                kk.dma("pool", W[:, :, 0:1536], swa_w_qkv.rearrange("(kc p) n -> p kc n", p=128), w=[R_Wl[0]])
                kk.dma("pool", W[:, :, 1536:2816], swa_w_perm.rearrange("(kc p) n -> p kc n", p=128), w=[R_Wl[1]])
                fm = [(0, 8, qT_d, "swa", 1536), (1024, 2, kT_d, "swa", 1536 + 1024)]
                tm = [(1280, 256, v_d, AF.Identity, "v")]
                cs_d = swa_cs
            ht = [sbt(ph, "pht%d" % i, [128, DC, NT], F32) for i in range(2)]
            R_ht = [Res(), Res()]
            xl2 = [sbt(ph, "pxl%d" % i, [128, DC, NT], BF16) for i in range(2)]
            R_xl2 = [Res(), Res()]
            stg = [sbt(ph, "pst%d" % i, [128, DC, NT], BF16) for i in range(2)]
            R_stg = [Res(), Res()]
            R_stg2 = [Res(), Res()]
            stv = sbt(ph, "pstv", [128, 2048], BF16)
            R_stv = Res()
            stgg = sbt(ph, "pstg", [128, 2048], F32)
            R_stgg = Res()
            cs2 = [sbt(ph, "pcs%d" % i, [128, 2, NT], F32) for i in range(2)]
            R_cs2 = [Res(), Res()]
            t1 = sbt(ph, "pt1", [128, NT], F32)
            t2 = sbt(ph, "pt2", [128, NT], F32)
            R_t1, R_t2 = Res(), Res()
            t3 = sbt(ph, "pt3", [128, NT], F32)
            t4 = sbt(ph, "pt4", [128, NT], F32)
            R_t3, R_t4 = Res(), Res()
            sAB = sbt(ph, "psAB", [128, 2, NT], F32)
            R_sAB = Res()
            o = norm_mod(ph, "p")
            def prep(ti_, steps):
                t0_, n_, w__ = tl[ti_]
                b_ = ti_ % 2
                kk.dma("sp", ht[b_][:, :, :n_], h_in[:, :, t0_:t0_ + n_], w=[R_ht[b_]])
                if cs_d is not None:
                    kk.dma("sp", cs2[b_][:, :, :n_], cs_d[:, :, t0_:t0_ + n_], w=[R_cs2[b_]])
                emit_rstd(o, ht[b_], R_ht[b_], n_, steps)
                emit_xl(o, ht[b_], R_ht[b_], n_, li, 1, w__, xl2[b_], R_xl2[b_], steps)

            prep(0, None)
            for ti, (t0, n, w_) in enumerate(tl):
                b = ti % 2
                xl, R_xl = xl2[b], R_xl2[b]
                cs, R_cs = cs2[b], R_cs2[b]
                steps = []
                if ti + 1 < len(tl):
                    prep(ti + 1, steps)

                def proj(off, m):
                    (pa, Ra) = bankS()
                    mm_acc(pa[:, :n], Ra, [(W[:, kc, off + m * 128:off + (m + 1) * 128], xl[:, kc, :n], R_Wl + [R_xl])
                                            for kc in range(DC)])
                    if steps:
                        steps.pop(0)()
                    return pa, Ra

                def tt(out, a, b_, op, r, w):
                    kk.op("dve", lambda e: e.tensor_tensor(out=out, in0=a, in1=b_, op=op), r=r, w=w)

                def ttp(out, a, b_, op, r, w):
                    kk.op("pool", lambda e: e.tensor_tensor(out=out, in0=a, in1=b_, op=op), r=r, w=w)

                for fi, (off, nch, dst, mode, poff) in enumerate(fm):
                    sg, R_sg, R_sg2 = stg[fi], R_stg[fi], R_stg2[fi]
                    if mode == "ret":
                        for hh in range(nch // 2):
                            pa, Ra = proj(off, 2 * hh)
                            pb_, Rb = proj(off, 2 * hh + 1)
                            kk.op("act", lambda e, pa=pa: e.activation(out=sAB[:, 0, :n], in_=pa[:, :n], func=AF.Identity),
                                  r=[Ra], w=[R_sAB])
                            kk.op("act", lambda e, pb_=pb_: e.activation(out=sAB[:, 1, :n], in_=pb_[:, :n], func=AF.Identity),
                                  r=[Rb], w=[R_sAB])
                            tt(t1[:, :n], sAB[:, 0, :n], cs[:, 0, :n], ALU.mult, [R_sAB, R_cs], [R_t1])
                            tt(t2[:, :n], sAB[:, 1, :n], cs[:, 1, :n], ALU.mult, [R_sAB, R_cs], [R_t2])
                            tt(sg[:, 2 * hh, :n], t1[:, :n], t2[:, :n], ALU.subtract, [R_t1, R_t2], [R_sg])
                            ttp(t3[:, :n], sAB[:, 1, :n], cs[:, 0, :n], ALU.mult, [R_sAB, R_cs], [R_t3])
                            ttp(t4[:, :n], sAB[:, 0, :n], cs[:, 1, :n], ALU.mult, [R_sAB, R_cs], [R_t4])
                            ttp(sg[:, 2 * hh + 1, :n], t3[:, :n], t4[:, :n], ALU.add, [R_t3, R_t4], [R_sg2])
                    elif mode == "swa":
                        for m in range(nch):
                            pa, Ra = proj(off, m)
                            pb_, Rb = proj(poff, m)
                            tt(t1[:, :n], pa[:, :n], cs[:, 0, :n], ALU.mult, [Ra, R_cs], [R_t1])
                            tt(t2[:, :n], pb_[:, :n], cs[:, 1, :n], ALU.mult, [Rb, R_cs], [R_t2])
                            tt(sg[:, m, :n], t1[:, :n], t2[:, :n], ALU.add, [R_t1, R_t2], [R_sg])
                    else:
                        for m in range(nch):
                            pa, Ra = proj(off, m)
                            kk.op("act", lambda e, pa=pa, m=m, sg=sg: e.activation(out=sg[:, m, :n], in_=pa[:, :n], func=AF.Identity),
                                  r=[Ra], w=[R_sg])
                    kk.dma("sp", dst[:, 0:nch, t0:t0 + n], sg[:, 0:nch, :n], r=[R_sg, R_sg2])
                while steps:
                    steps.pop(0)()
                for sub in range(n // 128):
                    for (off, ncols, dstd, fn, nm) in tm:
                        st_t, R_st = (stv, R_stv) if nm == "v" else (stgg, R_stgg)
                        for blk in range((ncols + 511) // 512):
                            cw = min(512, ncols - blk * 512)
                            (pa, Ra) = bankS()
                            mm_acc(pa[:, :cw], Ra, [(xl[:, kc, sub * 128:(sub + 1) * 128],
                                                     W[:, kc, off + blk * 512:off + blk * 512 + cw], R_Wl + [R_xl])
                                                    for kc in range(DC)])
                            kk.op("act", lambda e, pa=pa, blk=blk, cw=cw, st_t=st_t, fn=fn: e.activation(
                                out=st_t[:, blk * 512:blk * 512 + cw], in_=pa[:, :cw], func=fn), r=[Ra], w=[R_st])
                        kk.dma("sp", dstd[t0 + sub * 128:t0 + (sub + 1) * 128, 0:ncols], st_t[:, 0:ncols], r=[R_st])
            kk.barrier()

    def attn_phase(kind, do_ctx):
        with contextlib.ExitStack() as ph:
            ntyp, nch = (5, 7) if kind == "nat" else (3, 5)
            bias = sbt(ph, "abias", [128, ntyp, nch, 128], F32)
            R_bias = Res()
            ones_b = sbt(ph, "aones", [128, 64], BF16)
            R_c = Res()
            kk.op("dve", lambda e: e.memset(ones_b[:], 1.0), w=[R_c])
            sk = sbt(ph, "ask", [128, 16], F32)
            if kind == "swa":
                kk.dma("sp", bias[:], swa_bias, w=[R_bias])
                kk.op("act", lambda e: e.activation(out=bias[:], in_=bias[:], func=AF.Exp), r=[R_bias], w=[R_bias])
                kk.dma("sp", sk[:], swa_sink.partition_broadcast(128), w=[R_c])
                kk.op("act", lambda e: e.activation(out=sk[:], in_=sk[:], func=AF.Exp), r=[R_c], w=[R_c])
            qh = [sbt(ph, "aqh%d" % i, [64, T_ALL], BF16) for i in range(2)]
            kh = [sbt(ph, "akh%d" % i, [64, T_ALL], BF16) for i in range(2)]
            vh = [sbt(ph, "avh%d" % i, [128, 34, 65], BF16) for i in range(2)]
            R_qh, R_kh, R_vh = [Res(), Res()], [Res(), Res()], [Res(), Res()]
            for i in range(2):
                kk.op("dve", lambda e, i=i: e.memset(vh[i][:, :, 64:65], 1.0), w=[R_vh[i]])
            otm = [sbt(ph, "aotm%d" % i, [128, 34, 128], BF16) for i in range(2)]
            R_otm = [Res(), Res()]
            ostg = sbt(ph, "aostg", [128, 34 * 128], BF16)
            R_ostg = Res()
            ident_b = sbt(ph, "aident", [128, 128], BF16)
            kk.op("dve", lambda e: e.tensor_copy(out=ident_b[:], in_=identf[:]), r=[R_const], w=[R_c])
            psT = ps_stat[:].bitcast(BF16)
            bias2 = [bias, sbt(ph, "abias2", [128, ntyp, nch, 128], F32)] if kind == "nat" else [bias, bias]
            R_bias2 = [R_bias, Res()] if kind == "nat" else [R_bias, R_bias]
            tmp = [sbt(ph, "atmp%d" % i, [128, nch * 128], F32) for i in range(2)]
            R_tmp = [Res(), Res()]
            pt = [sbt(ph, "apt%d" % i, [128, nch * 128], BF16) for i in range(2)]
            R_pt = [Res(), Res()]
            den = [sbt(ph, "aden%d" % i, [128, 1], F32) for i in range(2)]
            R_den = [Res(), Res()]
            Tq = T_ALL if do_ctx else T_LAT

            def load_head(h):
                hs = h % 2
                kvh = h if kind == "nat" else h // 4
                kk.dma("sp", qh[hs][:], qT_d[(h % 2) * 64:(h % 2) * 64 + 64, h // 2, :], w=[R_qh[hs]])
                kk.dma("sp", kh[hs][:], kT_d[(kvh % 2) * 64:(kvh % 2) * 64 + 64, kvh // 2, :], w=[R_kh[hs]])
                kk.dma("sp", vh[hs][:, :, 0:64], v_d[:, kvh * 64:(kvh + 1) * 64].rearrange("(n p) d -> p n d", p=128),
                       w=[R_vh[hs]])
                if kind == "nat":
                    kk.dma("sp", bias2[hs][:], nat_bias[h], w=[R_bias2[hs]])
                    kk.op("act", lambda e, hs=hs: e.activation(out=bias2[hs][:], in_=bias2[hs][:], func=AF.Exp),
                          r=[R_bias2[hs]], w=[R_bias2[hs]])

            units = []
            for h in range(16):
                for qi in range(32):
                    if kind == "nat":
                        typ = 0 if 2 <= qi <= 29 else {0: 1, 1: 2, 30: 3, 31: 4}[qi]
                        base = min(max(qi - 2, 0), 27)
                        chunks = [base + i for i in range(5)] + [32, 33]
                    else:
                        typ = 1 if qi == 0 else (2 if qi == 31 else 0)
                        chunks = [max(qi - 1, 0), qi, min(qi + 1, 31), 32, 33]
                    units.append([h, qi * 128, chunks, typ, qi == 0, False])
                if do_ctx:
                    for qc in (32, 33):
                        units.append([h, qc * 128, [32, 33], None, False, False])
                units[-1][5] = True
            state = {}

            def s1(ui):
                h, q0, chunks, typ, first, lasth = units[ui]
                hs = h % 2
                ncu = len(chunks)
                u2 = ui % 2
                banks = []
                for c0 in range(0, ncu, 4):
                    cn = min(4, ncu - c0)
                    (pS, RS) = bankS()
                    banks.append((pS, RS, c0, cn))
                    for ci in range(c0, c0 + cn):
                        kc_ = chunks[ci]
                        kk.op("pe", lambda e, pS=pS, ci=ci, c0=c0, kc_=kc_, q0=q0, hs=hs: e.matmul(
                            pS[:, (ci - c0) * 128:(ci - c0 + 1) * 128], kh[hs][:, kc_ * 128:(kc_ + 1) * 128],
                            qh[hs][:, q0:q0 + 128], start=True, stop=True),
                            r=[R_kh[hs], R_qh[hs]], w=[RS], inc=(ci == c0 + cn - 1))
                if typ is not None and kind == "swa":
                    for (pS, RS, c0, cn) in banks:
                        kk.op("act", lambda e, pS=pS, c0=c0, cn=cn, u2=u2: e.activation(
                            out=pt[u2][:, c0 * 128:(c0 + cn) * 128], in_=pS[:, :cn * 128], func=AF.Exp, scale=0.125),
                            r=[RS], w=[R_pt[u2]])
                    for ch in (0, 2):
                        kk.op("pool", lambda e, ch=ch, u2=u2, typ=typ: e.tensor_tensor(
                            out=pt[u2][:, ch * 128:(ch + 1) * 128], in0=pt[u2][:, ch * 128:(ch + 1) * 128],
                            in1=bias[:, typ, ch, :], op=ALU.mult), r=[R_bias], w=[R_pt[u2]])
                elif typ is not None:
                    for (pS, RS, c0, cn) in banks:
                        nloc = max(0, min(cn, 5 - c0))
                        if nloc > 0:
                            kk.op("act", lambda e, pS=pS, c0=c0, nloc=nloc, u2=u2: e.activation(
                                out=tmp[u2][:, c0 * 128:(c0 + nloc) * 128], in_=pS[:, :nloc * 128], func=AF.Exp, scale=0.125),
                                r=[RS], w=[R_tmp[u2]])
                        if cn > nloc:
                            kk.op("act", lambda e, pS=pS, c0=c0, cn=cn, nloc=nloc, u2=u2: e.activation(
                                out=pt[u2][:, (c0 + nloc) * 128:(c0 + cn) * 128], in_=pS[:, nloc * 128:cn * 128],
                                func=AF.Exp, scale=0.125), r=[RS], w=[R_pt[u2]])
                    kk.op("pool", lambda e, u2=u2, typ=typ, hs=hs: e.tensor_tensor(
                        out=pt[u2][:, 0:640].rearrange("p (c q) -> p c q", q=128),
                        in0=tmp[u2][:, 0:640].rearrange("p (c q) -> p c q", q=128),
                        in1=bias2[hs][:, typ, 0:5, :], op=ALU.mult), r=[R_tmp[u2], R_bias2[hs]], w=[R_pt[u2]])
                else:
                    for (pS, RS, c0, cn) in banks:
                        kk.op("act", lambda e, pS=pS, c0=c0, cn=cn, u2=u2: e.activation(
                            out=pt[u2][:, c0 * 128:(c0 + cn) * 128], in_=pS[:, :cn * 128], func=AF.Exp, scale=0.125),
                            r=[RS], w=[R_pt[u2]])

            ntile = 34 if do_ctx else 32

            def s2(ui):
                h, q0, chunks, typ, first, lasth = units[ui]
                hs = h % 2
                if first and h + 1 < 16:
                    load_head(h + 1)
                ncu = len(chunks)
                u2 = ui % 2
                pp = (h // 2) % 2
                par = h % 2
                qt = q0 // 128
                (pO, RO) = bankO()
                mm_acc(pO[:, 0:65], RO, [(pt[u2][:, ci * 128:(ci + 1) * 128], vh[hs][:, chunks[ci], 0:65], [R_vh[hs], R_pt[u2]])
                                         for ci in range(ncu)])
                if kind == "swa":
                    kk.op("dve", lambda e, pO=pO, u2=u2, h=h: e.tensor_scalar(
                        out=den[u2][:], in0=pO[:, 64:65], scalar1=sk[:, h:h + 1], scalar2=None, op0=ALU.add),
                        r=[RO, R_c], w=[R_den[u2]])
                    kk.op("dve", lambda e, u2=u2: e.reciprocal(out=den[u2][:], in_=den[u2][:]), r=[R_den[u2]], w=[R_den[u2]])
                else:
                    kk.op("dve", lambda e, pO=pO, u2=u2: e.reciprocal(out=den[u2][:], in_=pO[:, 64:65]),
                          r=[RO], w=[R_den[u2]])
                kk.op("dve", lambda e, pO=pO, u2=u2, pp=pp, par=par, qt=qt: e.tensor_scalar(
                    out=otm[pp][:, qt, par * 64:(par + 1) * 64], in0=pO[:, 0:64], scalar1=den[u2][:, 0:1], scalar2=None,
                    op0=ALU.mult), r=[RO, R_den[u2]], w=[R_otm[pp]])
                if lasth and par == 1:
                    for t in range(ntile):
                        reg = (t % 8) * 128
                        kk.op("pe", lambda e, t=t, reg=reg, pp=pp: e.transpose(psT[:, reg:reg + 128], otm[pp][:, t, :], ident_b[:]),
                              r=[R_otm[pp], R_c], w=[R_ps_stat], inc=(t % 4 == 3 or t == ntile - 1))
                        if t % 4 == 3 or t == ntile - 1:
                            t0_ = t - (t % 4)
                            half = ((t0_ % 8) // 4) * 512
                            nn = (t - t0_ + 1) * 128
                            kk.op("act", lambda e, t0_=t0_, half=half, nn=nn: e.activation(
                                out=ostg[:, t0_ * 128:t0_ * 128 + nn], in_=psT[:, half:half + nn], func=AF.Identity),
                                r=[R_ps_stat], w=[R_ostg])
                    kk.dma("sp", oT_d[:, h // 2, 0:Tq], ostg[:, 0:Tq], r=[R_ostg])

            LA = 1
            load_head(0)
            for k in range(len(units) + LA):
                if k < len(units):
                    s1(k)
                if k - LA >= 0:
                    s2(k - LA)
            kk.barrier()

    def oproj_phase(li, Wd, KC, h_in, h_out, tl):
        with contextlib.ExitStack() as ph:
            W = sbt(ph, "ow", [128, KC, D], BF16)
            R_W = Res()
            kk.dma("pool", W[:], Wd.rearrange("(c p) d -> p c d", p=128), w=[R_W])
            ht = [sbt(ph, "oht%d" % i, [128, DC, NT], F32) for i in range(2)]
            R_ht = [Res(), Res()]
            ot = [sbt(ph, "oot%d" % i, [128, KC, NT], BF16) for i in range(2)]
            R_ot = [Res(), Res()]
            def load(ti_):
                t0_, n_, _w = tl[ti_]
                b_ = ti_ % 2
                kk.dma("sp", ht[b_][:, :, :n_], h_in[:, :, t0_:t0_ + n_], w=[R_ht[b_]])
                kk.dma("sp", ot[b_][:, :, :n_], oT_d[:, 0:KC, t0_:t0_ + n_], w=[R_ot[b_]])

            load(0)
            for ti, (t0, n, w_) in enumerate(tl):
                b = ti % 2
                if ti + 1 < len(tl):
                    load(ti + 1)
                for d in range(DC):
                    (pO, RO) = bankO()
                    mm_acc(pO[:, :n], RO, [(W[:, c, d * 128:(d + 1) * 128], ot[b][:, c, :n], [R_W, R_ot[b]]) for c in range(KC)])
                    kk.op("dve", lambda e, d=d, pO=pO, b=b: e.scalar_tensor_tensor(
                        out=ht[b][:, d, :n], in0=pO[:, :n], scalar=gat[:, li, 1, d, w_:w_ + 1],
                        in1=ht[b][:, d, :n], op0=ALU.mult, op1=ALU.add),
                        r=[RO, R_mods, R_ht[b]], w=[R_ht[b]])
                kk.dma("sp", h_out[:, :, t0:t0 + n], ht[b][:, :, :n], r=[R_ht[b]])
            kk.barrier()

    def ret_phase(do_ctx):
        with contextlib.ExitStack() as ph:
            rt = sbt(ph, "rtabs", [128, 8, 128], F32)
            R_rt = Res()
            kk.dma("sp", rt[:], ret_tabs, w=[R_rt])
            lg = sbt(ph, "rlg", [128, 8], F32)
            kk.dma("sp", lg[:], ret_decay.partition_broadcast(128), w=[R_rt])
            kk.op("act", lambda e: e.activation(out=lg[:], in_=lg[:], func=AF.Sigmoid), r=[R_rt], w=[R_rt])
            kk.op("act", lambda e: e.activation(out=lg[:], in_=lg[:], func=AF.Ln), r=[R_rt], w=[R_rt])
            lgn = sbt(ph, "rlgn", [128, 8], F32)
            kk.op("dve", lambda e: e.tensor_scalar(out=lgn[:], in0=lg[:], scalar1=-1.0, scalar2=None, op0=ALU.mult),
                  r=[R_rt], w=[R_rt])
            ident = sbt(ph, "rident", [128, 128], BF16)
            kk.op("dve", lambda e: e.tensor_copy(out=ident[:], in_=rt[:, 7, :]), r=[R_rt], w=[R_rt])
            gng = sbt(ph, "rgng", [128, 2048], F32)
            kk.dma("sp", gng[:], ret_gn.partition_broadcast(128), w=[R_rt])
            tb = sbt(ph, "rtb", [128, 4, 5, 128], F32)
            tA = sbt(ph, "rtA", [128, 128], F32)
            tB = sbt(ph, "rtB", [128, 128], F32)
            for h in range(4):
                lf = lg[:, h:h + 1]
                lb = lg[:, 4 + h:5 + h]
                for (slot, tab, sc_) in ((0, 0, lf), (1, 1, lb), (2, 2, lf), (3, 3, lb)):
                    kk.op("act", lambda e, h=h, slot=slot, tab=tab, sc_=sc_: e.activation(
                        out=tb[:, h, slot, :], in_=rt[:, tab, :], func=AF.Exp, scale=sc_), r=[R_rt], w=[R_rt])
                kk.op("act", lambda e, lf=lf: e.activation(out=tA[:], in_=rt[:, 4, :], func=AF.Exp, scale=lf), r=[R_rt], w=[R_rt])
                kk.op("act", lambda e, lb=lb: e.activation(out=tB[:], in_=rt[:, 5, :], func=AF.Exp, scale=lb), r=[R_rt], w=[R_rt])
                kk.op("dve", lambda e: e.tensor_tensor(out=tA[:], in0=tA[:], in1=tB[:], op=ALU.subtract), r=[R_rt], w=[R_rt])
                kk.op("dve", lambda e: e.tensor_tensor(out=tA[:], in0=tA[:], in1=rt[:, 6, :], op=ALU.mult), r=[R_rt], w=[R_rt])
                kk.op("dve", lambda e, h=h: e.tensor_tensor(out=tb[:, h, 4, :], in0=tA[:], in1=tB[:], op=ALU.add), r=[R_rt], w=[R_rt])
                kk.op("dve", lambda e, h=h: e.tensor_scalar(out=tb[:, h, 2:5, :], in0=tb[:, h, 2:5, :], scalar1=1.0 / 16.0,
                                                            scalar2=None, op0=ALU.mult), r=[R_rt], w=[R_rt])
                kk.op("act", lambda e, h=h: e.activation(out=tA[:], in_=rt[:, 0, :], func=AF.Exp, scale=lgn[:, h:h + 1]),
                      r=[R_rt], w=[R_rt])
                kk.op("dve", lambda e, h=h: e.tensor_tensor(out=tb[:, h, 4, :], in0=tb[:, h, 4, :], in1=tA[:], op=ALU.mult),
                      r=[R_rt], w=[R_rt])
            cc = sbt(ph, "rcc", [128, 8], F32)
            kk.op("act", lambda e: e.activation(out=cc[:], in_=lg[:], func=AF.Exp, scale=128.0), r=[R_rt], w=[R_rt])
            qh = sbt(ph, "rqh", [128, 2, T_ALL], BF16)
            qf = sbt(ph, "rqf", [128, 2, T_ALL], BF16)
            qb = sbt(ph, "rqb", [128, 2, T_ALL], BF16)
            kh = sbt(ph, "rkh", [128, 2, T_ALL], BF16)
            vh = sbt(ph, "rvh", [128, 34, 512], BF16)
            R_qh, R_qf, R_qb, R_kh, R_vh = Res(), Res(), Res(), Res(), Res()
            Sf = sbt(ph, "rSf", [128, 2, 512], F32)
            Sbp = [sbt(ph, "rSb%d" % i, [128, 2, 512], F32) for i in range(2)]
            R_Sf, R_Sbp = Res(), [Res(), Res()]
            Sfb = [sbt(ph, "rSfb%d" % i, [128, 2, 512], BF16) for i in range(2)]
            R_Sfb = [Res(), Res()]
            sstg = [sbt(ph, "rsstg%d" % i, [128, 2, 512], BF16) for i in range(2)]
            R_sstg = [Res(), Res()]
            sbl = [sbt(ph, "rsbl%d" % i, [128, 2, 512], BF16) for i in range(3)]
            R_sbl = [Res(), Res(), Res()]
            kt = [sbt(ph, "rkt%d" % i, [128, 256], BF16) for i in range(2)]
            R_kt = [Res(), Res()]
            pm = [sbt(ph, "rpm%d" % i, [128, 128], BF16) for i in range(2)]
            R_pm = [Res(), Res()]
            ocn = [sbt(ph, "rocn%d" % i, [128, 512], F32) for i in range(4)]
            R_ocn = [Res() for _ in range(4)]
            junk = sbt(ph, "rjunk", [128, 512], F32)
            R_junk = Res()
            gt = [sbt(ph, "rgt%d" % i, [128, 512], F32) for i in range(4)]
            R_gt = [Res() for _ in range(4)]
            gated = [sbt(ph, "rgated%d" % i, [128, 512], BF16) for i in range(4)]
            R_gated = [Res() for _ in range(4)]
            ost = [sbt(ph, "rost%d" % i, [128, 4, 128], BF16) for i in range(4)]
            R_ost = [Res() for _ in range(4)]
            sm = [sbt(ph, "rsm%d" % i, [128, 4], F32) for i in range(4)]
            R_sm = [Res() for _ in range(4)]
            psT = ps_stat[:].bitcast(BF16)
            psXb = ps_x[:].bitcast(BF16)
            R_sbd = [Res() for _ in range(34)]
            Ubank = [[(ps_a[0], R_ps_a[0]), (ps_a[1], R_ps_a[1])], [(ps_b[0], R_ps_b[0]), (ps_b[1], R_ps_b[1])]]
            Obank = [(ps_o[0], R_ps_o[0]), (ps_o[1], R_ps_o[1])]

            def prescale_q(h):
                for (dst, R_dst, slot, eng) in ((qf, R_qf, 0, "dve"), (qb, R_qb, 1, "pool")):
                    kk.op(eng, lambda e, dst=dst, slot=slot: e.tensor_tensor(
                        out=dst[:].rearrange("p c (n i) -> p c n i", i=128),
                        in0=qh[:].rearrange("p c (n i) -> p c n i", i=128),
                        in1=tb[:, h, slot, :].unsqueeze(1).unsqueeze(1).to_broadcast([128, 2, 34, 128]), op=ALU.mult),
                        r=[R_qh, R_rt], w=[R_dst])

            def u_prep(h, m, direction, par):
                reg = 512 + par * 256
                for dc in range(2):
                    kk.op("pe", lambda e, dc=dc: e.transpose(psXb[:, reg + dc * 128:reg + (dc + 1) * 128],
                                                             kh[:, dc, m * 128:(m + 1) * 128], ident[:]),
                          r=[R_kh, R_rt], w=[R_ps_x], inc=(dc == 1))
                slot = 2 if direction == "f" else 3
                kk.op("act", lambda e: e.activation(out=kt[par][:], in_=psXb[:, reg:reg + 256], func=AF.Identity,
                                                    scale=tb[:, h, slot, 0:1]), r=[R_ps_x, R_rt], w=[R_kt[par]])
                for dc in range(2):
                    (pU, RU) = Ubank[par][dc]
                    kk.op("pe", lambda e, dc=dc, pU=pU: e.matmul(pU[:, :512], kt[par][:, dc * 128:(dc + 1) * 128], vh[:, m, :],
                                                                  start=True, stop=True), r=[R_kt[par], R_vh], w=[RU])

            def s_update(h, S, R_S, direction, par, S2=None, R_S2=None):
                cidx = h if direction == "f" else 4 + h
                if S2 is None:
                    S2, R_S2 = S, R_S
                for dc in range(2):
                    (pU, RU) = Ubank[par][dc]
                    kk.op("dve", lambda e, dc=dc, pU=pU: e.scalar_tensor_tensor(
                        out=S2[:, dc, :], in0=S[:, dc, :], scalar=cc[:, cidx:cidx + 1], in1=pU[:, :512],
                        op0=ALU.mult, op1=ALU.add), r=[RU, R_S, R_rt], w=[R_S2])

            def epiA(h, q0, tx, pO, RO):
                t2 = tx % 4
                kk.op("dve", lambda e: e.reduce_sum(out=sm[t2][:, 0:1], in_=pO[:, :512], axis=AX.X), r=[RO], w=[R_sm[t2]])
                kk.op("dve", lambda e: e.tensor_scalar(out=sm[t2][:, 0:1], in0=sm[t2][:, 0:1], scalar1=-1.0 / 512.0, scalar2=None,
                                                       op0=ALU.mult), r=[R_sm[t2]], w=[R_sm[t2]])
                kk.op("act", lambda e: e.activation(out=ocn[t2][:], in_=pO[:, :512], func=AF.Identity, bias=sm[t2][:, 0:1]),
                      r=[RO, R_sm[t2]], w=[R_ocn[t2]])
                kk.op("act", lambda e: e.activation(out=junk[:], in_=ocn[t2][:], func=AF.Square, accum_out=sm[t2][:, 1:2]),
                      r=[R_ocn[t2]], w=[R_sm[t2], R_junk])
                kk.op("act", lambda e: e.activation(out=sm[t2][:, 2:3], in_=sm[t2][:, 1:2], func=AF.Sqrt, scale=1.0 / 512.0, bias=EPS),
                      r=[R_sm[t2]], w=[R_sm[t2]])

            def epiB(h, q0, tx):
                t2 = tx % 4
                kk.op("dve", lambda e: e.reciprocal(out=sm[t2][:, 2:3], in_=sm[t2][:, 2:3]), r=[R_sm[t2]], w=[R_sm[t2]])
                kk.op("dve", lambda e: e.scalar_tensor_tensor(
                    out=ocn[t2][:], in0=ocn[t2][:], scalar=sm[t2][:, 2:3], in1=gng[:, h * 512:(h + 1) * 512],
                    op0=ALU.mult, op1=ALU.mult), r=[R_ocn[t2], R_sm[t2], R_rt], w=[R_ocn[t2]])
                kk.op("dve", lambda e: e.tensor_tensor(out=gated[t2][:], in0=ocn[t2][:], in1=gt[t2][:], op=ALU.mult),
                      r=[R_ocn[t2], R_gt[t2]], w=[R_gated[t2]])

            def epiC(h, q0, tx):
                t2 = tx % 4
                for blk in range(4):
                    kk.op("pe", lambda e, blk=blk: e.transpose(psT[:, blk * 128:(blk + 1) * 128],
                                                               gated[t2][:, blk * 128:(blk + 1) * 128], ident[:]),
                          r=[R_gated[t2], R_rt], w=[R_ps_stat], inc=(blk == 3))
                kk.op("act", lambda e: e.activation(out=ost[t2][:].rearrange("p a b -> p (a b)"), in_=psT[:, 0:512],
                                                    func=AF.Identity), r=[R_ps_stat], w=[R_ost[t2]])
                kk.dma("sp", oT_d[:, 4 * h:4 * h + 4, q0:q0 + 128], ost[t2][:], r=[R_ost[t2]])

            tx = 0
            pending = []

            def tick():
                for it in pending:
                    it[0] -= 1
                while pending and pending[0][0] <= 0:
                    pending.pop(0)[1]()

            for h in range(4):
                kk.dma("sp", qh[:], qT_d[:, 2 * h:2 * h + 2, :], w=[R_qh])
                kk.dma("sp", kh[:], kT_d[:, 2 * h:2 * h + 2, :], w=[R_kh])
                kk.dma("sp", vh[:], v_d[:, h * 512:(h + 1) * 512].rearrange("(n p) d -> p n d", p=128), w=[R_vh])
                prescale_q(h)
                kk.op("dve", lambda e: e.memset(Sbp[0][:], 0.0), w=[R_Sbp[0]])
                orderB = [33, 32] + list(range(31, -1, -1))
                for bi, m in enumerate(orderB):
                    par = bi % 2
                    cur = bi % 2
                    u_prep(h, m, "b", par)
                    need = (m < 32) or (do_ctx and m == 32)
                    if need:
                        sg = bi % 2
                        kk.op("act", lambda e, sg=sg, cur=cur: e.activation(out=sstg[sg][:], in_=Sbp[cur][:], func=AF.Identity),
                              r=[R_Sbp[cur]], w=[R_sstg[sg]])
                        kk.dma("sp", sb_d[m], sstg[sg][:], r=[R_sstg[sg]], w=[R_sbd[m]])
                    if bi < len(orderB) - 1:
                        s_update(h, Sbp[cur], R_Sbp[cur], "b", par, Sbp[1 - cur], R_Sbp[1 - cur])
                kk.op("dve", lambda e: e.memset(Sf[:], 0.0), w=[R_Sf])
                orderF = [32, 33] + list(range(32))
                u_prep(h, orderF[0], "f", 0)
                for fi, n in enumerate(orderF):
                    par = fi % 2
                    if fi + 1 < len(orderF):
                        u_prep(h, orderF[fi + 1], "f", 1 - par)
                    out_needed = (n < 32) or do_ctx
                    if out_needed:
                        q0 = n * 128
                        t4 = tx % 4
                        kk.dma("sp", gt[t4][:], g_d[q0:q0 + 128, h * 512:(h + 1) * 512], w=[R_gt[t4]])
                        has_b = not (n == 33)
                        has_f = fi > 0
                        if has_b:
                            sl = tx % 3
                            kk.dma("sp", sbl[sl][:], sb_d[n], r=[R_sbd[n]], w=[R_sbl[sl]])
                        (pO, RO) = Obank[tx % 2]
                        mm_acc(ps_x[:, :128], R_ps_x, [(kh[:, dc, q0:q0 + 128], qf[:, dc, q0:q0 + 128], [R_kh, R_qf]) for dc in range(2)])
                        p2 = tx % 2
                        kk.op("dve", lambda e, p2=p2: e.tensor_tensor(out=pm[p2][:], in0=ps_x[:, :128], in1=tb[:, h, 4, :], op=ALU.mult),
                              r=[R_ps_x, R_rt], w=[R_pm[p2]])
                        terms = [(pm[p2][:], vh[:, n, :], [R_pm[p2], R_vh])]
                        if has_b:
                            terms += [(qb[:, dc, q0:q0 + 128], sbl[sl][:, dc, :], [R_qb, R_sbl[sl]]) for dc in range(2)]
                        if has_f:
                            fb = (fi - 1) % 2
                            terms += [(qf[:, dc, q0:q0 + 128], Sfb[fb][:, dc, :], [R_qf, R_Sfb[fb]]) for dc in range(2)]
                        for ti_, (l_, r_, rd) in enumerate(terms):
                            kk.op("pe", lambda e, l_=l_, r_=r_, ti_=ti_, nt_=len(terms), pO=pO: e.matmul(
                                pO[:, :512], l_, r_, start=(ti_ == 0), stop=(ti_ == nt_ - 1)), r=rd, w=[RO],
                                inc=(ti_ == len(terms) - 1))
                    if fi < len(orderF) - 1:
                        s_update(h, Sf, R_Sf, "f", par)
                        fb = fi % 2
                        kk.op("act", lambda e, fb=fb: e.activation(out=Sfb[fb][:], in_=Sf[:], func=AF.Identity),
                              r=[R_Sf], w=[R_Sfb[fb]])
                    while pending:
                        pending.pop(0)[1]()
                    if out_needed:
                        epiA(h, q0, tx, pO, RO)
                        pending.append([1, (lambda h=h, q0=q0, tx=tx: epiB(h, q0, tx))])
                        pending.append([2, (lambda h=h, q0=q0, tx=tx: epiC(h, q0, tx))])
                        tx += 1
            while pending:
                pending.pop(0)[1]()
            kk.barrier()


    def pool_phase(li, h_in, h_out, tl, do_ctx):
        with contextlib.ExitStack() as ph:
            ht = [sbt(ph, "qht%d" % i, [128, DC, NT], F32) for i in range(2)]
            R_ht = [Res(), Res()]
            xf = [sbt(ph, "qxf%d" % i, [128, DC, NT], F32) for i in range(2)]
            R_xf = [Res(), Res()]
            o = norm_mod(ph, "q")
            for ti, (t0, n, w_) in enumerate(tl):
                b = ti % 2
                kk.dma("sp", ht[b][:, :, :n], h_in[:, :, t0:t0 + n], w=[R_ht[b]])
                emit_rstd(o, ht[b], R_ht[b], n)
                emit_xl(o, ht[b], R_ht[b], n, li, 1, w_, xf[b], R_xf[b])
                kk.dma("sp", xl_d[:, :, t0:t0 + n], xf[b][:, :, :n], r=[R_xf[b]])
            kk.barrier()
        with contextlib.ExitStack() as ph:
            PAD = 8
            TB = T_LAT + 2 * PAD
            X = sbt(ph, "qX", [128, 2, TB], F32)
            Y = sbt(ph, "qY", [128, 2, TB], F32)
            Zb = sbt(ph, "qZ", [128, 2, TB], F32)
            R_X, R_Y, R_Z = [Res(), Res()], [Res(), Res()], [Res(), Res()]
            icn = sbt(ph, "qicn", [128, T_LAT], F32)
            R_icn = Res()
            pbf = sbt(ph, "qpb", [128, 2, T_LAT], BF16)
            R_pb = [Res(), Res()]
            wg = sbt(ph, "qwg", [128, 2, 256], BF16)
            R_wg = Res()
            psc = sbt(ph, "qpsc", [128, DC], F32)
            gp = sbt(ph, "qgp", [128, DC, 2], F32)
            R_gp = Res()
            kk.dma("sp", psc[:], pool_sc, w=[R_gp])
            for w_ in range(2):
                kk.op("dve", lambda e, w_=w_: e.tensor_tensor(out=gp[:, :, w_], in0=gat[:, li, 1, :, w_], in1=psc[:], op=ALU.mult),
                      r=[R_gp, R_mods], w=[R_gp])
            hc = [sbt(ph, "qhc%d" % i, [128, 512], F32) for i in range(2)]
            R_hc = [Res(), Res()]
            seqs = [(0, T_LAT, 0)] + ([(T_LAT, T_CTX, 1)] if do_ctx else [])
            hi_ = 0
            for g_ in range(4):
                kk.dma("pool", wg[:], pool_w[g_].rearrange("(kc p) n -> p kc n", p=128), w=[R_wg])
                for (s0, T, w_) in seqs:
                    L = T + 2 * PAD
                    kk.op("pool", lambda e, L=L: e.memset(X[:, :, 0:L], 0.0), w=R_X)
                    kk.dma("sp", X[:, :, PAD:PAD + T], xl_d[:, 2 * g_:2 * g_ + 2, s0:s0 + T], w=R_X)
                    kk.dma("sp", icn[:, :T], pool_icnt[g_:g_ + 1, s0:s0 + T].partition_broadcast(128), w=[R_icn])
                    EN = ("dve", "pool")
                    kk.op("pool", lambda e, L=L: e.memset(Y[:, :, 0:L], 0.0), w=R_Y)
                    for c in range(2):
                        kk.op(EN[c], lambda e, L=L, c=c: e.tensor_tensor(
                            out=Y[:, c, 1:L], in0=X[:, c, 1:L], in1=X[:, c, 0:L - 1], op=ALU.add), r=[R_X[c]], w=[R_Y[c]])
                    lv_src, R_lsrc = Y, R_Y
                    sh = 1
                    for lev in range(g_):
                        a_, Ra_, b_, Rb_ = (Y, R_Y, Zb, R_Z) if lev % 2 == 0 else (Zb, R_Z, Y, R_Y)
                        kk.op("pool", lambda e, L=L, b_=b_: e.memset(b_[:, :, 0:L], 0.0), w=Rb_)
                        for c in range(2):
                            kk.op(EN[c], lambda e, L=L, a_=a_, b_=b_, sh=sh, c=c: e.tensor_tensor(
                                out=b_[:, c, sh:L - sh], in0=a_[:, c, 0:L - 2 * sh], in1=a_[:, c, 2 * sh:L], op=ALU.add),
                                r=[Ra_[c]], w=[Rb_[c]])
                        lv_src, R_lsrc = b_, Rb_
                        sh *= 2
                    for c in range(2):
                        kk.op(EN[c], lambda e, c=c, T=T, lv_src=lv_src: e.tensor_tensor(
                            out=lv_src[:, c, PAD:PAD + T], in0=lv_src[:, c, PAD:PAD + T], in1=icn[:, :T], op=ALU.mult),
                            r=[R_lsrc[c], R_icn], w=[R_lsrc[c]])
                    for c in range(2):
                        kk.op(EN[c], lambda e, c=c, T=T, lv_src=lv_src: e.tensor_tensor(
                            out=pbf[:, c, :T], in0=lv_src[:, c, PAD:PAD + T], in1=X[:, c, PAD:PAD + T], op=ALU.subtract),
                            r=[R_lsrc[c], R_X[c]], w=[R_pb[c]])
                    for m in range(2):
                        c = 2 * g_ + m
                        for tt0 in range(0, T, 512):
                            nn = min(512, T - tt0)
                            hb = hi_ % 2
                            hi_ += 1
                            kk.dma("sp", hc[hb][:, :nn], h_in[:, c, s0 + tt0:s0 + tt0 + nn], w=[R_hc[hb]])
                            (pO, RO) = bankO()
                            mm_acc(pO[:, :nn], RO, [(wg[:, kc, m * 128:(m + 1) * 128], pbf[:, kc, tt0:tt0 + nn], [R_wg] + R_pb)
                                                    for kc in range(2)])
                            kk.op("dve", lambda e, pO=pO, hb=hb, nn=nn, c=c, w_=w_: e.scalar_tensor_tensor(
                                out=hc[hb][:, :nn], in0=pO[:, :nn], scalar=gp[:, c, w_:w_ + 1], in1=hc[hb][:, :nn],
                                op0=ALU.mult, op1=ALU.add), r=[RO, R_gp, R_hc[hb]], w=[R_hc[hb]])
                            kk.dma("sp", h_out[:, c, s0 + tt0:s0 + tt0 + nn], hc[hb][:, :nn], r=[R_hc[hb]])
            kk.barrier()

    kinds = cfg.get("kinds", ["ret", "nat", "pool", "swa"])
    cur = xT
    nxt = 0
    lat_tiles = [t for t in tiles if t[2] == 0]
    for li in range(n_layers):
        kind = kinds[li]
        last = (li == n_layers - 1)
        ctx_live = (not last) or kind != "pool"
        tl1 = tiles if ctx_live else lat_tiles
        tl2 = lat_tiles if last else tiles
        ffn_phase(li, 0, 0, cur, hbufs[nxt], tl1)
        cur = hbufs[nxt]
        nxt ^= 1
        if mixers:
            if kind == "pool":
                pool_phase(li, cur, hbufs[nxt], tl2, not last)
            else:
                proj_phase(li, kind, cur, tl1)
                if kind == "ret":
                    ret_phase(not last)
                    oproj_phase(li, ret_w_out, 16, cur, hbufs[nxt], tl2)
                else:
                    attn_phase(kind, not last)
                    oproj_phase(li, nat_w_o if kind == "nat" else swa_w_o, 8, cur, hbufs[nxt], tl2)
            cur = hbufs[nxt]
            nxt ^= 1
        ffn_phase(li, 1, 2, cur, hbufs[nxt], tl2)
        cur = hbufs[nxt]
        nxt ^= 1
    final_phase(cur)
    kk.barrier()
    kk.ninst_total = kk.ninst
    nc._kk = kk
    return nc


def _fm(a):
    t = a.shape[0]
    return np.ascontiguousarray(a.T.reshape(DC, 128, t).transpose(1, 0, 2))


def _vec_fm(v):
    lead = v.shape[:-1]
    x = v.reshape(*lead, DC, 128)
    x = np.moveaxis(x, -1, 0)
    return np.ascontiguousarray(x)


def _consts():
    c = {}
    p = np.arange(128, dtype=np.float64)[:, None]
    i = np.arange(128, dtype=np.float64)[None, :]
    tabs = np.zeros((128, 8, 128), np.float64)
    tabs[:, 0] = i + 0 * p
    tabs[:, 1] = 127 - i + 0 * p
    tabs[:, 2] = 128 * i + 128 - p
    tabs[:, 3] = 128 * i + p + 1
    tabs[:, 4] = np.maximum(i - p, 0)
    tabs[:, 5] = np.maximum(p - i, 0)
    tabs[:, 6] = (i >= p)
    tabs[:, 7] = (i == p)
    c["ret_tabs"] = tabs.astype(np.float32)
    t = np.arange(T_LAT, dtype=np.float32)[None, :]
    inv = (10000.0 ** (-(np.arange(0, 256, 2, dtype=np.float32)) / 256.0)).astype(np.float32)[:, None]
    ang = (t * inv).astype(np.float32)
    cs = np.zeros((128, 2, T_ALL), np.float32)
    cs[:, 0, :T_LAT] = np.cos(ang)
    cs[:, 1, :T_LAT] = np.sin(ang)
    cs[:, 0, T_LAT:] = 1.0
    c["ret_cs"] = cs
    d = np.arange(128) % 64
    tt = np.arange(T_LAT)
    pos = np.where((d < 32)[:, None], (tt // 64)[None, :], (tt % 64)[None, :]).astype(np.float32)
    inv16 = (10000.0 ** (-(np.arange(0, 32, 2, dtype=np.float32)) / 32.0)).astype(np.float32)
    invd = inv16[(d % 32) % 16][:, None]
    ang = (pos * invd).astype(np.float32)
    sign = np.where((d % 32) < 16, -1.0, 1.0).astype(np.float32)[:, None]
    cs = np.zeros((128, 2, T_ALL), np.float32)
    cs[:, 0, :T_LAT] = np.cos(ang)
    cs[:, 1, :T_LAT] = np.sin(ang) * sign
    cs[:, 0, T_LAT:] = 1.0
    c["swa_cs"] = cs
    NEG = -30000.0
    j = np.arange(128)[:, None]
    q = np.arange(128)[None, :]
    sb_ = np.zeros((128, 3, 5, 128), np.float32)
    for typ in range(3):
        sb_[:, typ, 0] = np.where(j >= q, 0.0, NEG)
        sb_[:, typ, 2] = np.where(j <= q, 0.0, NEG)
    sb_[:, 1, 0] = NEG
    sb_[:, 2, 2] = NEG
    c["swa_bias"] = sb_
    ic = np.zeros((4, T_ALL), np.float32)
    for g_, w in enumerate((2, 4, 8, 16)):
        for (s0, T) in ((0, T_LAT), (T_LAT, T_CTX)):
            tq = np.arange(T)
            lo = np.clip(tq - w // 2, 0, T)
            hi = np.clip(tq + w // 2, 0, T)
            ic[g_, s0:s0 + T] = 1.0 / (hi - lo).astype(np.float32)
    c["pool_icnt"] = ic
    return c


def _nat_bias(rpb):
    NEG = -30000.0
    out = np.zeros((16, 128, 5, 7, 128), np.float32)
    u = (np.arange(128) // 64)
    n = (np.arange(128) % 64)
    cfgs = [(2, 0), (0, 0), (1, 0), (30, 27), (31, 27)]
    for typ, (qi, base) in enumerate(cfgs):
        r = (2 * qi + u)[None, :]
        cq = n[None, :]
        r0 = np.clip(r - 4, 0, 56)
        c0 = np.clip(cq - 8, 0, 48)
        for ch in range(5):
            a = (2 * (base + ch) + u)[:, None]
            nk = n[:, None]
            valid = (a >= r0) & (a < r0 + 8) & (nk >= c0) & (nk < c0 + 16)
            ri = np.clip(a - r + 7, 0, 14)
            ci = np.clip(nk - cq + 15, 0, 30)
            vals = rpb[:, ri, ci]
            out[:, :, typ, ch, :] = np.where(valid[None], vals, NEG)
    return out


def make_in_maps(inputs, cores=range(N_CORES)):
    f = lambda a: np.ascontiguousarray(np.asarray(a, dtype=np.float32))
    x, c, ctx, c_ctx = f(inputs["x"]), f(inputs["c"]), f(inputs["ctx"]), f(inputs["c_ctx"])
    b_mod = f(inputs["b_mod"])
    shared = {
        "w_mod": f(inputs["w_mod"]),
        "bmodT": np.ascontiguousarray(b_mod.reshape(DEPTH, 72, 128).transpose(2, 0, 1)),
        "normgT": _vec_fm(f(inputs["norm_g"])),
        "fnormgT": _vec_fm(f(inputs["final_norm_g"])),
        "ffn_w_in": f(inputs["ffn_w_in"]),
        "ffn_w_out": f(inputs["ffn_w_out"]),
        "ret_w_in": f(inputs["ret_w_in"][0]),
        "ret_w_out": f(inputs["ret_w_out"][0]),
        "ret_gn": f(inputs["ret_gn_g"][0:1]),
        "ret_decay": np.ascontiguousarray(np.concatenate([f(inputs["ret_decay_f"][0]), f(inputs["ret_decay_b"][0])])[None, :]),
        "nat_w_qkv": f(inputs["nat_w_qkv"][0]),
        "nat_w_o": f(inputs["nat_w_o"][0]),
        "nat_bias": _nat_bias(f(inputs["nat_rpb"][0])),
        "pool_w": f(inputs["pool_w"][0]),
        "pool_sc": _vec_fm(f(inputs["pool_scale"][0])),
        "swa_w_qkv": f(inputs["swa_w_qkv"][0]),
        "swa_w_o": f(inputs["swa_w_o"][0]),
        "swa_sink": f(inputs["swa_sink"][0:1]),
    }
    wq = shared["swa_w_qkv"]
    dd = np.arange(64)
    partner = np.where((dd % 32) < 16, dd + 16, dd - 16)
    colq = (np.arange(16)[:, None] * 64 + partner[None, :]).reshape(-1)
    colk = 1024 + (np.arange(4)[:, None] * 64 + partner[None, :]).reshape(-1)
    shared["swa_w_perm"] = np.ascontiguousarray(wq[:, np.concatenate([colq, colk])])
    shared.update(_consts())
    maps = []
    for b in cores:
        m = dict(shared)
        m["xT"] = _fm(np.concatenate([x[b], ctx[b]], axis=0))
        m["cT"] = np.ascontiguousarray(np.stack([c[b], c_ctx], axis=0).reshape(2, DC, 128).transpose(2, 1, 0))
        maps.append(m)
    return maps


_NC_CACHE = {}


def kernel(**inputs):
    if "nc" not in _NC_CACHE:
        _NC_CACHE["nc"] = build()
    nc = _NC_CACHE["nc"]
    in_maps = make_in_maps(inputs)
    res = run_bass_kernel_spmd(nc, in_maps, core_ids=list(range(N_CORES)))
    outs = []
    for b in range(N_CORES):
        o = res.results[b]["outT"]
        outs.append(o.transpose(1, 0, 2).reshape(D, T_LAT).T)
    return np.ascontiguousarray(np.stack(outs, axis=0).astype(np.float32))
```

```python
import contextlib
import numpy as np
import concourse.bass as bass
import concourse.mybir as mybir
from concourse.bass_utils import run_bass_kernel_spmd

F32 = mybir.dt.float32
BF16 = mybir.dt.bfloat16
AF = mybir.ActivationFunctionType
ALU = mybir.AluOpType
AX = mybir.AxisListType

D = 1024
DC = 8
T_LAT = 4096
T_CTX = 256
T_ALL = T_LAT + T_CTX
DEPTH = 4
FF = 2816
FJ = 22
EPS = 1e-6
NT = 256
N_CORES = 4


class Res:
    __slots__ = ("name", "w", "r")

    def __init__(self, name=""):
        self.name = name
        self.w = None
        self.r = {}


class _Eng:
    def __init__(self, kk, name, handle):
        self.name = name
        self.h = handle
        self.sem = kk.new_sem("e_" + name)
        self.count = 0
        self.waited = {}
        self.pend_r = []
        self.pend_w = []


class K:
    def __init__(self, nc, n_dma_sems=16):
        self.nc = nc
        self.st = contextlib.ExitStack()
        self.sems = {}
        self.nsem = 0
        self.eng = {}
        for name, h in (("pe", nc.tensor), ("act", nc.scalar), ("dve", nc.vector),
                        ("pool", nc.gpsimd), ("sp", nc.sync)):
            self.eng[name] = _Eng(self, name, h)
        self.dma_pool = {}
        for q in ("sp", "pool"):
            self.dma_pool[q] = [[self.new_sem("d_%s%d" % (q, i)), 0] for i in range(n_dma_sems)]
        self.dma_rr = {"sp": 0, "pool": 0}
        self.ninst = 0

    def new_sem(self, name):
        s = self.st.enter_context(self.nc.semaphore(name))
        sid = self.nsem
        self.nsem += 1
        self.sems[sid] = s
        return sid

    def _wait(self, e, ev):
        sid, val = ev
        if e.waited.get(sid, 0) >= val:
            return
        e.waited[sid] = val
        e.h.wait_ge(self.sems[sid], val)
        self.ninst += 1

    def _deps(self, e, r, w, nowaw=False):
        evs = {}

        def add(ev):
            if ev is not None and evs.get(ev[0], 0) < ev[1]:
                evs[ev[0]] = ev[1]
        for x in r:
            add(x.w)
        for x in w:
            if not nowaw:
                add(x.w)
            for sid, val in x.r.items():
                add((sid, val))
        for sid, val in evs.items():
            if e.name == "pe" and sid == e.sem:
                continue
            self._wait(e, (sid, val))

    def _commit(self, ev, r, w):
        for x in r:
            if x.r.get(ev[0], 0) < ev[1]:
                x.r[ev[0]] = ev[1]
        for x in w:
            x.w = ev
            x.r = {}

    def op(self, eng, fn, r=(), w=(), inc=True):
        e = self.eng[eng]
        self._deps(e, r, w)
        inst = fn(e.h)
        self.ninst += 1
        if not inc:
            e.pend_r.extend(r)
            e.pend_w.extend(w)
            return None
        if e.count >= 30000:
            e.sem = self.new_sem("e_%s_%d" % (eng, self.nsem))
            e.count = 0
        e.count += 1
        inst.then_inc(self.sems[e.sem], 1)
        ev = (e.sem, e.count)
        self._commit(ev, list(r) + e.pend_r, list(w) + e.pend_w)
        e.pend_r = []
        e.pend_w = []
        return ev

    def dma(self, q, out, in_, r=(), w=(), nowaw=False):
        e = self.eng[q]
        self._deps(e, r, w, nowaw=nowaw)
        pool = self.dma_pool[q]
        i = self.dma_rr[q]
        self.dma_rr[q] = (i + 1) % len(pool)
        slot = pool[i]
        if slot[1] > 0:
            self._wait(e, (slot[0], slot[1]))
        slot[1] += 16
        e.h.dma_start(out=out, in_=in_).then_inc(self.sems[slot[0]], 16)
        self.ninst += 1
        ev = (slot[0], slot[1])
        self._commit(ev, r, w)
        return ev

    def barrier(self, engs=("pe", "act", "dve", "pool", "sp")):
        for x in engs:
            e = self.eng[x]
            for y in self.eng.values():
                if y is not e and y.count > 0:
                    self._wait(e, (y.sem, y.count))
            for pool in self.dma_pool.values():
                for sid, val in pool:
                    if val > 0:
                        self._wait(e, (sid, val))


def build(cfg=None):
    cfg = cfg or {}
    n_layers = cfg.get("n_layers", DEPTH)
    mixers = cfg.get("mixers", True)
    dbg = cfg.get("dbg", False)

    nc = bass.Bass("TRN2", target_bir_lowering=False)
    kk = K(nc)
    st = kk.st

    def dram_in(name, shape, dt=F32):
        return nc.dram_tensor(name, list(shape), dt, kind="ExternalInput").ap()

    xT = dram_in("xT", [128, DC, T_ALL])
    cT = dram_in("cT", [128, DC, 2])
    w_mod = dram_in("w_mod", [DEPTH, D, 9 * D])
    bmodT = dram_in("bmodT", [128, DEPTH, 72])
    normgT = dram_in("normgT", [128, DEPTH, 3, DC])
    fnormgT = dram_in("fnormgT", [128, DC])
    ffn_w_in = dram_in("ffn_w_in", [DEPTH, 2, D, 2 * FF])
    ffn_w_out = dram_in("ffn_w_out", [DEPTH, 2, FF, D])
    outT = nc.dram_tensor("outT", [128, DC, T_LAT], F32, kind="ExternalOutput").ap()
    hA = nc.dram_tensor("hA", [128, DC, T_ALL], F32).ap()
    hB = nc.dram_tensor("hB", [128, DC, T_ALL], F32).ap()
    hbufs = [hA, hB]
    ret_w_in = dram_in("ret_w_in", [D, 6144])
    ret_w_out = dram_in("ret_w_out", [2048, D])
    ret_gn = dram_in("ret_gn", [1, 2048])
    ret_decay = dram_in("ret_decay", [1, 8])
    ret_tabs = dram_in("ret_tabs", [128, 8, 128])
    ret_cs = dram_in("ret_cs", [128, 2, T_ALL])
    nat_w_qkv = dram_in("nat_w_qkv", [D, 3072])
    nat_w_o = dram_in("nat_w_o", [D, D])
    nat_bias = dram_in("nat_bias", [16, 128, 5, 7, 128])
    pool_w = dram_in("pool_w", [4, 256, 256])
    pool_sc = dram_in("pool_sc", [128, DC])
    pool_icnt = dram_in("pool_icnt", [4, T_ALL])
    swa_w_qkv = dram_in("swa_w_qkv", [D, 1536])
    swa_w_perm = dram_in("swa_w_perm", [D, 1280])
    swa_w_o = dram_in("swa_w_o", [D, D])
    swa_sink = dram_in("swa_sink", [1, 16])
    swa_cs = dram_in("swa_cs", [128, 2, T_ALL])
    swa_bias = dram_in("swa_bias", [128, 3, 5, 128])
    qT_d = nc.dram_tensor("qT_d", [128, DC, T_ALL], BF16).ap()
    kT_d = nc.dram_tensor("kT_d", [128, DC, T_ALL], BF16).ap()
    v_d = nc.dram_tensor("v_d", [T_ALL, 2048], BF16).ap()
    g_d = nc.dram_tensor("g_d", [T_ALL, 2048], F32).ap()
    oT_d = nc.dram_tensor("oT_d", [128, 16, T_ALL], BF16).ap()
    xl_d = nc.dram_tensor("xl_d", [128, DC, T_ALL], F32).ap()
    sb_d = nc.dram_tensor("sb_d", [34, 128, 2, 512], BF16).ap()

    def sb(name, shape, dt):
        return st.enter_context(nc.sbuf_tensor(name, list(shape), dt))

    def ps(name, shape, dt=F32):
        return st.enter_context(nc.psum_tensor(name, list(shape), dt))

    _uid = [0]

    def sbt(ph, name, shape, dt):
        _uid[0] += 1
        return ph.enter_context(nc.sbuf_tensor("%s_%d" % (name, _uid[0]), list(shape), dt))

    ones_f = sb("ones_f", [128, 128], F32)
    mods = sb("mods", [128, DEPTH, 72, 2], F32)
    gsc = sb("gsc", [128, DEPTH, 3, DC, 2], F32)
    gat = sb("gat", [128, DEPTH, 3, DC, 2], F32)
    ng = sb("ng", [128, DEPTH, 3, DC], F32)
    fng = sb("fng", [128, DC], F32)
    bm = sb("bm", [128, DEPTH, 72], F32)
    sT = sb("sT", [128, DC, 2], F32)
    R_const = Res("const")
    R_mods = Res("mods")

    kk.op("dve", lambda e: e.memset(ones_f[:], 1.0), w=[R_const])
    kk.dma("sp", ng[:], normgT, w=[R_const])
    kk.dma("sp", fng[:], fnormgT, w=[R_const])
    kk.dma("sp", bm[:], bmodT, w=[R_const])
    kk.dma("sp", sT[:], cT, w=[R_const])
    kk.op("act", lambda e: e.activation(out=sT[:], in_=sT[:], func=AF.Silu), r=[R_const], w=[R_const])

    ps_stat = ps("ps_stat", [128, 512])
    ps_a = [ps("ps_a%d" % i, [128, 512]) for i in range(2)]
    ps_b = [ps("ps_b%d" % i, [128, 512]) for i in range(2)]
    ps_o = [ps("ps_o%d" % i, [128, 512]) for i in range(2)]
    R_ps_stat = Res()
    R_ps_a = [Res(), Res()]
    R_ps_b = [Res(), Res()]
    R_ps_o = [Res(), Res()]
    ps_x = ps("ps_x", [128, 512])
    R_ps_x = Res()
    poolS = [(ps_a[0], R_ps_a[0]), (ps_a[1], R_ps_a[1]), (ps_b[0], R_ps_b[0]), (ps_b[1], R_ps_b[1])]
    poolO = [(ps_o[0], R_ps_o[0]), (ps_o[1], R_ps_o[1]), (ps_x, R_ps_x)]
    _rrS = [0]
    _rrO = [0]

    def bankS():
        _rrS[0] = (_rrS[0] + 1) % len(poolS)
        return poolS[_rrS[0]]

    def bankO():
        _rrO[0] = (_rrO[0] + 1) % len(poolO)
        return poolO[_rrO[0]]

    def mm_acc(out_ap, R_out, pairs):
        last = len(pairs) - 1
        ev = None
        for idx, (l_, r_, rd) in enumerate(pairs):
            ev = kk.op("pe", lambda e, l_=l_, r_=r_, idx=idx: e.matmul(out_ap, l_, r_, start=(idx == 0), stop=(idx == last)),
                       r=rd, w=[R_out], inc=(idx == last))
        return ev

    identf = sb("identf", [128, 128], F32)
    kk.dma("sp", identf[:], ret_tabs[:, 7, :], w=[R_const])
    with contextlib.ExitStack() as ph:
        wm = [sbt(ph, "wm%d" % i, [128, DC, 1024], F32) for i in range(2)]
        R_wm = [Res(), Res()]
        modrow = sbt(ph, "modrow", [2, 9 * D], F32)
        R_modrow = Res()
        blk = 0
        for li in range(n_layers):
            for nb in range(9):
                s = blk % 2
                blk += 1
                src = w_mod[li, :, nb * 1024:(nb + 1) * 1024].rearrange("(kc p) n -> p kc n", p=128)
                kk.dma("sp", wm[s][:], src, w=[R_wm[s]])
                for half in range(2):
                    (pS, RS) = bankS()
                    mm_acc(pS[0:2, :512], RS, [(sT[:, kc, :], wm[s][:, kc, half * 512:(half + 1) * 512], [R_wm[s], R_const])
                                               for kc in range(DC)])
                    c0 = nb * 1024 + half * 512
                    kk.op("act", lambda e, pS=pS, c0=c0: e.activation(out=modrow[0:2, c0:c0 + 512], in_=pS[0:2, :512],
                                                                       func=AF.Identity), r=[RS], w=[R_modrow])
            for n in range(72):
                kk.op("pe", lambda e, n=n: e.transpose(ps_stat[:, n * 2:(n + 1) * 2], modrow[0:2, n * 128:(n + 1) * 128],
                                                       identf[0:2, 0:2]),
                      r=[R_modrow, R_const], w=[R_ps_stat], inc=(n == 71))
            kk.op("dve", lambda e, li=li: e.tensor_tensor(
                out=mods[:, li, :, :], in0=ps_stat[:, 0:144].rearrange("p (n w) -> p n w", w=2),
                in1=bm[:, li, :].unsqueeze(2).to_broadcast([128, 72, 2]),
                op=ALU.add), r=[R_ps_stat, R_const], w=[R_mods])
        kk.barrier()

    for li in range(n_layers):
        for j in range(3):
            sc = mods[:, li, (j * 3 + 1) * 8:(j * 3 + 2) * 8, :]
            gt = mods[:, li, (j * 3 + 2) * 8:(j * 3 + 3) * 8, :]
            for w_ in range(2):
                kk.op("dve", lambda e, li=li, j=j, w_=w_, sc=sc: e.scalar_tensor_tensor(
                    out=gsc[:, li, j, :, w_], in0=sc[:, :, w_], scalar=1.0, in1=ng[:, li, j, :],
                    op0=ALU.add, op1=ALU.mult), r=[R_mods, R_const], w=[R_mods])
            kk.op("dve", lambda e, li=li, j=j, gt=gt: e.tensor_scalar(
                out=gat[:, li, j, :, :], in0=gt, scalar1=(1.0 if j == 1 else 0.5), scalar2=None,
                op0=ALU.mult), r=[R_mods], w=[R_mods])
    kk.barrier()

    tiles = [(t0, NT, 0) for t0 in range(0, T_LAT, NT)] + [(T_LAT, T_CTX, 1)]

    def norm_mod(ph, name):
        o = {}
        o["sqt"] = sbt(ph, name + "_sqt", [128, DC, NT], F32)
        o["Rsqt"] = Res()
        o["ssum"] = sbt(ph, name + "_ssum", [128, NT], F32)
        o["Rssum"] = Res()
        o["rstd"] = sbt(ph, name + "_rstd", [128, NT], F32)
        o["Rrstd"] = Res()
        return o

    def _emit(steps, eng, fn, r, w):
        if steps is None:
            kk.op(eng, fn, r=r, w=w)
        else:
            steps.append(lambda: kk.op(eng, fn, r=r, w=w))

    def emit_rstd(o, ht, R_ht, n, steps=None):
        _emit(steps, "dve", lambda e: e.tensor_tensor(out=o["sqt"][:, :, :n], in0=ht[:, :, :n], in1=ht[:, :, :n], op=ALU.mult),
              [R_ht], [o["Rsqt"]])
        _emit(steps, "dve", lambda e: e.tensor_reduce(out=o["ssum"][:, :n], in_=o["sqt"][:, :, :n].rearrange("p c n -> p n c"),
                                                      axis=AX.X, op=ALU.add), [o["Rsqt"]], [o["Rssum"]])
        _emit(steps, "pe", lambda e: e.matmul(ps_stat[:, :n], ones_f[:], o["ssum"][:, :n], start=True, stop=True),
              [o["Rssum"], R_const], [R_ps_stat])
        _emit(steps, "act", lambda e: e.activation(out=o["rstd"][:, :n], in_=ps_stat[:, :n], func=AF.Sqrt,
                                                   scale=1.0 / D, bias=EPS), [R_ps_stat], [o["Rrstd"]])
        _emit(steps, "dve", lambda e: e.reciprocal(out=o["rstd"][:, :n], in_=o["rstd"][:, :n]), [o["Rrstd"]], [o["Rrstd"]])

    def emit_xl(o, ht, R_ht, n, li, j, w_, dst, R_dst, steps=None):
        _emit(steps, "dve", lambda e: e.tensor_tensor(
            out=o["sqt"][:, :, :n], in0=ht[:, :, :n], in1=o["rstd"][:, :n].unsqueeze(1).to_broadcast([128, DC, n]),
            op=ALU.mult), [R_ht, o["Rrstd"]], [o["Rsqt"]])
        for c in range(DC):
            _emit(steps, "act", lambda e, c=c: e.activation(
                out=dst[:, c, :n], in_=o["sqt"][:, c, :n], func=AF.Identity,
                scale=gsc[:, li, j, c, w_:w_ + 1], bias=mods[:, li, (j * 3) * 8 + c, w_:w_ + 1]),
                [o["Rsqt"], R_mods], [R_dst])

    def ffn_phase(li, s_, j, h_in, h_out, tl):
        with contextlib.ExitStack() as ph:
            win = sbt(ph, "win", [128, DC, 2 * FF], BF16)
            wout = sbt(ph, "wout", [128, FJ, D], BF16)
            jblocks = [(0, 2), (2, 5), (5, 8), (8, 11), (11, 14), (14, 17), (17, 20), (20, 22)]
            blk_of_j = {}
            for bi_, (j0, j1) in enumerate(jblocks):
                for jx in range(j0, j1):
                    blk_of_j[jx] = bi_
            R_wina = [Res() for _ in jblocks]
            R_winb = [Res() for _ in jblocks]
            R_woutb = [Res() for _ in range(4)]
            w_in_src = ffn_w_in[li, s_].rearrange("(kc p) n -> p kc n", p=128)
            for bi_, (j0, j1) in enumerate(jblocks):
                for base, RR in ((0, R_wina), (FF, R_winb)):
                    c0, c1 = base + j0 * 128, base + j1 * 128
                    kk.dma("pool", win[:, :, c0:c1], w_in_src[:, :, c0:c1], w=[RR[bi_]])
                if bi_ == 1:
                    w_out_src = ffn_w_out[li, s_].rearrange("(j p) d -> p j d", p=128)
                    for ob in range(4):
                        o0, o1 = ob * 6, min(FJ, ob * 6 + 6)
                        kk.dma("pool", wout[:, o0:o1, :], w_out_src[:, o0:o1, :], w=[R_woutb[ob]])
            ht = [sbt(ph, "ht%d" % i, [128, DC, NT], F32) for i in range(3)]
            R_ht = [Res(), Res(), Res()]
            xl = [sbt(ph, "xl%d" % i, [128, DC, NT], BF16) for i in range(2)]
            R_xl = [Res(), Res()]
            g = sbt(ph, "g", [128, FJ, NT], BF16)
            R_g = [Res() for _ in range(FJ)]
            sa = [sbt(ph, "sa%d" % i, [128, NT], F32) for i in range(2)]
            R_sa = [Res(), Res()]
            o = norm_mod(ph, "f")

            def prep(ti, steps):
                t0, n, w_ = tl[ti]
                hb_ = ti % 3
                xb_ = ti % 2
                kk.dma("sp", ht[hb_][:, :, :n], h_in[:, :, t0:t0 + n], w=[R_ht[hb_]])
                emit_rstd(o, ht[hb_], R_ht[hb_], n, steps)
                emit_xl(o, ht[hb_], R_ht[hb_], n, li, j, w_, xl[xb_], R_xl[xb_], steps)

            prep(0, None)
            for ti, (t0, n, w_) in enumerate(tl):
                b = ti % 2
                hb = ti % 3
                steps = []
                if ti + 1 < len(tl):
                    prep(ti + 1, steps)
                for jj in range(FJ):
                    pb = jj % 2
                    for kc in range(DC):
                        kk.op("pe", lambda e, jj=jj, kc=kc, pb=pb: e.matmul(
                            ps_a[pb][:, :n], win[:, kc, jj * 128:(jj + 1) * 128], xl[b][:, kc, :n],
                            start=(kc == 0), stop=(kc == DC - 1)),
                            r=[R_wina[blk_of_j[jj]], R_xl[b]], w=[R_ps_a[pb]], inc=(kc == DC - 1))
                    for kc in range(DC):
                        kk.op("pe", lambda e, jj=jj, kc=kc, pb=pb: e.matmul(
                            ps_b[pb][:, :n], win[:, kc, FF + jj * 128:FF + (jj + 1) * 128], xl[b][:, kc, :n],
                            start=(kc == 0), stop=(kc == DC - 1)),
                            r=[R_winb[blk_of_j[jj]], R_xl[b]], w=[R_ps_b[pb]], inc=(kc == DC - 1))
                    kk.op("act", lambda e, pb=pb: e.activation(out=sa[pb][:, :n], in_=ps_a[pb][:, :n], func=AF.Silu),
                          r=[R_ps_a[pb]], w=[R_sa[pb]])
                    kk.op("dve", lambda e, pb=pb, jj=jj: e.tensor_tensor(out=g[:, jj, :n], in0=sa[pb][:, :n],
                                                                         in1=ps_b[pb][:, :n], op=ALU.mult),
                          r=[R_sa[pb], R_ps_b[pb]], w=[R_g[jj]])
                    if jj >= 6 and steps:
                        steps.pop(0)()
                while steps:
                    steps.pop(0)()
                for d in range(DC):
                    pb = d % 2
                    for jj in range(FJ):
                        kk.op("pe", lambda e, jj=jj, d=d, pb=pb: e.matmul(
                            ps_o[pb][:, :n], wout[:, jj, d * 128:(d + 1) * 128], g[:, jj, :n],
                            start=(jj == 0), stop=(jj == FJ - 1)),
                            r=[R_woutb[jj // 6], R_g[jj]], w=[R_ps_o[pb]], inc=(jj == FJ - 1))
                    kk.op("dve", lambda e, d=d, pb=pb, hb=hb: e.scalar_tensor_tensor(
                        out=ht[hb][:, d, :n], in0=ps_o[pb][:, :n], scalar=gat[:, li, j, d, w_:w_ + 1],
                        in1=ht[hb][:, d, :n], op0=ALU.mult, op1=ALU.add),
                        r=[R_ps_o[pb], R_mods, R_ht[hb]], w=[R_ht[hb]])
                kk.dma("sp", h_out[:, :, t0:t0 + n], ht[hb][:, :, :n], r=[R_ht[hb]])
            kk.barrier()

    def final_phase(h_in):
        with contextlib.ExitStack() as ph:
            ht = [sbt(ph, "fht%d" % i, [128, DC, NT], F32) for i in range(2)]
            R_ht = [Res(), Res()]
            ot = [sbt(ph, "fot%d" % i, [128, DC, NT], F32) for i in range(2)]
            R_ot = [Res(), Res()]
            o = norm_mod(ph, "fn")
            for ti, (t0, n, w_) in enumerate(tiles):
                if w_ == 1:
                    continue
                b = ti % 2
                kk.dma("sp", ht[b][:, :, :n], h_in[:, :, t0:t0 + n], w=[R_ht[b]])
                emit_rstd(o, ht[b], R_ht[b], n)
                for c in range(DC):
                    kk.op("dve", lambda e, c=c, b=b: e.scalar_tensor_tensor(
                        out=ot[b][:, c, :n], in0=ht[b][:, c, :n], scalar=fng[:, c:c + 1], in1=o["rstd"][:, :n],
                        op0=ALU.mult, op1=ALU.mult), r=[R_ht[b], o["Rrstd"], R_const], w=[R_ot[b]])
                kk.dma("sp", outT[:, :, t0:t0 + n], ot[b][:, :, :n], r=[R_ot[b]])
            kk.barrier()

    def proj_phase(li, kind, h_in, tl):
        with contextlib.ExitStack() as ph:
            if kind == "ret":
                ncol = 6144
                W = sbt(ph, "pw", [128, DC, ncol], BF16)
                R_Wl = [Res() for _ in range(DC)]
                for kc in range(DC):
                    kk.dma("pool", W[:, kc, :], ret_w_in[kc * 128:(kc + 1) * 128, :], w=[R_Wl[kc]])
                fm = [(0, 8, qT_d, "ret", 0), (1024, 8, kT_d, "ret", 0)]
                tm = [(2048, 2048, v_d, AF.Identity, "v"), (4096, 2048, g_d, AF.Silu, "g")]
                cs_d = ret_cs
            elif kind == "nat":
                ncol = 3072
                W = sbt(ph, "pw", [128, DC, ncol], BF16)
                R_Wl = [Res()]
                kk.dma("pool", W[:], nat_w_qkv.rearrange("(kc p) n -> p kc n", p=128), w=[R_Wl[0]])
                fm = [(0, 8, qT_d, None, 0), (1024, 8, kT_d, None, 0)]
                tm = [(2048, 1024, v_d, AF.Identity, "v")]
                cs_d = None
            else:
                ncol = 1536 + 1280
                W = sbt(ph, "pw", [128, DC, ncol], BF16)
                R_Wl = [Res(), Res()]
                kk.dma("pool", W[:, :, 0:1536], swa_w_qkv.rearrange("(kc p) n -> p kc n", p=128), w=[R_Wl[0]])
                kk.dma("pool", W[:, :, 1536:2816], swa_w_perm.rearrange("(kc p) n -> p kc n", p=128), w=[R_Wl[1]])
                fm = [(0, 8, qT_d, "swa", 1536), (1024, 2, kT_d, "swa", 1536 + 1024)]
                tm = [(1280, 256, v_d, AF.Identity, "v")]
                cs_d = swa_cs
            ht = [sbt(ph, "pht%d" % i, [128, DC, NT], F32) for i in range(2)]
            R_ht = [Res(), Res()]
            xl2 = [sbt(ph, "pxl%d" % i, [128, DC, NT], BF16) for i in range(2)]
            R_xl2 = [Res(), Res()]
            stg = [sbt(ph, "pst%d" % i, [128, DC, NT], BF16) for i in range(2)]
            R_stg = [Res(), Res()]
            R_stg2 = [Res(), Res()]
            stv = sbt(ph, "pstv", [128, 2048], BF16)
            R_stv = Res()
            stgg = sbt(ph, "pstg", [128, 2048], F32)
            R_stgg = Res()
            cs2 = [sbt(ph, "pcs%d" % i, [128, 2, NT], F32) for i in range(2)]
            R_cs2 = [Res(), Res()]
            t1 = sbt(ph, "pt1", [128, NT], F32)
            t2 = sbt(ph, "pt2", [128, NT], F32)
            R_t1, R_t2 = Res(), Res()
            t3 = sbt(ph, "pt3", [128, NT], F32)
            t4 = sbt(ph, "pt4", [128, NT], F32)
            R_t3, R_t4 = Res(), Res()
            sAB = sbt(ph, "psAB", [128, 2, NT], F32)
            R_sAB = Res()
            o = norm_mod(ph, "p")
            def prep(ti_, steps):
                t0_, n_, w__ = tl[ti_]
                b_ = ti_ % 2
                kk.dma("sp", ht[b_][:, :, :n_], h_in[:, :, t0_:t0_ + n_], w=[R_ht[b_]])
                if cs_d is not None:
                    kk.dma("sp", cs2[b_][:, :, :n_], cs_d[:, :, t0_:t0_ + n_], w=[R_cs2[b_]])
                emit_rstd(o, ht[b_], R_ht[b_], n_, steps)
                emit_xl(o, ht[b_], R_ht[b_], n_, li, 1, w__, xl2[b_], R_xl2[b_], steps)

            prep(0, None)
            for ti, (t0, n, w_) in enumerate(tl):
                b = ti % 2
                xl, R_xl = xl2[b], R_xl2[b]
                cs, R_cs = cs2[b], R_cs2[b]
                steps = []
                npj = [0]
                if ti + 1 < len(tl):
                    prep(ti + 1, steps)

                def proj(off, m):
                    (pa, Ra) = bankS()
                    mm_acc(pa[:, :n], Ra, [(W[:, kc, off + m * 128:off + (m + 1) * 128], xl[:, kc, :n], R_Wl + [R_xl])
                                            for kc in range(DC)])
                    npj[0] += 1
                    if steps and npj[0] > 4:
                        steps.pop(0)()
                    return pa, Ra

                def tt(out, a, b_, op, r, w):
                    kk.op("dve", lambda e: e.tensor_tensor(out=out, in0=a, in1=b_, op=op), r=r, w=w)

                def ttp(out, a, b_, op, r, w):
                    kk.op("pool", lambda e: e.tensor_tensor(out=out, in0=a, in1=b_, op=op), r=r, w=w)

                for fi, (off, nch, dst, mode, poff) in enumerate(fm):
                    sg, R_sg, R_sg2 = stg[fi], R_stg[fi], R_stg2[fi]
                    if mode == "ret":
                        for hh in range(nch // 2):
                            pa, Ra = proj(off, 2 * hh)
                            pb_, Rb = proj(off, 2 * hh + 1)
                            kk.op("act", lambda e, pa=pa: e.activation(out=sAB[:, 0, :n], in_=pa[:, :n], func=AF.Identity),
                                  r=[Ra], w=[R_sAB])
                            kk.op("act", lambda e, pb_=pb_: e.activation(out=sAB[:, 1, :n], in_=pb_[:, :n], func=AF.Identity),
                                  r=[Rb], w=[R_sAB])
                            tt(t1[:, :n], sAB[:, 0, :n], cs[:, 0, :n], ALU.mult, [R_sAB, R_cs], [R_t1])
                            tt(t2[:, :n], sAB[:, 1, :n], cs[:, 1, :n], ALU.mult, [R_sAB, R_cs], [R_t2])
                            tt(sg[:, 2 * hh, :n], t1[:, :n], t2[:, :n], ALU.subtract, [R_t1, R_t2], [R_sg])
                            ttp(t3[:, :n], sAB[:, 1, :n], cs[:, 0, :n], ALU.mult, [R_sAB, R_cs], [R_t3])
                            ttp(t4[:, :n], sAB[:, 0, :n], cs[:, 1, :n], ALU.mult, [R_sAB, R_cs], [R_t4])
                            ttp(sg[:, 2 * hh + 1, :n], t3[:, :n], t4[:, :n], ALU.add, [R_t3, R_t4], [R_sg2])
                    elif mode == "swa":
                        for m in range(nch):
                            pa, Ra = proj(off, m)
                            pb_, Rb = proj(poff, m)
                            tt(t1[:, :n], pa[:, :n], cs[:, 0, :n], ALU.mult, [Ra, R_cs], [R_t1])
                            tt(t2[:, :n], pb_[:, :n], cs[:, 1, :n], ALU.mult, [Rb, R_cs], [R_t2])
                            tt(sg[:, m, :n], t1[:, :n], t2[:, :n], ALU.add, [R_t1, R_t2], [R_sg])
                    else:
                        for m in range(nch):
                            pa, Ra = proj(off, m)
                            kk.op("act", lambda e, pa=pa, m=m, sg=sg: e.activation(out=sg[:, m, :n], in_=pa[:, :n], func=AF.Identity),
                                  r=[Ra], w=[R_sg])
                    kk.dma("sp", dst[:, 0:nch, t0:t0 + n], sg[:, 0:nch, :n], r=[R_sg, R_sg2])
                while steps:
                    steps.pop(0)()
                for sub in range(n // 128):
                    for (off, ncols, dstd, fn, nm) in tm:
                        st_t, R_st = (stv, R_stv) if nm == "v" else (stgg, R_stgg)
                        for blk in range((ncols + 511) // 512):
                            cw = min(512, ncols - blk * 512)
                            (pa, Ra) = bankS()
                            mm_acc(pa[:, :cw], Ra, [(xl[:, kc, sub * 128:(sub + 1) * 128],
                                                     W[:, kc, off + blk * 512:off + blk * 512 + cw], R_Wl + [R_xl])
                                                    for kc in range(DC)])
                            kk.op("act", lambda e, pa=pa, blk=blk, cw=cw, st_t=st_t, fn=fn: e.activation(
                                out=st_t[:, blk * 512:blk * 512 + cw], in_=pa[:, :cw], func=fn), r=[Ra], w=[R_st])
                        kk.dma("sp", dstd[t0 + sub * 128:t0 + (sub + 1) * 128, 0:ncols], st_t[:, 0:ncols], r=[R_st])
            kk.barrier()

    def attn_phase(kind, do_ctx):
        with contextlib.ExitStack() as ph:
            ntyp, nch = (5, 7) if kind == "nat" else (3, 5)
            bias = sbt(ph, "abias", [128, ntyp, nch, 128], F32)
            R_bias = Res()
            ones_b = sbt(ph, "aones", [128, 64], BF16)
            R_c = Res()
            kk.op("dve", lambda e: e.memset(ones_b[:], 1.0), w=[R_c])
            sk = sbt(ph, "ask", [128, 16], F32)
            if kind == "swa":
                kk.dma("sp", bias[:], swa_bias, w=[R_bias])
                kk.op("act", lambda e: e.activation(out=bias[:], in_=bias[:], func=AF.Exp), r=[R_bias], w=[R_bias])
                kk.dma("sp", sk[:], swa_sink.partition_broadcast(128), w=[R_c])
                kk.op("act", lambda e: e.activation(out=sk[:], in_=sk[:], func=AF.Exp), r=[R_c], w=[R_c])
            qh = [sbt(ph, "aqh%d" % i, [64, T_ALL], BF16) for i in range(2)]
            kh = [sbt(ph, "akh%d" % i, [64, T_ALL], BF16) for i in range(2)]
            vh = [sbt(ph, "avh%d" % i, [128, 34, 65], BF16) for i in range(2)]
            R_qh, R_kh, R_vh = [Res(), Res()], [Res(), Res()], [Res(), Res()]
            for i in range(2):
                kk.op("dve", lambda e, i=i: e.memset(vh[i][:, :, 64:65], 1.0), w=[R_vh[i]])
            otm = [sbt(ph, "aotm%d" % i, [128, 34, 128], BF16) for i in range(2)]
            R_otm = [Res(), Res()]
            ostg = sbt(ph, "aostg", [128, 34 * 128], BF16)
            R_ostg = Res()
            ident_b = sbt(ph, "aident", [128, 128], BF16)
            kk.op("dve", lambda e: e.tensor_copy(out=ident_b[:], in_=identf[:]), r=[R_const], w=[R_c])
            psT = ps_stat[:].bitcast(BF16)
            bias2 = [bias, sbt(ph, "abias2", [128, ntyp, nch, 128], F32)] if kind == "nat" else [bias, bias]
            R_bias2 = [R_bias, Res()] if kind == "nat" else [R_bias, R_bias]
            tmp = [sbt(ph, "atmp%d" % i, [128, nch * 128], F32) for i in range(2)]
            R_tmp = [Res(), Res()]
            pt = [sbt(ph, "apt%d" % i, [128, nch * 128], BF16) for i in range(2)]
            R_pt = [Res(), Res()]
            den = [sbt(ph, "aden%d" % i, [128, 1], F32) for i in range(2)]
            R_den = [Res(), Res()]
            Tq = T_ALL if do_ctx else T_LAT

            def load_head(h):
                hs = h % 2
                kvh = h if kind == "nat" else h // 4
                kk.dma("sp", qh[hs][:], qT_d[(h % 2) * 64:(h % 2) * 64 + 64, h // 2, :], w=[R_qh[hs]])
                kk.dma("sp", kh[hs][:], kT_d[(kvh % 2) * 64:(kvh % 2) * 64 + 64, kvh // 2, :], w=[R_kh[hs]])
                kk.dma("sp", vh[hs][:, :, 0:64], v_d[:, kvh * 64:(kvh + 1) * 64].rearrange("(n p) d -> p n d", p=128),
                       w=[R_vh[hs]])
                if kind == "nat":
                    kk.dma("sp", bias2[hs][:], nat_bias[h], w=[R_bias2[hs]])
                    kk.op("act", lambda e, hs=hs: e.activation(out=bias2[hs][:], in_=bias2[hs][:], func=AF.Exp),
                          r=[R_bias2[hs]], w=[R_bias2[hs]])

            units = []
            for h in range(16):
                for qi in range(32):
                    if kind == "nat":
                        typ = 0 if 2 <= qi <= 29 else {0: 1, 1: 2, 30: 3, 31: 4}[qi]
                        base = min(max(qi - 2, 0), 27)
                        chunks = [base + i for i in range(5)] + [32, 33]
                    else:
                        typ = 1 if qi == 0 else (2 if qi == 31 else 0)
                        chunks = [max(qi - 1, 0), qi, min(qi + 1, 31), 32, 33]
                    units.append([h, qi * 128, chunks, typ, qi == 0, False])
                if do_ctx:
                    for qc in (32, 33):
                        units.append([h, qc * 128, [32, 33], None, False, False])
                units[-1][5] = True
            state = {}

            def s1(ui):
                h, q0, chunks, typ, first, lasth = units[ui]
                hs = h % 2
                ncu = len(chunks)
                u2 = ui % 2
                banks = []
                for c0 in range(0, ncu, 4):
                    cn = min(4, ncu - c0)
                    (pS, RS) = bankS()
                    banks.append((pS, RS, c0, cn))
                    for ci in range(c0, c0 + cn):
                        kc_ = chunks[ci]
                        kk.op("pe", lambda e, pS=pS, ci=ci, c0=c0, kc_=kc_, q0=q0, hs=hs: e.matmul(
                            pS[:, (ci - c0) * 128:(ci - c0 + 1) * 128], kh[hs][:, kc_ * 128:(kc_ + 1) * 128],
                            qh[hs][:, q0:q0 + 128], start=True, stop=True),
                            r=[R_kh[hs], R_qh[hs]], w=[RS], inc=(ci == c0 + cn - 1))
                if typ is not None and kind == "swa":
                    for (pS, RS, c0, cn) in banks:
                        kk.op("act", lambda e, pS=pS, c0=c0, cn=cn, u2=u2: e.activation(
                            out=pt[u2][:, c0 * 128:(c0 + cn) * 128], in_=pS[:, :cn * 128], func=AF.Exp, scale=0.125),
                            r=[RS], w=[R_pt[u2]])
                    for ch in (0, 2):
                        kk.op("pool", lambda e, ch=ch, u2=u2, typ=typ: e.tensor_tensor(
                            out=pt[u2][:, ch * 128:(ch + 1) * 128], in0=pt[u2][:, ch * 128:(ch + 1) * 128],
                            in1=bias[:, typ, ch, :], op=ALU.mult), r=[R_bias], w=[R_pt[u2]])
                elif typ is not None:
                    for (pS, RS, c0, cn) in banks:
                        nloc = max(0, min(cn, 5 - c0))
                        if nloc > 0:
                            kk.op("act", lambda e, pS=pS, c0=c0, nloc=nloc, u2=u2: e.activation(
                                out=tmp[u2][:, c0 * 128:(c0 + nloc) * 128], in_=pS[:, :nloc * 128], func=AF.Exp, scale=0.125),
                                r=[RS], w=[R_tmp[u2]])
                        if cn > nloc:
                            kk.op("act", lambda e, pS=pS, c0=c0, cn=cn, nloc=nloc, u2=u2: e.activation(
                                out=pt[u2][:, (c0 + nloc) * 128:(c0 + cn) * 128], in_=pS[:, nloc * 128:cn * 128],
                                func=AF.Exp, scale=0.125), r=[RS], w=[R_pt[u2]])
                    kk.op("pool", lambda e, u2=u2, typ=typ, hs=hs: e.tensor_tensor(
                        out=pt[u2][:, 0:640].rearrange("p (c q) -> p c q", q=128),
                        in0=tmp[u2][:, 0:640].rearrange("p (c q) -> p c q", q=128),
                        in1=bias2[hs][:, typ, 0:5, :], op=ALU.mult), r=[R_tmp[u2], R_bias2[hs]], w=[R_pt[u2]])
                else:
                    for (pS, RS, c0, cn) in banks:
                        kk.op("act", lambda e, pS=pS, c0=c0, cn=cn, u2=u2: e.activation(
                            out=pt[u2][:, c0 * 128:(c0 + cn) * 128], in_=pS[:, :cn * 128], func=AF.Exp, scale=0.125),
                            r=[RS], w=[R_pt[u2]])

            ntile = 34 if do_ctx else 32

            def s2(ui):
                h, q0, chunks, typ, first, lasth = units[ui]
                hs = h % 2
                if first and h + 1 < 16:
                    load_head(h + 1)
                ncu = len(chunks)
                u2 = ui % 2
                pp = (h // 2) % 2
                par = h % 2
                qt = q0 // 128
                (pO, RO) = bankO()
                mm_acc(pO[:, 0:65], RO, [(pt[u2][:, ci * 128:(ci + 1) * 128], vh[hs][:, chunks[ci], 0:65], [R_vh[hs], R_pt[u2]])
                                         for ci in range(ncu)])
                if kind == "swa":
                    kk.op("dve", lambda e, pO=pO, u2=u2, h=h: e.tensor_scalar(
                        out=den[u2][:], in0=pO[:, 64:65], scalar1=sk[:, h:h + 1], scalar2=None, op0=ALU.add),
                        r=[RO, R_c], w=[R_den[u2]])
                    kk.op("dve", lambda e, u2=u2: e.reciprocal(out=den[u2][:], in_=den[u2][:]), r=[R_den[u2]], w=[R_den[u2]])
                else:
                    kk.op("dve", lambda e, pO=pO, u2=u2: e.reciprocal(out=den[u2][:], in_=pO[:, 64:65]),
                          r=[RO], w=[R_den[u2]])
                kk.op("dve", lambda e, pO=pO, u2=u2, pp=pp, par=par, qt=qt: e.tensor_scalar(
                    out=otm[pp][:, qt, par * 64:(par + 1) * 64], in0=pO[:, 0:64], scalar1=den[u2][:, 0:1], scalar2=None,
                    op0=ALU.mult), r=[RO, R_den[u2]], w=[R_otm[pp]])
                if lasth and par == 1:
                    for t in range(ntile):
                        reg = (t % 8) * 128
                        kk.op("pe", lambda e, t=t, reg=reg, pp=pp: e.transpose(psT[:, reg:reg + 128], otm[pp][:, t, :], ident_b[:]),
                              r=[R_otm[pp], R_c], w=[R_ps_stat], inc=(t % 4 == 3 or t == ntile - 1))
                        if t % 4 == 3 or t == ntile - 1:
                            t0_ = t - (t % 4)
                            half = ((t0_ % 8) // 4) * 512
                            nn = (t - t0_ + 1) * 128
                            kk.op("act", lambda e, t0_=t0_, half=half, nn=nn: e.activation(
                                out=ostg[:, t0_ * 128:t0_ * 128 + nn], in_=psT[:, half:half + nn], func=AF.Identity),
                                r=[R_ps_stat], w=[R_ostg])
                    kk.dma("sp", oT_d[:, h // 2, 0:Tq], ostg[:, 0:Tq], r=[R_ostg])

            LA = 1
            load_head(0)
            for k in range(len(units) + LA):
                if k < len(units):
                    s1(k)
                if k - LA >= 0:
                    s2(k - LA)
            kk.barrier()

    def oproj_phase(li, Wd, KC, h_in, h_out, tl):
        with contextlib.ExitStack() as ph:
            W = sbt(ph, "ow", [128, KC, D], BF16)
            R_W = Res()
            kk.dma("pool", W[:], Wd.rearrange("(c p) d -> p c d", p=128), w=[R_W])
            ht = [sbt(ph, "oht%d" % i, [128, DC, NT], F32) for i in range(3)]
            R_ht = [Res(), Res(), Res()]
            ot = [sbt(ph, "oot%d" % i, [128, KC, NT], BF16) for i in range(2)]
            R_ot = [Res(), Res()]
            def load(ti_):
                t0_, n_, _w = tl[ti_]
                b_ = ti_ % 2
                kk.dma("sp", ht[ti_ % 3][:, :, :n_], h_in[:, :, t0_:t0_ + n_], w=[R_ht[ti_ % 3]])
                kk.dma("sp", ot[b_][:, :, :n_], oT_d[:, 0:KC, t0_:t0_ + n_], w=[R_ot[b_]])

            load(0)
            for ti, (t0, n, w_) in enumerate(tl):
                b = ti % 2
                hb = ti % 3
                if ti + 1 < len(tl):
                    load(ti + 1)
                for d in range(DC):
                    (pO, RO) = bankO()
                    mm_acc(pO[:, :n], RO, [(W[:, c, d * 128:(d + 1) * 128], ot[b][:, c, :n], [R_W, R_ot[b]]) for c in range(KC)])
                    kk.op("dve", lambda e, d=d, pO=pO, hb=hb: e.scalar_tensor_tensor(
                        out=ht[hb][:, d, :n], in0=pO[:, :n], scalar=gat[:, li, 1, d, w_:w_ + 1],
                        in1=ht[hb][:, d, :n], op0=ALU.mult, op1=ALU.add),
                        r=[RO, R_mods, R_ht[hb]], w=[R_ht[hb]])
                kk.dma("sp", h_out[:, :, t0:t0 + n], ht[hb][:, :, :n], r=[R_ht[hb]])
            kk.barrier()

    def ret_phase(do_ctx):
        with contextlib.ExitStack() as ph:
            rt = sbt(ph, "rtabs", [128, 8, 128], F32)
            R_rt = Res()
            kk.dma("sp", rt[:], ret_tabs, w=[R_rt])
            lg = sbt(ph, "rlg", [128, 8], F32)
            kk.dma("sp", lg[:], ret_decay.partition_broadcast(128), w=[R_rt])
            kk.op("act", lambda e: e.activation(out=lg[:], in_=lg[:], func=AF.Sigmoid), r=[R_rt], w=[R_rt])
            kk.op("act", lambda e: e.activation(out=lg[:], in_=lg[:], func=AF.Ln), r=[R_rt], w=[R_rt])
            lgn = sbt(ph, "rlgn", [128, 8], F32)
            kk.op("dve", lambda e: e.tensor_scalar(out=lgn[:], in0=lg[:], scalar1=-1.0, scalar2=None, op0=ALU.mult),
                  r=[R_rt], w=[R_rt])
            ident = sbt(ph, "rident", [128, 128], BF16)
            kk.op("dve", lambda e: e.tensor_copy(out=ident[:], in_=rt[:, 7, :]), r=[R_rt], w=[R_rt])
            gng = sbt(ph, "rgng", [128, 2048], F32)
            kk.dma("sp", gng[:], ret_gn.partition_broadcast(128), w=[R_rt])
            tb = sbt(ph, "rtb", [128, 4, 5, 128], F32)
            tA = sbt(ph, "rtA", [128, 128], F32)
            tB = sbt(ph, "rtB", [128, 128], F32)
            for h in range(4):
                lf = lg[:, h:h + 1]
                lb = lg[:, 4 + h:5 + h]
                for (slot, tab, sc_) in ((0, 0, lf), (1, 1, lb), (2, 2, lf), (3, 3, lb)):
                    kk.op("act", lambda e, h=h, slot=slot, tab=tab, sc_=sc_: e.activation(
                        out=tb[:, h, slot, :], in_=rt[:, tab, :], func=AF.Exp, scale=sc_), r=[R_rt], w=[R_rt])
                kk.op("act", lambda e, lf=lf: e.activation(out=tA[:], in_=rt[:, 4, :], func=AF.Exp, scale=lf), r=[R_rt], w=[R_rt])
                kk.op("act", lambda e, lb=lb: e.activation(out=tB[:], in_=rt[:, 5, :], func=AF.Exp, scale=lb), r=[R_rt], w=[R_rt])
                kk.op("dve", lambda e: e.tensor_tensor(out=tA[:], in0=tA[:], in1=tB[:], op=ALU.subtract), r=[R_rt], w=[R_rt])
                kk.op("dve", lambda e: e.tensor_tensor(out=tA[:], in0=tA[:], in1=rt[:, 6, :], op=ALU.mult), r=[R_rt], w=[R_rt])
                kk.op("dve", lambda e, h=h: e.tensor_tensor(out=tb[:, h, 4, :], in0=tA[:], in1=tB[:], op=ALU.add), r=[R_rt], w=[R_rt])
                kk.op("dve", lambda e, h=h: e.tensor_scalar(out=tb[:, h, 2:5, :], in0=tb[:, h, 2:5, :], scalar1=1.0 / 16.0,
                                                            scalar2=None, op0=ALU.mult), r=[R_rt], w=[R_rt])
                kk.op("act", lambda e, h=h: e.activation(out=tA[:], in_=rt[:, 0, :], func=AF.Exp, scale=lgn[:, h:h + 1]),
                      r=[R_rt], w=[R_rt])
                kk.op("dve", lambda e, h=h: e.tensor_tensor(out=tb[:, h, 4, :], in0=tb[:, h, 4, :], in1=tA[:], op=ALU.mult),
                      r=[R_rt], w=[R_rt])
            cc = sbt(ph, "rcc", [128, 8], F32)
            kk.op("act", lambda e: e.activation(out=cc[:], in_=lg[:], func=AF.Exp, scale=128.0), r=[R_rt], w=[R_rt])
            qh = sbt(ph, "rqh", [128, 2, T_ALL], BF16)
            qf = sbt(ph, "rqf", [128, 2, T_ALL], BF16)
            qb = sbt(ph, "rqb", [128, 2, T_ALL], BF16)
            kh = sbt(ph, "rkh", [128, 2, T_ALL], BF16)
            vh = sbt(ph, "rvh", [128, 34, 512], BF16)
            R_qh, R_qf, R_qb, R_kh, R_vh = Res(), Res(), Res(), Res(), Res()
            Sf = sbt(ph, "rSf", [128, 2, 512], F32)
            Sbp = [sbt(ph, "rSb%d" % i, [128, 2, 512], F32) for i in range(2)]
            R_Sf, R_Sbp = Res(), [Res(), Res()]
            Sfb = [sbt(ph, "rSfb%d" % i, [128, 2, 512], BF16) for i in range(2)]
            R_Sfb = [Res(), Res()]
            sstg = [sbt(ph, "rsstg%d" % i, [128, 2, 512], BF16) for i in range(2)]
            R_sstg = [Res(), Res()]
            sbl = [sbt(ph, "rsbl%d" % i, [128, 2, 512], BF16) for i in range(3)]
            R_sbl = [Res(), Res(), Res()]
            kt = [sbt(ph, "rkt%d" % i, [128, 256], BF16) for i in range(2)]
            R_kt = [Res(), Res()]
            pm = [sbt(ph, "rpm%d" % i, [128, 128], BF16) for i in range(2)]
            R_pm = [Res(), Res()]
            ocn = [sbt(ph, "rocn%d" % i, [128, 512], F32) for i in range(4)]
            R_ocn = [Res() for _ in range(4)]
            junk = sbt(ph, "rjunk", [128, 512], F32)
            R_junk = Res()
            gt = [sbt(ph, "rgt%d" % i, [128, 512], F32) for i in range(4)]
            R_gt = [Res() for _ in range(4)]
            gated = [sbt(ph, "rgated%d" % i, [128, 512], BF16) for i in range(4)]
            R_gated = [Res() for _ in range(4)]
            ost = [sbt(ph, "rost%d" % i, [128, 4, 128], BF16) for i in range(4)]
            R_ost = [Res() for _ in range(4)]
            sm = [sbt(ph, "rsm%d" % i, [128, 4], F32) for i in range(4)]
            R_sm = [Res() for _ in range(4)]
            psT = ps_stat[:].bitcast(BF16)
            psXb = ps_x[:].bitcast(BF16)
            R_sbd = [Res() for _ in range(34)]
            Ubank = [[(ps_a[0], R_ps_a[0]), (ps_a[1], R_ps_a[1])], [(ps_b[0], R_ps_b[0]), (ps_b[1], R_ps_b[1])]]
            Obank = [(ps_o[0], R_ps_o[0]), (ps_o[1], R_ps_o[1])]

            def prescale_q(h):
                for (dst, R_dst, slot, eng) in ((qf, R_qf, 0, "dve"), (qb, R_qb, 1, "pool")):
                    kk.op(eng, lambda e, dst=dst, slot=slot: e.tensor_tensor(
                        out=dst[:].rearrange("p c (n i) -> p c n i", i=128),
                        in0=qh[:].rearrange("p c (n i) -> p c n i", i=128),
                        in1=tb[:, h, slot, :].unsqueeze(1).unsqueeze(1).to_broadcast([128, 2, 34, 128]), op=ALU.mult),
                        r=[R_qh, R_rt], w=[R_dst])

            def u_prep(h, m, direction, par):
                reg = 512 + par * 256
                for dc in range(2):
                    kk.op("pe", lambda e, dc=dc: e.transpose(psXb[:, reg + dc * 128:reg + (dc + 1) * 128],
                                                             kh[:, dc, m * 128:(m + 1) * 128], ident[:]),
                          r=[R_kh, R_rt], w=[R_ps_x], inc=(dc == 1))
                slot = 2 if direction == "f" else 3
                kk.op("act", lambda e: e.activation(out=kt[par][:], in_=psXb[:, reg:reg + 256], func=AF.Identity,
                                                    scale=tb[:, h, slot, 0:1]), r=[R_ps_x, R_rt], w=[R_kt[par]])
                for dc in range(2):
                    (pU, RU) = Ubank[par][dc]
                    kk.op("pe", lambda e, dc=dc, pU=pU: e.matmul(pU[:, :512], kt[par][:, dc * 128:(dc + 1) * 128], vh[:, m, :],
                                                                  start=True, stop=True), r=[R_kt[par], R_vh], w=[RU])

            def s_update(h, S, R_S, direction, par, S2=None, R_S2=None):
                cidx = h if direction == "f" else 4 + h
                if S2 is None:
                    S2, R_S2 = S, R_S
                for dc in range(2):
                    (pU, RU) = Ubank[par][dc]
                    kk.op("dve", lambda e, dc=dc, pU=pU: e.scalar_tensor_tensor(
                        out=S2[:, dc, :], in0=S[:, dc, :], scalar=cc[:, cidx:cidx + 1], in1=pU[:, :512],
                        op0=ALU.mult, op1=ALU.add), r=[RU, R_S, R_rt], w=[R_S2])

            def epiA(h, q0, tx, pO, RO):
                t2 = tx % 4
                kk.op("dve", lambda e: e.reduce_sum(out=sm[t2][:, 0:1], in_=pO[:, :512], axis=AX.X), r=[RO], w=[R_sm[t2]])
                kk.op("dve", lambda e: e.tensor_scalar(out=sm[t2][:, 0:1], in0=sm[t2][:, 0:1], scalar1=-1.0 / 512.0, scalar2=None,
                                                       op0=ALU.mult), r=[R_sm[t2]], w=[R_sm[t2]])
                kk.op("act", lambda e: e.activation(out=ocn[t2][:], in_=pO[:, :512], func=AF.Identity, bias=sm[t2][:, 0:1]),
                      r=[RO, R_sm[t2]], w=[R_ocn[t2]])
                kk.op("act", lambda e: e.activation(out=junk[:], in_=ocn[t2][:], func=AF.Square, accum_out=sm[t2][:, 1:2]),
                      r=[R_ocn[t2]], w=[R_sm[t2], R_junk])
                kk.op("act", lambda e: e.activation(out=sm[t2][:, 2:3], in_=sm[t2][:, 1:2], func=AF.Sqrt, scale=1.0 / 512.0, bias=EPS),
                      r=[R_sm[t2]], w=[R_sm[t2]])

            def epiB(h, q0, tx):
                t2 = tx % 4
                kk.op("dve", lambda e: e.reciprocal(out=sm[t2][:, 2:3], in_=sm[t2][:, 2:3]), r=[R_sm[t2]], w=[R_sm[t2]])
                kk.op("dve", lambda e: e.scalar_tensor_tensor(
                    out=ocn[t2][:], in0=ocn[t2][:], scalar=sm[t2][:, 2:3], in1=gng[:, h * 512:(h + 1) * 512],
                    op0=ALU.mult, op1=ALU.mult), r=[R_ocn[t2], R_sm[t2], R_rt], w=[R_ocn[t2]])
                kk.op("dve", lambda e: e.tensor_tensor(out=gated[t2][:], in0=ocn[t2][:], in1=gt[t2][:], op=ALU.mult),
                      r=[R_ocn[t2], R_gt[t2]], w=[R_gated[t2]])

            def epiC(h, q0, tx):
                t2 = tx % 4
                for blk in range(4):
                    kk.op("pe", lambda e, blk=blk: e.transpose(psT[:, blk * 128:(blk + 1) * 128],
                                                               gated[t2][:, blk * 128:(blk + 1) * 128], ident[:]),
                          r=[R_gated[t2], R_rt], w=[R_ps_stat], inc=(blk == 3))
                kk.op("act", lambda e: e.activation(out=ost[t2][:].rearrange("p a b -> p (a b)"), in_=psT[:, 0:512],
                                                    func=AF.Identity), r=[R_ps_stat], w=[R_ost[t2]])
                kk.dma("sp", oT_d[:, 4 * h:4 * h + 4, q0:q0 + 128], ost[t2][:], r=[R_ost[t2]])

            tx = 0
            pending = []

            def tick():
                for it in pending:
                    it[0] -= 1
                while pending and pending[0][0] <= 0:
                    pending.pop(0)[1]()

            for h in range(4):
                kk.dma("sp", qh[:], qT_d[:, 2 * h:2 * h + 2, :], w=[R_qh])
                kk.dma("sp", kh[:], kT_d[:, 2 * h:2 * h + 2, :], w=[R_kh])
                kk.dma("sp", vh[:], v_d[:, h * 512:(h + 1) * 512].rearrange("(n p) d -> p n d", p=128), w=[R_vh])
                prescale_q(h)
                kk.op("dve", lambda e: e.memset(Sbp[0][:], 0.0), w=[R_Sbp[0]])
                orderB = [33, 32] + list(range(31, -1, -1))
                for bi, m in enumerate(orderB):
                    par = bi % 2
                    cur = bi % 2
                    u_prep(h, m, "b", par)
                    need = (m < 32) or (do_ctx and m == 32)
                    if need:
                        sg = bi % 2
                        kk.op("act", lambda e, sg=sg, cur=cur: e.activation(out=sstg[sg][:], in_=Sbp[cur][:], func=AF.Identity),
                              r=[R_Sbp[cur]], w=[R_sstg[sg]])
                        kk.dma("sp", sb_d[m], sstg[sg][:], r=[R_sstg[sg]], w=[R_sbd[m]])
                    if bi < len(orderB) - 1:
                        s_update(h, Sbp[cur], R_Sbp[cur], "b", par, Sbp[1 - cur], R_Sbp[1 - cur])
                kk.op("dve", lambda e: e.memset(Sf[:], 0.0), w=[R_Sf])
                orderF = [32, 33] + list(range(32))
                u_prep(h, orderF[0], "f", 0)
                for fi, n in enumerate(orderF):
                    par = fi % 2
                    if fi + 1 < len(orderF):
                        u_prep(h, orderF[fi + 1], "f", 1 - par)
                    out_needed = (n < 32) or do_ctx
                    if out_needed:
                        q0 = n * 128
                        t4 = tx % 4
                        kk.dma("sp", gt[t4][:], g_d[q0:q0 + 128, h * 512:(h + 1) * 512], w=[R_gt[t4]])
                        has_b = not (n == 33)
                        has_f = fi > 0
                        if has_b:
                            sl = tx % 3
                            kk.dma("sp", sbl[sl][:], sb_d[n], r=[R_sbd[n]], w=[R_sbl[sl]])
                        (pO, RO) = Obank[tx % 2]
                        mm_acc(ps_x[:, :128], R_ps_x, [(kh[:, dc, q0:q0 + 128], qf[:, dc, q0:q0 + 128], [R_kh, R_qf]) for dc in range(2)])
                        p2 = tx % 2
                        kk.op("dve", lambda e, p2=p2: e.tensor_tensor(out=pm[p2][:], in0=ps_x[:, :128], in1=tb[:, h, 4, :], op=ALU.mult),
                              r=[R_ps_x, R_rt], w=[R_pm[p2]])
                        terms = [(pm[p2][:], vh[:, n, :], [R_pm[p2], R_vh])]
                        if has_b:
                            terms += [(qb[:, dc, q0:q0 + 128], sbl[sl][:, dc, :], [R_qb, R_sbl[sl]]) for dc in range(2)]
                        if has_f:
                            fb = (fi - 1) % 2
                            terms += [(qf[:, dc, q0:q0 + 128], Sfb[fb][:, dc, :], [R_qf, R_Sfb[fb]]) for dc in range(2)]
                        for ti_, (l_, r_, rd) in enumerate(terms):
                            kk.op("pe", lambda e, l_=l_, r_=r_, ti_=ti_, nt_=len(terms), pO=pO: e.matmul(
                                pO[:, :512], l_, r_, start=(ti_ == 0), stop=(ti_ == nt_ - 1)), r=rd, w=[RO],
                                inc=(ti_ == len(terms) - 1))
                    if fi < len(orderF) - 1:
                        s_update(h, Sf, R_Sf, "f", par)
                        fb = fi % 2
                        kk.op("act", lambda e, fb=fb: e.activation(out=Sfb[fb][:], in_=Sf[:], func=AF.Identity),
                              r=[R_Sf], w=[R_Sfb[fb]])
                    while pending:
                        pending.pop(0)[1]()
                    if out_needed:
                        epiA(h, q0, tx, pO, RO)
                        pending.append([1, (lambda h=h, q0=q0, tx=tx: epiB(h, q0, tx))])
                        pending.append([2, (lambda h=h, q0=q0, tx=tx: epiC(h, q0, tx))])
                        tx += 1
            while pending:
                pending.pop(0)[1]()
            kk.barrier()


    def pool_phase(li, h_in, h_out, tl, do_ctx):
        with contextlib.ExitStack() as ph:
            ht = [sbt(ph, "qht%d" % i, [128, DC, NT], F32) for i in range(2)]
            R_ht = [Res(), Res()]
            xf = [sbt(ph, "qxf%d" % i, [128, DC, NT], F32) for i in range(2)]
            R_xf = [Res(), Res()]
            o = norm_mod(ph, "q")
            for ti, (t0, n, w_) in enumerate(tl):
                b = ti % 2
                kk.dma("sp", ht[b][:, :, :n], h_in[:, :, t0:t0 + n], w=[R_ht[b]])
                emit_rstd(o, ht[b], R_ht[b], n)
                emit_xl(o, ht[b], R_ht[b], n, li, 1, w_, xf[b], R_xf[b])
                kk.dma("sp", xl_d[:, :, t0:t0 + n], xf[b][:, :, :n], r=[R_xf[b]])
            kk.barrier()
        with contextlib.ExitStack() as ph:
            PAD = 8
            TB = T_LAT + 2 * PAD
            X = sbt(ph, "qX", [128, 2, TB], F32)
            Y = sbt(ph, "qY", [128, 2, TB], F32)
            Zb = sbt(ph, "qZ", [128, 2, TB], F32)
            R_X, R_Y, R_Z = [Res(), Res()], [Res(), Res()], [Res(), Res()]
            icn = sbt(ph, "qicn", [128, T_LAT], F32)
            R_icn = Res()
            pbf = sbt(ph, "qpb", [128, 2, T_LAT], BF16)
            R_pb = [Res(), Res()]
            wg = sbt(ph, "qwg", [128, 2, 256], BF16)
            R_wg = Res()
            psc = sbt(ph, "qpsc", [128, DC], F32)
            gp = sbt(ph, "qgp", [128, DC, 2], F32)
            R_gp = Res()
            kk.dma("sp", psc[:], pool_sc, w=[R_gp])
            for w_ in range(2):
                kk.op("dve", lambda e, w_=w_: e.tensor_tensor(out=gp[:, :, w_], in0=gat[:, li, 1, :, w_], in1=psc[:], op=ALU.mult),
                      r=[R_gp, R_mods], w=[R_gp])
            hc = [sbt(ph, "qhc%d" % i, [128, 512], F32) for i in range(2)]
            R_hc = [Res(), Res()]
            seqs = [(0, T_LAT, 0)] + ([(T_LAT, T_CTX, 1)] if do_ctx else [])
            hi_ = 0
            for g_ in range(4):
                kk.dma("pool", wg[:], pool_w[g_].rearrange("(kc p) n -> p kc n", p=128), w=[R_wg])
                for (s0, T, w_) in seqs:
                    L = T + 2 * PAD
                    kk.op("pool", lambda e, L=L: e.memset(X[:, :, 0:L], 0.0), w=R_X)
                    kk.dma("sp", X[:, :, PAD:PAD + T], xl_d[:, 2 * g_:2 * g_ + 2, s0:s0 + T], w=R_X)
                    kk.dma("sp", icn[:, :T], pool_icnt[g_:g_ + 1, s0:s0 + T].partition_broadcast(128), w=[R_icn])
                    EN = ("dve", "pool")
                    kk.op("pool", lambda e, L=L: e.memset(Y[:, :, 0:L], 0.0), w=R_Y)
                    for c in range(2):
                        kk.op(EN[c], lambda e, L=L, c=c: e.tensor_tensor(
                            out=Y[:, c, 1:L], in0=X[:, c, 1:L], in1=X[:, c, 0:L - 1], op=ALU.add), r=[R_X[c]], w=[R_Y[c]])
                    lv_src, R_lsrc = Y, R_Y
                    sh = 1
                    for lev in range(g_):
                        a_, Ra_, b_, Rb_ = (Y, R_Y, Zb, R_Z) if lev % 2 == 0 else (Zb, R_Z, Y, R_Y)
                        kk.op("pool", lambda e, L=L, b_=b_: e.memset(b_[:, :, 0:L], 0.0), w=Rb_)
                        for c in range(2):
                            kk.op(EN[c], lambda e, L=L, a_=a_, b_=b_, sh=sh, c=c: e.tensor_tensor(
                                out=b_[:, c, sh:L - sh], in0=a_[:, c, 0:L - 2 * sh], in1=a_[:, c, 2 * sh:L], op=ALU.add),
                                r=[Ra_[c]], w=[Rb_[c]])
                        lv_src, R_lsrc = b_, Rb_
                        sh *= 2
                    for c in range(2):
                        kk.op(EN[c], lambda e, c=c, T=T, lv_src=lv_src: e.tensor_tensor(
                            out=lv_src[:, c, PAD:PAD + T], in0=lv_src[:, c, PAD:PAD + T], in1=icn[:, :T], op=ALU.mult),
                            r=[R_lsrc[c], R_icn], w=[R_lsrc[c]])
                    for c in range(2):
                        kk.op(EN[c], lambda e, c=c, T=T, lv_src=lv_src: e.tensor_tensor(
                            out=pbf[:, c, :T], in0=lv_src[:, c, PAD:PAD + T], in1=X[:, c, PAD:PAD + T], op=ALU.subtract),
                            r=[R_lsrc[c], R_X[c]], w=[R_pb[c]])
                    for m in range(2):
                        c = 2 * g_ + m
                        for tt0 in range(0, T, 512):
                            nn = min(512, T - tt0)
                            hb = hi_ % 2
                            hi_ += 1
                            kk.dma("sp", hc[hb][:, :nn], h_in[:, c, s0 + tt0:s0 + tt0 + nn], w=[R_hc[hb]])
                            (pO, RO) = bankO()
                            mm_acc(pO[:, :nn], RO, [(wg[:, kc, m * 128:(m + 1) * 128], pbf[:, kc, tt0:tt0 + nn], [R_wg] + R_pb)
                                                    for kc in range(2)])
                            kk.op("dve", lambda e, pO=pO, hb=hb, nn=nn, c=c, w_=w_: e.scalar_tensor_tensor(
                                out=hc[hb][:, :nn], in0=pO[:, :nn], scalar=gp[:, c, w_:w_ + 1], in1=hc[hb][:, :nn],
                                op0=ALU.mult, op1=ALU.add), r=[RO, R_gp, R_hc[hb]], w=[R_hc[hb]])
                            kk.dma("sp", h_out[:, c, s0 + tt0:s0 + tt0 + nn], hc[hb][:, :nn], r=[R_hc[hb]])
            kk.barrier()

    kinds = cfg.get("kinds", ["ret", "nat", "pool", "swa"])
    cur = xT
    nxt = 0
    lat_tiles = [t for t in tiles if t[2] == 0]
    for li in range(n_layers):
        kind = kinds[li]
        last = (li == n_layers - 1)
        ctx_live = (not last) or kind != "pool"
        tl1 = tiles if ctx_live else lat_tiles
        tl2 = lat_tiles if last else tiles
        ffn_phase(li, 0, 0, cur, hbufs[nxt], tl1)
        cur = hbufs[nxt]
        nxt ^= 1
        if mixers:
            if kind == "pool":
                pool_phase(li, cur, hbufs[nxt], tl2, not last)
            else:
                proj_phase(li, kind, cur, tl1)
                if kind == "ret":
                    ret_phase(not last)
                    oproj_phase(li, ret_w_out, 16, cur, hbufs[nxt], tl2)
                else:
                    attn_phase(kind, not last)
                    oproj_phase(li, nat_w_o if kind == "nat" else swa_w_o, 8, cur, hbufs[nxt], tl2)
            cur = hbufs[nxt]
            nxt ^= 1
        ffn_phase(li, 1, 2, cur, hbufs[nxt], tl2)
        cur = hbufs[nxt]
        nxt ^= 1
    final_phase(cur)
    kk.barrier()
    kk.ninst_total = kk.ninst
    nc._kk = kk
    return nc


def _fm(a):
    t = a.shape[0]
    return np.ascontiguousarray(a.T.reshape(DC, 128, t).transpose(1, 0, 2))


def _vec_fm(v):
    lead = v.shape[:-1]
    x = v.reshape(*lead, DC, 128)
    x = np.moveaxis(x, -1, 0)
    return np.ascontiguousarray(x)


def _consts():
    c = {}
    p = np.arange(128, dtype=np.float64)[:, None]
    i = np.arange(128, dtype=np.float64)[None, :]
    tabs = np.zeros((128, 8, 128), np.float64)
    tabs[:, 0] = i + 0 * p
    tabs[:, 1] = 127 - i + 0 * p
    tabs[:, 2] = 128 * i + 128 - p
    tabs[:, 3] = 128 * i + p + 1
    tabs[:, 4] = np.maximum(i - p, 0)
    tabs[:, 5] = np.maximum(p - i, 0)
    tabs[:, 6] = (i >= p)
    tabs[:, 7] = (i == p)
    c["ret_tabs"] = tabs.astype(np.float32)
    t = np.arange(T_LAT, dtype=np.float32)[None, :]
    inv = (10000.0 ** (-(np.arange(0, 256, 2, dtype=np.float32)) / 256.0)).astype(np.float32)[:, None]
    ang = (t * inv).astype(np.float32)
    cs = np.zeros((128, 2, T_ALL), np.float32)
    cs[:, 0, :T_LAT] = np.cos(ang)
    cs[:, 1, :T_LAT] = np.sin(ang)
    cs[:, 0, T_LAT:] = 1.0
    c["ret_cs"] = cs
    d = np.arange(128) % 64
    tt = np.arange(T_LAT)
    pos = np.where((d < 32)[:, None], (tt // 64)[None, :], (tt % 64)[None, :]).astype(np.float32)
    inv16 = (10000.0 ** (-(np.arange(0, 32, 2, dtype=np.float32)) / 32.0)).astype(np.float32)
    invd = inv16[(d % 32) % 16][:, None]
    ang = (pos * invd).astype(np.float32)
    sign = np.where((d % 32) < 16, -1.0, 1.0).astype(np.float32)[:, None]
    cs = np.zeros((128, 2, T_ALL), np.float32)
    cs[:, 0, :T_LAT] = np.cos(ang)
    cs[:, 1, :T_LAT] = np.sin(ang) * sign
    cs[:, 0, T_LAT:] = 1.0
    c["swa_cs"] = cs
    NEG = -30000.0
    j = np.arange(128)[:, None]
    q = np.arange(128)[None, :]
    sb_ = np.zeros((128, 3, 5, 128), np.float32)
    for typ in range(3):
        sb_[:, typ, 0] = np.where(j >= q, 0.0, NEG)
        sb_[:, typ, 2] = np.where(j <= q, 0.0, NEG)
    sb_[:, 1, 0] = NEG
    sb_[:, 2, 2] = NEG
    c["swa_bias"] = sb_
    ic = np.zeros((4, T_ALL), np.float32)
    for g_, w in enumerate((2, 4, 8, 16)):
        for (s0, T) in ((0, T_LAT), (T_LAT, T_CTX)):
            tq = np.arange(T)
            lo = np.clip(tq - w // 2, 0, T)
            hi = np.clip(tq + w // 2, 0, T)
            ic[g_, s0:s0 + T] = 1.0 / (hi - lo).astype(np.float32)
    c["pool_icnt"] = ic
    return c


def _nat_bias(rpb):
    NEG = -30000.0
    out = np.zeros((16, 128, 5, 7, 128), np.float32)
    u = (np.arange(128) // 64)
    n = (np.arange(128) % 64)
    cfgs = [(2, 0), (0, 0), (1, 0), (30, 27), (31, 27)]
    for typ, (qi, base) in enumerate(cfgs):
        r = (2 * qi + u)[None, :]
        cq = n[None, :]
        r0 = np.clip(r - 4, 0, 56)
        c0 = np.clip(cq - 8, 0, 48)
        for ch in range(5):
            a = (2 * (base + ch) + u)[:, None]
            nk = n[:, None]
            valid = (a >= r0) & (a < r0 + 8) & (nk >= c0) & (nk < c0 + 16)
            ri = np.clip(a - r + 7, 0, 14)
            ci = np.clip(nk - cq + 15, 0, 30)
            vals = rpb[:, ri, ci]
            out[:, :, typ, ch, :] = np.where(valid[None], vals, NEG)
    return out


def make_in_maps(inputs, cores=range(N_CORES)):
    f = lambda a: np.ascontiguousarray(np.asarray(a, dtype=np.float32))
    x, c, ctx, c_ctx = f(inputs["x"]), f(inputs["c"]), f(inputs["ctx"]), f(inputs["c_ctx"])
    b_mod = f(inputs["b_mod"])
    shared = {
        "w_mod": f(inputs["w_mod"]),
        "bmodT": np.ascontiguousarray(b_mod.reshape(DEPTH, 72, 128).transpose(2, 0, 1)),
        "normgT": _vec_fm(f(inputs["norm_g"])),
        "fnormgT": _vec_fm(f(inputs["final_norm_g"])),
        "ffn_w_in": f(inputs["ffn_w_in"]),
        "ffn_w_out": f(inputs["ffn_w_out"]),
        "ret_w_in": f(inputs["ret_w_in"][0]),
        "ret_w_out": f(inputs["ret_w_out"][0]),
        "ret_gn": f(inputs["ret_gn_g"][0:1]),
        "ret_decay": np.ascontiguousarray(np.concatenate([f(inputs["ret_decay_f"][0]), f(inputs["ret_decay_b"][0])])[None, :]),
        "nat_w_qkv": f(inputs["nat_w_qkv"][0]),
        "nat_w_o": f(inputs["nat_w_o"][0]),
        "nat_bias": _nat_bias(f(inputs["nat_rpb"][0])),
        "pool_w": f(inputs["pool_w"][0]),
        "pool_sc": _vec_fm(f(inputs["pool_scale"][0])),
        "swa_w_qkv": f(inputs["swa_w_qkv"][0]),
        "swa_w_o": f(inputs["swa_w_o"][0]),
        "swa_sink": f(inputs["swa_sink"][0:1]),
    }
    wq = shared["swa_w_qkv"]
    dd = np.arange(64)
    partner = np.where((dd % 32) < 16, dd + 16, dd - 16)
    colq = (np.arange(16)[:, None] * 64 + partner[None, :]).reshape(-1)
    colk = 1024 + (np.arange(4)[:, None] * 64 + partner[None, :]).reshape(-1)
    shared["swa_w_perm"] = np.ascontiguousarray(wq[:, np.concatenate([colq, colk])])
    shared.update(_consts())
    maps = []
    for b in cores:
        m = dict(shared)
        m["xT"] = _fm(np.concatenate([x[b], ctx[b]], axis=0))
        m["cT"] = np.ascontiguousarray(np.stack([c[b], c_ctx], axis=0).reshape(2, DC, 128).transpose(2, 1, 0))
        maps.append(m)
    return maps


_NC_CACHE = {}


def kernel(**inputs):
    if "nc" not in _NC_CACHE:
        _NC_CACHE["nc"] = build()
    nc = _NC_CACHE["nc"]
    in_maps = make_in_maps(inputs)
    res = run_bass_kernel_spmd(nc, in_maps, core_ids=list(range(N_CORES)))
    outs = []
    for b in range(N_CORES):
        o = res.results[b]["outT"]
        outs.append(o.transpose(1, 0, 2).reshape(D, T_LAT).T)
    return np.ascontiguousarray(np.stack(outs, axis=0).astype(np.float32))
```

```python
import contextlib
import numpy as np
import concourse.bass as bass
import concourse.mybir as mybir
from concourse.bass_utils import run_bass_kernel_spmd

F32 = mybir.dt.float32
BF16 = mybir.dt.bfloat16
AF = mybir.ActivationFunctionType
ALU = mybir.AluOpType
AX = mybir.AxisListType

D = 1024
DC = 8
T_LAT = 4096
T_CTX = 256
T_ALL = T_LAT + T_CTX
DEPTH = 4
FF = 2816
FJ = 22
EPS = 1e-6
NT = 256
N_CORES = 4


class Res:
    __slots__ = ("name", "w", "r")

    def __init__(self, name=""):
        self.name = name
        self.w = None
        self.r = {}


class _Eng:
    def __init__(self, kk, name, handle):
        self.name = name
        self.h = handle
        self.sem = kk.new_sem("e_" + name)
        self.count = 0
        self.waited = {}
        self.pend_r = []
        self.pend_w = []


class K:
    def __init__(self, nc, n_dma_sems=16):
        self.nc = nc
        self.st = contextlib.ExitStack()
        self.sems = {}
        self.nsem = 0
        self.eng = {}
        for name, h in (("pe", nc.tensor), ("act", nc.scalar), ("dve", nc.vector),
                        ("pool", nc.gpsimd), ("sp", nc.sync)):
            self.eng[name] = _Eng(self, name, h)
        self.dma_pool = {}
        for q in ("sp", "pool"):
            self.dma_pool[q] = [[self.new_sem("d_%s%d" % (q, i)), 0] for i in range(n_dma_sems)]
        self.dma_rr = {"sp": 0, "pool": 0}
        self.ninst = 0

    def new_sem(self, name):
        s = self.st.enter_context(self.nc.semaphore(name))
        sid = self.nsem
        self.nsem += 1
        self.sems[sid] = s
        return sid

    def _wait(self, e, ev):
        sid, val = ev
        if e.waited.get(sid, 0) >= val:
            return
        e.waited[sid] = val
        e.h.wait_ge(self.sems[sid], val)
        self.ninst += 1

    def _deps(self, e, r, w, nowaw=False):
        evs = {}

        def add(ev):
            if ev is not None and evs.get(ev[0], 0) < ev[1]:
                evs[ev[0]] = ev[1]
        for x in r:
            add(x.w)
        for x in w:
            if not nowaw:
                add(x.w)
            for sid, val in x.r.items():
                add((sid, val))
        for sid, val in evs.items():
            if e.name == "pe" and sid == e.sem:
                continue
            self._wait(e, (sid, val))

    def _commit(self, ev, r, w):
        for x in r:
            if x.r.get(ev[0], 0) < ev[1]:
                x.r[ev[0]] = ev[1]
        for x in w:
            x.w = ev
            x.r = {}

    def op(self, eng, fn, r=(), w=(), inc=True):
        e = self.eng[eng]
        self._deps(e, r, w)
        inst = fn(e.h)
        self.ninst += 1
        if not inc:
            e.pend_r.extend(r)
            e.pend_w.extend(w)
            return None
        if e.count >= 30000:
            e.sem = self.new_sem("e_%s_%d" % (eng, self.nsem))
            e.count = 0
        e.count += 1
        inst.then_inc(self.sems[e.sem], 1)
        ev = (e.sem, e.count)
        self._commit(ev, list(r) + e.pend_r, list(w) + e.pend_w)
        e.pend_r = []
        e.pend_w = []
        return ev

    def dma(self, q, out, in_, r=(), w=(), nowaw=False):
        e = self.eng[q]
        self._deps(e, r, w, nowaw=nowaw)
        pool = self.dma_pool[q]
        i = self.dma_rr[q]
        self.dma_rr[q] = (i + 1) % len(pool)
        slot = pool[i]
        if slot[1] > 0:
            self._wait(e, (slot[0], slot[1]))
        slot[1] += 16
        e.h.dma_start(out=out, in_=in_).then_inc(self.sems[slot[0]], 16)
        self.ninst += 1
        ev = (slot[0], slot[1])
        self._commit(ev, r, w)
        return ev

    def barrier(self, engs=("pe", "act", "dve", "pool", "sp")):
        for x in engs:
            e = self.eng[x]
            for y in self.eng.values():
                if y is not e and y.count > 0:
                    self._wait(e, (y.sem, y.count))
            for pool in self.dma_pool.values():
                for sid, val in pool:
                    if val > 0:
                        self._wait(e, (sid, val))


def build(cfg=None):
    cfg = cfg or {}
    n_layers = cfg.get("n_layers", DEPTH)
    mixers = cfg.get("mixers", True)
    dbg = cfg.get("dbg", False)

    nc = bass.Bass("TRN2", target_bir_lowering=False)
    kk = K(nc)
    st = kk.st

    def dram_in(name, shape, dt=F32):
        return nc.dram_tensor(name, list(shape), dt, kind="ExternalInput").ap()

    xT = dram_in("xT", [128, DC, T_ALL])
    cT = dram_in("cT", [128, DC, 2])
    w_mod = dram_in("w_mod", [DEPTH, D, 9 * D])
    bmodT = dram_in("bmodT", [128, DEPTH, 72])
    normgT = dram_in("normgT", [128, DEPTH, 3, DC])
    fnormgT = dram_in("fnormgT", [128, DC])
    ffn_w_in = dram_in("ffn_w_in", [DEPTH, 2, D, 2 * FF])
    ffn_w_out = dram_in("ffn_w_out", [DEPTH, 2, FF, D])
    outT = nc.dram_tensor("outT", [128, DC, T_LAT], F32, kind="ExternalOutput").ap()
    hA = nc.dram_tensor("hA", [128, DC, T_ALL], F32).ap()
    hB = nc.dram_tensor("hB", [128, DC, T_ALL], F32).ap()
    hbufs = [hA, hB]
    ret_w_in = dram_in("ret_w_in", [D, 6144])
    ret_w_out = dram_in("ret_w_out", [2048, D])
    ret_gn = dram_in("ret_gn", [1, 2048])
    ret_decay = dram_in("ret_decay", [1, 8])
    ret_tabs = dram_in("ret_tabs", [128, 8, 128])
    ret_cs = dram_in("ret_cs", [128, 2, T_ALL])
    nat_w_qkv = dram_in("nat_w_qkv", [D, 3072])
    nat_w_o = dram_in("nat_w_o", [D, D])
    nat_bias = dram_in("nat_bias", [16, 128, 5, 7, 128])
    pool_w = dram_in("pool_w", [4, 256, 256])
    pool_sc = dram_in("pool_sc", [128, DC])
    pool_icnt = dram_in("pool_icnt", [4, T_ALL])
    swa_w_qkv = dram_in("swa_w_qkv", [D, 1536])
    swa_w_perm = dram_in("swa_w_perm", [D, 1280])
    swa_w_o = dram_in("swa_w_o", [D, D])
    swa_sink = dram_in("swa_sink", [1, 16])
    swa_cs = dram_in("swa_cs", [128, 2, T_ALL])
    swa_bias = dram_in("swa_bias", [128, 3, 5, 128])
    qT_d = nc.dram_tensor("qT_d", [128, DC, T_ALL], BF16).ap()
    kT_d = nc.dram_tensor("kT_d", [128, DC, T_ALL], BF16).ap()
    v_d = nc.dram_tensor("v_d", [T_ALL, 2048], BF16).ap()
    g_d = nc.dram_tensor("g_d", [T_ALL, 2048], F32).ap()
    oT_d = nc.dram_tensor("oT_d", [128, 16, T_ALL], BF16).ap()
    xl_d = nc.dram_tensor("xl_d", [128, DC, T_ALL], F32).ap()
    sb_d = nc.dram_tensor("sb_d", [34, 128, 2, 512], BF16).ap()

    def sb(name, shape, dt):
        return st.enter_context(nc.sbuf_tensor(name, list(shape), dt))

    def ps(name, shape, dt=F32):
        return st.enter_context(nc.psum_tensor(name, list(shape), dt))

    _uid = [0]

    def sbt(ph, name, shape, dt):
        _uid[0] += 1
        return ph.enter_context(nc.sbuf_tensor("%s_%d" % (name, _uid[0]), list(shape), dt))

    ones_f = sb("ones_f", [128, 128], F32)
    mods = sb("mods", [128, DEPTH, 72, 2], F32)
    gsc = sb("gsc", [128, DEPTH, 3, DC, 2], F32)
    gat = sb("gat", [128, DEPTH, 3, DC, 2], F32)
    ng = sb("ng", [128, DEPTH, 3, DC], F32)
    fng = sb("fng", [128, DC], F32)
    bm = sb("bm", [128, DEPTH, 72], F32)
    sT = sb("sT", [128, DC, 2], F32)
    R_const = Res("const")
    R_mods = Res("mods")

    kk.op("dve", lambda e: e.memset(ones_f[:], 1.0), w=[R_const])
    kk.dma("sp", ng[:], normgT, w=[R_const])
    kk.dma("sp", fng[:], fnormgT, w=[R_const])
    kk.dma("sp", bm[:], bmodT, w=[R_const])
    kk.dma("sp", sT[:], cT, w=[R_const])
    kk.op("act", lambda e: e.activation(out=sT[:], in_=sT[:], func=AF.Silu), r=[R_const], w=[R_const])

    ps_stat = ps("ps_stat", [128, 512])
    ps_a = [ps("ps_a%d" % i, [128, 512]) for i in range(2)]
    ps_b = [ps("ps_b%d" % i, [128, 512]) for i in range(2)]
    ps_o = [ps("ps_o%d" % i, [128, 512]) for i in range(2)]
    R_ps_stat = Res()
    R_ps_a = [Res(), Res()]
    R_ps_b = [Res(), Res()]
    R_ps_o = [Res(), Res()]
    ps_x = ps("ps_x", [128, 512])
    R_ps_x = Res()
    poolS = [(ps_a[0], R_ps_a[0]), (ps_a[1], R_ps_a[1]), (ps_b[0], R_ps_b[0]), (ps_b[1], R_ps_b[1])]
    poolO = [(ps_o[0], R_ps_o[0]), (ps_o[1], R_ps_o[1]), (ps_x, R_ps_x)]
    _rrS = [0]
    _rrO = [0]

    def bankS():
        _rrS[0] = (_rrS[0] + 1) % len(poolS)
        return poolS[_rrS[0]]

    def bankO():
        _rrO[0] = (_rrO[0] + 1) % len(poolO)
        return poolO[_rrO[0]]

    def mm_acc(out_ap, R_out, pairs):
        last = len(pairs) - 1
        ev = None
        for idx, (l_, r_, rd) in enumerate(pairs):
            ev = kk.op("pe", lambda e, l_=l_, r_=r_, idx=idx: e.matmul(out_ap, l_, r_, start=(idx == 0), stop=(idx == last)),
                       r=rd, w=[R_out], inc=(idx == last))
        return ev

    identf = sb("identf", [128, 128], F32)
    kk.dma("sp", identf[:], ret_tabs[:, 7, :], w=[R_const])
    with contextlib.ExitStack() as ph:
        wm = [sbt(ph, "wm%d" % i, [128, DC, 1024], F32) for i in range(2)]
        R_wm = [Res(), Res()]
        modrow = sbt(ph, "modrow", [2, 9 * D], F32)
        R_modrow = Res()
        blk = 0
        for li in range(n_layers):
            for nb in range(9):
                s = blk % 2
                blk += 1
                src = w_mod[li, :, nb * 1024:(nb + 1) * 1024].rearrange("(kc p) n -> p kc n", p=128)
                kk.dma("sp", wm[s][:], src, w=[R_wm[s]])
                for half in range(2):
                    (pS, RS) = bankS()
                    mm_acc(pS[0:2, :512], RS, [(sT[:, kc, :], wm[s][:, kc, half * 512:(half + 1) * 512], [R_wm[s], R_const])
                                               for kc in range(DC)])
                    c0 = nb * 1024 + half * 512
                    kk.op("act", lambda e, pS=pS, c0=c0: e.activation(out=modrow[0:2, c0:c0 + 512], in_=pS[0:2, :512],
                                                                       func=AF.Identity), r=[RS], w=[R_modrow])
            for n in range(72):
                kk.op("pe", lambda e, n=n: e.transpose(ps_stat[:, n * 2:(n + 1) * 2], modrow[0:2, n * 128:(n + 1) * 128],
                                                       identf[0:2, 0:2]),
                      r=[R_modrow, R_const], w=[R_ps_stat], inc=(n == 71))
            kk.op("dve", lambda e, li=li: e.tensor_tensor(
                out=mods[:, li, :, :], in0=ps_stat[:, 0:144].rearrange("p (n w) -> p n w", w=2),
                in1=bm[:, li, :].unsqueeze(2).to_broadcast([128, 72, 2]),
                op=ALU.add), r=[R_ps_stat, R_const], w=[R_mods])
        kk.barrier()

    for li in range(n_layers):
        for j in range(3):
            sc = mods[:, li, (j * 3 + 1) * 8:(j * 3 + 2) * 8, :]
            gt = mods[:, li, (j * 3 + 2) * 8:(j * 3 + 3) * 8, :]
            for w_ in range(2):
                kk.op("dve", lambda e, li=li, j=j, w_=w_, sc=sc: e.scalar_tensor_tensor(
                    out=gsc[:, li, j, :, w_], in0=sc[:, :, w_], scalar=1.0, in1=ng[:, li, j, :],
                    op0=ALU.add, op1=ALU.mult), r=[R_mods, R_const], w=[R_mods])
            kk.op("dve", lambda e, li=li, j=j, gt=gt: e.tensor_scalar(
                out=gat[:, li, j, :, :], in0=gt, scalar1=(1.0 if j == 1 else 0.5), scalar2=None,
                op0=ALU.mult), r=[R_mods], w=[R_mods])
    kk.barrier()

    tiles = [(t0, NT, 0) for t0 in range(0, T_LAT, NT)] + [(T_LAT, T_CTX, 1)]

    def norm_mod(ph, name):
        o = {}
        o["sqt"] = sbt(ph, name + "_sqt", [128, DC, NT], F32)
        o["Rsqt"] = Res()
        o["ssum"] = sbt(ph, name + "_ssum", [128, NT], F32)
        o["Rssum"] = Res()
        o["rstd"] = sbt(ph, name + "_rstd", [128, NT], F32)
        o["Rrstd"] = Res()
        return o

    def _emit(steps, eng, fn, r, w):
        if steps is None:
            kk.op(eng, fn, r=r, w=w)
        else:
            steps.append(lambda: kk.op(eng, fn, r=r, w=w))

    def emit_rstd(o, ht, R_ht, n, steps=None):
        _emit(steps, "dve", lambda e: e.tensor_tensor(out=o["sqt"][:, :, :n], in0=ht[:, :, :n], in1=ht[:, :, :n], op=ALU.mult),
              [R_ht], [o["Rsqt"]])
        _emit(steps, "dve", lambda e: e.tensor_reduce(out=o["ssum"][:, :n], in_=o["sqt"][:, :, :n].rearrange("p c n -> p n c"),
                                                      axis=AX.X, op=ALU.add), [o["Rsqt"]], [o["Rssum"]])
        _emit(steps, "pe", lambda e: e.matmul(ps_stat[:, :n], ones_f[:], o["ssum"][:, :n], start=True, stop=True),
              [o["Rssum"], R_const], [R_ps_stat])
        _emit(steps, "act", lambda e: e.activation(out=o["rstd"][:, :n], in_=ps_stat[:, :n], func=AF.Sqrt,
                                                   scale=1.0 / D, bias=EPS), [R_ps_stat], [o["Rrstd"]])
        _emit(steps, "dve", lambda e: e.reciprocal(out=o["rstd"][:, :n], in_=o["rstd"][:, :n]), [o["Rrstd"]], [o["Rrstd"]])

    def emit_xl(o, ht, R_ht, n, li, j, w_, dst, R_dst, steps=None):
        _emit(steps, "dve", lambda e: e.tensor_tensor(
            out=o["sqt"][:, :, :n], in0=ht[:, :, :n], in1=o["rstd"][:, :n].unsqueeze(1).to_broadcast([128, DC, n]),
            op=ALU.mult), [R_ht, o["Rrstd"]], [o["Rsqt"]])
        for c in range(DC):
            _emit(steps, "act", lambda e, c=c: e.activation(
                out=dst[:, c, :n], in_=o["sqt"][:, c, :n], func=AF.Identity,
                scale=gsc[:, li, j, c, w_:w_ + 1], bias=mods[:, li, (j * 3) * 8 + c, w_:w_ + 1]),
                [o["Rsqt"], R_mods], [R_dst])

    def ffn_phase(li, s_, j, h_in, h_out, tl):
        with contextlib.ExitStack() as ph:
            win = sbt(ph, "win", [128, DC, 2 * FF], BF16)
            wout = sbt(ph, "wout", [128, FJ, D], BF16)
            jblocks = [(0, 2), (2, 5), (5, 8), (8, 11), (11, 14), (14, 17), (17, 20), (20, 22)]
            blk_of_j = {}
            for bi_, (j0, j1) in enumerate(jblocks):
                for jx in range(j0, j1):
                    blk_of_j[jx] = bi_
            R_wina = [Res() for _ in jblocks]
            R_winb = [Res() for _ in jblocks]
            R_woutb = [Res() for _ in range(4)]
            w_in_src = ffn_w_in[li, s_].rearrange("(kc p) n -> p kc n", p=128)
            for bi_, (j0, j1) in enumerate(jblocks):
                for base, RR in ((0, R_wina), (FF, R_winb)):
                    c0, c1 = base + j0 * 128, base + j1 * 128
                    kk.dma("pool", win[:, :, c0:c1], w_in_src[:, :, c0:c1], w=[RR[bi_]])
                if bi_ == 1:
                    w_out_src = ffn_w_out[li, s_].rearrange("(j p) d -> p j d", p=128)
                    for ob in range(4):
                        o0, o1 = ob * 6, min(FJ, ob * 6 + 6)
                        kk.dma("pool", wout[:, o0:o1, :], w_out_src[:, o0:o1, :], w=[R_woutb[ob]])
            ht = [sbt(ph, "ht%d" % i, [128, DC, NT], F32) for i in range(3)]
            R_ht = [Res(), Res(), Res()]
            xl = [sbt(ph, "xl%d" % i, [128, DC, NT], BF16) for i in range(2)]
            R_xl = [Res(), Res()]
            g = sbt(ph, "g", [128, FJ, NT], BF16)
            R_g = [Res() for _ in range(FJ)]
            sa = [sbt(ph, "sa%d" % i, [128, NT], F32) for i in range(2)]
            R_sa = [Res(), Res()]
            o = norm_mod(ph, "f")

            def prep(ti, steps):
                t0, n, w_ = tl[ti]
                hb_ = ti % 3
                xb_ = ti % 2
                kk.dma("sp", ht[hb_][:, :, :n], h_in[:, :, t0:t0 + n], w=[R_ht[hb_]])
                emit_rstd(o, ht[hb_], R_ht[hb_], n, steps)
                emit_xl(o, ht[hb_], R_ht[hb_], n, li, j, w_, xl[xb_], R_xl[xb_], steps)

            prep(0, None)
            for ti, (t0, n, w_) in enumerate(tl):
                b = ti % 2
                hb = ti % 3
                steps = []
                if ti + 1 < len(tl):
                    prep(ti + 1, steps)
                    steps.insert(2, lambda: None)
                    steps.insert(2, lambda: None)
                for jj in range(FJ):
                    pb = jj % 2
                    for kc in range(DC):
                        kk.op("pe", lambda e, jj=jj, kc=kc, pb=pb: e.matmul(
                            ps_a[pb][:, :n], win[:, kc, jj * 128:(jj + 1) * 128], xl[b][:, kc, :n],
                            start=(kc == 0), stop=(kc == DC - 1)),
                            r=[R_wina[blk_of_j[jj]], R_xl[b]], w=[R_ps_a[pb]], inc=(kc == DC - 1))
                    for kc in range(DC):
                        kk.op("pe", lambda e, jj=jj, kc=kc, pb=pb: e.matmul(
                            ps_b[pb][:, :n], win[:, kc, FF + jj * 128:FF + (jj + 1) * 128], xl[b][:, kc, :n],
                            start=(kc == 0), stop=(kc == DC - 1)),
                            r=[R_winb[blk_of_j[jj]], R_xl[b]], w=[R_ps_b[pb]], inc=(kc == DC - 1))
                    kk.op("act", lambda e, pb=pb: e.activation(out=sa[pb][:, :n], in_=ps_a[pb][:, :n], func=AF.Silu),
                          r=[R_ps_a[pb]], w=[R_sa[pb]])
                    kk.op("dve", lambda e, pb=pb, jj=jj: e.tensor_tensor(out=g[:, jj, :n], in0=sa[pb][:, :n],
                                                                         in1=ps_b[pb][:, :n], op=ALU.mult),
                          r=[R_sa[pb], R_ps_b[pb]], w=[R_g[jj]])
                    if jj >= 5 and steps:
                        steps.pop(0)()
                while steps:
                    steps.pop(0)()
                for d in range(DC):
                    pb = d % 2
                    for jj in range(FJ):
                        kk.op("pe", lambda e, jj=jj, d=d, pb=pb: e.matmul(
                            ps_o[pb][:, :n], wout[:, jj, d * 128:(d + 1) * 128], g[:, jj, :n],
                            start=(jj == 0), stop=(jj == FJ - 1)),
                            r=[R_woutb[jj // 6], R_g[jj]], w=[R_ps_o[pb]], inc=(jj == FJ - 1))
                    kk.op("dve", lambda e, d=d, pb=pb, hb=hb: e.scalar_tensor_tensor(
                        out=ht[hb][:, d, :n], in0=ps_o[pb][:, :n], scalar=gat[:, li, j, d, w_:w_ + 1],
                        in1=ht[hb][:, d, :n], op0=ALU.mult, op1=ALU.add),
                        r=[R_ps_o[pb], R_mods, R_ht[hb]], w=[R_ht[hb]])
                kk.dma("sp", h_out[:, :, t0:t0 + n], ht[hb][:, :, :n], r=[R_ht[hb]])
            kk.barrier()

    def final_phase(h_in):
        with contextlib.ExitStack() as ph:
            ht = [sbt(ph, "fht%d" % i, [128, DC, NT], F32) for i in range(2)]
            R_ht = [Res(), Res()]
            ot = [sbt(ph, "fot%d" % i, [128, DC, NT], F32) for i in range(2)]
            R_ot = [Res(), Res()]
            o = norm_mod(ph, "fn")
            for ti, (t0, n, w_) in enumerate(tiles):
                if w_ == 1:
                    continue
                b = ti % 2
                kk.dma("sp", ht[b][:, :, :n], h_in[:, :, t0:t0 + n], w=[R_ht[b]])
                emit_rstd(o, ht[b], R_ht[b], n)
                for c in range(DC):
                    kk.op("dve", lambda e, c=c, b=b: e.scalar_tensor_tensor(
                        out=ot[b][:, c, :n], in0=ht[b][:, c, :n], scalar=fng[:, c:c + 1], in1=o["rstd"][:, :n],
                        op0=ALU.mult, op1=ALU.mult), r=[R_ht[b], o["Rrstd"], R_const], w=[R_ot[b]])
                kk.dma("sp", outT[:, :, t0:t0 + n], ot[b][:, :, :n], r=[R_ot[b]])
            kk.barrier()

    def proj_phase(li, kind, h_in, tl):
        with contextlib.ExitStack() as ph:
            if kind == "ret":
                ncol = 6144
                W = sbt(ph, "pw", [128, DC, ncol], BF16)
                R_Wl = [Res() for _ in range(DC)]
                for kc in range(DC):
                    kk.dma("pool", W[:, kc, :], ret_w_in[kc * 128:(kc + 1) * 128, :], w=[R_Wl[kc]])
                fm = [(0, 8, qT_d, "ret", 0), (1024, 8, kT_d, "ret", 0)]
                tm = [(2048, 2048, v_d, AF.Identity, "v"), (4096, 2048, g_d, AF.Silu, "g")]
                cs_d = ret_cs
            elif kind == "nat":
                ncol = 3072
                W = sbt(ph, "pw", [128, DC, ncol], BF16)
                R_Wl = [Res()]
                kk.dma("pool", W[:], nat_w_qkv.rearrange("(kc p) n -> p kc n", p=128), w=[R_Wl[0]])
                fm = [(0, 8, qT_d, None, 0), (1024, 8, kT_d, None, 0)]
                tm = [(2048, 1024, v_d, AF.Identity, "v")]
                cs_d = None
            else:
                ncol = 1536 + 1280
                W = sbt(ph, "pw", [128, DC, ncol], BF16)
                R_Wl = [Res(), Res()]
                kk.dma("pool", W[:, :, 0:1536], swa_w_qkv.rearrange("(kc p) n -> p kc n", p=128), w=[R_Wl[0]])
                kk.dma("pool", W[:, :, 1536:2816], swa_w_perm.rearrange("(kc p) n -> p kc n", p=128), w=[R_Wl[1]])
                fm = [(0, 8, qT_d, "swa", 1536), (1024, 2, kT_d, "swa", 1536 + 1024)]
                tm = [(1280, 256, v_d, AF.Identity, "v")]
                cs_d = swa_cs
            ht = [sbt(ph, "pht%d" % i, [128, DC, NT], F32) for i in range(2)]
            R_ht = [Res(), Res()]
            xl2 = [sbt(ph, "pxl%d" % i, [128, DC, NT], BF16) for i in range(2)]
            R_xl2 = [Res(), Res()]
            stg = [sbt(ph, "pst%d" % i, [128, DC, NT], BF16) for i in range(2)]
            R_stg = [Res(), Res()]
            R_stg2 = [Res(), Res()]
            stv = sbt(ph, "pstv", [128, 2048], BF16)
            R_stv = Res()
            stgg = sbt(ph, "pstg", [128, 2048], F32)
            R_stgg = Res()
            cs2 = [sbt(ph, "pcs%d" % i, [128, 2, NT], F32) for i in range(2)]
            R_cs2 = [Res(), Res()]
            t1 = sbt(ph, "pt1", [128, NT], F32)
            t2 = sbt(ph, "pt2", [128, NT], F32)
            R_t1, R_t2 = Res(), Res()
            t3 = sbt(ph, "pt3", [128, NT], F32)
            t4 = sbt(ph, "pt4", [128, NT], F32)
            R_t3, R_t4 = Res(), Res()
            sAB = sbt(ph, "psAB", [128, 2, NT], F32)
            R_sAB = Res()
            o = norm_mod(ph, "p")
            def prep(ti_, steps):
                t0_, n_, w__ = tl[ti_]
                b_ = ti_ % 2
                kk.dma("sp", ht[b_][:, :, :n_], h_in[:, :, t0_:t0_ + n_], w=[R_ht[b_]])
                if cs_d is not None:
                    kk.dma("sp", cs2[b_][:, :, :n_], cs_d[:, :, t0_:t0_ + n_], w=[R_cs2[b_]])
                emit_rstd(o, ht[b_], R_ht[b_], n_, steps)
                emit_xl(o, ht[b_], R_ht[b_], n_, li, 1, w__, xl2[b_], R_xl2[b_], steps)

            prep(0, None)
            for ti, (t0, n, w_) in enumerate(tl):
                b = ti % 2
                xl, R_xl = xl2[b], R_xl2[b]
                cs, R_cs = cs2[b], R_cs2[b]
                steps = []
                npj = [0]
                if ti + 1 < len(tl):
                    prep(ti + 1, steps)

                def proj(off, m):
                    (pa, Ra) = bankS()
                    mm_acc(pa[:, :n], Ra, [(W[:, kc, off + m * 128:off + (m + 1) * 128], xl[:, kc, :n], R_Wl + [R_xl])
                                            for kc in range(DC)])
                    npj[0] += 1
                    if steps and npj[0] > 4:
                        steps.pop(0)()
                    return pa, Ra

                def tt(out, a, b_, op, r, w):
                    kk.op("dve", lambda e: e.tensor_tensor(out=out, in0=a, in1=b_, op=op), r=r, w=w)

                def ttp(out, a, b_, op, r, w):
                    kk.op("pool", lambda e: e.tensor_tensor(out=out, in0=a, in1=b_, op=op), r=r, w=w)

                for fi, (off, nch, dst, mode, poff) in enumerate(fm):
                    sg, R_sg, R_sg2 = stg[fi], R_stg[fi], R_stg2[fi]
                    if mode == "ret":
                        for hh in range(nch // 2):
                            pa, Ra = proj(off, 2 * hh)
                            pb_, Rb = proj(off, 2 * hh + 1)
                            kk.op("act", lambda e, pa=pa: e.activation(out=sAB[:, 0, :n], in_=pa[:, :n], func=AF.Identity),
                                  r=[Ra], w=[R_sAB])
                            kk.op("act", lambda e, pb_=pb_: e.activation(out=sAB[:, 1, :n], in_=pb_[:, :n], func=AF.Identity),
                                  r=[Rb], w=[R_sAB])
                            tt(t1[:, :n], sAB[:, 0, :n], cs[:, 0, :n], ALU.mult, [R_sAB, R_cs], [R_t1])
                            tt(t2[:, :n], sAB[:, 1, :n], cs[:, 1, :n], ALU.mult, [R_sAB, R_cs], [R_t2])
                            tt(sg[:, 2 * hh, :n], t1[:, :n], t2[:, :n], ALU.subtract, [R_t1, R_t2], [R_sg])
                            ttp(t3[:, :n], sAB[:, 1, :n], cs[:, 0, :n], ALU.mult, [R_sAB, R_cs], [R_t3])
                            ttp(t4[:, :n], sAB[:, 0, :n], cs[:, 1, :n], ALU.mult, [R_sAB, R_cs], [R_t4])
                            ttp(sg[:, 2 * hh + 1, :n], t3[:, :n], t4[:, :n], ALU.add, [R_t3, R_t4], [R_sg2])
                    elif mode == "swa":
                        for m in range(nch):
                            pa, Ra = proj(off, m)
                            pb_, Rb = proj(poff, m)
                            tt(t1[:, :n], pa[:, :n], cs[:, 0, :n], ALU.mult, [Ra, R_cs], [R_t1])
                            tt(t2[:, :n], pb_[:, :n], cs[:, 1, :n], ALU.mult, [Rb, R_cs], [R_t2])
                            tt(sg[:, m, :n], t1[:, :n], t2[:, :n], ALU.add, [R_t1, R_t2], [R_sg])
                    else:
                        for m in range(nch):
                            pa, Ra = proj(off, m)
                            kk.op("act", lambda e, pa=pa, m=m, sg=sg: e.activation(out=sg[:, m, :n], in_=pa[:, :n], func=AF.Identity),
                                  r=[Ra], w=[R_sg])
                    kk.dma("sp", dst[:, 0:nch, t0:t0 + n], sg[:, 0:nch, :n], r=[R_sg, R_sg2])
                while steps:
                    steps.pop(0)()
                for sub in range(n // 128):
                    for (off, ncols, dstd, fn, nm) in tm:
                        st_t, R_st = (stv, R_stv) if nm == "v" else (stgg, R_stgg)
                        for blk in range((ncols + 511) // 512):
                            cw = min(512, ncols - blk * 512)
                            (pa, Ra) = bankS()
                            mm_acc(pa[:, :cw], Ra, [(xl[:, kc, sub * 128:(sub + 1) * 128],
                                                     W[:, kc, off + blk * 512:off + blk * 512 + cw], R_Wl + [R_xl])
                                                    for kc in range(DC)])
                            kk.op("act", lambda e, pa=pa, blk=blk, cw=cw, st_t=st_t, fn=fn: e.activation(
                                out=st_t[:, blk * 512:blk * 512 + cw], in_=pa[:, :cw], func=fn), r=[Ra], w=[R_st])
                        kk.dma("sp", dstd[t0 + sub * 128:t0 + (sub + 1) * 128, 0:ncols], st_t[:, 0:ncols], r=[R_st])
            kk.barrier()

    def attn_phase(kind, do_ctx):
        with contextlib.ExitStack() as ph:
            ntyp, nch = (5, 7) if kind == "nat" else (3, 5)
            bias = sbt(ph, "abias", [128, ntyp, nch, 128], F32)
            R_bias = Res()
            ones_b = sbt(ph, "aones", [128, 64], BF16)
            R_c = Res()
            kk.op("dve", lambda e: e.memset(ones_b[:], 1.0), w=[R_c])
            sk = sbt(ph, "ask", [128, 16], F32)
            if kind == "swa":
                kk.dma("sp", bias[:], swa_bias, w=[R_bias])
                kk.op("act", lambda e: e.activation(out=bias[:], in_=bias[:], func=AF.Exp), r=[R_bias], w=[R_bias])
                kk.dma("sp", sk[:], swa_sink.partition_broadcast(128), w=[R_c])
                kk.op("act", lambda e: e.activation(out=sk[:], in_=sk[:], func=AF.Exp), r=[R_c], w=[R_c])
            qh = [sbt(ph, "aqh%d" % i, [64, T_ALL], BF16) for i in range(2)]
            kh = [sbt(ph, "akh%d" % i, [64, T_ALL], BF16) for i in range(2)]
            vh = [sbt(ph, "avh%d" % i, [128, 34, 65], BF16) for i in range(2)]
            R_qh, R_kh, R_vh = [Res(), Res()], [Res(), Res()], [Res(), Res()]
            for i in range(2):
                kk.op("dve", lambda e, i=i: e.memset(vh[i][:, :, 64:65], 1.0), w=[R_vh[i]])
            otm = [sbt(ph, "aotm%d" % i, [128, 34, 128], BF16) for i in range(2)]
            R_otm = [Res(), Res()]
            ostg = sbt(ph, "aostg", [128, 34 * 128], BF16)
            R_ostg = Res()
            ident_b = sbt(ph, "aident", [128, 128], BF16)
            kk.op("dve", lambda e: e.tensor_copy(out=ident_b[:], in_=identf[:]), r=[R_const], w=[R_c])
            psT = ps_stat[:].bitcast(BF16)
            bias2 = [bias, sbt(ph, "abias2", [128, ntyp, nch, 128], F32)] if kind == "nat" else [bias, bias]
            R_bias2 = [R_bias, Res()] if kind == "nat" else [R_bias, R_bias]
            tmp = [sbt(ph, "atmp%d" % i, [128, nch * 128], F32) for i in range(2)]
            R_tmp = [Res(), Res()]
            pt = [sbt(ph, "apt%d" % i, [128, nch * 128], BF16) for i in range(2)]
            R_pt = [Res(), Res()]
            den = [sbt(ph, "aden%d" % i, [128, 1], F32) for i in range(2)]
            R_den = [Res(), Res()]
            Tq = T_ALL if do_ctx else T_LAT

            def load_head(h):
                hs = h % 2
                kvh = h if kind == "nat" else h // 4
                kk.dma("sp", qh[hs][:], qT_d[(h % 2) * 64:(h % 2) * 64 + 64, h // 2, :], w=[R_qh[hs]])
                kk.dma("sp", kh[hs][:], kT_d[(kvh % 2) * 64:(kvh % 2) * 64 + 64, kvh // 2, :], w=[R_kh[hs]])
                kk.dma("sp", vh[hs][:, :, 0:64], v_d[:, kvh * 64:(kvh + 1) * 64].rearrange("(n p) d -> p n d", p=128),
                       w=[R_vh[hs]])
                if kind == "nat":
                    kk.dma("sp", bias2[hs][:], nat_bias[h], w=[R_bias2[hs]])
                    kk.op("act", lambda e, hs=hs: e.activation(out=bias2[hs][:], in_=bias2[hs][:], func=AF.Exp),
                          r=[R_bias2[hs]], w=[R_bias2[hs]])

            units = []
            for h in range(16):
                for qi in range(32):
                    if kind == "nat":
                        typ = 0 if 2 <= qi <= 29 else {0: 1, 1: 2, 30: 3, 31: 4}[qi]
                        base = min(max(qi - 2, 0), 27)
                        chunks = [base + i for i in range(5)] + [32, 33]
                    else:
                        typ = 1 if qi == 0 else (2 if qi == 31 else 0)
                        chunks = [max(qi - 1, 0), qi, min(qi + 1, 31), 32, 33]
                    units.append([h, qi * 128, chunks, typ, qi == 0, False])
                if do_ctx:
                    for qc in (32, 33):
                        units.append([h, qc * 128, [32, 33], None, False, False])
                units[-1][5] = True
            state = {}

            def s1(ui):
                h, q0, chunks, typ, first, lasth = units[ui]
                hs = h % 2
                ncu = len(chunks)
                u2 = ui % 2
                banks = []
                for c0 in range(0, ncu, 4):
                    cn = min(4, ncu - c0)
                    (pS, RS) = bankS()
                    banks.append((pS, RS, c0, cn))
                    for ci in range(c0, c0 + cn):
                        kc_ = chunks[ci]
                        kk.op("pe", lambda e, pS=pS, ci=ci, c0=c0, kc_=kc_, q0=q0, hs=hs: e.matmul(
                            pS[:, (ci - c0) * 128:(ci - c0 + 1) * 128], kh[hs][:, kc_ * 128:(kc_ + 1) * 128],
                            qh[hs][:, q0:q0 + 128], start=True, stop=True),
                            r=[R_kh[hs], R_qh[hs]], w=[RS], inc=(ci == c0 + cn - 1))
                if typ is not None and kind == "swa":
                    for (pS, RS, c0, cn) in banks:
                        kk.op("act", lambda e, pS=pS, c0=c0, cn=cn, u2=u2: e.activation(
                            out=pt[u2][:, c0 * 128:(c0 + cn) * 128], in_=pS[:, :cn * 128], func=AF.Exp, scale=0.125),
                            r=[RS], w=[R_pt[u2]])
                    for ch in (0, 2):
                        kk.op("pool", lambda e, ch=ch, u2=u2, typ=typ: e.tensor_tensor(
                            out=pt[u2][:, ch * 128:(ch + 1) * 128], in0=pt[u2][:, ch * 128:(ch + 1) * 128],
                            in1=bias[:, typ, ch, :], op=ALU.mult), r=[R_bias], w=[R_pt[u2]])
                elif typ is not None:
                    for (pS, RS, c0, cn) in banks:
                        nloc = max(0, min(cn, 5 - c0))
                        if nloc > 0:
                            kk.op("act", lambda e, pS=pS, c0=c0, nloc=nloc, u2=u2: e.activation(
                                out=tmp[u2][:, c0 * 128:(c0 + nloc) * 128], in_=pS[:, :nloc * 128], func=AF.Exp, scale=0.125),
                                r=[RS], w=[R_tmp[u2]])
                        if cn > nloc:
                            kk.op("act", lambda e, pS=pS, c0=c0, cn=cn, nloc=nloc, u2=u2: e.activation(
                                out=pt[u2][:, (c0 + nloc) * 128:(c0 + cn) * 128], in_=pS[:, nloc * 128:cn * 128],
                                func=AF.Exp, scale=0.125), r=[RS], w=[R_pt[u2]])
                    kk.op("pool", lambda e, u2=u2, typ=typ, hs=hs: e.tensor_tensor(
                        out=pt[u2][:, 0:640].rearrange("p (c q) -> p c q", q=128),
                        in0=tmp[u2][:, 0:640].rearrange("p (c q) -> p c q", q=128),
                        in1=bias2[hs][:, typ, 0:5, :], op=ALU.mult), r=[R_tmp[u2], R_bias2[hs]], w=[R_pt[u2]])
                else:
                    for (pS, RS, c0, cn) in banks:
                        kk.op("act", lambda e, pS=pS, c0=c0, cn=cn, u2=u2: e.activation(
                            out=pt[u2][:, c0 * 128:(c0 + cn) * 128], in_=pS[:, :cn * 128], func=AF.Exp, scale=0.125),
                            r=[RS], w=[R_pt[u2]])

            ntile = 34 if do_ctx else 32

            def s2(ui):
                h, q0, chunks, typ, first, lasth = units[ui]
                hs = h % 2
                if first and h + 1 < 16:
                    load_head(h + 1)
                ncu = len(chunks)
                u2 = ui % 2
                pp = (h // 2) % 2
                par = h % 2
                qt = q0 // 128
                (pO, RO) = bankO()
                mm_acc(pO[:, 0:65], RO, [(pt[u2][:, ci * 128:(ci + 1) * 128], vh[hs][:, chunks[ci], 0:65], [R_vh[hs], R_pt[u2]])
                                         for ci in range(ncu)])
                if kind == "swa":
                    kk.op("dve", lambda e, pO=pO, u2=u2, h=h: e.tensor_scalar(
                        out=den[u2][:], in0=pO[:, 64:65], scalar1=sk[:, h:h + 1], scalar2=None, op0=ALU.add),
                        r=[RO, R_c], w=[R_den[u2]])
                    kk.op("dve", lambda e, u2=u2: e.reciprocal(out=den[u2][:], in_=den[u2][:]), r=[R_den[u2]], w=[R_den[u2]])
                else:
                    kk.op("dve", lambda e, pO=pO, u2=u2: e.reciprocal(out=den[u2][:], in_=pO[:, 64:65]),
                          r=[RO], w=[R_den[u2]])
                kk.op("dve", lambda e, pO=pO, u2=u2, pp=pp, par=par, qt=qt: e.tensor_scalar(
                    out=otm[pp][:, qt, par * 64:(par + 1) * 64], in0=pO[:, 0:64], scalar1=den[u2][:, 0:1], scalar2=None,
                    op0=ALU.mult), r=[RO, R_den[u2]], w=[R_otm[pp]])
                if lasth and par == 1:
                    for t in range(ntile):
                        reg = (t % 8) * 128
                        kk.op("pe", lambda e, t=t, reg=reg, pp=pp: e.transpose(psT[:, reg:reg + 128], otm[pp][:, t, :], ident_b[:]),
                              r=[R_otm[pp], R_c], w=[R_ps_stat], inc=(t % 4 == 3 or t == ntile - 1))
                        if t % 4 == 3 or t == ntile - 1:
                            t0_ = t - (t % 4)
                            half = ((t0_ % 8) // 4) * 512
                            nn = (t - t0_ + 1) * 128
                            kk.op("act", lambda e, t0_=t0_, half=half, nn=nn: e.activation(
                                out=ostg[:, t0_ * 128:t0_ * 128 + nn], in_=psT[:, half:half + nn], func=AF.Identity),
                                r=[R_ps_stat], w=[R_ostg])
                    kk.dma("sp", oT_d[:, h // 2, 0:Tq], ostg[:, 0:Tq], r=[R_ostg])

            LA = 1
            load_head(0)
            for k in range(len(units) + LA):
                if k < len(units):
                    s1(k)
                if k - LA >= 0:
                    s2(k - LA)
            kk.barrier()

    def oproj_phase(li, Wd, KC, h_in, h_out, tl):
        with contextlib.ExitStack() as ph:
            W = sbt(ph, "ow", [128, KC, D], BF16)
            R_W = Res()
            kk.dma("pool", W[:], Wd.rearrange("(c p) d -> p c d", p=128), w=[R_W])
            ht = [sbt(ph, "oht%d" % i, [128, DC, NT], F32) for i in range(3)]
            R_ht = [Res(), Res(), Res()]
            ot = [sbt(ph, "oot%d" % i, [128, KC, NT], BF16) for i in range(2)]
            R_ot = [Res(), Res()]
            def load(ti_):
                t0_, n_, _w = tl[ti_]
                b_ = ti_ % 2
                kk.dma("sp", ht[ti_ % 3][:, :, :n_], h_in[:, :, t0_:t0_ + n_], w=[R_ht[ti_ % 3]])
                kk.dma("sp", ot[b_][:, :, :n_], oT_d[:, 0:KC, t0_:t0_ + n_], w=[R_ot[b_]])

            load(0)
            for ti, (t0, n, w_) in enumerate(tl):
                b = ti % 2
                hb = ti % 3
                if ti + 1 < len(tl):
                    load(ti + 1)
                for d in range(DC):
                    (pO, RO) = bankO()
                    mm_acc(pO[:, :n], RO, [(W[:, c, d * 128:(d + 1) * 128], ot[b][:, c, :n], [R_W, R_ot[b]]) for c in range(KC)])
                    kk.op("dve", lambda e, d=d, pO=pO, hb=hb: e.scalar_tensor_tensor(
                        out=ht[hb][:, d, :n], in0=pO[:, :n], scalar=gat[:, li, 1, d, w_:w_ + 1],
                        in1=ht[hb][:, d, :n], op0=ALU.mult, op1=ALU.add),
                        r=[RO, R_mods, R_ht[hb]], w=[R_ht[hb]])
                kk.dma("sp", h_out[:, :, t0:t0 + n], ht[hb][:, :, :n], r=[R_ht[hb]])
            kk.barrier()

    def ret_phase(do_ctx):
        with contextlib.ExitStack() as ph:
            rt = sbt(ph, "rtabs", [128, 8, 128], F32)
            R_rt = Res()
            kk.dma("sp", rt[:], ret_tabs, w=[R_rt])
            lg = sbt(ph, "rlg", [128, 8], F32)
            kk.dma("sp", lg[:], ret_decay.partition_broadcast(128), w=[R_rt])
            kk.op("act", lambda e: e.activation(out=lg[:], in_=lg[:], func=AF.Sigmoid), r=[R_rt], w=[R_rt])
            kk.op("act", lambda e: e.activation(out=lg[:], in_=lg[:], func=AF.Ln), r=[R_rt], w=[R_rt])
            lgn = sbt(ph, "rlgn", [128, 8], F32)
            kk.op("dve", lambda e: e.tensor_scalar(out=lgn[:], in0=lg[:], scalar1=-1.0, scalar2=None, op0=ALU.mult),
                  r=[R_rt], w=[R_rt])
            ident = sbt(ph, "rident", [128, 128], BF16)
            kk.op("dve", lambda e: e.tensor_copy(out=ident[:], in_=rt[:, 7, :]), r=[R_rt], w=[R_rt])
            gng = sbt(ph, "rgng", [128, 2048], F32)
            kk.dma("sp", gng[:], ret_gn.partition_broadcast(128), w=[R_rt])
            tb = sbt(ph, "rtb", [128, 4, 5, 128], F32)
            tA = sbt(ph, "rtA", [128, 128], F32)
            tB = sbt(ph, "rtB", [128, 128], F32)
            for h in range(4):
                lf = lg[:, h:h + 1]
                lb = lg[:, 4 + h:5 + h]
                for (slot, tab, sc_) in ((0, 0, lf), (1, 1, lb), (2, 2, lf), (3, 3, lb)):
                    kk.op("act", lambda e, h=h, slot=slot, tab=tab, sc_=sc_: e.activation(
                        out=tb[:, h, slot, :], in_=rt[:, tab, :], func=AF.Exp, scale=sc_), r=[R_rt], w=[R_rt])
                kk.op("act", lambda e, lf=lf: e.activation(out=tA[:], in_=rt[:, 4, :], func=AF.Exp, scale=lf), r=[R_rt], w=[R_rt])
                kk.op("act", lambda e, lb=lb: e.activation(out=tB[:], in_=rt[:, 5, :], func=AF.Exp, scale=lb), r=[R_rt], w=[R_rt])
                kk.op("dve", lambda e: e.tensor_tensor(out=tA[:], in0=tA[:], in1=tB[:], op=ALU.subtract), r=[R_rt], w=[R_rt])
                kk.op("dve", lambda e: e.tensor_tensor(out=tA[:], in0=tA[:], in1=rt[:, 6, :], op=ALU.mult), r=[R_rt], w=[R_rt])
                kk.op("dve", lambda e, h=h: e.tensor_tensor(out=tb[:, h, 4, :], in0=tA[:], in1=tB[:], op=ALU.add), r=[R_rt], w=[R_rt])
                kk.op("dve", lambda e, h=h: e.tensor_scalar(out=tb[:, h, 2:5, :], in0=tb[:, h, 2:5, :], scalar1=1.0 / 16.0,
                                                            scalar2=None, op0=ALU.mult), r=[R_rt], w=[R_rt])
                kk.op("act", lambda e, h=h: e.activation(out=tA[:], in_=rt[:, 0, :], func=AF.Exp, scale=lgn[:, h:h + 1]),
                      r=[R_rt], w=[R_rt])
                kk.op("dve", lambda e, h=h: e.tensor_tensor(out=tb[:, h, 4, :], in0=tb[:, h, 4, :], in1=tA[:], op=ALU.mult),
                      r=[R_rt], w=[R_rt])
            cc = sbt(ph, "rcc", [128, 8], F32)
            kk.op("act", lambda e: e.activation(out=cc[:], in_=lg[:], func=AF.Exp, scale=128.0), r=[R_rt], w=[R_rt])
            qh = sbt(ph, "rqh", [128, 2, T_ALL], BF16)
            qf = sbt(ph, "rqf", [128, 2, T_ALL], BF16)
            qb = sbt(ph, "rqb", [128, 2, T_ALL], BF16)
            kh = sbt(ph, "rkh", [128, 2, T_ALL], BF16)
            vh = sbt(ph, "rvh", [128, 34, 512], BF16)
            R_qh, R_qf, R_qb, R_kh, R_vh = Res(), Res(), Res(), Res(), Res()
            Sf = sbt(ph, "rSf", [128, 2, 512], F32)
            Sbp = [sbt(ph, "rSb%d" % i, [128, 2, 512], F32) for i in range(2)]
            R_Sf, R_Sbp = Res(), [Res(), Res()]
            Sfb = [sbt(ph, "rSfb%d" % i, [128, 2, 512], BF16) for i in range(2)]
            R_Sfb = [Res(), Res()]
            sstg = [sbt(ph, "rsstg%d" % i, [128, 2, 512], BF16) for i in range(2)]
            R_sstg = [Res(), Res()]
            sbl = [sbt(ph, "rsbl%d" % i, [128, 2, 512], BF16) for i in range(3)]
            R_sbl = [Res(), Res(), Res()]
            kt = [sbt(ph, "rkt%d" % i, [128, 256], BF16) for i in range(2)]
            R_kt = [Res(), Res()]
            pm = [sbt(ph, "rpm%d" % i, [128, 128], BF16) for i in range(2)]
            R_pm = [Res(), Res()]
            ocn = [sbt(ph, "rocn%d" % i, [128, 512], F32) for i in range(4)]
            R_ocn = [Res() for _ in range(4)]
            junk = sbt(ph, "rjunk", [128, 512], F32)
            R_junk = Res()
            gt = [sbt(ph, "rgt%d" % i, [128, 512], F32) for i in range(4)]
            R_gt = [Res() for _ in range(4)]
            gated = [sbt(ph, "rgated%d" % i, [128, 512], BF16) for i in range(4)]
            R_gated = [Res() for _ in range(4)]
            ost = [sbt(ph, "rost%d" % i, [128, 4, 128], BF16) for i in range(4)]
            R_ost = [Res() for _ in range(4)]
            sm = [sbt(ph, "rsm%d" % i, [128, 4], F32) for i in range(4)]
            R_sm = [Res() for _ in range(4)]
            psT = ps_stat[:].bitcast(BF16)
            psXb = ps_x[:].bitcast(BF16)
            R_sbd = [Res() for _ in range(34)]
            Ubank = [[(ps_a[0], R_ps_a[0]), (ps_a[1], R_ps_a[1])], [(ps_b[0], R_ps_b[0]), (ps_b[1], R_ps_b[1])]]
            Obank = [(ps_o[0], R_ps_o[0]), (ps_o[1], R_ps_o[1])]

            def prescale_q(h):
                for (dst, R_dst, slot, eng) in ((qf, R_qf, 0, "dve"), (qb, R_qb, 1, "pool")):
                    kk.op(eng, lambda e, dst=dst, slot=slot: e.tensor_tensor(
                        out=dst[:].rearrange("p c (n i) -> p c n i", i=128),
                        in0=qh[:].rearrange("p c (n i) -> p c n i", i=128),
                        in1=tb[:, h, slot, :].unsqueeze(1).unsqueeze(1).to_broadcast([128, 2, 34, 128]), op=ALU.mult),
                        r=[R_qh, R_rt], w=[R_dst])

            def u_prep(h, m, direction, par):
                reg = 512 + par * 256
                for dc in range(2):
                    kk.op("pe", lambda e, dc=dc: e.transpose(psXb[:, reg + dc * 128:reg + (dc + 1) * 128],
                                                             kh[:, dc, m * 128:(m + 1) * 128], ident[:]),
                          r=[R_kh, R_rt], w=[R_ps_x], inc=(dc == 1))
                slot = 2 if direction == "f" else 3
                kk.op("act", lambda e: e.activation(out=kt[par][:], in_=psXb[:, reg:reg + 256], func=AF.Identity,
                                                    scale=tb[:, h, slot, 0:1]), r=[R_ps_x, R_rt], w=[R_kt[par]])
                for dc in range(2):
                    (pU, RU) = Ubank[par][dc]
                    kk.op("pe", lambda e, dc=dc, pU=pU: e.matmul(pU[:, :512], kt[par][:, dc * 128:(dc + 1) * 128], vh[:, m, :],
                                                                  start=True, stop=True), r=[R_kt[par], R_vh], w=[RU])

            def s_update(h, S, R_S, direction, par, S2=None, R_S2=None):
                cidx = h if direction == "f" else 4 + h
                if S2 is None:
                    S2, R_S2 = S, R_S
                for dc in range(2):
                    (pU, RU) = Ubank[par][dc]
                    kk.op("dve", lambda e, dc=dc, pU=pU: e.scalar_tensor_tensor(
                        out=S2[:, dc, :], in0=S[:, dc, :], scalar=cc[:, cidx:cidx + 1], in1=pU[:, :512],
                        op0=ALU.mult, op1=ALU.add), r=[RU, R_S, R_rt], w=[R_S2])

            def epiA(h, q0, tx, pO, RO):
                t2 = tx % 4
                kk.op("dve", lambda e: e.reduce_sum(out=sm[t2][:, 0:1], in_=pO[:, :512], axis=AX.X), r=[RO], w=[R_sm[t2]])
                kk.op("dve", lambda e: e.tensor_scalar(out=sm[t2][:, 0:1], in0=sm[t2][:, 0:1], scalar1=-1.0 / 512.0, scalar2=None,
                                                       op0=ALU.mult), r=[R_sm[t2]], w=[R_sm[t2]])
                kk.op("act", lambda e: e.activation(out=ocn[t2][:], in_=pO[:, :512], func=AF.Identity, bias=sm[t2][:, 0:1]),
                      r=[RO, R_sm[t2]], w=[R_ocn[t2]])
                kk.op("act", lambda e: e.activation(out=junk[:], in_=ocn[t2][:], func=AF.Square, accum_out=sm[t2][:, 1:2]),
                      r=[R_ocn[t2]], w=[R_sm[t2], R_junk])
                kk.op("act", lambda e: e.activation(out=sm[t2][:, 2:3], in_=sm[t2][:, 1:2], func=AF.Sqrt, scale=1.0 / 512.0, bias=EPS),
                      r=[R_sm[t2]], w=[R_sm[t2]])

            def epiB(h, q0, tx):
                t2 = tx % 4
                kk.op("dve", lambda e: e.reciprocal(out=sm[t2][:, 2:3], in_=sm[t2][:, 2:3]), r=[R_sm[t2]], w=[R_sm[t2]])
                kk.op("dve", lambda e: e.scalar_tensor_tensor(
                    out=ocn[t2][:], in0=ocn[t2][:], scalar=sm[t2][:, 2:3], in1=gng[:, h * 512:(h + 1) * 512],
                    op0=ALU.mult, op1=ALU.mult), r=[R_ocn[t2], R_sm[t2], R_rt], w=[R_ocn[t2]])
                kk.op("dve", lambda e: e.tensor_tensor(out=gated[t2][:], in0=ocn[t2][:], in1=gt[t2][:], op=ALU.mult),
                      r=[R_ocn[t2], R_gt[t2]], w=[R_gated[t2]])

            def epiC(h, q0, tx):
                t2 = tx % 4
                for blk in range(4):
                    kk.op("pe", lambda e, blk=blk: e.transpose(psT[:, blk * 128:(blk + 1) * 128],
                                                               gated[t2][:, blk * 128:(blk + 1) * 128], ident[:]),
                          r=[R_gated[t2], R_rt], w=[R_ps_stat], inc=(blk == 3))
                kk.op("act", lambda e: e.activation(out=ost[t2][:].rearrange("p a b -> p (a b)"), in_=psT[:, 0:512],
                                                    func=AF.Identity), r=[R_ps_stat], w=[R_ost[t2]])
                kk.dma("sp", oT_d[:, 4 * h:4 * h + 4, q0:q0 + 128], ost[t2][:], r=[R_ost[t2]])

            tx = 0
            pending = []

            def tick():
                for it in pending:
                    it[0] -= 1
                while pending and pending[0][0] <= 0:
                    pending.pop(0)[1]()

            for h in range(4):
                kk.dma("sp", qh[:], qT_d[:, 2 * h:2 * h + 2, :], w=[R_qh])
                kk.dma("sp", kh[:], kT_d[:, 2 * h:2 * h + 2, :], w=[R_kh])
                kk.dma("sp", vh[:], v_d[:, h * 512:(h + 1) * 512].rearrange("(n p) d -> p n d", p=128), w=[R_vh])
                prescale_q(h)
                kk.op("dve", lambda e: e.memset(Sbp[0][:], 0.0), w=[R_Sbp[0]])
                orderB = [33, 32] + list(range(31, -1, -1))
                for bi, m in enumerate(orderB):
                    par = bi % 2
                    cur = bi % 2
                    u_prep(h, m, "b", par)
                    need = (m < 32) or (do_ctx and m == 32)
                    if need:
                        sg = bi % 2
                        kk.op("act", lambda e, sg=sg, cur=cur: e.activation(out=sstg[sg][:], in_=Sbp[cur][:], func=AF.Identity),
                              r=[R_Sbp[cur]], w=[R_sstg[sg]])
                        kk.dma("sp", sb_d[m], sstg[sg][:], r=[R_sstg[sg]], w=[R_sbd[m]])
                    if bi < len(orderB) - 1:
                        s_update(h, Sbp[cur], R_Sbp[cur], "b", par, Sbp[1 - cur], R_Sbp[1 - cur])
                kk.op("dve", lambda e: e.memset(Sf[:], 0.0), w=[R_Sf])
                orderF = [32, 33] + list(range(32))
                u_prep(h, orderF[0], "f", 0)
                for fi, n in enumerate(orderF):
                    par = fi % 2
                    if fi + 1 < len(orderF):
                        u_prep(h, orderF[fi + 1], "f", 1 - par)
                    out_needed = (n < 32) or do_ctx
                    if out_needed:
                        q0 = n * 128
                        t4 = tx % 4
                        kk.dma("sp", gt[t4][:], g_d[q0:q0 + 128, h * 512:(h + 1) * 512], w=[R_gt[t4]])
                        has_b = not (n == 33)
                        has_f = fi > 0
                        if has_b:
                            sl = tx % 3
                            kk.dma("sp", sbl[sl][:], sb_d[n], r=[R_sbd[n]], w=[R_sbl[sl]])
                        (pO, RO) = Obank[tx % 2]
                        mm_acc(ps_x[:, :128], R_ps_x, [(kh[:, dc, q0:q0 + 128], qf[:, dc, q0:q0 + 128], [R_kh, R_qf]) for dc in range(2)])
                        p2 = tx % 2
                        kk.op("dve", lambda e, p2=p2: e.tensor_tensor(out=pm[p2][:], in0=ps_x[:, :128], in1=tb[:, h, 4, :], op=ALU.mult),
                              r=[R_ps_x, R_rt], w=[R_pm[p2]])
                        terms = [(pm[p2][:], vh[:, n, :], [R_pm[p2], R_vh])]
                        if has_b:
                            terms += [(qb[:, dc, q0:q0 + 128], sbl[sl][:, dc, :], [R_qb, R_sbl[sl]]) for dc in range(2)]
                        if has_f:
                            fb = (fi - 1) % 2
                            terms += [(qf[:, dc, q0:q0 + 128], Sfb[fb][:, dc, :], [R_qf, R_Sfb[fb]]) for dc in range(2)]
                        for ti_, (l_, r_, rd) in enumerate(terms):
                            kk.op("pe", lambda e, l_=l_, r_=r_, ti_=ti_, nt_=len(terms), pO=pO: e.matmul(
                                pO[:, :512], l_, r_, start=(ti_ == 0), stop=(ti_ == nt_ - 1)), r=rd, w=[RO],
                                inc=(ti_ == len(terms) - 1))
                    if fi < len(orderF) - 1:
                        s_update(h, Sf, R_Sf, "f", par)
                        fb = fi % 2
                        kk.op("act", lambda e, fb=fb: e.activation(out=Sfb[fb][:], in_=Sf[:], func=AF.Identity),
                              r=[R_Sf], w=[R_Sfb[fb]])
                    while pending:
                        pending.pop(0)[1]()
                    if out_needed:
                        epiA(h, q0, tx, pO, RO)
                        pending.append([1, (lambda h=h, q0=q0, tx=tx: epiB(h, q0, tx))])
                        pending.append([2, (lambda h=h, q0=q0, tx=tx: epiC(h, q0, tx))])
                        tx += 1
            while pending:
                pending.pop(0)[1]()
            kk.barrier()


    def pool_phase(li, h_in, h_out, tl, do_ctx):
        with contextlib.ExitStack() as ph:
            ht = [sbt(ph, "qht%d" % i, [128, DC, NT], F32) for i in range(2)]
            R_ht = [Res(), Res()]
            xf = [sbt(ph, "qxf%d" % i, [128, DC, NT], F32) for i in range(2)]
            R_xf = [Res(), Res()]
            o = norm_mod(ph, "q")
            for ti, (t0, n, w_) in enumerate(tl):
                b = ti % 2
                kk.dma("sp", ht[b][:, :, :n], h_in[:, :, t0:t0 + n], w=[R_ht[b]])
                emit_rstd(o, ht[b], R_ht[b], n)
                emit_xl(o, ht[b], R_ht[b], n, li, 1, w_, xf[b], R_xf[b])
                kk.dma("sp", xl_d[:, :, t0:t0 + n], xf[b][:, :, :n], r=[R_xf[b]])
            kk.barrier()
        with contextlib.ExitStack() as ph:
            PAD = 8
            TB = T_LAT + 2 * PAD
            X = sbt(ph, "qX", [128, 2, TB], F32)
            Y = sbt(ph, "qY", [128, 2, TB], F32)
            Zb = sbt(ph, "qZ", [128, 2, TB], F32)
            R_X, R_Y, R_Z = [Res(), Res()], [Res(), Res()], [Res(), Res()]
            icn = sbt(ph, "qicn", [128, T_LAT], F32)
            R_icn = Res()
            pbf = sbt(ph, "qpb", [128, 2, T_LAT], BF16)
            R_pb = [Res(), Res()]
            wg = sbt(ph, "qwg", [128, 2, 256], BF16)
            R_wg = Res()
            psc = sbt(ph, "qpsc", [128, DC], F32)
            gp = sbt(ph, "qgp", [128, DC, 2], F32)
            R_gp = Res()
            kk.dma("sp", psc[:], pool_sc, w=[R_gp])
            for w_ in range(2):
                kk.op("dve", lambda e, w_=w_: e.tensor_tensor(out=gp[:, :, w_], in0=gat[:, li, 1, :, w_], in1=psc[:], op=ALU.mult),
                      r=[R_gp, R_mods], w=[R_gp])
            hc = [sbt(ph, "qhc%d" % i, [128, 512], F32) for i in range(2)]
            R_hc = [Res(), Res()]
            seqs = [(0, T_LAT, 0)] + ([(T_LAT, T_CTX, 1)] if do_ctx else [])
            hi_ = 0
            for g_ in range(4):
                kk.dma("pool", wg[:], pool_w[g_].rearrange("(kc p) n -> p kc n", p=128), w=[R_wg])
                for (s0, T, w_) in seqs:
                    L = T + 2 * PAD
                    kk.op("pool", lambda e, L=L: e.memset(X[:, :, 0:L], 0.0), w=R_X)
                    kk.dma("sp", X[:, :, PAD:PAD + T], xl_d[:, 2 * g_:2 * g_ + 2, s0:s0 + T], w=R_X)
                    kk.dma("sp", icn[:, :T], pool_icnt[g_:g_ + 1, s0:s0 + T].partition_broadcast(128), w=[R_icn])
                    EN = ("dve", "pool")
                    kk.op("pool", lambda e, L=L: e.memset(Y[:, :, 0:L], 0.0), w=R_Y)
                    for c in range(2):
                        kk.op(EN[c], lambda e, L=L, c=c: e.tensor_tensor(
                            out=Y[:, c, 1:L], in0=X[:, c, 1:L], in1=X[:, c, 0:L - 1], op=ALU.add), r=[R_X[c]], w=[R_Y[c]])
                    lv_src, R_lsrc = Y, R_Y
                    sh = 1
                    for lev in range(g_):
                        a_, Ra_, b_, Rb_ = (Y, R_Y, Zb, R_Z) if lev % 2 == 0 else (Zb, R_Z, Y, R_Y)
                        kk.op("pool", lambda e, L=L, b_=b_: e.memset(b_[:, :, 0:L], 0.0), w=Rb_)
                        for c in range(2):
                            kk.op(EN[c], lambda e, L=L, a_=a_, b_=b_, sh=sh, c=c: e.tensor_tensor(
                                out=b_[:, c, sh:L - sh], in0=a_[:, c, 0:L - 2 * sh], in1=a_[:, c, 2 * sh:L], op=ALU.add),
                                r=[Ra_[c]], w=[Rb_[c]])
                        lv_src, R_lsrc = b_, Rb_
                        sh *= 2
                    for c in range(2):
                        kk.op(EN[c], lambda e, c=c, T=T, lv_src=lv_src: e.tensor_tensor(
                            out=lv_src[:, c, PAD:PAD + T], in0=lv_src[:, c, PAD:PAD + T], in1=icn[:, :T], op=ALU.mult),
                            r=[R_lsrc[c], R_icn], w=[R_lsrc[c]])
                    for c in range(2):
                        kk.op(EN[c], lambda e, c=c, T=T, lv_src=lv_src: e.tensor_tensor(
                            out=pbf[:, c, :T], in0=lv_src[:, c, PAD:PAD + T], in1=X[:, c, PAD:PAD + T], op=ALU.subtract),
                            r=[R_lsrc[c], R_X[c]], w=[R_pb[c]])
                    for m in range(2):
                        c = 2 * g_ + m
                        for tt0 in range(0, T, 512):
                            nn = min(512, T - tt0)
                            hb = hi_ % 2
                            hi_ += 1
                            kk.dma("sp", hc[hb][:, :nn], h_in[:, c, s0 + tt0:s0 + tt0 + nn], w=[R_hc[hb]])
                            (pO, RO) = bankO()
                            mm_acc(pO[:, :nn], RO, [(wg[:, kc, m * 128:(m + 1) * 128], pbf[:, kc, tt0:tt0 + nn], [R_wg] + R_pb)
                                                    for kc in range(2)])
                            kk.op("dve", lambda e, pO=pO, hb=hb, nn=nn, c=c, w_=w_: e.scalar_tensor_tensor(
                                out=hc[hb][:, :nn], in0=pO[:, :nn], scalar=gp[:, c, w_:w_ + 1], in1=hc[hb][:, :nn],
                                op0=ALU.mult, op1=ALU.add), r=[RO, R_gp, R_hc[hb]], w=[R_hc[hb]])
                            kk.dma("sp", h_out[:, c, s0 + tt0:s0 + tt0 + nn], hc[hb][:, :nn], r=[R_hc[hb]])
            kk.barrier()

    kinds = cfg.get("kinds", ["ret", "nat", "pool", "swa"])
    cur = xT
    nxt = 0
    lat_tiles = [t for t in tiles if t[2] == 0]
    for li in range(n_layers):
        kind = kinds[li]
        last = (li == n_layers - 1)
        ctx_live = (not last) or kind != "pool"
        tl1 = tiles if ctx_live else lat_tiles
        tl2 = lat_tiles if last else tiles
        ffn_phase(li, 0, 0, cur, hbufs[nxt], tl1)
        cur = hbufs[nxt]
        nxt ^= 1
        if mixers:
            if kind == "pool":
                pool_phase(li, cur, hbufs[nxt], tl2, not last)
            else:
                proj_phase(li, kind, cur, tl1)
                if kind == "ret":
                    ret_phase(not last)
                    oproj_phase(li, ret_w_out, 16, cur, hbufs[nxt], tl2)
                else:
                    attn_phase(kind, not last)
                    oproj_phase(li, nat_w_o if kind == "nat" else swa_w_o, 8, cur, hbufs[nxt], tl2)
            cur = hbufs[nxt]
            nxt ^= 1
        ffn_phase(li, 1, 2, cur, hbufs[nxt], tl2)
        cur = hbufs[nxt]
        nxt ^= 1
    final_phase(cur)
    kk.barrier()
    kk.ninst_total = kk.ninst
    nc._kk = kk
    return nc


def _fm(a):
    t = a.shape[0]
    return np.ascontiguousarray(a.T.reshape(DC, 128, t).transpose(1, 0, 2))


def _vec_fm(v):
    lead = v.shape[:-1]
    x = v.reshape(*lead, DC, 128)
    x = np.moveaxis(x, -1, 0)
    return np.ascontiguousarray(x)


def _consts():
    c = {}
    p = np.arange(128, dtype=np.float64)[:, None]
    i = np.arange(128, dtype=np.float64)[None, :]
    tabs = np.zeros((128, 8, 128), np.float64)
    tabs[:, 0] = i + 0 * p
    tabs[:, 1] = 127 - i + 0 * p
    tabs[:, 2] = 128 * i + 128 - p
    tabs[:, 3] = 128 * i + p + 1
    tabs[:, 4] = np.maximum(i - p, 0)
    tabs[:, 5] = np.maximum(p - i, 0)
    tabs[:, 6] = (i >= p)
    tabs[:, 7] = (i == p)
    c["ret_tabs"] = tabs.astype(np.float32)
    t = np.arange(T_LAT, dtype=np.float32)[None, :]
    inv = (10000.0 ** (-(np.arange(0, 256, 2, dtype=np.float32)) / 256.0)).astype(np.float32)[:, None]
    ang = (t * inv).astype(np.float32)
    cs = np.zeros((128, 2, T_ALL), np.float32)
    cs[:, 0, :T_LAT] = np.cos(ang)
    cs[:, 1, :T_LAT] = np.sin(ang)
    cs[:, 0, T_LAT:] = 1.0
    c["ret_cs"] = cs
    d = np.arange(128) % 64
    tt = np.arange(T_LAT)
    pos = np.where((d < 32)[:, None], (tt // 64)[None, :], (tt % 64)[None, :]).astype(np.float32)
    inv16 = (10000.0 ** (-(np.arange(0, 32, 2, dtype=np.float32)) / 32.0)).astype(np.float32)
    invd = inv16[(d % 32) % 16][:, None]
    ang = (pos * invd).astype(np.float32)
    sign = np.where((d % 32) < 16, -1.0, 1.0).astype(np.float32)[:, None]
    cs = np.zeros((128, 2, T_ALL), np.float32)
    cs[:, 0, :T_LAT] = np.cos(ang)
    cs[:, 1, :T_LAT] = np.sin(ang) * sign
    cs[:, 0, T_LAT:] = 1.0
    c["swa_cs"] = cs
    NEG = -30000.0
    j = np.arange(128)[:, None]
    q = np.arange(128)[None, :]
    sb_ = np.zeros((128, 3, 5, 128), np.float32)
    for typ in range(3):
        sb_[:, typ, 0] = np.where(j >= q, 0.0, NEG)
        sb_[:, typ, 2] = np.where(j <= q, 0.0, NEG)
    sb_[:, 1, 0] = NEG
    sb_[:, 2, 2] = NEG
    c["swa_bias"] = sb_
    ic = np.zeros((4, T_ALL), np.float32)
    for g_, w in enumerate((2, 4, 8, 16)):
        for (s0, T) in ((0, T_LAT), (T_LAT, T_CTX)):
            tq = np.arange(T)
            lo = np.clip(tq - w // 2, 0, T)
            hi = np.clip(tq + w // 2, 0, T)
            ic[g_, s0:s0 + T] = 1.0 / (hi - lo).astype(np.float32)
    c["pool_icnt"] = ic
    return c


def _nat_bias(rpb):
    NEG = -30000.0
    out = np.zeros((16, 128, 5, 7, 128), np.float32)
    u = (np.arange(128) // 64)
    n = (np.arange(128) % 64)
    cfgs = [(2, 0), (0, 0), (1, 0), (30, 27), (31, 27)]
    for typ, (qi, base) in enumerate(cfgs):
        r = (2 * qi + u)[None, :]
        cq = n[None, :]
        r0 = np.clip(r - 4, 0, 56)
        c0 = np.clip(cq - 8, 0, 48)
        for ch in range(5):
            a = (2 * (base + ch) + u)[:, None]
            nk = n[:, None]
            valid = (a >= r0) & (a < r0 + 8) & (nk >= c0) & (nk < c0 + 16)
            ri = np.clip(a - r + 7, 0, 14)
            ci = np.clip(nk - cq + 15, 0, 30)
            vals = rpb[:, ri, ci]
            out[:, :, typ, ch, :] = np.where(valid[None], vals, NEG)
    return out


def make_in_maps(inputs, cores=range(N_CORES)):
    f = lambda a: np.ascontiguousarray(np.asarray(a, dtype=np.float32))
    x, c, ctx, c_ctx = f(inputs["x"]), f(inputs["c"]), f(inputs["ctx"]), f(inputs["c_ctx"])
    b_mod = f(inputs["b_mod"])
    shared = {
        "w_mod": f(inputs["w_mod"]),
        "bmodT": np.ascontiguousarray(b_mod.reshape(DEPTH, 72, 128).transpose(2, 0, 1)),
        "normgT": _vec_fm(f(inputs["norm_g"])),
        "fnormgT": _vec_fm(f(inputs["final_norm_g"])),
        "ffn_w_in": f(inputs["ffn_w_in"]),
        "ffn_w_out": f(inputs["ffn_w_out"]),
        "ret_w_in": f(inputs["ret_w_in"][0]),
        "ret_w_out": f(inputs["ret_w_out"][0]),
        "ret_gn": f(inputs["ret_gn_g"][0:1]),
        "ret_decay": np.ascontiguousarray(np.concatenate([f(inputs["ret_decay_f"][0]), f(inputs["ret_decay_b"][0])])[None, :]),
        "nat_w_qkv": f(inputs["nat_w_qkv"][0]),
        "nat_w_o": f(inputs["nat_w_o"][0]),
        "nat_bias": _nat_bias(f(inputs["nat_rpb"][0])),
        "pool_w": f(inputs["pool_w"][0]),
        "pool_sc": _vec_fm(f(inputs["pool_scale"][0])),
        "swa_w_qkv": f(inputs["swa_w_qkv"][0]),
        "swa_w_o": f(inputs["swa_w_o"][0]),
        "swa_sink": f(inputs["swa_sink"][0:1]),
    }
    wq = shared["swa_w_qkv"]
    dd = np.arange(64)
    partner = np.where((dd % 32) < 16, dd + 16, dd - 16)
    colq = (np.arange(16)[:, None] * 64 + partner[None, :]).reshape(-1)
    colk = 1024 + (np.arange(4)[:, None] * 64 + partner[None, :]).reshape(-1)
    shared["swa_w_perm"] = np.ascontiguousarray(wq[:, np.concatenate([colq, colk])])
    shared.update(_consts())
    maps = []
    for b in cores:
        m = dict(shared)
        m["xT"] = _fm(np.concatenate([x[b], ctx[b]], axis=0))
        m["cT"] = np.ascontiguousarray(np.stack([c[b], c_ctx], axis=0).reshape(2, DC, 128).transpose(2, 1, 0))
        maps.append(m)
    return maps


_NC_CACHE = {}


def kernel(**inputs):
    if "nc" not in _NC_CACHE:
        _NC_CACHE["nc"] = build()
    nc = _NC_CACHE["nc"]
    in_maps = make_in_maps(inputs)
    res = run_bass_kernel_spmd(nc, in_maps, core_ids=list(range(N_CORES)))
    outs = []
    for b in range(N_CORES):
        o = res.results[b]["outT"]
        outs.append(o.transpose(1, 0, 2).reshape(D, T_LAT).T)
    return np.ascontiguousarray(np.stack(outs, axis=0).astype(np.float32))
```
